# Optimizing a Trainium2 kernel written in Bass

```python
import math
import jax, jax.numpy as jnp
from jax import lax
import numpy as np

D_MODEL = 1024
BATCH = 32
SEQ = 2048
DEPTH = 4

D_RNN = 1024
RNN_BLOCKS = 16
RNN_BLOCK = D_RNN // RNN_BLOCKS
CONV_WIDTH = 4
LRU_C = 8.0
N_HEADS = 16
QK_NOPE = 64
QK_ROPE = 32
V_HEAD = 64
Q_LORA = 384
KV_LORA = 256
ROPE_THETA = 10000.0
Q_BLOCK = 128
D_MIX = N_HEADS * V_HEAD
IN_WIDTH = 2 * D_RNN + Q_LORA + KV_LORA + QK_ROPE + 2 * D_MIX
D_FF = 3 * D_MODEL
FFN_CONV_WIDTH = 3
ALPHA = (2 * DEPTH) ** 0.25
BETA = (8 * DEPTH) ** -0.25
EPS = 1e-6
NEG_INF = -1e30

kernel_name = "hybrid_rglru_mla_convffn_deepnorm"


def _split_points():
    sizes = (D_RNN, D_RNN, Q_LORA, KV_LORA, QK_ROPE, D_MIX, D_MIX)
    pts, acc = [], 0
    for s in sizes[:-1]:
        acc += s
        pts.append(acc)
    return pts


def layer_norm(x, g, b):
    xf = x.astype(jnp.float32)
    mu = xf.mean(-1, keepdims=True)
    var = jnp.square(xf - mu).mean(-1, keepdims=True)
    y = (xf - mu) * lax.rsqrt(var + EPS) * g.astype(jnp.float32) + b.astype(jnp.float32)
    return y.astype(x.dtype)


def rms_norm(x, g):
    xf = x.astype(jnp.float32)
    y = xf * lax.rsqrt(jnp.mean(xf * xf, -1, keepdims=True) + EPS) * g.astype(jnp.float32)
    return y.astype(x.dtype)


def causal_dwconv(x, w, b):
    width, c = w.shape
    y = lax.conv_general_dilated(
        x, w[:, None, :].astype(x.dtype), window_strides=(1,), padding=[(width - 1, 0)],
        dimension_numbers=("NWC", "WIO", "NWC"), feature_group_count=c)
    return y + b.astype(x.dtype)


def rope_tables(positions, dtype):
    inv_freq = ROPE_THETA ** (-jnp.arange(0, QK_ROPE, 2, dtype=jnp.float32) / QK_ROPE)
    ang = positions.astype(jnp.float32)[..., None] * inv_freq
    return jnp.cos(ang)[:, :, None, :].astype(dtype), jnp.sin(ang)[:, :, None, :].astype(dtype)


def apply_rope(x, cos, sin):
    x1, x2 = jnp.split(x, 2, axis=-1)
    return jnp.concatenate([x1 * cos - x2 * sin, x2 * cos + x1 * sin], axis=-1)


def rg_lru(x, gx_w, gx_b, ga_w, ga_b, lru_lambda):
    B, S, _ = x.shape
    xb = x.reshape(B, S, RNN_BLOCKS, RNN_BLOCK)
    gate_x = jax.nn.sigmoid(jnp.einsum("bshi,hij->bshj", xb, gx_w).reshape(B, S, D_RNN) + gx_b)
    gate_a = jax.nn.sigmoid(jnp.einsum("bshi,hij->bshj", xb, ga_w).reshape(B, S, D_RNN) + ga_b)
    log_a = -LRU_C * gate_a.astype(jnp.float32) * jax.nn.softplus(-lru_lambda.astype(jnp.float32))
    a = jnp.exp(log_a)
    mult = jnp.sqrt(-jnp.expm1(2.0 * log_a))
    u = mult * (gate_x * x).astype(jnp.float32)

    def step(h, au):
        a_t, u_t = au
        h = a_t * h + u_t
        return h, h

    _, hs = lax.scan(step, jnp.zeros((B, D_RNN), jnp.float32),
                     (jnp.swapaxes(a, 0, 1), jnp.swapaxes(u, 0, 1)))
    return jnp.swapaxes(hs, 0, 1).astype(x.dtype)


def mla_attention(q_lat, kv_lat, k_rope_raw, cos, sin, q_norm_g, w_uq, kv_norm_g, w_ukv):
    B, S, _ = q_lat.shape
    q = (rms_norm(q_lat, q_norm_g) @ w_uq).reshape(B, S, N_HEADS, QK_NOPE + QK_ROPE)
    q_nope, q_pe = q[..., :QK_NOPE], apply_rope(q[..., QK_NOPE:], cos, sin)
    kv = (rms_norm(kv_lat, kv_norm_g) @ w_ukv).reshape(B, S, N_HEADS, QK_NOPE + V_HEAD)
    k_nope, v = kv[..., :QK_NOPE], kv[..., QK_NOPE:]
    k_pe = apply_rope(k_rope_raw[:, :, None, :], cos, sin)[:, :, 0, :]
    scale = (QK_NOPE + QK_ROPE) ** -0.5
    outs = []
    for start in range(0, S, Q_BLOCK):
        end = start + Q_BLOCK
        s = (jnp.einsum("bqhd,bkhd->bhqk", q_nope[:, start:end], k_nope[:, :end])
             + jnp.einsum("bqhd,bkd->bhqk", q_pe[:, start:end], k_pe[:, :end]))
        s = s.astype(jnp.float32) * scale
        causal = (start + jnp.arange(Q_BLOCK))[:, None] >= jnp.arange(end)[None, :]
        p = jax.nn.softmax(jnp.where(causal, s, NEG_INF), axis=-1).astype(v.dtype)
        outs.append(jnp.einsum("bhqk,bkhd->bqhd", p, v[:, :end]))
    return jnp.concatenate(outs, axis=1).reshape(B, S, D_MIX)


def mixer_sublayer(x, cos, sin, w_in, conv_w, conv_b, gx_w, gx_b, ga_w, ga_b, lru_lambda,
                   q_norm_g, w_uq, kv_norm_g, w_ukv, w_out):
    proj = x @ w_in
    x_rnn, g_rnn, q_lat, kv_lat, k_rope, gate_a, gate_b = jnp.split(proj, _split_points(), axis=-1)
    y_rnn = jax.nn.gelu(g_rnn) * rg_lru(causal_dwconv(x_rnn, conv_w, conv_b),
                                        gx_w, gx_b, ga_w, ga_b, lru_lambda)
    y_mla = mla_attention(q_lat, kv_lat, k_rope, cos, sin, q_norm_g, w_uq, kv_norm_g, w_ukv)
    merged = jax.nn.sigmoid(gate_a) * y_rnn + jax.nn.sigmoid(gate_b) * y_mla
    return merged @ w_out


def conv_ffn(x, w_up, ffn_conv_w, ffn_conv_b, w_down):
    h = causal_dwconv(x @ w_up, ffn_conv_w, ffn_conv_b)
    h_gate, h_val = jnp.split(h, 2, axis=-1)
    return (jax.nn.gelu(h_gate) * h_val) @ w_down


def setup_inputs(seed: int = 0) -> dict:
    key = jax.random.key(seed)
    ks = jax.random.split(key, 24)
    f32 = jnp.float32

    def nrm(k, shape, scale):
        return jax.random.normal(k, shape, f32) * scale

    x = jax.random.normal(ks[0], (BATCH, SEQ, D_MODEL), f32)
    offsets = jax.random.randint(ks[1], (BATCH, 1), 0, 4096, dtype=jnp.int32)
    positions = (jnp.arange(SEQ, dtype=jnp.int32)[None, :] + offsets).astype(jnp.int32)
    u = jax.random.uniform(ks[2], (DEPTH, D_RNN), f32, 0.9, 0.999)
    a0 = u ** (1.0 / LRU_C)
    lru_lambda = jnp.log(a0) - jnp.log1p(-a0)
    return {
        "x": x,
        "positions": positions,
        "w_in": nrm(ks[3], (DEPTH, D_MODEL, IN_WIDTH), D_MODEL ** -0.5),
        "conv_w": nrm(ks[4], (DEPTH, CONV_WIDTH, D_RNN), CONV_WIDTH ** -0.5),
        "conv_b": nrm(ks[5], (DEPTH, D_RNN), 0.02),
        "gx_w": nrm(ks[6], (DEPTH, RNN_BLOCKS, RNN_BLOCK, RNN_BLOCK), RNN_BLOCK ** -0.5),
        "gx_b": nrm(ks[7], (DEPTH, D_RNN), 0.02),
        "ga_w": nrm(ks[8], (DEPTH, RNN_BLOCKS, RNN_BLOCK, RNN_BLOCK), RNN_BLOCK ** -0.5),
        "ga_b": nrm(ks[9], (DEPTH, D_RNN), 0.02),
        "lru_lambda": lru_lambda,
        "q_norm_g": 1.0 + nrm(ks[10], (DEPTH, Q_LORA), 0.02),
        "w_uq": nrm(ks[11], (DEPTH, Q_LORA, N_HEADS * (QK_NOPE + QK_ROPE)), Q_LORA ** -0.5),
        "kv_norm_g": 1.0 + nrm(ks[12], (DEPTH, KV_LORA), 0.02),
        "w_ukv": nrm(ks[13], (DEPTH, KV_LORA, N_HEADS * (QK_NOPE + V_HEAD)), KV_LORA ** -0.5),
        "w_out": nrm(ks[14], (DEPTH, D_MIX, D_MODEL), BETA * D_MIX ** -0.5),
        "ln1_g": 1.0 + nrm(ks[15], (DEPTH, D_MODEL), 0.02),
        "ln1_b": nrm(ks[16], (DEPTH, D_MODEL), 0.02),
        "w_up": nrm(ks[17], (DEPTH, D_MODEL, 2 * D_FF), D_MODEL ** -0.5),
        "ffn_conv_w": nrm(ks[18], (DEPTH, FFN_CONV_WIDTH, 2 * D_FF), FFN_CONV_WIDTH ** -0.5),
        "ffn_conv_b": nrm(ks[19], (DEPTH, 2 * D_FF), 0.02),
        "w_down": nrm(ks[20], (DEPTH, D_FF, D_MODEL), BETA * D_FF ** -0.5),
        "ln2_g": 1.0 + nrm(ks[21], (DEPTH, D_MODEL), 0.02),
        "ln2_b": nrm(ks[22], (DEPTH, D_MODEL), 0.02),
    }


def reference(x, positions, w_in, conv_w, conv_b, gx_w, gx_b, ga_w, ga_b, lru_lambda,
              q_norm_g, w_uq, kv_norm_g, w_ukv, w_out, ln1_g, ln1_b,
              w_up, ffn_conv_w, ffn_conv_b, w_down, ln2_g, ln2_b):
    cos, sin = rope_tables(positions, x.dtype)
    for l in range(DEPTH):
        mix = mixer_sublayer(x, cos, sin, w_in[l], conv_w[l], conv_b[l], gx_w[l], gx_b[l],
                             ga_w[l], ga_b[l], lru_lambda[l], q_norm_g[l], w_uq[l],
                             kv_norm_g[l], w_ukv[l], w_out[l])
        x = layer_norm(ALPHA * x + mix, ln1_g[l], ln1_b[l])
        ffn = conv_ffn(x, w_up[l], ffn_conv_w[l], ffn_conv_b[l], w_down[l])
        x = layer_norm(ALPHA * x + ffn, ln2_g[l], ln2_b[l])
    return x
```

```python
from contextlib import ExitStack
import math
import numpy as np
import concourse.bass as bass
import concourse.mybir as mybir
from concourse.bass_utils import run_bass_kernel_spmd

F32 = mybir.dt.float32
BF16 = mybir.dt.bfloat16
I32 = mybir.dt.int32
AF = mybir.ActivationFunctionType
ALU = mybir.AluOpType

ENGS = ["pe", "act", "dve", "pool", "sp"]

D = 1024
T = 2048
TS = 512
NT = T // TS
L_ALL = 4
INW = 4768
DFF = 3072
ALPHA = float((2 * L_ALL) ** 0.25)
EPS = 1e-6
SCALE = float(96 ** -0.5)
PL = 296
O_CW, O_CB, O_GXB, O_GAB, O_LAM = 0, 32, 40, 48, 56
O_L1G, O_L1B, O_L2G, O_L2B = 64, 72, 80, 88
O_FCW, O_FCB, O_QG, O_KVG = 96, 240, 288, 291
TWO_PI = float(2 * math.pi)
PI = float(math.pi)


class Buf:
    __slots__ = ("name", "lw", "rd")

    def __init__(self, name=""):
        self.name = name
        self.lw = None
        self.rd = {}


class Sched:
    K = 8
    R = 32

    def __init__(self, nc, stack):
        self.nc = nc
        self.sems = {e: [stack.enter_context(nc.semaphore(f"s_{e}_{i}")) for i in range(self.K)]
                     for e in ENGS}
        self.dsems = [stack.enter_context(nc.semaphore(f"d_{i}")) for i in range(self.R)]
        self.cnt = {e: 0 for e in ENGS}
        self.seen = {e: {f: 0 for f in ENGS} for e in ENGS}
        self.seen_d = {e: [0] * self.R for e in ENGS}
        self.clock = {e: [] for e in ENGS}
        self.nd = 0
        self.thunks = {e: [] for e in ENGS}
        self.bufs = {}

    def b(self, key):
        x = self.bufs.get(key)
        if x is None:
            x = self.bufs[key] = Buf(str(key))
        return x

    def _need(self, e, tok, out):
        if tok is None:
            return
        if tok[0] == "e":
            _, f, i = tok
            if f == e and e == "pe":
                return
            if self.seen[e][f] >= i + 1:
                return
            out.append(tok)
        else:
            n = tok[1]
            if self.seen_d[e][n % self.R] >= n // self.R + 1:
                return
            out.append(tok)

    def _emit_waits(self, e, toks):
        best = {}
        dm = {}
        for t in toks:
            if t[0] == "e":
                best[t[1]] = max(best.get(t[1], -1), t[2])
            else:
                s = t[1] % self.R
                dm[s] = max(dm.get(s, -1), t[1])
        for f, i in best.items():
            if self.seen[e][f] >= i + 1:
                continue
            self.thunks[e].append(("w", self.sems[f][i % self.K], i // self.K + 1))
            self.seen[e][f] = i + 1
            clk = self.clock[f][i]
            for g in ENGS:
                if g != e and clk[g] > self.seen[e][g]:
                    self.seen[e][g] = clk[g]
        for s, n in dm.items():
            v = n // self.R + 1
            if self.seen_d[e][s] >= v:
                continue
            self.thunks[e].append(("w", self.dsems[s], 16 * v))
            self.seen_d[e][s] = v

    def _deps(self, e, reads, writes):
        toks = []
        for r in reads:
            self._need(e, r.lw, toks)
        for w in writes:
            self._need(e, w.lw, toks)
            for f, i in w.rd.items():
                if f == "d":
                    for n in i:
                        self._need(e, ("d", n), toks)
                else:
                    self._need(e, ("e", f, i), toks)
        return toks

    def _bl(self, xs):
        return [x if isinstance(x, Buf) else self.b(x) for x in xs]

    def op(self, e, name, reads=(), writes=(), **kw):
        fn = (name, kw)
        reads = self._bl(reads)
        writes = self._bl(writes)
        self._emit_waits(e, self._deps(e, reads, writes))
        i = self.cnt[e]
        self.cnt[e] = i + 1
        self.thunks[e].append(("i", fn, self.sems[e][i % self.K], 1))
        self.clock[e].append(dict(self.seen[e]))
        tok = ("e", e, i)
        for r in reads:
            if r.rd.get(e, -1) < i:
                r.rd[e] = i
        for w in writes:
            w.lw = tok
            w.rd = {}
        return tok

    def dma(self, out, in_, reads=(), writes=(), q="sp"):
        fn = ("dma_start", dict(out=out, in_=in_))
        reads = self._bl(reads)
        writes = self._bl(writes)
        n = self.nd
        self.nd += 1
        s = n % self.R
        toks = self._deps(q, reads, writes)
        if n >= self.R:
            self._need(q, ("d", n - self.R), toks)
        self._emit_waits(q, toks)
        self.thunks[q].append(("i", fn, self.dsems[s], 16))
        tok = ("d", n)
        for r in reads:
            r.rd.setdefault("d", []).append(n)
        for w in writes:
            w.lw = tok
            w.rd = {}
        return tok

    def barrier_all(self):
        for e in ENGS:
            toks = []
            for f in ENGS:
                if self.cnt[f] > 0:
                    self._need(e, ("e", f, self.cnt[f] - 1), toks)
            for n in range(max(0, self.nd - self.R), self.nd):
                self._need(e, ("d", n), toks)
            self._emit_waits(e, toks)

    def run(self):
        nc = self.nc
        with nc.Block() as block:
            def mk(e):
                def body(eng):
                    for t in self.thunks[e]:
                        if t[0] == "w":
                            eng.wait_ge(t[1], t[2])
                        else:
                            getattr(eng, t[1][0])(**t[1][1]).then_inc(t[2], t[3])
                return body
            block.tensor(mk("pe"))
            block.scalar(mk("act"))
            block.vector(mk("dve"))
            block.gpsimd(mk("pool"))
            block.sync(mk("sp"))


def build(NS, layers, nlw=L_ALL):
    nc = bass.Bass("TRN2", target_bir_lowering=False)
    dt_in = lambda n, s, d=F32: nc.dram_tensor(n, s, d, kind="ExternalInput").ap()
    x_d = dt_in("x", [NS, T, D])
    pos_d = dt_in("pos", [NS, 32, T], I32)
    w_in_d = dt_in("w_in", [nlw, D, INW])
    w_uq_d = dt_in("w_uq", [nlw, 384, 1536])
    w_ukv_d = dt_in("w_ukv", [nlw, 256, 2048])
    w_out_d = dt_in("w_out", [nlw, D, D])
    w_up_d = dt_in("w_up", [nlw, D, 2 * DFF])
    w_dn_d = dt_in("w_down", [nlw, DFF, D])
    gxw_d = dt_in("gxw", [nlw, 128, 8 * 128])
    gaw_d = dt_in("gaw", [nlw, 128, 8 * 128])
    pv_d = dt_in("pv", [128, nlw * PL])
    cst_d = dt_in("cst", [128, 258])
    y_d = nc.dram_tensor("y", [NS, T, D], F32, kind="ExternalOutput").ap()

    scr = lambda n, s: nc.dram_tensor(n, s, BF16, kind="Internal").ap()
    winL_s = scr("winL", [nlw, 128, 8, 672])
    winB_s = scr("winB", [nlw, 128, 8, 8, 512])
    wkrot_s = scr("wkrot", [nlw, 128, 8 * 96])
    wuqP_s = scr("wuqP", [nlw, 128, 8, 3, 192])
    wqrP_s = scr("wqrP", [nlw, 128, 8 * 3 * 192])
    wukvP_s = scr("wukvP", [nlw, 128, 8, 2, 256])
    wout_s = scr("wout", [nlw, 128, 8, 1024])
    wupP_s = scr("wupP", [nlw, 128, 24, 8, 256])
    wdnP_s = scr("wdnP", [nlw, 128, 8, 24, 128])
    gxw_s = scr("gxws", [nlw, 128, 1024])
    gaw_s = scr("gaws", [nlw, 128, 1024])

    with ExitStack() as st:
        S = Sched(nc, st)
        sb = lambda n, s, d: st.enter_context(nc.sbuf_tensor("sb_" + n, s, d))
        xhi = sb("xhi", [128, 8, T], BF16)
        xlo = sb("xlo", [128, 8, T], BF16)
        big = sb("big", [128, 26624], BF16)
        tmp = sb("tmp", [128, 9, TS], F32)
        xr = sb("xr", [128, 3 + TS], F32)
        cosT = sb("cosT", [128, T], F32)
        sinT = sb("sinT", [128, T], F32)
        KT = sb("KT", [128, 2, T], BF16)
        VA = sb("VA", [128, 16, 2, 128], BF16)
        QT = sb("QT", [128, 2, TS], BF16)
        PT = sb("PT", [128, 3, TS], BF16)
        wbuf = sb("wbuf", [128, 8192], BF16)
        wsm = sb("wsm", [128, 2, 1920], BF16)
        pv = sb("pv", [128, nlw * PL], F32)
        pvd = sb("pvd", [128, nlw * 16], F32)
        cst = sb("cst", [128, 258], F32)
        onesb = sb("onesb", [128, 128], BF16)
        trib = sb("trib", [128, 128], BF16)
        hal = sb("hal", [128, 48, 2], F32)
        hlast = sb("hlast", [128, 2], F32)
        ps = [st.enter_context(nc.psum_tensor(f"ps{i}", [128, TS], F32)) for i in range(8)]
        PSB = [S.b(("ps", i)) for i in range(8)]
        ident = cst[:, 0:128]
        invf = cst[:, 256:257]
        epsc = cst[:, 257:258]
        wkr = wsm[:, 1, 0:768].rearrange("p (k d) -> p k d", k=8)

        merged = big[:, 0:16384].rearrange("p (c t) -> p c t", c=8)
        qn = big[:, 16384:22528].rearrange("p (c t) -> p c t", c=3)
        kvn = big[:, 22528:26624].rearrange("p (c t) -> p c t", c=2)
        abuf = big[:, 0:24576].rearrange("p (j t) -> p j t", j=24)
        wdn_sb = [KT[:].rearrange("p a t -> p (a t)")[:, 0:3072].rearrange("p (j q) -> p j q", j=24),
                  VA[:].rearrange("p a b c -> p (a b c)")[:, 0:3072].rearrange("p (j q) -> p j q", j=24)]

        def ACT(out, in_, func, rd, wr, **kw):
            S.op("act", "activation", rd, wr, out=out, in_=in_, func=func, **kw)

        def TT(e, out, in0, in1, op, rd, wr):
            S.op(e, "tensor_tensor", rd, wr, out=out, in0=in0, in1=in1, op=op)

        def TSC(e, out, in0, s1, s2, op0, op1, rd, wr):
            if s2 is None:
                S.op(e, "tensor_scalar", rd, wr, out=out, in0=in0, scalar1=s1, scalar2=None, op0=op0)
            else:
                S.op(e, "tensor_scalar", rd, wr, out=out, in0=in0, scalar1=s1, scalar2=s2, op0=op0, op1=op1)

        def STT(e, out, in0, scalar, in1, op0, op1, rd, wr):
            S.op(e, "scalar_tensor_tensor", rd, wr, out=out, in0=in0, scalar=scalar, in1=in1, op0=op0, op1=op1)

        def CP(e, out, in_, rd, wr):
            S.op(e, "tensor_copy", rd, wr, out=out, in_=in_)

        def RCP(out, in_, rd, wr):
            S.op("dve", "reciprocal", rd, wr, out=out, in_=in_)

        def MM(out, lhsT, rhs, start, stop, rd, wr):
            S.op("pe", "matmul", rd, wr, out=out, lhsT=lhsT, rhs=rhs, start=start, stop=stop)

        def TR(out, in_, rd, wr):
            S.op("pe", "transpose", rd, wr, out=out, in_=in_, identity=ident)

        def MS(e, ap, v, wr):
            S.op(e, "memset", (), wr, ap=ap, constant=v)

        DMA = S.dma
        proj_rr = [0]

        def pbank():
            i = proj_rr[0] % 3
            proj_rr[0] += 1
            return i

        def mm_group(pb, lhs_list, rhs_list, reads, out_ap=None):
            n = len(lhs_list)
            o = ps[pb][:] if out_ap is None else out_ap
            for i in range(n):
                MM(o, lhs_list[i], rhs_list[i], i == 0, i == n - 1, reads, [PSB[pb]])

        def tsl(tt):
            return slice(tt * TS, (tt + 1) * TS)

        XB = lambda k, tt: ("xhi", k, tt)
        XL = lambda k, tt: ("xlo", k, tt)

        DMA(cst[:], cst_d, (), ["cst"])
        DMA(pv[:], pv_d, (), ["pv"])
        MS("pool", onesb[:], 1.0, ["onesb"])
        CP("dve", trib[:], cst[:, 128:256], ["cst"], ["trib"])
        MS("pool", hal[:], 0.0, ["hal"])
        for l in layers:
            lam = pv[:, l * PL + O_LAM: l * PL + O_LAM + 8]
            t0 = tmp[:, 0, 0:8]
            ACT(t0, lam, AF.Exp, ["pv"], ["t0"], scale=-1.0)
            ACT(t0, t0, AF.Ln, ["t0"], ["t0"], bias=1.0)
            TSC("dve", pvd[:, l * 16: l * 16 + 8], t0, -8.0, None, ALU.mult, None, ["t0"], ["pvd"])
            TSC("dve", pvd[:, l * 16 + 8: l * 16 + 16], t0, -16.0, None, ALU.mult, None, ["t0"], ["pvd"])
        S.barrier_all()

        stf = big[:, 0:24576].bitcast(F32).rearrange("p (a b) -> p a b", a=2)
        stb = xhi[:].rearrange("p k t -> p (k t)")[:, 0:12288].rearrange("p (a b) -> p a b", a=2)
        tb16 = tmp[:].rearrange("p a b -> p (a b)").bitcast(BF16)
        wqr_sb = tb16[:, 0:4608].rearrange("p (c k h d) -> p c k h d", c=8, k=3, h=2)
        rr = [0]
        ceng = ["dve", "pool"]

        def stage_load(src_ap, ncols):
            i = rr[0] % 2
            rr[0] += 1
            DMA(stf[:, i, 0:ncols], src_ap, (), [("stf", i)])
            return i

        def cast(i, out_ap, in_ap, scale=None, eng=None):
            e_ = eng or ceng[rr[0] % 2]
            if scale is None:
                CP(e_, out_ap, in_ap, [("stf", i)], [("stb", i)])
            else:
                TSC(e_, out_ap, in_ap, scale, None, ALU.mult, None, [("stf", i), "pv"], [("stb", i)])

        MS("pool", wkr, 0.0, ["wkr"])
        MS("pool", tb16[:, 0:4608], 0.0, ["wqr_sb"])
        for l in layers:
            base = l * PL
            for k in range(8):
                i = stage_load(w_in_d[l, k * 128:(k + 1) * 128, :], INW)
                sf = stf[:, i, :]
                so = stb[:, i, :]
                soB = so[:, 0:4096].rearrange("p (c j q) -> p c j q", c=8, j=4)
                for j, b0 in enumerate([0, 1024, 2720, 3744]):
                    cast(i, soB[:, :, j, :], sf[:, b0:b0 + 1024].rearrange("p (c q) -> p c q", c=8), eng=ceng[j % 2])
                cast(i, so[:, 4096:4768], sf[:, 2048:2720], eng="dve")
                TSC("pool", wkr[:, k, 64:80], sf[:, 2704:2720], -1.0, None, ALU.mult, None, [("stf", i)], ["wkr"])
                CP("pool", wkr[:, k, 80:96], sf[:, 2688:2704], [("stf", i)], ["wkr"])
                DMA(winB_s[l][:, :, k, :], so[:, 0:4096].rearrange("p (c q) -> p c q", c=8), [("stb", i)], [("winB", l)])
                DMA(winL_s[l][:, k, :], so[:, 4096:4768], [("stb", i)], [("winL", l)])
            DMA(wkrot_s[l], wsm[:, 1, 0:768], ["wkr"], [("wkrot", l)])
            for k in range(3):
                i = stage_load(w_uq_d[l, k * 128:(k + 1) * 128, :], 1536)
                sf = stf[:, i, :]
                so = stb[:, i, :]
                g = pv[:, base + O_QG + k: base + O_QG + k + 1]
                cast(i, so[:, 0:1536], sf[:, 0:1536], scale=g)
                sfv = sf[:, 0:1536].rearrange("p (c h d) -> p c h d", c=8, h=2)
                TSC("pool", wqr_sb[:, :, k, :, 64:80], sfv[:, :, :, 80:96], g, -1.0, ALU.mult, ALU.mult,
                    [("stf", i), "pv"], ["wqr_sb"])
                TSC("pool", wqr_sb[:, :, k, :, 80:96], sfv[:, :, :, 64:80], g, None, ALU.mult, None,
                    [("stf", i), "pv"], ["wqr_sb"])
                DMA(wuqP_s[l][:, :, k, :], so[:, 0:1536].rearrange("p (c q) -> p c q", c=8), [("stb", i)], [("wuqP", l)])
            DMA(wqrP_s[l], tb16[:, 0:4608], ["wqr_sb"], [("wqrP", l)])
            for k in range(2):
                i = stage_load(w_ukv_d[l, k * 128:(k + 1) * 128, :], 2048)
                g = pv[:, base + O_KVG + k: base + O_KVG + k + 1]
                cast(i, stb[:, i, 0:2048], stf[:, i, 0:2048], scale=g)
                DMA(wukvP_s[l][:, :, k, :], stb[:, i, 0:2048].rearrange("p (c q) -> p c q", c=8), [("stb", i)], [("wukvP", l)])
            for k in range(8):
                i = stage_load(w_out_d[l, k * 128:(k + 1) * 128, :], 1024)
                cast(i, stb[:, i, 0:1024], stf[:, i, 0:1024])
                DMA(wout_s[l][:, k, :], stb[:, i, 0:1024], [("stb", i)], [("wout", l)])
            for k in range(8):
                i = stage_load(w_up_d[l, k * 128:(k + 1) * 128, :], 6144)
                sfv = stf[:, i, :].rearrange("p (g j q) -> p g j q", g=2, j=24)
                sov = stb[:, i, :].rearrange("p (j g q) -> p j g q", j=24, g=2)
                cast(i, sov[:, :, 0, :], sfv[:, 0, :, :], eng="dve")
                cast(i, sov[:, :, 1, :], sfv[:, 1, :, :], eng="pool")
                DMA(wupP_s[l][:, :, k, :], stb[:, i, :].rearrange("p (j q) -> p j q", j=24), [("stb", i)], [("wupP", l)])
            for k in range(24):
                i = stage_load(w_dn_d[l, k * 128:(k + 1) * 128, :], 1024)
                cast(i, stb[:, i, 0:1024], stf[:, i, 0:1024])
                DMA(wdnP_s[l][:, :, k, :], stb[:, i, 0:1024].rearrange("p (m q) -> p m q", m=8), [("stb", i)], [("wdnP", l)])
            for src, dst, nm in [(gxw_d, gxw_s, "gxws"), (gaw_d, gaw_s, "gaws")]:
                i = stage_load(src[l], 1024)
                cast(i, stb[:, i, 0:1024], stf[:, i, 0:1024])
                DMA(dst[l], stb[:, i, 0:1024], [("stb", i)], [(nm, l)])
        S.barrier_all()

        P = slice(64, 96)
        for s in range(NS):
            for tb in range(T // 128):
                tt = tb // 4
                xt = tmp[:, 2 * (tb % 2):2 * (tb % 2) + 2, :].rearrange("p a b -> p (a b)")
                xtb = ("xt", tb % 2)
                DMA(xt, x_d[s, tb * 128:(tb + 1) * 128, :], (), [xtb])
                for g in range(2):
                    pb = pbank()
                    for kk in range(4):
                        k = g * 4 + kk
                        TR(ps[pb][:, kk * 128:(kk + 1) * 128], xt[:, k * 128:(k + 1) * 128], [xtb, "cst"], [PSB[pb]])
                    hi = xhi[:, g * 4:(g + 1) * 4, tb * 128:(tb + 1) * 128]
                    lo = xlo[:, g * 4:(g + 1) * 4, tb * 128:(tb + 1) * 128]
                    pv3 = ps[pb][:].rearrange("p (a b) -> p a b", a=4)
                    hb = [XB(k, tt) for k in range(g * 4, g * 4 + 4)]
                    lb = [XL(k, tt) for k in range(g * 4, g * 4 + 4)]
                    ACT(hi, pv3, AF.Copy, [PSB[pb]], hb)
                    TT("dve", lo, pv3, hi, ALU.subtract, [PSB[pb]] + hb, lb)
            posi = tmp[:, 0:4, :].rearrange("p a b -> p (a b)").bitcast(I32)
            ang = tmp[:, 4:8, :].rearrange("p a b -> p (a b)")
            kf = tmp[:, 0:4, :].rearrange("p a b -> p (a b)")
            S.barrier_all()
            DMA(posi[P, :], pos_d[s], (), ["posi"])
            CP("dve", ang[P, :], posi[P, :], ["posi"], ["ang"])
            TSC("dve", ang[P, :], ang[P, :], invf[P, :], None, ALU.mult, None, ["ang", "cst"], ["ang"])
            TSC("dve", posi[P, :], ang[P, :], 1.0 / TWO_PI, None, ALU.mult, None, ["ang"], ["posi"])
            CP("dve", kf[P, :], posi[P, :], ["posi"], ["posi"])
            STT("dve", ang[P, :], kf[P, :], -TWO_PI, ang[P, :], ALU.mult, ALU.add, ["posi", "ang"], ["ang"])

            def wrap():
                TSC("dve", kf[P, :], ang[P, :], PI, TWO_PI, ALU.is_gt, ALU.mult, ["ang"], ["posi"])
                TT("dve", ang[P, :], ang[P, :], kf[P, :], ALU.subtract, ["ang", "posi"], ["ang"])
                TSC("dve", kf[P, :], ang[P, :], -PI, TWO_PI, ALU.is_lt, ALU.mult, ["ang"], ["posi"])
                TT("dve", ang[P, :], ang[P, :], kf[P, :], ALU.add, ["ang", "posi"], ["ang"])
            wrap()
            ACT(sinT[P, :], ang[P, :], AF.Sin, ["ang"], ["sinT"])
            TSC("dve", ang[P, :], ang[P, :], PI / 2, None, ALU.add, None, ["ang"], ["ang"])
            wrap()
            ACT(cosT[P, :], ang[P, :], AF.Sin, ["ang"], ["cosT"])
            S.barrier_all()

            for li, l in enumerate(layers):
                base = l * PL
                pcol = lambda o, base=base: pv[:, base + o: base + o + 1]
                wlat = wbuf[:, 0:8 * 672].rearrange("p (k q) -> p k q", k=8)
                DMA(wlat, winL_s[l], [("winL", l)], ["wbA", "wbB"])
                DMA(wsm[:, 1, 0:768], wkrot_s[l], [("wkrot", l)], ["wkr"])
                for tt in range(NT):
                    xh = [xhi[:, k, tsl(tt)] for k in range(8)]
                    xrd = [XB(k, tt) for k in range(8)]
                    for (nch, c0, dst, inv_n, nm) in [(3, 0, qn, 1.0 / 384, "qn"), (2, 384, kvn, 1.0 / 256, "kvn")]:
                        for oc in range(nch):
                            pb = pbank()
                            mm_group(pb, [wlat[:, k, c0 + oc * 128: c0 + (oc + 1) * 128] for k in range(8)], xh, xrd + ["wbA"])
                            ACT(tmp[:, oc, :], ps[pb][:], AF.Copy, [PSB[pb]], [("t", oc)])
                            ACT(PT[:, oc, :], ps[pb][:], AF.Square, [PSB[pb]], [("pt", oc)])
                        mm_group(7, [onesb[:]] * nch, [PT[:, oc, :] for oc in range(nch)],
                                 ["onesb"] + [("pt", oc) for oc in range(nch)])
                        ACT(tmp[:, 4, :], ps[7][:], AF.Sqrt, [PSB[7], "cst"], [("t", 4)], bias=epsc, scale=inv_n)
                        RCP(tmp[:, 5, :], tmp[:, 4, :], [("t", 4)], [("t", 5)])
                        for oc in range(nch):
                            TT("pool", dst[:, oc, tsl(tt)], tmp[:, oc, :], tmp[:, 5, :], ALU.mult,
                               [("t", oc), ("t", 5)], [(nm, oc, tt)])
                    pa = pbank()
                    mm_group(pa, [wlat[:, k, 576:672] for k in range(8)], xh, xrd + ["wbA"], out_ap=ps[pa][0:96, :])
                    pb2 = pbank()
                    mm_group(pb2, [wkr[:, k, :] for k in range(8)], xh, xrd + ["wkr"], out_ap=ps[pb2][0:96, :])
                    TT("dve", tmp[P, 6, :], ps[pa][P, :], cosT[P, tsl(tt)], ALU.mult, [PSB[pa], "cosT"], [("t", 6)])
                    TT("dve", tmp[P, 7, :], ps[pb2][P, :], sinT[P, tsl(tt)], ALU.mult, [PSB[pb2], "sinT"], [("t", 7)])
                    TT("pool", KT[P, 0, tsl(tt)], tmp[P, 6, :], tmp[P, 7, :], ALU.add, [("t", 6), ("t", 7)], [("kpe", tt)])
                    CP("pool", KT[P, 1, tsl(tt)], KT[P, 0, tsl(tt)], [("kpe", tt)], [("kpe1", tt)])
                S.barrier_all()

                def load_pair(c, l=l):
                    hb_ = c % 2
                    wb = wbuf[:, hb_ * 4096:(hb_ + 1) * 4096]
                    DMA(wb.rearrange("p (k q) -> p k q", k=8), winB_s[l][:, c, :, :], [("winB", l)], ["wbA" if hb_ == 0 else "wbB"])
                    sm = wsm[:, hb_, :]
                    DMA(sm[:, 0:576].rearrange("p (k q) -> p k q", k=3), wuqP_s[l][:, c, :, :], [("wuqP", l)], [("wsm", hb_, 0)])
                    DMA(sm[:, 576:1152], wqrP_s[l][:, c * 576:(c + 1) * 576], [("wqrP", l)], [("wsm", hb_, 1)])
                    DMA(sm[:, 1152:1664].rearrange("p (k q) -> p k q", k=2), wukvP_s[l][:, c, :, :], [("wukvP", l)], [("wsm", hb_, 2)])
                    DMA(sm[:, 1664:1792], gxw_s[l][:, c * 128:(c + 1) * 128], [("gxws", l)], [("wsm", hb_, 3)])
                    DMA(sm[:, 1792:1920], gaw_s[l][:, c * 128:(c + 1) * 128], [("gaws", l)], [("wsm", hb_, 4)])

                MS("pool", VA[:, :, 0, 64:128], 1.0, ["VA"])
                MS("pool", VA[:, :, 1, 0:64], 1.0, ["VA"])
                load_pair(0)
                for c in range(8):
                    hb_ = c % 2
                    WB = "wbA" if hb_ == 0 else "wbB"
                    if c + 1 < 8:
                        load_pair(c + 1)
                    wc = wbuf[:, hb_ * 4096:(hb_ + 1) * 4096].rearrange("p (k j q) -> p k j q", k=8, j=4)
                    sm = wsm[:, hb_, :]
                    wuq = sm[:, 0:576].rearrange("p (k q) -> p k q", k=3)
                    wqr = sm[:, 576:1152].rearrange("p (k h q) -> p k h q", k=3, h=2)
                    wukv = sm[:, 1152:1664].rearrange("p (k q) -> p k q", k=2)
                    gxw = sm[:, 1664:1792]
                    gaw = sm[:, 1792:1920]
                    cwc = lambda tap, c=c: pcol(O_CW + tap * 8 + c)
                    for hh in range(2):
                        for tt in range(NT):
                            pb = pbank()
                            mm_group(pb, [wukv[:, k, hh * 128: hh * 128 + 64] for k in range(2)],
                                     [kvn[:, k, tsl(tt)] for k in range(2)], [("kvn", 0, tt), ("kvn", 1, tt), ("wsm", hb_, 2)],
                                     out_ap=ps[pb][0:64, :])
                            ACT(KT[0:64, hh, tsl(tt)], ps[pb][0:64, :], AF.Copy, [PSB[pb]], [("KT", hh, tt)])
                    for g4 in range(4):
                        pb = pbank()
                        pv4 = ps[pb][:].rearrange("p (t h d) -> p t h d", t=4, h=2)
                        for t4 in range(4):
                            tb = g4 * 4 + t4
                            for k in range(2):
                                MM(pv4[:, t4, :, :], kvn[:, k, tb * 128:(tb + 1) * 128],
                                   wukv[:, k, :].rearrange("p (h d) -> p h d", h=2)[:, :, 64:128], k == 0, k == 1,
                                   [("kvn", k, g4), ("wsm", hb_, 2)], [PSB[pb]])
                        ACT(VA[:, g4 * 4:(g4 + 1) * 4, 0, 0:64], pv4[:, :, 0, :], AF.Copy, [PSB[pb]], [("VA", g4)])
                        CP("dve", VA[:, g4 * 4:(g4 + 1) * 4, 1, 64:128], pv4[:, :, 1, :], [PSB[pb]], [("VA", g4)])
                    for qt in range(NT):
                        xh = [xhi[:, k, tsl(qt)] for k in range(8)]
                        xrd = [XB(k, qt) for k in range(8)]
                        if qt == 0:
                            MS("pool", xr[:, 0:3], 0.0, ["xr"])
                        else:
                            CP("pool", xr[:, 0:3], xr[:, TS:TS + 3], ["xr"], ["xr"])
                        px = pbank()
                        mm_group(px, [wc[:, k, 0, :] for k in range(8)], xh, xrd + [WB])
                        ACT(xr[:, 3:3 + TS], ps[px][:], AF.Copy, [PSB[px]], ["xr"])
                        ACT(tmp[:, 0, :], ps[px][:], AF.Identity, [PSB[px], "pv"], [("t", 0)], bias=pcol(O_CB + c), scale=cwc(3))
                        for tap in (2, 1, 0):
                            STT("dve", tmp[:, 0, :], xr[:, tap:tap + TS], cwc(tap), tmp[:, 0, :], ALU.mult, ALU.add,
                                ["xr", ("t", 0), "pv"], [("t", 0)])
                        CP("pool", PT[:, 2, :], tmp[:, 0, :], [("t", 0)], [("pt", 2)])
                        pgx = pbank()
                        mm_group(pgx, [gxw], [PT[:, 2, :]], [("pt", 2), ("wsm", hb_, 3)])
                        ACT(tmp[:, 1, :], ps[pgx][:], AF.Sigmoid, [PSB[pgx], "pv"], [("t", 1)], bias=pcol(O_GXB + c))
                        pga = pbank()
                        mm_group(pga, [gaw], [PT[:, 2, :]], [("pt", 2), ("wsm", hb_, 4)])
                        ACT(tmp[:, 2, :], ps[pga][:], AF.Sigmoid, [PSB[pga], "pv"], [("t", 2)], bias=pcol(O_GAB + c))
                        ACT(tmp[:, 3, :], tmp[:, 2, :], AF.Exp, [("t", 2), "pvd"], [("t", 3)],
                            scale=pvd[:, l * 16 + c: l * 16 + c + 1])
                        ACT(tmp[:, 2, :], tmp[:, 2, :], AF.Exp, [("t", 2), "pvd"], [("t", 2)],
                            scale=pvd[:, l * 16 + 8 + c: l * 16 + 9 + c])
                        ACT(tmp[:, 2, :], tmp[:, 2, :], AF.Sqrt, [("t", 2)], [("t", 2)], bias=1.0, scale=-1.0)
                        TT("pool", tmp[:, 1, :], tmp[:, 1, :], tmp[:, 0, :], ALU.mult, [("t", 1), ("t", 0)], [("t", 1)])
                        TT("dve", tmp[:, 1, :], tmp[:, 1, :], tmp[:, 2, :], ALU.mult, [("t", 1), ("t", 2)], [("t", 1)])
                        init = 0.0 if qt == 0 else hlast[:, 0:1]
                        S.op("dve", "tensor_tensor_scan", [("t", 3), ("t", 1), "hlast"], [("t", 2)],
                             out=tmp[:, 2, :], data0=tmp[:, 3, :], data1=tmp[:, 1, :], initial=init, op0=ALU.mult, op1=ALU.add)
                        CP("dve", hlast[:, 0:1], tmp[:, 2, TS - 1:TS], [("t", 2)], ["hlast"])
                        pg = pbank()
                        mm_group(pg, [wc[:, k, 1, :] for k in range(8)], xh, xrd + [WB])
                        ACT(tmp[:, 0, :], ps[pg][:], AF.Gelu_apprx_tanh, [PSB[pg]], [("t", 0)])
                        TT("pool", tmp[:, 0, :], tmp[:, 0, :], tmp[:, 2, :], ALU.mult, [("t", 0), ("t", 2)], [("t", 0)])
                        pa_ = pbank()
                        mm_group(pa_, [wc[:, k, 2, :] for k in range(8)], xh, xrd + [WB])
                        ACT(tmp[:, 1, :], ps[pa_][:], AF.Sigmoid, [PSB[pa_]], [("t", 1)])
                        TT("pool", tmp[:, 0, :], tmp[:, 0, :], tmp[:, 1, :], ALU.mult, [("t", 0), ("t", 1)], [("t", 0)])
                        pbg = pbank()
                        mm_group(pbg, [wc[:, k, 3, :] for k in range(8)], xh, xrd + [WB])
                        ACT(tmp[:, 4, :], ps[pbg][:], AF.Sigmoid, [PSB[pbg]], [("t", 4)])
                        for hh in range(2):
                            qr = [("qn", k, qt) for k in range(3)]
                            qrh = [qn[:, k, tsl(qt)] for k in range(3)]
                            pq = pbank()
                            mm_group(pq, [wuq[:, k, hh * 96:(hh + 1) * 96] for k in range(3)], qrh,
                                     qr + [("wsm", hb_, 0)], out_ap=ps[pq][0:96, :])
                            pr = pbank()
                            mm_group(pr, [wqr[:, k, hh, :] for k in range(3)], qrh,
                                     qr + [("wsm", hb_, 1)], out_ap=ps[pr][0:96, :])
                            ACT(QT[0:64, hh, :], ps[pq][0:64, :], AF.Copy, [PSB[pq]], [("QT", hh)])
                            TT("dve", tmp[P, 5, :], ps[pq][P, :], cosT[P, tsl(qt)], ALU.mult, [PSB[pq], "cosT"], [("t", 5)])
                            TT("dve", tmp[P, 6, :], ps[pr][P, :], sinT[P, tsl(qt)], ALU.mult, [PSB[pr], "sinT"], [("t", 6)])
                            TT("pool", QT[P, hh, :], tmp[P, 5, :], tmp[P, 6, :], ALU.add, [("t", 5), ("t", 6)], [("QT", hh)])
                            po = 5 + hh
                            nkb = 4 * qt + 4
                            for kb in range(nkb):
                                j = kb - 4 * qt
                                n0 = max(j, 0) * 128
                                sb_ = 3 + (kb % 2)
                                pti = kb % 2
                                MM(ps[sb_][:, n0:TS], KT[0:96, hh, kb * 128:(kb + 1) * 128], QT[0:96, hh, n0:TS], True, True,
                                   [("KT", hh, kb // 4), ("kpe" if hh == 0 else "kpe1", kb // 4), ("QT", hh)], [PSB[sb_]])
                                ACT(PT[:, pti, n0:TS], ps[sb_][:, n0:TS], AF.Exp, [PSB[sb_]], [("pt", pti)], scale=SCALE)
                                if j >= 0:
                                    TT("pool", PT[:, pti, n0:n0 + 128], PT[:, pti, n0:n0 + 128], trib[:], ALU.mult,
                                       [("pt", pti), "trib"], [("pt", pti)])
                                MM(ps[po][:, n0:TS], VA[:, kb, hh, :], PT[:, pti, n0:TS], kb == 0, kb == nkb - 1,
                                   [("VA", kb // 4), ("pt", pti), "VA"], [PSB[po]])
                            if hh == 0:
                                RCP(tmp[0:64, 7, :], ps[po][64:128, :], [PSB[po]], [("t", 7)])
                                TT("dve", tmp[0:64, 8, :], ps[po][0:64, :], tmp[0:64, 7, :], ALU.mult, [PSB[po], ("t", 7)], [("t", 8)])
                            else:
                                RCP(tmp[64:128, 7, :], ps[po][0:64, :], [PSB[po]], [("t", 7)])
                                TT("dve", tmp[64:128, 8, :], ps[po][64:128, :], tmp[64:128, 7, :], ALU.mult, [PSB[po], ("t", 7)], [("t", 8)])
                        TT("pool", tmp[:, 8, :], tmp[:, 8, :], tmp[:, 4, :], ALU.mult, [("t", 8), ("t", 4)], [("t", 8)])
                        TT("pool", merged[:, c, tsl(qt)], tmp[:, 8, :], tmp[:, 0, :], ALU.add, [("t", 8), ("t", 0)], [("mg", c, qt)])
                S.barrier_all()

                def ln_finish(tiles, og, ob):
                    for (tt, pm, pq_) in tiles:
                        TSC("dve", tmp[:, 4, :], ps[pm][:], 1.0 / D, None, ALU.mult, None, [PSB[pm]], [("t", 4)])
                        TT("pool", tmp[:, 5, :], tmp[:, 4, :], tmp[:, 4, :], ALU.mult, [("t", 4)], [("t", 5)])
                        STT("dve", tmp[:, 5, :], ps[pq_][:], 1.0 / D, tmp[:, 5, :], ALU.mult, ALU.subtract,
                            [PSB[pq_], ("t", 5)], [("t", 5)])
                        ACT(tmp[:, 5, :], tmp[:, 5, :], AF.Sqrt, [("t", 5), "cst"], [("t", 5)], bias=epsc, scale=1.0)
                        RCP(tmp[:, 6, :], tmp[:, 5, :], [("t", 5)], [("t", 6)])
                        for m in range(8):
                            zt = tmp[:, 7 + (m % 2), :]
                            ZB = ("t", 7 + (m % 2))
                            TT("pool", zt, xhi[:, m, tsl(tt)], xlo[:, m, tsl(tt)], ALU.add, [XB(m, tt), XL(m, tt)], [ZB])
                            TT("dve", zt, zt, tmp[:, 4, :], ALU.subtract, [ZB, ("t", 4)], [ZB])
                            TT("pool", zt, zt, tmp[:, 6, :], ALU.mult, [ZB, ("t", 6)], [ZB])
                            ACT(zt, zt, AF.Identity, [ZB, "pv"], [ZB], bias=pcol(ob + m), scale=pcol(og + m))
                            ACT(xhi[:, m, tsl(tt)], zt, AF.Copy, [ZB], [XB(m, tt)])
                            TT("dve", xlo[:, m, tsl(tt)], zt, xhi[:, m, tsl(tt)], ALU.subtract, [ZB, XB(m, tt)], [XL(m, tt)])

                def resid(pb, m, tt, pm, pq_):
                    zt = tmp[:, (m % 2), :]
                    ZB = ("t", m % 2)
                    STT("dve", zt, xhi[:, m, tsl(tt)], ALPHA, ps[pb][:], ALU.mult, ALU.add, [XB(m, tt), PSB[pb]], [ZB])
                    STT("dve", zt, xlo[:, m, tsl(tt)], ALPHA, zt, ALU.mult, ALU.add, [XL(m, tt), ZB], [ZB])
                    ACT(xhi[:, m, tsl(tt)], zt, AF.Copy, [ZB], [XB(m, tt)])
                    TT("pool", xlo[:, m, tsl(tt)], zt, xhi[:, m, tsl(tt)], ALU.subtract, [ZB, XB(m, tt)], [XL(m, tt)])
                    sq = PT[:, m % 2, :]
                    ACT(sq, zt, AF.Square, [ZB], [("pt", m % 2)])
                    MM(ps[pm][:], onesb[:], xhi[:, m, tsl(tt)], m == 0, m == 7, [XB(m, tt), "onesb"], [PSB[pm]])
                    MM(ps[pq_][:], onesb[:], sq, m == 0, m == 7, [("pt", m % 2), "onesb"], [PSB[pq_]])

                wo = wbuf[:].rearrange("p (k q) -> p k q", k=8)
                DMA(wo, wout_s[l], [("wout", l)], ["wbA", "wbB"])
                for tt in range(NT):
                    for m in range(8):
                        pb = pbank()
                        mm_group(pb, [wo[:, cc, m * 128:(m + 1) * 128] for cc in range(8)], [merged[:, cc, tsl(tt)] for cc in range(8)],
                                 [("mg", cc, tt) for cc in range(8)] + ["wbA"])
                        resid(pb, m, tt, 6, 7)
                    ln_finish([(tt, 6, 7)], O_L1G, O_L1B)
                S.barrier_all()

                def load_up(j, l=l):
                    bi = j % 4
                    DMA(wbuf[:, bi * 2048:(bi + 1) * 2048].rearrange("p (k q) -> p k q", k=8), wupP_s[l][:, j, :, :],
                        [("wupP", l)], [("wup", bi)])

                def load_dn(m, l=l):
                    DMA(wdn_sb[m % 2], wdnP_s[l][:, m, :, :], [("wdnP", l)], [("wdn", m % 2)])

                for half in range(2):
                    for j in range(3):
                        load_up(j)
                    for j in range(24):
                        if j + 3 < 24:
                            load_up(j + 3)
                        wu = wbuf[:, (j % 4) * 2048:(j % 4 + 1) * 2048].rearrange("p (k g q) -> p k g q", k=8, g=2)
                        for t2 in range(2):
                            tt = half * 2 + t2
                            xh = [xhi[:, k, tsl(tt)] for k in range(8)]
                            xrd = [XB(k, tt) for k in range(8)]
                            for gv in range(2):
                                jj = gv * 24 + j
                                pb = pbank()
                                mm_group(pb, [wu[:, k, gv, :] for k in range(8)], xh, xrd + [("wup", j % 4)])
                                ht = tmp[:, 2 + gv, :]
                                HB = ("t", 2 + gv)
                                fw = lambda tap, jj=jj: pcol(O_FCW + tap * 48 + jj)
                                ACT(ht, ps[pb][:], AF.Identity, [PSB[pb], "pv"], [HB], bias=pcol(O_FCB + jj), scale=fw(2))
                                STT("dve", ht[:, 1:TS], ps[pb][:, 0:TS - 1], fw(1), ht[:, 1:TS], ALU.mult, ALU.add, [PSB[pb], HB, "pv"], [HB])
                                STT("dve", ht[:, 2:TS], ps[pb][:, 0:TS - 2], fw(0), ht[:, 2:TS], ALU.mult, ALU.add, [PSB[pb], HB, "pv"], [HB])
                                if tt > 0:
                                    STT("dve", ht[:, 0:1], hal[:, jj, 1:2], fw(1), ht[:, 0:1], ALU.mult, ALU.add, [("hal", jj), HB, "pv"], [HB])
                                    STT("dve", ht[:, 0:2], hal[:, jj, 0:2], fw(0), ht[:, 0:2], ALU.mult, ALU.add, [("hal", jj), HB, "pv"], [HB])
                                if tt < NT - 1:
                                    ACT(hal[:, jj, :], ps[pb][:, TS - 2:TS], AF.Copy, [PSB[pb]], [("hal", jj)])
                            ACT(tmp[:, 2, :], tmp[:, 2, :], AF.Gelu_apprx_tanh, [("t", 2)], [("t", 2)])
                            TT("pool", abuf[:, j, t2 * TS:(t2 + 1) * TS], tmp[:, 2, :], tmp[:, 3, :], ALU.mult,
                               [("t", 2), ("t", 3)], [("a", j, t2)])
                    S.barrier_all()
                    load_dn(0)
                    for m in range(8):
                        if m + 1 < 8:
                            load_dn(m + 1)
                        wd = wdn_sb[m % 2]
                        for t2 in range(2):
                            tt = half * 2 + t2
                            pb = pbank()
                            mm_group(pb, [wd[:, j, :] for j in range(24)], [abuf[:, j, t2 * TS:(t2 + 1) * TS] for j in range(24)],
                                     [("a", j, t2) for j in range(24)] + [("wdn", m % 2)])
                            resid(pb, m, tt, 4 + 2 * t2, 5 + 2 * t2)
                    ln_finish([(half * 2, 4, 5), (half * 2 + 1, 6, 7)], O_L2G, O_L2B)
                    S.barrier_all()

            S.barrier_all()
            cnt = 0
            for tt in range(NT):
                for g in range(2):
                    for kk in range(4):
                        m = g * 4 + kk
                        TT("pool", tmp[:, kk, :], xhi[:, m, tsl(tt)], xlo[:, m, tsl(tt)], ALU.add, [XB(m, tt), XL(m, tt)], [("t", kk)])
                    for t4 in range(4):
                        tb = tt * 4 + t4
                        pb = pbank()
                        for kk in range(4):
                            TR(ps[pb][:, kk * 128:(kk + 1) * 128], tmp[:, kk, t4 * 128:(t4 + 1) * 128], [("t", kk), "cst"], [PSB[pb]])
                        oi = 4 + cnt % 4
                        cnt += 1
                        if cnt % 2 == 0:
                            ACT(tmp[:, oi, :], ps[pb][:], AF.Copy, [PSB[pb]], [("t", oi)])
                        else:
                            CP("dve", tmp[:, oi, :], ps[pb][:], [PSB[pb]], [("t", oi)])
                        DMA(y_d[s, tb * 128:(tb + 1) * 128, g * 512:(g + 1) * 512], tmp[:, oi, :], [("t", oi)], [("y", s, tb, g)])
            S.barrier_all()
        S.barrier_all()
        S.run()
    return nc


def _fm(v):
    v = np.asarray(v, np.float32)
    return np.ascontiguousarray(v.reshape(-1, 128).T)


def _host_params(inp, nl):
    pvs = np.zeros((128, nl * PL), np.float32)
    for l in range(nl):
        b = l * PL
        for tap in range(4):
            pvs[:, b + O_CW + tap * 8: b + O_CW + tap * 8 + 8] = _fm(inp["conv_w"][l, tap])
        pvs[:, b + O_CB: b + O_CB + 8] = _fm(inp["conv_b"][l])
        pvs[:, b + O_GXB: b + O_GXB + 8] = _fm(inp["gx_b"][l])
        pvs[:, b + O_GAB: b + O_GAB + 8] = _fm(inp["ga_b"][l])
        pvs[:, b + O_LAM: b + O_LAM + 8] = _fm(inp["lru_lambda"][l])
        pvs[:, b + O_L1G: b + O_L1G + 8] = _fm(inp["ln1_g"][l])
        pvs[:, b + O_L1B: b + O_L1B + 8] = _fm(inp["ln1_b"][l])
        pvs[:, b + O_L2G: b + O_L2G + 8] = _fm(inp["ln2_g"][l])
        pvs[:, b + O_L2B: b + O_L2B + 8] = _fm(inp["ln2_b"][l])
        for tap in range(3):
            pvs[:, b + O_FCW + tap * 48: b + O_FCW + tap * 48 + 48] = _fm(inp["ffn_conv_w"][l, tap])
        pvs[:, b + O_FCB: b + O_FCB + 48] = _fm(inp["ffn_conv_b"][l])
        pvs[:, b + O_QG: b + O_QG + 3] = _fm(inp["q_norm_g"][l])
        pvs[:, b + O_KVG: b + O_KVG + 2] = _fm(inp["kv_norm_g"][l])
    return pvs


def _host_bd(w, nl):
    w = np.asarray(w, np.float32)
    out = np.zeros((nl, 128, 8, 128), np.float32)
    for c in range(8):
        for b in range(2):
            out[:, b * 64:(b + 1) * 64, c, b * 64:(b + 1) * 64] = w[:, 2 * c + b]
    return out.reshape(nl, 128, 1024)


def _consts():
    c = np.zeros((128, 258), np.float32)
    c[:, 0:128] = np.eye(128, dtype=np.float32)
    k = np.arange(128)[:, None]
    q = np.arange(128)[None, :]
    c[:, 128:256] = (q >= k).astype(np.float32)
    inv = (10000.0 ** (-np.arange(0, 32, 2, dtype=np.float32) / np.float32(32))).astype(np.float32)
    for p in range(64, 96):
        c[p, 256] = inv[(p - 64) % 16]
    c[:, 257] = EPS
    return c


_NC_CACHE = {}


def run(inp, NS, layers, ncores, seq_of_core):
    nl = L_ALL
    key = (NS, tuple(layers))
    if key not in _NC_CACHE:
        _NC_CACHE[key] = build(NS, list(layers))
    nc = _NC_CACHE[key]
    shared = {
        "w_in": np.ascontiguousarray(inp["w_in"], np.float32),
        "w_uq": np.ascontiguousarray(inp["w_uq"], np.float32),
        "w_ukv": np.ascontiguousarray(inp["w_ukv"], np.float32),
        "w_out": np.ascontiguousarray(inp["w_out"], np.float32),
        "w_up": np.ascontiguousarray(inp["w_up"], np.float32),
        "w_down": np.ascontiguousarray(inp["w_down"], np.float32),
        "gxw": _host_bd(inp["gx_w"], nl),
        "gaw": _host_bd(inp["ga_w"], nl),
        "pv": _host_params(inp, nl),
        "cst": _consts(),
    }
    x = np.asarray(inp["x"], np.float32)
    pos = np.asarray(inp["positions"], np.int32)
    in_maps = []
    for ci in range(ncores):
        seqs = seq_of_core[ci]
        m = dict(shared)
        m["x"] = np.ascontiguousarray(x[seqs])
        m["pos"] = np.ascontiguousarray(np.broadcast_to(pos[seqs][:, None, :], (len(seqs), 32, T)))
        in_maps.append(m)
    res = run_bass_kernel_spmd(nc, in_maps, core_ids=list(range(ncores)))
    return [r["y"] for r in res.results]


def kernel(**inputs):
    B = inputs["x"].shape[0]
    ncores = 8
    NS = B // ncores
    seq_of_core = [list(range(ci * NS, (ci + 1) * NS)) for ci in range(ncores)]
    outs = run(inputs, NS, list(range(L_ALL)), ncores, seq_of_core)
    return np.concatenate(outs, axis=0).astype(np.float32)
```

```python
from contextlib import ExitStack
import math
import numpy as np
import concourse.bass as bass
import concourse.mybir as mybir
from concourse.bass_utils import run_bass_kernel_spmd

F32 = mybir.dt.float32
BF16 = mybir.dt.bfloat16
I32 = mybir.dt.int32
AF = mybir.ActivationFunctionType
ALU = mybir.AluOpType

ENGS = ["pe", "act", "dve", "pool", "sp"]

D = 1024
T = 2048
TS = 512
NT = T // TS
L_ALL = 4
INW = 4768
DFF = 3072
ALPHA = float((2 * L_ALL) ** 0.25)
EPS = 1e-6
SCALE = float(96 ** -0.5)
PL = 296
O_CW, O_CB, O_GXB, O_GAB, O_LAM = 0, 32, 40, 48, 56
O_L1G, O_L1B, O_L2G, O_L2B = 64, 72, 80, 88
O_FCW, O_FCB, O_QG, O_KVG = 96, 240, 288, 291
PAIR_EXP = True
TWO_PI = float(2 * math.pi)
PI = float(math.pi)


class Buf:
    __slots__ = ("name", "lw", "rd")

    def __init__(self, name=""):
        self.name = name
        self.lw = None
        self.rd = {}


class Sched:
    K = 8
    R = 32

    def __init__(self, nc, stack):
        self.nc = nc
        self.sems = {e: [stack.enter_context(nc.semaphore(f"s_{e}_{i}")) for i in range(self.K)]
                     for e in ENGS}
        self.dsems = [stack.enter_context(nc.semaphore(f"d_{i}")) for i in range(self.R)]
        self.cnt = {e: 0 for e in ENGS}
        self.seen = {e: {f: 0 for f in ENGS} for e in ENGS}
        self.seen_d = {e: [0] * self.R for e in ENGS}
        self.clock = {e: [] for e in ENGS}
        self.nd = 0
        self.thunks = {e: [] for e in ENGS}
        self.bufs = {}

    def b(self, key):
        x = self.bufs.get(key)
        if x is None:
            x = self.bufs[key] = Buf(str(key))
        return x

    def _need(self, e, tok, out):
        if tok is None:
            return
        if tok[0] == "e":
            _, f, i = tok
            if f == e and e == "pe":
                return
            if self.seen[e][f] >= i + 1:
                return
            out.append(tok)
        else:
            n = tok[1]
            if self.seen_d[e][n % self.R] >= n // self.R + 1:
                return
            out.append(tok)

    def _emit_waits(self, e, toks):
        best = {}
        dm = {}
        for t in toks:
            if t[0] == "e":
                best[t[1]] = max(best.get(t[1], -1), t[2])
            else:
                s = t[1] % self.R
                dm[s] = max(dm.get(s, -1), t[1])
        for f, i in best.items():
            if self.seen[e][f] >= i + 1:
                continue
            self.thunks[e].append(("w", self.sems[f][i % self.K], i // self.K + 1))
            self.seen[e][f] = i + 1
            clk = self.clock[f][i]
            for g in ENGS:
                if g != e and clk[g] > self.seen[e][g]:
                    self.seen[e][g] = clk[g]
        for s, n in dm.items():
            v = n // self.R + 1
            if self.seen_d[e][s] >= v:
                continue
            self.thunks[e].append(("w", self.dsems[s], 16 * v))
            self.seen_d[e][s] = v

    def _deps(self, e, reads, writes):
        toks = []
        for r in reads:
            self._need(e, r.lw, toks)
        for w in writes:
            self._need(e, w.lw, toks)
            for f, i in w.rd.items():
                if f == "d":
                    for n in i:
                        self._need(e, ("d", n), toks)
                else:
                    self._need(e, ("e", f, i), toks)
        return toks

    def _bl(self, xs):
        return [x if isinstance(x, Buf) else self.b(x) for x in xs]

    def op(self, e, name, reads=(), writes=(), **kw):
        fn = (name, kw)
        reads = self._bl(reads)
        writes = self._bl(writes)
        self._emit_waits(e, self._deps(e, reads, writes))
        i = self.cnt[e]
        self.cnt[e] = i + 1
        self.thunks[e].append(("i", fn, self.sems[e][i % self.K], 1))
        self.clock[e].append(dict(self.seen[e]))
        tok = ("e", e, i)
        for r in reads:
            if r.rd.get(e, -1) < i:
                r.rd[e] = i
        for w in writes:
            w.lw = tok
            w.rd = {}
        return tok

    def dma(self, out, in_, reads=(), writes=(), q="sp"):
        fn = ("dma_start", dict(out=out, in_=in_))
        reads = self._bl(reads)
        writes = self._bl(writes)
        n = self.nd
        self.nd += 1
        s = n % self.R
        toks = self._deps(q, reads, writes)
        if n >= self.R:
            self._need(q, ("d", n - self.R), toks)
        self._emit_waits(q, toks)
        self.thunks[q].append(("i", fn, self.dsems[s], 16))
        tok = ("d", n)
        for r in reads:
            r.rd.setdefault("d", []).append(n)
        for w in writes:
            w.lw = tok
            w.rd = {}
        return tok

    def barrier_all(self):
        for e in ENGS:
            toks = []
            for f in ENGS:
                if self.cnt[f] > 0:
                    self._need(e, ("e", f, self.cnt[f] - 1), toks)
            for n in range(max(0, self.nd - self.R), self.nd):
                self._need(e, ("d", n), toks)
            self._emit_waits(e, toks)

    def run(self):
        nc = self.nc
        with nc.Block() as block:
            def mk(e):
                def body(eng):
                    for t in self.thunks[e]:
                        if t[0] == "w":
                            eng.wait_ge(t[1], t[2])
                        else:
                            getattr(eng, t[1][0])(**t[1][1]).then_inc(t[2], t[3])
                return body
            block.tensor(mk("pe"))
            block.scalar(mk("act"))
            block.vector(mk("dve"))
            block.gpsimd(mk("pool"))
            block.sync(mk("sp"))


def build(NS, layers, nlw=L_ALL):
    nc = bass.Bass("TRN2", target_bir_lowering=False)
    dt_in = lambda n, s, d=F32: nc.dram_tensor(n, s, d, kind="ExternalInput").ap()
    x_d = dt_in("x", [NS, T, D])
    pos_d = dt_in("pos", [NS, 32, T], I32)
    w_in_d = dt_in("w_in", [nlw, D, INW])
    w_uq_d = dt_in("w_uq", [nlw, 384, 1536])
    w_ukv_d = dt_in("w_ukv", [nlw, 256, 2048])
    w_out_d = dt_in("w_out", [nlw, D, D])
    w_up_d = dt_in("w_up", [nlw, D, 2 * DFF])
    w_dn_d = dt_in("w_down", [nlw, DFF, D])
    gxw_d = dt_in("gxw", [nlw, 128, 8 * 128])
    gaw_d = dt_in("gaw", [nlw, 128, 8 * 128])
    pv_d = dt_in("pv", [128, nlw * PL])
    cst_d = dt_in("cst", [128, 258])
    y_d = nc.dram_tensor("y", [NS, T, D], F32, kind="ExternalOutput").ap()

    scr = lambda n, s: nc.dram_tensor(n, s, BF16, kind="Internal").ap()
    winL_s = scr("winL", [nlw, 128, 8, 672])
    winB_s = scr("winB", [nlw, 128, 8, 8, 512])
    wkrot_s = scr("wkrot", [nlw, 128, 8 * 96])
    wuqP_s = scr("wuqP", [nlw, 128, 8, 3, 192])
    wqrP_s = scr("wqrP", [nlw, 128, 8 * 3 * 192])
    wukvP_s = scr("wukvP", [nlw, 128, 8, 2, 256])
    wout_s = scr("wout", [nlw, 128, 8, 1024])
    wupP_s = scr("wupP", [nlw, 128, 24, 8, 256])
    wdnP_s = scr("wdnP", [nlw, 128, 8, 24, 128])
    gxw_s = scr("gxws", [nlw, 128, 1024])
    gaw_s = scr("gaws", [nlw, 128, 1024])

    with ExitStack() as st:
        S = Sched(nc, st)
        sb = lambda n, s, d: st.enter_context(nc.sbuf_tensor("sb_" + n, s, d))
        xhi = sb("xhi", [128, 8, T], BF16)
        xlo = sb("xlo", [128, 8, T], BF16)
        big = sb("big", [128, 26624], BF16)
        tmp = sb("tmp", [128, 9, TS], F32)
        hr = sb("hr", [128, 4], F32)
        tmo = sb("tmo", [128, 2, TS], F32)
        cosT = sb("cosT", [128, T], F32)
        sinT = sb("sinT", [128, T], F32)
        KT = sb("KT", [128, 2, T], BF16)
        VA = sb("VA", [128, 16, 2, 128], BF16)
        QT = sb("QT", [128, 2, TS], BF16)
        PT = sb("PT", [128, 5, TS], BF16)
        PT4 = PT[:, 0:4, :].rearrange("p (s h) t -> p s h t", s=2)
        wbuf = sb("wbuf", [128, 8192], BF16)
        wsm = sb("wsm", [128, 2, 1920], BF16)
        pv = sb("pv", [128, nlw * PL], F32)
        pvd = sb("pvd", [128, nlw * 32], F32)
        cst = sb("cst", [128, 258], F32)
        onesb = sb("onesb", [128, 128], BF16)
        trib = sb("trib", [128, 128], BF16)
        hal = sb("hal", [128, 48, 2], F32)
        hlast = sb("hlast", [128, 2], F32)
        ps = [st.enter_context(nc.psum_tensor(f"ps{i}", [128, TS], F32)) for i in range(4)]
        ppair = [st.enter_context(nc.psum_tensor(f"pp{i}", [128, 2, TS], F32)) for i in range(2)]
        ps += [ppair[0][:, 0, :], ppair[0][:, 1, :], ppair[1][:, 0, :], ppair[1][:, 1, :]]
        PSB = [S.b(("ps", i)) for i in range(8)]
        ident = cst[:, 0:128]
        invf = cst[:, 256:257]
        epsc = cst[:, 257:258]
        wkr = wsm[:, 1, 0:768].rearrange("p (k d) -> p k d", k=8)

        merged = big[:, 0:16384].rearrange("p (c t) -> p c t", c=8)
        qn = big[:, 16384:22528].rearrange("p (c t) -> p c t", c=3)
        kvn = big[:, 22528:26624].rearrange("p (c t) -> p c t", c=2)
        abuf = big[:, 0:24576].rearrange("p (j t) -> p j t", j=24)
        wdn_sb = [KT[:].rearrange("p a t -> p (a t)")[:, 0:3072].rearrange("p (j q) -> p j q", j=24),
                  VA[:].rearrange("p a b c -> p (a b c)")[:, 0:3072].rearrange("p (j q) -> p j q", j=24)]

        def ACT(out, in_, func, rd, wr, **kw):
            S.op("act", "activation", rd, wr, out=out, in_=in_, func=func, **kw)

        def TT(e, out, in0, in1, op, rd, wr):
            S.op(e, "tensor_tensor", rd, wr, out=out, in0=in0, in1=in1, op=op)

        def TSC(e, out, in0, s1, s2, op0, op1, rd, wr):
            if s2 is None:
                S.op(e, "tensor_scalar", rd, wr, out=out, in0=in0, scalar1=s1, scalar2=None, op0=op0)
            else:
                S.op(e, "tensor_scalar", rd, wr, out=out, in0=in0, scalar1=s1, scalar2=s2, op0=op0, op1=op1)

        def STT(e, out, in0, scalar, in1, op0, op1, rd, wr):
            S.op(e, "scalar_tensor_tensor", rd, wr, out=out, in0=in0, scalar=scalar, in1=in1, op0=op0, op1=op1)

        def CP(e, out, in_, rd, wr):
            S.op(e, "tensor_copy", rd, wr, out=out, in_=in_)

        def RCP(out, in_, rd, wr):
            S.op("dve", "reciprocal", rd, wr, out=out, in_=in_)

        def MM(out, lhsT, rhs, start, stop, rd, wr):
            S.op("pe", "matmul", rd, wr, out=out, lhsT=lhsT, rhs=rhs, start=start, stop=stop)

        def TR(out, in_, rd, wr):
            S.op("pe", "transpose", rd, wr, out=out, in_=in_, identity=ident)

        def MS(e, ap, v, wr):
            S.op(e, "memset", (), wr, ap=ap, constant=v)

        DMA = S.dma
        proj_rr = [0]

        nproj = [3]

        def pbank():
            i = proj_rr[0] % nproj[0]
            proj_rr[0] += 1
            return i

        rb_rr = [0]
        ab_rr = [0]

        def rbank():
            i = rb_rr[0] % 2
            rb_rr[0] += 1
            return i

        def abank():
            i = 4 + ab_rr[0] % 4
            ab_rr[0] += 1
            return i

        def mm_group(pb, lhs_list, rhs_list, reads, out_ap=None):
            n = len(lhs_list)
            o = ps[pb][:] if out_ap is None else out_ap
            for i in range(n):
                MM(o, lhs_list[i], rhs_list[i], i == 0, i == n - 1, reads, [PSB[pb]])

        def tsl(tt):
            return slice(tt * TS, (tt + 1) * TS)

        XB = lambda k, tt: ("xhi", k, tt)
        XL = lambda k, tt: ("xlo", k, tt)

        DMA(cst[:], cst_d, (), ["cst"])
        DMA(pv[:], pv_d, (), ["pv"])
        MS("pool", onesb[:], 1.0, ["onesb"])
        CP("dve", trib[:], cst[:, 128:256], ["cst"], ["trib"])
        MS("pool", hal[:], 0.0, ["hal"])
        for l in layers:
            lam = pv[:, l * PL + O_LAM: l * PL + O_LAM + 8]
            t0 = tmp[:, 0, 0:8]
            ACT(t0, lam, AF.Exp, ["pv"], ["t0"], scale=-1.0)
            ACT(t0, t0, AF.Ln, ["t0"], ["t0"], bias=1.0)
            TSC("dve", pvd[:, l * 32: l * 32 + 8], t0, -4.0, None, ALU.mult, None, ["t0"], ["pvd"])
            TSC("dve", pvd[:, l * 32 + 8: l * 32 + 16], t0, -8.0, None, ALU.mult, None, ["t0"], ["pvd"])
            TSC("dve", pvd[:, l * 32 + 16: l * 32 + 24], pv[:, l * PL + O_GXB: l * PL + O_GXB + 8], 0.5, None, ALU.mult, None, ["pv"], ["pvd"])
            TSC("dve", pvd[:, l * 32 + 24: l * 32 + 32], pv[:, l * PL + O_GAB: l * PL + O_GAB + 8], 0.5, None, ALU.mult, None, ["pv"], ["pvd"])
        S.barrier_all()

        stf = big[:, 0:24576].bitcast(F32).rearrange("p (a b) -> p a b", a=2)
        stb = xhi[:].rearrange("p k t -> p (k t)")[:, 0:12288].rearrange("p (a b) -> p a b", a=2)
        tb16 = tmp[:].rearrange("p a b -> p (a b)").bitcast(BF16)
        wqr_sb = tb16[:, 0:4608].rearrange("p (c k h d) -> p c k h d", c=8, k=3, h=2)
        rr = [0]
        ceng = ["dve", "pool"]

        def stage_load(src_ap, ncols):
            i = rr[0] % 2
            rr[0] += 1
            DMA(stf[:, i, 0:ncols], src_ap, (), [("stf", i)])
            return i

        def cast(i, out_ap, in_ap, scale=None, eng=None):
            e_ = eng or ceng[rr[0] % 2]
            if scale is None:
                CP(e_, out_ap, in_ap, [("stf", i)], [("stb", i)])
            else:
                TSC(e_, out_ap, in_ap, scale, None, ALU.mult, None, [("stf", i), "pv"], [("stb", i)])

        MS("pool", wkr, 0.0, ["wkr"])
        MS("pool", tb16[:, 0:4608], 0.0, ["wqr_sb"])
        for l in layers:
            base = l * PL
            for k in range(8):
                i = stage_load(w_in_d[l, k * 128:(k + 1) * 128, :], INW)
                sf = stf[:, i, :]
                so = stb[:, i, :]
                soB = so[:, 0:4096].rearrange("p (c j q) -> p c j q", c=8, j=4)
                for j, b0 in enumerate([0, 1024, 2720, 3744]):
                    cast(i, soB[:, :, j, :], sf[:, b0:b0 + 1024].rearrange("p (c q) -> p c q", c=8), eng=ceng[j % 2])
                cast(i, so[:, 4096:4768], sf[:, 2048:2720], eng="dve")
                TSC("pool", wkr[:, k, 64:80], sf[:, 2704:2720], -1.0, None, ALU.mult, None, [("stf", i)], ["wkr"])
                CP("pool", wkr[:, k, 80:96], sf[:, 2688:2704], [("stf", i)], ["wkr"])
                DMA(winB_s[l][:, :, k, :], so[:, 0:4096].rearrange("p (c q) -> p c q", c=8), [("stb", i)], [("winB", l)])
                DMA(winL_s[l][:, k, :], so[:, 4096:4768], [("stb", i)], [("winL", l)])
            DMA(wkrot_s[l], wsm[:, 1, 0:768], ["wkr"], [("wkrot", l)])
            for k in range(3):
                i = stage_load(w_uq_d[l, k * 128:(k + 1) * 128, :], 1536)
                sf = stf[:, i, :]
                so = stb[:, i, :]
                g = pv[:, base + O_QG + k: base + O_QG + k + 1]
                cast(i, so[:, 0:1536], sf[:, 0:1536], scale=g)
                sfv = sf[:, 0:1536].rearrange("p (c h d) -> p c h d", c=8, h=2)
                TSC("pool", wqr_sb[:, :, k, :, 64:80], sfv[:, :, :, 80:96], g, -1.0, ALU.mult, ALU.mult,
                    [("stf", i), "pv"], ["wqr_sb"])
                TSC("pool", wqr_sb[:, :, k, :, 80:96], sfv[:, :, :, 64:80], g, None, ALU.mult, None,
                    [("stf", i), "pv"], ["wqr_sb"])
                DMA(wuqP_s[l][:, :, k, :], so[:, 0:1536].rearrange("p (c q) -> p c q", c=8), [("stb", i)], [("wuqP", l)])
            DMA(wqrP_s[l], tb16[:, 0:4608], ["wqr_sb"], [("wqrP", l)])
            for k in range(2):
                i = stage_load(w_ukv_d[l, k * 128:(k + 1) * 128, :], 2048)
                g = pv[:, base + O_KVG + k: base + O_KVG + k + 1]
                cast(i, stb[:, i, 0:2048], stf[:, i, 0:2048], scale=g)
                DMA(wukvP_s[l][:, :, k, :], stb[:, i, 0:2048].rearrange("p (c q) -> p c q", c=8), [("stb", i)], [("wukvP", l)])
            for k in range(8):
                i = stage_load(w_out_d[l, k * 128:(k + 1) * 128, :], 1024)
                cast(i, stb[:, i, 0:1024], stf[:, i, 0:1024])
                DMA(wout_s[l][:, k, :], stb[:, i, 0:1024], [("stb", i)], [("wout", l)])
            for k in range(8):
                i = stage_load(w_up_d[l, k * 128:(k + 1) * 128, :], 6144)
                sfv = stf[:, i, :].rearrange("p (g j q) -> p g j q", g=2, j=24)
                sov = stb[:, i, :].rearrange("p (j g q) -> p j g q", j=24, g=2)
                cast(i, sov[:, :, 0, :], sfv[:, 0, :, :], eng="dve")
                cast(i, sov[:, :, 1, :], sfv[:, 1, :, :], eng="pool")
                DMA(wupP_s[l][:, :, k, :], stb[:, i, :].rearrange("p (j q) -> p j q", j=24), [("stb", i)], [("wupP", l)])
            for k in range(24):
                i = stage_load(w_dn_d[l, k * 128:(k + 1) * 128, :], 1024)
                cast(i, stb[:, i, 0:1024], stf[:, i, 0:1024])
                DMA(wdnP_s[l][:, :, k, :], stb[:, i, 0:1024].rearrange("p (m q) -> p m q", m=8), [("stb", i)], [("wdnP", l)])
            for src, dst, nm in [(gxw_d, gxw_s, "gxws"), (gaw_d, gaw_s, "gaws")]:
                i = stage_load(src[l], 1024)
                cast(i, stb[:, i, 0:1024], stf[:, i, 0:1024])
                DMA(dst[l], stb[:, i, 0:1024], [("stb", i)], [(nm, l)])
        S.barrier_all()

        P = slice(64, 96)
        for s in range(NS):
            for tb in range(T // 128):
                tt = tb // 4
                xt = tmp[:, 2 * (tb % 2):2 * (tb % 2) + 2, :].rearrange("p a b -> p (a b)")
                xtb = ("xt", tb % 2)
                DMA(xt, x_d[s, tb * 128:(tb + 1) * 128, :], (), [xtb])
                for g in range(2):
                    pb = pbank()
                    for kk in range(4):
                        k = g * 4 + kk
                        TR(ps[pb][:, kk * 128:(kk + 1) * 128], xt[:, k * 128:(k + 1) * 128], [xtb, "cst"], [PSB[pb]])
                    hi = xhi[:, g * 4:(g + 1) * 4, tb * 128:(tb + 1) * 128]
                    lo = xlo[:, g * 4:(g + 1) * 4, tb * 128:(tb + 1) * 128]
                    pv3 = ps[pb][:].rearrange("p (a b) -> p a b", a=4)
                    hb = [XB(k, tt) for k in range(g * 4, g * 4 + 4)]
                    lb = [XL(k, tt) for k in range(g * 4, g * 4 + 4)]
                    ACT(hi, pv3, AF.Copy, [PSB[pb]], hb)
                    TT("dve", lo, pv3, hi, ALU.subtract, [PSB[pb]] + hb, lb)
            posi = tmp[:, 0:4, :].rearrange("p a b -> p (a b)").bitcast(I32)
            ang = tmp[:, 4:8, :].rearrange("p a b -> p (a b)")
            kf = tmp[:, 0:4, :].rearrange("p a b -> p (a b)")
            S.barrier_all()
            DMA(posi[P, :], pos_d[s], (), ["posi"])
            CP("dve", ang[P, :], posi[P, :], ["posi"], ["ang"])
            TSC("dve", ang[P, :], ang[P, :], invf[P, :], None, ALU.mult, None, ["ang", "cst"], ["ang"])
            TSC("dve", posi[P, :], ang[P, :], 1.0 / TWO_PI, None, ALU.mult, None, ["ang"], ["posi"])
            CP("dve", kf[P, :], posi[P, :], ["posi"], ["posi"])
            STT("dve", ang[P, :], kf[P, :], -TWO_PI, ang[P, :], ALU.mult, ALU.add, ["posi", "ang"], ["ang"])

            def wrap():
                TSC("dve", kf[P, :], ang[P, :], PI, TWO_PI, ALU.is_gt, ALU.mult, ["ang"], ["posi"])
                TT("dve", ang[P, :], ang[P, :], kf[P, :], ALU.subtract, ["ang", "posi"], ["ang"])
                TSC("dve", kf[P, :], ang[P, :], -PI, TWO_PI, ALU.is_lt, ALU.mult, ["ang"], ["posi"])
                TT("dve", ang[P, :], ang[P, :], kf[P, :], ALU.add, ["ang", "posi"], ["ang"])
            wrap()
            ACT(sinT[P, :], ang[P, :], AF.Sin, ["ang"], ["sinT"])
            TSC("dve", ang[P, :], ang[P, :], PI / 2, None, ALU.add, None, ["ang"], ["ang"])
            wrap()
            ACT(cosT[P, :], ang[P, :], AF.Sin, ["ang"], ["cosT"])
            S.barrier_all()

            for li, l in enumerate(layers):
                base = l * PL
                pcol = lambda o, base=base: pv[:, base + o: base + o + 1]
                wlat = wbuf[:, 0:8 * 672].rearrange("p (k q) -> p k q", k=8)
                DMA(wlat, winL_s[l], [("winL", l)], ["wbA", "wbB"])
                DMA(wsm[:, 1, 0:768], wkrot_s[l], [("wkrot", l)], ["wkr"])
                for tt in range(NT):
                    xh = [xhi[:, k, tsl(tt)] for k in range(8)]
                    xrd = [XB(k, tt) for k in range(8)]
                    for (nch, c0, dst, inv_n, nm) in [(3, 0, qn, 1.0 / 384, "qn"), (2, 384, kvn, 1.0 / 256, "kvn")]:
                        for oc in range(nch):
                            pb = pbank()
                            mm_group(pb, [wlat[:, k, c0 + oc * 128: c0 + (oc + 1) * 128] for k in range(8)], xh, xrd + ["wbA"])
                            ACT(tmp[:, oc, :], ps[pb][:], AF.Copy, [PSB[pb]], [("t", oc)])
                            ACT(PT[:, oc, :], ps[pb][:], AF.Square, [PSB[pb]], [("pt", oc)])
                        mm_group(7, [onesb[:]] * nch, [PT[:, oc, :] for oc in range(nch)],
                                 ["onesb"] + [("pt", oc) for oc in range(nch)])
                        ACT(tmp[:, 4, :], ps[7][:], AF.Sqrt, [PSB[7], "cst"], [("t", 4)], bias=epsc, scale=inv_n)
                        RCP(tmp[:, 5, :], tmp[:, 4, :], [("t", 4)], [("t", 5)])
                        for oc in range(nch):
                            TT("pool", dst[:, oc, tsl(tt)], tmp[:, oc, :], tmp[:, 5, :], ALU.mult,
                               [("t", oc), ("t", 5)], [(nm, oc, tt)])
                    pa = pbank()
                    mm_group(pa, [wlat[:, k, 576:672] for k in range(8)], xh, xrd + ["wbA"], out_ap=ps[pa][0:96, :])
                    pb2 = pbank()
                    mm_group(pb2, [wkr[:, k, :] for k in range(8)], xh, xrd + ["wkr"], out_ap=ps[pb2][0:96, :])
                    TT("dve", tmp[P, 6, :], ps[pa][P, :], cosT[P, tsl(tt)], ALU.mult, [PSB[pa], "cosT"], [("t", 6)])
                    TT("dve", tmp[P, 7, :], ps[pb2][P, :], sinT[P, tsl(tt)], ALU.mult, [PSB[pb2], "sinT"], [("t", 7)])
                    TT("pool", KT[P, 0, tsl(tt)], tmp[P, 6, :], tmp[P, 7, :], ALU.add, [("t", 6), ("t", 7)], [("kpe", tt)])
                    CP("pool", KT[P, 1, tsl(tt)], KT[P, 0, tsl(tt)], [("kpe", tt)], [("kpe1", tt)])
                S.barrier_all()

                def load_pair(c, l=l):
                    hb_ = c % 2
                    wb = wbuf[:, hb_ * 4096:(hb_ + 1) * 4096]
                    DMA(wb.rearrange("p (k q) -> p k q", k=8), winB_s[l][:, c, :, :], [("winB", l)], ["wbA" if hb_ == 0 else "wbB"])
                    sm = wsm[:, hb_, :]
                    DMA(sm[:, 0:576].rearrange("p (k q) -> p k q", k=3), wuqP_s[l][:, c, :, :], [("wuqP", l)], [("wsm", hb_, 0)])
                    DMA(sm[:, 576:1152], wqrP_s[l][:, c * 576:(c + 1) * 576], [("wqrP", l)], [("wsm", hb_, 1)])
                    DMA(sm[:, 1152:1664].rearrange("p (k q) -> p k q", k=2), wukvP_s[l][:, c, :, :], [("wukvP", l)], [("wsm", hb_, 2)])
                    DMA(sm[:, 1664:1792], gxw_s[l][:, c * 128:(c + 1) * 128], [("gxws", l)], [("wsm", hb_, 3)])
                    DMA(sm[:, 1792:1920], gaw_s[l][:, c * 128:(c + 1) * 128], [("gaws", l)], [("wsm", hb_, 4)])

                nproj[0] = 2
                MS("pool", VA[:, :, 0, 64:128], 1.0, ["VA"])
                MS("pool", VA[:, :, 1, 0:64], 1.0, ["VA"])
                load_pair(0)
                for c in range(8):
                    hb_ = c % 2
                    WB = "wbA" if hb_ == 0 else "wbB"
                    if c + 1 < 8:
                        load_pair(c + 1)
                    wc = wbuf[:, hb_ * 4096:(hb_ + 1) * 4096].rearrange("p (k j q) -> p k j q", k=8, j=4)
                    sm = wsm[:, hb_, :]
                    wuq = sm[:, 0:576].rearrange("p (k q) -> p k q", k=3)
                    wqr = sm[:, 576:1152].rearrange("p (k h q) -> p k h q", k=3, h=2)
                    wukv = sm[:, 1152:1664].rearrange("p (k q) -> p k q", k=2)
                    gxw = sm[:, 1664:1792]
                    gaw = sm[:, 1792:1920]
                    cwc = lambda tap, c=c: pcol(O_CW + tap * 8 + c)
                    for hh in range(2):
                        for tt in range(NT):
                            pb = abank()
                            mm_group(pb, [wukv[:, k, hh * 128: hh * 128 + 64] for k in range(2)],
                                     [kvn[:, k, tsl(tt)] for k in range(2)], [("kvn", 0, tt), ("kvn", 1, tt), ("wsm", hb_, 2)],
                                     out_ap=ps[pb][0:64, :])
                            ACT(KT[0:64, hh, tsl(tt)], ps[pb][0:64, :], AF.Copy, [PSB[pb]], [("KT", hh, tt)])
                    for g4 in range(4):
                        pb = abank()
                        pv4 = ps[pb][:].rearrange("p (t h d) -> p t h d", t=4, h=2)
                        for t4 in range(4):
                            tb = g4 * 4 + t4
                            for k in range(2):
                                MM(pv4[:, t4, :, :], kvn[:, k, tb * 128:(tb + 1) * 128],
                                   wukv[:, k, :].rearrange("p (h d) -> p h d", h=2)[:, :, 64:128], k == 0, k == 1,
                                   [("kvn", k, g4), ("wsm", hb_, 2)], [PSB[pb]])
                        ACT(VA[:, g4 * 4:(g4 + 1) * 4, 0, 0:64], pv4[:, :, 0, :], AF.Copy, [PSB[pb]], [("VA", g4)])
                        CP("dve", VA[:, g4 * 4:(g4 + 1) * 4, 1, 64:128], pv4[:, :, 1, :], [PSB[pb]], [("VA", g4)])
                    def rnn_p1(qt, c=c, wc=wc, gxw=gxw, gaw=gaw, hb_=hb_, WB=WB, cwc=cwc, l=l):
                        xh = [xhi[:, k, tsl(qt)] for k in range(8)]
                        xrd = [XB(k, qt) for k in range(8)]
                        pc = lambda o: pvd[:, l * 32 + o + c: l * 32 + o + c + 1]
                        px = rbank()
                        mm_group(px, [wc[:, k, 0, :] for k in range(8)], xh, xrd + [WB])
                        yield
                        X = ps[px]
                        cv = tmp[:, 0, :]
                        TSC("dve", cv, X[:], cwc(3), pcol(O_CB + c), ALU.mult, ALU.add, [PSB[px], "pv"], [("t", 0)])
                        yield
                        pg = rbank()
                        mm_group(pg, [wc[:, k, 1, :] for k in range(8)], xh, xrd + [WB])
                        yield
                        for sh, tap in ((1, 2), (2, 1), (3, 0)):
                            STT("dve", cv[:, sh:TS], X[:, 0:TS - sh], cwc(tap), cv[:, sh:TS], ALU.mult, ALU.add,
                                [PSB[px], ("t", 0), "pv"], [("t", 0)])
                            yield
                        if qt > 0:
                            for sh, tap in ((1, 2), (2, 1), (3, 0)):
                                STT("dve", cv[:, 0:sh], hr[:, 3 - sh:3], cwc(tap), cv[:, 0:sh], ALU.mult, ALU.add,
                                    ["hr", ("t", 0), "pv"], [("t", 0)])
                        if qt < NT - 1:
                            ACT(hr[:, 0:3], X[:, TS - 3:TS], AF.Copy, [PSB[px]], ["hr"])
                        yield
                        CP("pool", PT[:, 4, :], cv, [("t", 0)], [("pt", 4)])
                        ACT(tmp[:, 3, :], ps[pg][:], AF.Square, [PSB[pg]], [("t", 3)])
                        ACT(tmp[:, 4, :], ps[pg][:], AF.Copy, [PSB[pg]], [("t", 4)])
                        yield
                        TSC("dve", tmp[:, 3, :], tmp[:, 3, :], 0.044715, 1.0, ALU.mult, ALU.add, [("t", 3)], [("t", 3)])
                        yield
                        TT("dve", tmp[:, 3, :], tmp[:, 3, :], tmp[:, 4, :], ALU.mult, [("t", 3), ("t", 4)], [("t", 3)])
                        yield
                        pgx = rbank()
                        mm_group(pgx, [gxw], [PT[:, 4, :]], [("pt", 4), ("wsm", hb_, 3)])
                        pga = rbank()
                        mm_group(pga, [gaw], [PT[:, 4, :]], [("pt", 4), ("wsm", hb_, 4)])
                        yield
                        ACT(tmp[:, 3, :], tmp[:, 3, :], AF.Tanh, [("t", 3)], [("t", 3)], scale=0.7978845608028654)
                        yield
                        ACT(tmp[:, 1, :], ps[pgx][:], AF.Tanh, [PSB[pgx], "pvd"], [("t", 1)], bias=pc(16), scale=0.5)
                        yield
                        ACT(tmp[:, 2, :], ps[pga][:], AF.Tanh, [PSB[pga], "pvd"], [("t", 2)], bias=pc(24), scale=0.5)
                        STT("dve", tmp[:, 4, :], tmp[:, 3, :], 1.0, tmp[:, 4, :], ALU.add, ALU.mult, [("t", 3), ("t", 4)], [("t", 4)])
                        yield
                        ACT(tmp[:, 3, :], tmp[:, 2, :], AF.Exp, [("t", 2), "pvd"], [("t", 3)], bias=pc(0), scale=pc(0))
                        yield
                        ACT(tmp[:, 2, :], tmp[:, 2, :], AF.Exp, [("t", 2), "pvd"], [("t", 2)], bias=pc(8), scale=pc(8))
                        STT("dve", tmp[:, 1, :], tmp[:, 1, :], 1.0, tmp[:, 0, :], ALU.add, ALU.mult, [("t", 1), ("t", 0)], [("t", 1)])
                        yield
                        ACT(tmp[:, 2, :], tmp[:, 2, :], AF.Sqrt, [("t", 2)], [("t", 2)], bias=1.0, scale=-1.0)
                        yield
                        STT("dve", tmp[:, 1, :], tmp[:, 1, :], 0.5, tmp[:, 2, :], ALU.mult, ALU.mult, [("t", 1), ("t", 2)], [("t", 1)])
                        yield
                        init = 0.0 if qt == 0 else hlast[:, 0:1]
                        S.op("dve", "tensor_tensor_scan", [("t", 3), ("t", 1), "hlast"], [("t", 2)],
                             out=tmp[:, 2, :], data0=tmp[:, 3, :], data1=tmp[:, 1, :], initial=init, op0=ALU.mult, op1=ALU.add)
                        CP("dve", hlast[:, 0:1], tmp[:, 2, TS - 1:TS], [("t", 2)], ["hlast"])
                        yield
                        TT("pool", tmp[:, 4, :], tmp[:, 4, :], tmp[:, 2, :], ALU.mult, [("t", 4), ("t", 2)], [("t", 4)])
                        yield

                    def rnn_p2(qt, c=c, wc=wc, WB=WB):
                        xh = [xhi[:, k, tsl(qt)] for k in range(8)]
                        xrd = [XB(k, qt) for k in range(8)]
                        pa_ = rbank()
                        mm_group(pa_, [wc[:, k, 2, :] for k in range(8)], xh, xrd + [WB])
                        yield
                        ACT(tmp[:, 1, :], ps[pa_][:], AF.Tanh, [PSB[pa_]], [("t", 1)], scale=0.5)
                        yield
                        pbg = rbank()
                        mm_group(pbg, [wc[:, k, 3, :] for k in range(8)], xh, xrd + [WB])
                        yield
                        STT("dve", tmo[:, 0, :], tmp[:, 1, :], 1.0, tmp[:, 4, :], ALU.add, ALU.mult, [("t", 4), ("t", 1)], ["tM"])
                        ACT(tmo[:, 1, :], ps[pbg][:], AF.Tanh, [PSB[pbg]], ["tB"], scale=0.5)
                        yield

                    gq = []

                    def pump(n, limit):
                        k = 0
                        while k < n and gq_i[0] <= limit and gq_i[0] < len(gq):
                            try:
                                next(gq[gq_i[0]])
                                k += 1
                            except StopIteration:
                                gq_i[0] += 1

                    def flush(limit):
                        while gq_i[0] <= limit and gq_i[0] < len(gq):
                            try:
                                next(gq[gq_i[0]])
                            except StopIteration:
                                gq_i[0] += 1

                    nproj[0] = 2
                    gq_i = [0]
                    for q_ in range(NT):
                        gq.append(rnn_p1(q_))
                        gq.append(rnn_p2(q_))
                    for qt in range(NT):
                        LIM = 2 * qt + 2
                        pump(2, LIM)
                        qr = [("qn", k, qt) for k in range(3)]
                        qrh = [qn[:, k, tsl(qt)] for k in range(3)]
                        for hh in range(2):
                            pq = 4 + 2 * hh
                            mm_group(pq, [wuq[:, k, hh * 96:(hh + 1) * 96] for k in range(3)], qrh,
                                     qr + [("wsm", hb_, 0)], out_ap=ps[pq][0:96, :])
                            pr = 5 + 2 * hh
                            mm_group(pr, [wqr[:, k, hh, :] for k in range(3)], qrh,
                                     qr + [("wsm", hb_, 1)], out_ap=ps[pr][0:96, :])
                            ACT(QT[0:64, hh, :], ps[pq][0:64, :], AF.Copy, [PSB[pq]], [("QT", hh)])
                            TT("dve", tmp[P, 5, :], ps[pq][P, :], cosT[P, tsl(qt)], ALU.mult, [PSB[pq], "cosT"], [("t", 5)])
                            TT("dve", tmp[P, 6, :], ps[pr][P, :], sinT[P, tsl(qt)], ALU.mult, [PSB[pr], "sinT"], [("t", 6)])
                            TT("pool", QT[P, hh, :], tmp[P, 5, :], tmp[P, 6, :], ALU.add, [("t", 5), ("t", 6)], [("QT", hh)])
                            pump(2, LIM)
                        nkb = 4 * qt + 4
                        for i in range(nkb + 1):
                            if i < nkb:
                                kb = i
                                n0 = max(kb - 4 * qt, 0) * 128
                                sl = kb % 2
                                for hh in range(2):
                                    bk = 4 + 2 * sl + hh
                                    MM(ps[bk][:, n0:TS], KT[0:96, hh, kb * 128:(kb + 1) * 128], QT[0:96, hh, n0:TS], True, True,
                                       [("KT", hh, kb // 4), ("kpe" if hh == 0 else "kpe1", kb // 4), ("QT", hh)], [PSB[bk]])
                            if i >= 1:
                                kb = i - 1
                                j = kb - 4 * qt
                                n0 = max(j, 0) * 128
                                sl = kb % 2
                                ACT(PT4[:, sl, :, n0:TS], ppair[sl][:, :, n0:TS], AF.Exp,
                                    [PSB[4 + 2 * sl], PSB[5 + 2 * sl]], [("pt", sl)], scale=SCALE)
                                if j >= 0:
                                    for hh in range(2):
                                        TT("pool", PT4[:, sl, hh, n0:n0 + 128], PT4[:, sl, hh, n0:n0 + 128], trib[:], ALU.mult,
                                           [("pt", sl), "trib"], [("pt", sl)])
                                for hh in range(2):
                                    MM(ps[2 + hh][:, n0:TS], VA[:, kb, hh, :], PT4[:, sl, hh, n0:TS], kb == 0, kb == nkb - 1,
                                       [("VA", kb // 4), ("pt", sl), "VA"], [PSB[2 + hh]])
                            pump(3, LIM)
                        RCP(tmp[0:64, 7, :], ps[2][64:128, :], [PSB[2]], [("t", 7)])
                        TT("dve", tmp[0:64, 8, :], ps[2][0:64, :], tmp[0:64, 7, :], ALU.mult, [PSB[2], ("t", 7)], [("t", 8)])
                        RCP(tmp[64:128, 7, :], ps[3][0:64, :], [PSB[3]], [("t", 7)])
                        TT("dve", tmp[64:128, 8, :], ps[3][64:128, :], tmp[64:128, 7, :], ALU.mult, [PSB[3], ("t", 7)], [("t", 8)])
                        flush(2 * qt + 1)
                        STT("dve", tmp[:, 8, :], tmo[:, 1, :], 1.0, tmp[:, 8, :], ALU.add, ALU.mult, [("t", 8), "tB"], [("t", 8)])
                        STT("dve", tmp[:, 8, :], tmo[:, 0, :], 0.5, tmp[:, 8, :], ALU.mult, ALU.add, [("t", 8), "tM"], [("t", 8)])
                        ACT(merged[:, c, tsl(qt)], tmp[:, 8, :], AF.Copy, [("t", 8)], [("mg", c, qt)], scale=0.5)
                nproj[0] = 3
                S.barrier_all()

                def stat_evac(pm, pq_, tm, tr):
                    TSC("dve", tmp[:, tm, :], ps[pm][:], 1.0 / D, None, ALU.mult, None, [PSB[pm]], [("t", tm)])
                    TT("pool", tmp[:, tr, :], tmp[:, tm, :], tmp[:, tm, :], ALU.mult, [("t", tm)], [("t", tr)])
                    STT("dve", tmp[:, tr, :], ps[pq_][:], 1.0 / D, tmp[:, tr, :], ALU.mult, ALU.subtract,
                        [PSB[pq_], ("t", tr)], [("t", tr)])
                    ACT(tmp[:, tr, :], tmp[:, tr, :], AF.Sqrt, [("t", tr), "cst"], [("t", tr)], bias=epsc, scale=1.0)
                    RCP(tmp[:, tr, :], tmp[:, tr, :], [("t", tr)], [("t", tr)])

                def ln_norm_gen(tt, tm, tr, og, ob, zi):
                    zt = tmp[:, zi, :]
                    ZB = ("t", zi)
                    for m in range(8):
                        TT("pool", zt, xhi[:, m, tsl(tt)], xlo[:, m, tsl(tt)], ALU.add, [XB(m, tt), XL(m, tt)], [ZB])
                        yield
                        TT("dve", zt, zt, tmp[:, tm, :], ALU.subtract, [ZB, ("t", tm)], [ZB])
                        yield
                        TT("pool", zt, zt, tmp[:, tr, :], ALU.mult, [ZB, ("t", tr)], [ZB])
                        yield
                        ACT(zt, zt, AF.Identity, [ZB, "pv"], [ZB], bias=pcol(ob + m), scale=pcol(og + m))
                        yield
                        ACT(xhi[:, m, tsl(tt)], zt, AF.Copy, [ZB], [XB(m, tt)])
                        yield
                        TT("dve", xlo[:, m, tsl(tt)], zt, xhi[:, m, tsl(tt)], ALU.subtract, [ZB, XB(m, tt)], [XL(m, tt)])
                        yield

                def gpump(gen, n):
                    if gen is None:
                        return
                    for _ in range(n):
                        try:
                            next(gen)
                        except StopIteration:
                            return

                def chain2(g1, g2):
                    yield from g1
                    yield from g2

                rcnt = [0]

                def resid(pb, m, tt, pm, pq_):
                    k = rcnt[0]
                    rcnt[0] += 1
                    zt = tmp[:, k % 2, :]
                    ZB = ("t", k % 2)
                    STT("dve", zt, xhi[:, m, tsl(tt)], ALPHA, ps[pb][:], ALU.mult, ALU.add, [XB(m, tt), PSB[pb]], [ZB])
                    STT("dve", zt, xlo[:, m, tsl(tt)], ALPHA, zt, ALU.mult, ALU.add, [XL(m, tt), ZB], [ZB])
                    ACT(xhi[:, m, tsl(tt)], zt, AF.Copy, [ZB], [XB(m, tt)])
                    TT("pool", xlo[:, m, tsl(tt)], zt, xhi[:, m, tsl(tt)], ALU.subtract, [ZB, XB(m, tt)], [XL(m, tt)])
                    sq = PT[:, k % 4, :]
                    ACT(sq, zt, AF.Square, [ZB], [("pt", k % 4)])

                    def stats():
                        MM(ps[pm][:], onesb[:], xhi[:, m, tsl(tt)], m == 0, m == 7, [XB(m, tt), "onesb"], [PSB[pm]])
                        MM(ps[pq_][:], onesb[:], sq, m == 0, m == 7, [("pt", k % 4), "onesb"], [PSB[pq_]])
                    return stats

                wo = wbuf[:].rearrange("p (k q) -> p k q", k=8)
                DMA(wo, wout_s[l], [("wout", l)], ["wbA", "wbB"])
                gen = None
                for tt in range(NT):
                    pm, pq_ = (4, 5) if tt % 2 == 0 else (6, 7)
                    pending = None
                    for m in range(8):
                        pb = pbank()
                        mm_group(pb, [wo[:, cc, m * 128:(m + 1) * 128] for cc in range(8)], [merged[:, cc, tsl(tt)] for cc in range(8)],
                                 [("mg", cc, tt) for cc in range(8)] + ["wbA"])
                        if pending is not None:
                            pending()
                        pending = resid(pb, m, tt, pm, pq_)
                        gpump(gen, 6)
                    pending()
                    gpump(gen, 1000)
                    stat_evac(pm, pq_, 4, 5)
                    gen = ln_norm_gen(tt, 4, 5, O_L1G, O_L1B, 8)
                gpump(gen, 1000)
                S.barrier_all()

                def load_up(j, l=l):
                    bi = j % 4
                    DMA(wbuf[:, bi * 2048:(bi + 1) * 2048].rearrange("p (k q) -> p k q", k=8), wupP_s[l][:, j, :, :],
                        [("wupP", l)], [("wup", bi)])

                def load_dn(m, l=l):
                    DMA(wdn_sb[m % 2], wdnP_s[l][:, m, :, :], [("wdnP", l)], [("wdn", m % 2)])

                gen = None
                for half in range(2):
                    for j in range(3):
                        load_up(j)
                    for j in range(24):
                        if j + 3 < 24:
                            load_up(j + 3)
                        wu = wbuf[:, (j % 4) * 2048:(j % 4 + 1) * 2048].rearrange("p (k g q) -> p k g q", k=8, g=2)
                        for t2 in range(2):
                            tt = half * 2 + t2
                            xh = [xhi[:, k, tsl(tt)] for k in range(8)]
                            xrd = [XB(k, tt) for k in range(8)]
                            for gv in range(2):
                                jj = gv * 24 + j
                                pb = pbank()
                                mm_group(pb, [wu[:, k, gv, :] for k in range(8)], xh, xrd + [("wup", j % 4)])
                                ti = 2 * t2 + gv
                                ht = tmp[:, ti, :]
                                HB = ("t", ti)
                                fw = lambda tap, jj=jj: pcol(O_FCW + tap * 48 + jj)
                                ACT(ht, ps[pb][:], AF.Identity, [PSB[pb], "pv"], [HB], bias=pcol(O_FCB + jj), scale=fw(2))
                                STT("dve", ht[:, 1:TS], ps[pb][:, 0:TS - 1], fw(1), ht[:, 1:TS], ALU.mult, ALU.add, [PSB[pb], HB, "pv"], [HB])
                                STT("dve", ht[:, 2:TS], ps[pb][:, 0:TS - 2], fw(0), ht[:, 2:TS], ALU.mult, ALU.add, [PSB[pb], HB, "pv"], [HB])
                                if tt > 0:
                                    STT("dve", ht[:, 0:1], hal[:, jj, 1:2], fw(1), ht[:, 0:1], ALU.mult, ALU.add, [("hal", jj), HB, "pv"], [HB])
                                    STT("dve", ht[:, 0:2], hal[:, jj, 0:2], fw(0), ht[:, 0:2], ALU.mult, ALU.add, [("hal", jj), HB, "pv"], [HB])
                                if tt < NT - 1:
                                    ACT(hal[:, jj, :], ps[pb][:, TS - 2:TS], AF.Copy, [PSB[pb]], [("hal", jj)])
                            tg, tv = 2 * t2, 2 * t2 + 1
                            ACT(tmp[:, tg, :], tmp[:, tg, :], AF.Gelu_apprx_tanh, [("t", tg)], [("t", tg)])
                            TT("pool", abuf[:, j, t2 * TS:(t2 + 1) * TS], tmp[:, tg, :], tmp[:, tv, :], ALU.mult,
                               [("t", tg), ("t", tv)], [("a", j, t2)])
                            gpump(gen, 2)
                    gpump(gen, 1000)
                    load_dn(0)
                    pending = None
                    for m in range(8):
                        if m + 1 < 8:
                            load_dn(m + 1)
                        wd = wdn_sb[m % 2]
                        for t2 in range(2):
                            tt = half * 2 + t2
                            pb = pbank()
                            mm_group(pb, [wd[:, j, :] for j in range(24)], [abuf[:, j, t2 * TS:(t2 + 1) * TS] for j in range(24)],
                                     [("a", j, t2) for j in range(24)] + [("wdn", m % 2)])
                            if pending is not None:
                                pending()
                            pending = resid(pb, m, tt, 4 + 2 * t2, 5 + 2 * t2)
                    pending()
                    stat_evac(4, 5, 4, 5)
                    stat_evac(6, 7, 6, 7)
                    gen = chain2(ln_norm_gen(half * 2, 4, 5, O_L2G, O_L2B, 8), ln_norm_gen(half * 2 + 1, 6, 7, O_L2G, O_L2B, 8))
                gpump(gen, 1000)
                S.barrier_all()

            S.barrier_all()
            cnt = 0
            for tt in range(NT):
                for g in range(2):
                    for kk in range(4):
                        m = g * 4 + kk
                        TT("pool", tmp[:, kk, :], xhi[:, m, tsl(tt)], xlo[:, m, tsl(tt)], ALU.add, [XB(m, tt), XL(m, tt)], [("t", kk)])
                    for t4 in range(4):
                        tb = tt * 4 + t4
                        pb = pbank()
                        for kk in range(4):
                            TR(ps[pb][:, kk * 128:(kk + 1) * 128], tmp[:, kk, t4 * 128:(t4 + 1) * 128], [("t", kk), "cst"], [PSB[pb]])
                        oi = 4 + cnt % 4
                        cnt += 1
                        if cnt % 2 == 0:
                            ACT(tmp[:, oi, :], ps[pb][:], AF.Copy, [PSB[pb]], [("t", oi)])
                        else:
                            CP("dve", tmp[:, oi, :], ps[pb][:], [PSB[pb]], [("t", oi)])
                        DMA(y_d[s, tb * 128:(tb + 1) * 128, g * 512:(g + 1) * 512], tmp[:, oi, :], [("t", oi)], [("y", s, tb, g)])
            S.barrier_all()
        S.barrier_all()
        S.run()
    return nc


def _fm(v):
    v = np.asarray(v, np.float32)
    return np.ascontiguousarray(v.reshape(-1, 128).T)


def _host_params(inp, nl):
    pvs = np.zeros((128, nl * PL), np.float32)
    for l in range(nl):
        b = l * PL
        for tap in range(4):
            pvs[:, b + O_CW + tap * 8: b + O_CW + tap * 8 + 8] = _fm(inp["conv_w"][l, tap])
        pvs[:, b + O_CB: b + O_CB + 8] = _fm(inp["conv_b"][l])
        pvs[:, b + O_GXB: b + O_GXB + 8] = _fm(inp["gx_b"][l])
        pvs[:, b + O_GAB: b + O_GAB + 8] = _fm(inp["ga_b"][l])
        pvs[:, b + O_LAM: b + O_LAM + 8] = _fm(inp["lru_lambda"][l])
        pvs[:, b + O_L1G: b + O_L1G + 8] = _fm(inp["ln1_g"][l])
        pvs[:, b + O_L1B: b + O_L1B + 8] = _fm(inp["ln1_b"][l])
        pvs[:, b + O_L2G: b + O_L2G + 8] = _fm(inp["ln2_g"][l])
        pvs[:, b + O_L2B: b + O_L2B + 8] = _fm(inp["ln2_b"][l])
        for tap in range(3):
            pvs[:, b + O_FCW + tap * 48: b + O_FCW + tap * 48 + 48] = _fm(inp["ffn_conv_w"][l, tap])
        pvs[:, b + O_FCB: b + O_FCB + 48] = _fm(inp["ffn_conv_b"][l])
        pvs[:, b + O_QG: b + O_QG + 3] = _fm(inp["q_norm_g"][l])
        pvs[:, b + O_KVG: b + O_KVG + 2] = _fm(inp["kv_norm_g"][l])
    return pvs


def _host_bd(w, nl):
    w = np.asarray(w, np.float32)
    out = np.zeros((nl, 128, 8, 128), np.float32)
    for c in range(8):
        for b in range(2):
            out[:, b * 64:(b + 1) * 64, c, b * 64:(b + 1) * 64] = w[:, 2 * c + b]
    return out.reshape(nl, 128, 1024)


def _consts():
    c = np.zeros((128, 258), np.float32)
    c[:, 0:128] = np.eye(128, dtype=np.float32)
    k = np.arange(128)[:, None]
    q = np.arange(128)[None, :]
    c[:, 128:256] = (q >= k).astype(np.float32)
    inv = (10000.0 ** (-np.arange(0, 32, 2, dtype=np.float32) / np.float32(32))).astype(np.float32)
    for p in range(64, 96):
        c[p, 256] = inv[(p - 64) % 16]
    c[:, 257] = EPS
    return c


_NC_CACHE = {}
_TRACE = False
_LAST = [None]


def run(inp, NS, layers, ncores, seq_of_core):
    nl = L_ALL
    key = (NS, tuple(layers))
    if key not in _NC_CACHE:
        _NC_CACHE[key] = build(NS, list(layers))
    nc = _NC_CACHE[key]
    shared = {
        "w_in": np.ascontiguousarray(inp["w_in"], np.float32),
        "w_uq": np.ascontiguousarray(inp["w_uq"], np.float32),
        "w_ukv": np.ascontiguousarray(inp["w_ukv"], np.float32),
        "w_out": np.ascontiguousarray(inp["w_out"], np.float32),
        "w_up": np.ascontiguousarray(inp["w_up"], np.float32),
        "w_down": np.ascontiguousarray(inp["w_down"], np.float32),
        "gxw": _host_bd(inp["gx_w"], nl),
        "gaw": _host_bd(inp["ga_w"], nl),
        "pv": _host_params(inp, nl),
        "cst": _consts(),
    }
    x = np.asarray(inp["x"], np.float32)
    pos = np.asarray(inp["positions"], np.int32)
    in_maps = []
    for ci in range(ncores):
        seqs = seq_of_core[ci]
        m = dict(shared)
        m["x"] = np.ascontiguousarray(x[seqs])
        m["pos"] = np.ascontiguousarray(np.broadcast_to(pos[seqs][:, None, :], (len(seqs), 32, T)))
        in_maps.append(m)
    res = run_bass_kernel_spmd(nc, in_maps, core_ids=list(range(ncores)), trace=_TRACE)
    _LAST[0] = res
    return [r["y"] for r in res.results]


def kernel(**inputs):
    B = inputs["x"].shape[0]
    ncores = 8
    NS = B // ncores
    seq_of_core = [list(range(ci * NS, (ci + 1) * NS)) for ci in range(ncores)]
    outs = run(inputs, NS, list(range(L_ALL)), ncores, seq_of_core)
    return np.concatenate(outs, axis=0).astype(np.float32)
```

```python
from contextlib import ExitStack
import math
import numpy as np
import concourse.bass as bass
import concourse.mybir as mybir
from concourse.bass_utils import run_bass_kernel_spmd

F32 = mybir.dt.float32
BF16 = mybir.dt.bfloat16
I32 = mybir.dt.int32
AF = mybir.ActivationFunctionType
ALU = mybir.AluOpType

ENGS = ["pe", "act", "dve", "pool", "sp"]

D = 1024
T = 2048
TS = 512
NT = T // TS
L_ALL = 4
INW = 4768
DFF = 3072
ALPHA = float((2 * L_ALL) ** 0.25)
EPS = 1e-6
SCALE = float(96 ** -0.5)
PL = 296
O_CW, O_CB, O_GXB, O_GAB, O_LAM = 0, 32, 40, 48, 56
O_L1G, O_L1B, O_L2G, O_L2B = 64, 72, 80, 88
O_FCW, O_FCB, O_QG, O_KVG = 96, 240, 288, 291
PAIR_EXP = True
XC_RUNAHEAD = False
TWO_PI = float(2 * math.pi)
PI = float(math.pi)


class Buf:
    __slots__ = ("name", "lw", "rd")

    def __init__(self, name=""):
        self.name = name
        self.lw = None
        self.rd = {}


class Sched:
    K = 8
    R = 32

    def __init__(self, nc, stack):
        self.nc = nc
        self.sems = {e: [stack.enter_context(nc.semaphore(f"s_{e}_{i}")) for i in range(self.K)]
                     for e in ENGS}
        self.dsems = [stack.enter_context(nc.semaphore(f"d_{i}")) for i in range(self.R)]
        self.cnt = {e: 0 for e in ENGS}
        self.seen = {e: {f: 0 for f in ENGS} for e in ENGS}
        self.seen_d = {e: [0] * self.R for e in ENGS}
        self.clock = {e: [] for e in ENGS}
        self.nd = 0
        self.thunks = {e: [] for e in ENGS}
        self.bufs = {}

    def b(self, key):
        x = self.bufs.get(key)
        if x is None:
            x = self.bufs[key] = Buf(str(key))
        return x

    def _need(self, e, tok, out):
        if tok is None:
            return
        if tok[0] == "e":
            _, f, i = tok
            if f == e and e == "pe":
                return
            if self.seen[e][f] >= i + 1:
                return
            out.append(tok)
        else:
            n = tok[1]
            if self.seen_d[e][n % self.R] >= n // self.R + 1:
                return
            out.append(tok)

    def _emit_waits(self, e, toks):
        best = {}
        dm = {}
        for t in toks:
            if t[0] == "e":
                best[t[1]] = max(best.get(t[1], -1), t[2])
            else:
                s = t[1] % self.R
                dm[s] = max(dm.get(s, -1), t[1])
        for f, i in best.items():
            if self.seen[e][f] >= i + 1:
                continue
            self.thunks[e].append(("w", self.sems[f][i % self.K], i // self.K + 1))
            self.seen[e][f] = i + 1
            clk = self.clock[f][i]
            for g in ENGS:
                if g != e and clk[g] > self.seen[e][g]:
                    self.seen[e][g] = clk[g]
        for s, n in dm.items():
            v = n // self.R + 1
            if self.seen_d[e][s] >= v:
                continue
            self.thunks[e].append(("w", self.dsems[s], 16 * v))
            self.seen_d[e][s] = v

    def _deps(self, e, reads, writes):
        toks = []
        for r in reads:
            self._need(e, r.lw, toks)
        for w in writes:
            self._need(e, w.lw, toks)
            for f, i in w.rd.items():
                if f == "d":
                    for n in i:
                        self._need(e, ("d", n), toks)
                else:
                    self._need(e, ("e", f, i), toks)
        return toks

    def _bl(self, xs):
        return [x if isinstance(x, Buf) else self.b(x) for x in xs]

    def op(self, e, name, reads=(), writes=(), **kw):
        fn = (name, kw)
        reads = self._bl(reads)
        writes = self._bl(writes)
        self._emit_waits(e, self._deps(e, reads, writes))
        i = self.cnt[e]
        self.cnt[e] = i + 1
        self.thunks[e].append(("i", fn, self.sems[e][i % self.K], 1))
        self.clock[e].append(dict(self.seen[e]))
        tok = ("e", e, i)
        for r in reads:
            if r.rd.get(e, -1) < i:
                r.rd[e] = i
        for w in writes:
            w.lw = tok
            w.rd = {}
        return tok

    def dma(self, out, in_, reads=(), writes=(), q="sp"):
        fn = ("dma_start", dict(out=out, in_=in_))
        reads = self._bl(reads)
        writes = self._bl(writes)
        n = self.nd
        self.nd += 1
        s = n % self.R
        toks = self._deps(q, reads, writes)
        if n >= self.R:
            self._need(q, ("d", n - self.R), toks)
        self._emit_waits(q, toks)
        self.thunks[q].append(("i", fn, self.dsems[s], 16))
        tok = ("d", n)
        for r in reads:
            r.rd.setdefault("d", []).append(n)
        for w in writes:
            w.lw = tok
            w.rd = {}
        return tok

    def barrier_all(self):
        for e in ENGS:
            toks = []
            for f in ENGS:
                if self.cnt[f] > 0:
                    self._need(e, ("e", f, self.cnt[f] - 1), toks)
            for n in range(max(0, self.nd - self.R), self.nd):
                self._need(e, ("d", n), toks)
            self._emit_waits(e, toks)

    def run(self):
        nc = self.nc
        with nc.Block() as block:
            def mk(e):
                def body(eng):
                    for t in self.thunks[e]:
                        if t[0] == "w":
                            eng.wait_ge(t[1], t[2])
                        else:
                            getattr(eng, t[1][0])(**t[1][1]).then_inc(t[2], t[3])
                return body
            block.tensor(mk("pe"))
            block.scalar(mk("act"))
            block.vector(mk("dve"))
            block.gpsimd(mk("pool"))
            block.sync(mk("sp"))


def build(NS, layers, nlw=L_ALL):
    nc = bass.Bass("TRN2", target_bir_lowering=False)
    dt_in = lambda n, s, d=F32: nc.dram_tensor(n, s, d, kind="ExternalInput").ap()
    x_d = dt_in("x", [NS, T, D])
    pos_d = dt_in("pos", [NS, 32, T], I32)
    w_in_d = dt_in("w_in", [nlw, D, INW])
    w_uq_d = dt_in("w_uq", [nlw, 384, 1536])
    w_ukv_d = dt_in("w_ukv", [nlw, 256, 2048])
    w_out_d = dt_in("w_out", [nlw, D, D])
    w_up_d = dt_in("w_up", [nlw, D, 2 * DFF])
    w_dn_d = dt_in("w_down", [nlw, DFF, D])
    gxw_d = dt_in("gxw", [nlw, 128, 8 * 128])
    gaw_d = dt_in("gaw", [nlw, 128, 8 * 128])
    pv_d = dt_in("pv", [128, nlw * PL])
    cst_d = dt_in("cst", [128, 258])
    y_d = nc.dram_tensor("y", [NS, T, D], F32, kind="ExternalOutput").ap()

    scr = lambda n, s: nc.dram_tensor(n, s, BF16, kind="Internal").ap()
    winL_s = scr("winL", [nlw, 128, 8, 672])
    winB_s = scr("winB", [nlw, 128, 8, 8, 512])
    wkrot_s = scr("wkrot", [nlw, 128, 8 * 96])
    wuqP_s = scr("wuqP", [nlw, 128, 8, 3, 192])
    wqrP_s = scr("wqrP", [nlw, 128, 8 * 3 * 192])
    wukvP_s = scr("wukvP", [nlw, 128, 8, 2, 256])
    wout_s = scr("wout", [nlw, 128, 8, 1024])
    wupP_s = scr("wupP", [nlw, 128, 24, 8, 256])
    wdnP_s = scr("wdnP", [nlw, 128, 8, 24, 128])
    gxw_s = scr("gxws", [nlw, 128, 1024])
    gaw_s = scr("gaws", [nlw, 128, 1024])

    with ExitStack() as st:
        S = Sched(nc, st)
        sb = lambda n, s, d: st.enter_context(nc.sbuf_tensor("sb_" + n, s, d))
        xhi = sb("xhi", [128, 8, T], BF16)
        xlo = sb("xlo", [128, 8, T], BF16)
        big = sb("big", [128, 26624], BF16)
        tmp = sb("tmp", [128, 9, TS], F32)
        hr = sb("hr", [128, 4], F32)
        tmo = sb("tmo", [128, 2, TS], F32)
        cosT = sb("cosT", [128, T], F32)
        sinT = sb("sinT", [128, T], F32)
        KT = sb("KT", [128, 2, T], BF16)
        VA = sb("VA", [128, 16, 2, 128], BF16)
        QT = sb("QT", [128, 2, TS], BF16)
        PT = sb("PT", [128, 5, TS], BF16)
        PT4 = PT[:, 0:4, :].rearrange("p (s h) t -> p s h t", s=2)
        wbuf = sb("wbuf", [128, 8192], BF16)
        wsm = sb("wsm", [128, 2, 1920], BF16)
        pv = sb("pv", [128, nlw * PL], F32)
        pvd = sb("pvd", [128, nlw * 32], F32)
        cst = sb("cst", [128, 258], F32)
        onesb = sb("onesb", [128, 128], BF16)
        trib = sb("trib", [128, 128], BF16)
        hal = sb("hal", [128, 48, 2], F32)
        hlast = sb("hlast", [128, 2], F32)
        ps = [st.enter_context(nc.psum_tensor(f"ps{i}", [128, TS], F32)) for i in range(4)]
        ppair = [st.enter_context(nc.psum_tensor(f"pp{i}", [128, 2, TS], F32)) for i in range(2)]
        ps += [ppair[0][:, 0, :], ppair[0][:, 1, :], ppair[1][:, 0, :], ppair[1][:, 1, :]]
        PSB = [S.b(("ps", i)) for i in range(8)]
        ident = cst[:, 0:128]
        invf = cst[:, 256:257]
        epsc = cst[:, 257:258]
        wkr = wsm[:, 1, 0:768].rearrange("p (k d) -> p k d", k=8)

        merged = big[:, 0:16384].rearrange("p (c t) -> p c t", c=8)
        qn = big[:, 16384:22528].rearrange("p (c t) -> p c t", c=3)
        kvn = big[:, 22528:26624].rearrange("p (c t) -> p c t", c=2)
        abuf = big[:, 0:24576].rearrange("p (j t) -> p j t", j=24)
        wdn_sb = [KT[:].rearrange("p a t -> p (a t)")[:, 0:3072].rearrange("p (j q) -> p j q", j=24),
                  VA[:].rearrange("p a b c -> p (a b c)")[:, 0:3072].rearrange("p (j q) -> p j q", j=24)]

        def ACT(out, in_, func, rd, wr, **kw):
            S.op("act", "activation", rd, wr, out=out, in_=in_, func=func, **kw)

        def TT(e, out, in0, in1, op, rd, wr):
            S.op(e, "tensor_tensor", rd, wr, out=out, in0=in0, in1=in1, op=op)

        def TSC(e, out, in0, s1, s2, op0, op1, rd, wr):
            if s2 is None:
                S.op(e, "tensor_scalar", rd, wr, out=out, in0=in0, scalar1=s1, scalar2=None, op0=op0)
            else:
                S.op(e, "tensor_scalar", rd, wr, out=out, in0=in0, scalar1=s1, scalar2=s2, op0=op0, op1=op1)

        def STT(e, out, in0, scalar, in1, op0, op1, rd, wr):
            S.op(e, "scalar_tensor_tensor", rd, wr, out=out, in0=in0, scalar=scalar, in1=in1, op0=op0, op1=op1)

        def CP(e, out, in_, rd, wr):
            S.op(e, "tensor_copy", rd, wr, out=out, in_=in_)

        def RCP(out, in_, rd, wr):
            S.op("dve", "reciprocal", rd, wr, out=out, in_=in_)

        def MM(out, lhsT, rhs, start, stop, rd, wr):
            S.op("pe", "matmul", rd, wr, out=out, lhsT=lhsT, rhs=rhs, start=start, stop=stop)

        def TR(out, in_, rd, wr):
            S.op("pe", "transpose", rd, wr, out=out, in_=in_, identity=ident)

        def MS(e, ap, v, wr):
            S.op(e, "memset", (), wr, ap=ap, constant=v)

        DMA = S.dma
        proj_rr = [0]

        nproj = [3]

        def pbank():
            i = proj_rr[0] % nproj[0]
            proj_rr[0] += 1
            return i

        rb_rr = [0]
        ab_rr = [0]

        def rbank():
            i = rb_rr[0] % 2
            rb_rr[0] += 1
            return i

        def abank():
            i = 4 + ab_rr[0] % 4
            ab_rr[0] += 1
            return i

        def mm_group(pb, lhs_list, rhs_list, reads, out_ap=None):
            n = len(lhs_list)
            o = ps[pb][:] if out_ap is None else out_ap
            for i in range(n):
                MM(o, lhs_list[i], rhs_list[i], i == 0, i == n - 1, reads, [PSB[pb]])

        def tsl(tt):
            return slice(tt * TS, (tt + 1) * TS)

        XB = lambda k, tt: ("xhi", k, tt)
        XL = lambda k, tt: ("xlo", k, tt)

        DMA(cst[:], cst_d, (), ["cst"])
        DMA(pv[:], pv_d, (), ["pv"])
        MS("pool", onesb[:], 1.0, ["onesb"])
        CP("dve", trib[:], cst[:, 128:256], ["cst"], ["trib"])
        MS("pool", hal[:], 0.0, ["hal"])
        for l in layers:
            lam = pv[:, l * PL + O_LAM: l * PL + O_LAM + 8]
            t0 = tmp[:, 0, 0:8]
            ACT(t0, lam, AF.Exp, ["pv"], ["t0"], scale=-1.0)
            ACT(t0, t0, AF.Ln, ["t0"], ["t0"], bias=1.0)
            TSC("dve", pvd[:, l * 32: l * 32 + 8], t0, -4.0, None, ALU.mult, None, ["t0"], ["pvd"])
            TSC("dve", pvd[:, l * 32 + 8: l * 32 + 16], t0, -8.0, None, ALU.mult, None, ["t0"], ["pvd"])
            TSC("dve", pvd[:, l * 32 + 16: l * 32 + 24], pv[:, l * PL + O_GXB: l * PL + O_GXB + 8], 0.5, None, ALU.mult, None, ["pv"], ["pvd"])
            TSC("dve", pvd[:, l * 32 + 24: l * 32 + 32], pv[:, l * PL + O_GAB: l * PL + O_GAB + 8], 0.5, None, ALU.mult, None, ["pv"], ["pvd"])
        S.barrier_all()

        stf = big[:, 0:24576].bitcast(F32).rearrange("p (a b) -> p a b", a=2)
        stb = xhi[:].rearrange("p k t -> p (k t)")[:, 0:12288].rearrange("p (a b) -> p a b", a=2)
        tb16 = tmp[:].rearrange("p a b -> p (a b)").bitcast(BF16)
        wqr_sb = tb16[:, 0:4608].rearrange("p (c k h d) -> p c k h d", c=8, k=3, h=2)
        ccnt = [0]

        def cast(i, out_ap, in_ap, scale=None):
            if scale is None:
                ccnt[0] += 1
                if ccnt[0] % 2 == 0:
                    ACT(out_ap, in_ap, AF.Copy, [("stf", i)], [("stb", i)])
                else:
                    CP("dve", out_ap, in_ap, [("stf", i)], [("stb", i)])
            else:
                TSC("dve", out_ap, in_ap, scale, None, ALU.mult, None, [("stf", i), "pv"], [("stb", i)])

        MS("pool", wkr, 0.0, ["wkr"])
        MS("pool", tb16[:, 0:4608], 0.0, ["wqr_sb"])
        jobs = []
        for l in layers:
            base = l * PL
            for k in range(8):
                def job(i, l=l, k=k):
                    sf = stf[:, i, :]
                    so = stb[:, i, :]
                    soB = so[:, 0:4096].rearrange("p (c j q) -> p c j q", c=8, j=4)
                    for j, b0 in enumerate([0, 1024, 2720, 3744]):
                        cast(i, soB[:, :, j, :], sf[:, b0:b0 + 1024].rearrange("p (c q) -> p c q", c=8))
                    cast(i, so[:, 4096:4768], sf[:, 2048:2720])
                    TSC("pool", wkr[:, k, 64:80], sf[:, 2704:2720], -1.0, None, ALU.mult, None, [("stf", i)], ["wkr"])
                    CP("pool", wkr[:, k, 80:96], sf[:, 2688:2704], [("stf", i)], ["wkr"])
                    DMA(winB_s[l][:, :, k, :], so[:, 0:4096].rearrange("p (c q) -> p c q", c=8), [("stb", i)], [("winB", l)])
                    DMA(winL_s[l][:, k, :], so[:, 4096:4768], [("stb", i)], [("winL", l)])
                    if k == 7:
                        DMA(wkrot_s[l], wsm[:, 1, 0:768], ["wkr"], [("wkrot", l)])
                jobs.append((w_in_d[l, k * 128:(k + 1) * 128, :], lambda i: stf[:, i, 0:INW], job))

            def job_uq(i, l=l, base=base):
                sf3 = stf[:, i, 0:4608].rearrange("p (k q) -> p k q", k=3)
                so3 = stb[:, i, 0:4608].rearrange("p (k q) -> p k q", k=3)
                for k in range(3):
                    g = pv[:, base + O_QG + k: base + O_QG + k + 1]
                    cast(i, so3[:, k, :], sf3[:, k, :], scale=g)
                    sfv = sf3[:, k, :].rearrange("p (c h d) -> p c h d", c=8, h=2)
                    TSC("pool", wqr_sb[:, :, k, :, 64:80], sfv[:, :, :, 80:96], g, -1.0, ALU.mult, ALU.mult,
                        [("stf", i), "pv"], ["wqr_sb"])
                    TSC("pool", wqr_sb[:, :, k, :, 80:96], sfv[:, :, :, 64:80], g, None, ALU.mult, None,
                        [("stf", i), "pv"], ["wqr_sb"])
                    DMA(wuqP_s[l][:, :, k, :], so3[:, k, :].rearrange("p (c q) -> p c q", c=8), [("stb", i)], [("wuqP", l)])
                DMA(wqrP_s[l], tb16[:, 0:4608], ["wqr_sb"], [("wqrP", l)])
            jobs.append((w_uq_d[l].rearrange("(k p) q -> p k q", p=128),
                         lambda i: stf[:, i, 0:4608].rearrange("p (k q) -> p k q", k=3), job_uq))

            def job_ukv(i, l=l, base=base):
                sf2 = stf[:, i, 0:4096].rearrange("p (k q) -> p k q", k=2)
                so2 = stb[:, i, 0:4096].rearrange("p (k q) -> p k q", k=2)
                for k in range(2):
                    g = pv[:, base + O_KVG + k: base + O_KVG + k + 1]
                    cast(i, so2[:, k, :], sf2[:, k, :], scale=g)
                    DMA(wukvP_s[l][:, :, k, :], so2[:, k, :].rearrange("p (c q) -> p c q", c=8), [("stb", i)], [("wukvP", l)])
            jobs.append((w_ukv_d[l].rearrange("(k p) q -> p k q", p=128),
                         lambda i: stf[:, i, 0:4096].rearrange("p (k q) -> p k q", k=2), job_ukv))

            for k0 in range(0, 8, 4):
                def job_out(i, l=l, k0=k0):
                    cast(i, stb[:, i, 0:4096], stf[:, i, 0:4096])
                    DMA(wout_s[l][:, k0:k0 + 4, :], stb[:, i, 0:4096].rearrange("p (k q) -> p k q", k=4), [("stb", i)], [("wout", l)])
                jobs.append((w_out_d[l, k0 * 128:(k0 + 4) * 128, :].rearrange("(k p) q -> p k q", p=128),
                             lambda i: stf[:, i, 0:4096].rearrange("p (k q) -> p k q", k=4), job_out))

            for k in range(8):
                def job_up(i, l=l, k=k):
                    sfv = stf[:, i, :].rearrange("p (g j q) -> p g j q", g=2, j=24)
                    sov = stb[:, i, :].rearrange("p (j g q) -> p j g q", j=24, g=2)
                    cast(i, sov[:, :, 0, :], sfv[:, 0, :, :])
                    cast(i, sov[:, :, 1, :], sfv[:, 1, :, :])
                    DMA(wupP_s[l][:, :, k, :], stb[:, i, :].rearrange("p (j q) -> p j q", j=24), [("stb", i)], [("wupP", l)])
                jobs.append((w_up_d[l, k * 128:(k + 1) * 128, :], lambda i: stf[:, i, 0:6144], job_up))

            for k0 in range(0, 24, 4):
                def job_dn(i, l=l, k0=k0):
                    cast(i, stb[:, i, 0:4096], stf[:, i, 0:4096])
                    so4 = stb[:, i, 0:4096].rearrange("p (k m q) -> p k m q", k=4, m=8)
                    for kk in range(4):
                        DMA(wdnP_s[l][:, :, k0 + kk, :], so4[:, kk, :, :], [("stb", i)], [("wdnP", l)])
                jobs.append((w_dn_d[l, k0 * 128:(k0 + 4) * 128, :].rearrange("(k p) q -> p k q", p=128),
                             lambda i: stf[:, i, 0:4096].rearrange("p (k q) -> p k q", k=4), job_dn))

            for src, dst, nm in [(gxw_d, gxw_s, "gxws"), (gaw_d, gaw_s, "gaws")]:
                def job_g(i, l=l, dst=dst, nm=nm):
                    cast(i, stb[:, i, 0:1024], stf[:, i, 0:1024])
                    DMA(dst[l], stb[:, i, 0:1024], [("stb", i)], [(nm, l)])
                jobs.append((src[l], lambda i: stf[:, i, 0:1024], job_g))

        nj = len(jobs)

        def jload(n):
            i = n % 2
            DMA(jobs[n][1](i), jobs[n][0], (), [("stf", i)])

        if nj:
            jload(0)
        for n in range(nj):
            if n + 1 < nj:
                jload(n + 1)
            jobs[n][2](n % 2)
        S.barrier_all()

        P = slice(64, 96)
        for s in range(NS):
            for tb in range(T // 128):
                tt = tb // 4
                xt = tmp[:, 2 * (tb % 2):2 * (tb % 2) + 2, :].rearrange("p a b -> p (a b)")
                xtb = ("xt", tb % 2)
                DMA(xt, x_d[s, tb * 128:(tb + 1) * 128, :], (), [xtb])
                for g in range(2):
                    pb = pbank()
                    for kk in range(4):
                        k = g * 4 + kk
                        TR(ps[pb][:, kk * 128:(kk + 1) * 128], xt[:, k * 128:(k + 1) * 128], [xtb, "cst"], [PSB[pb]])
                    hi = xhi[:, g * 4:(g + 1) * 4, tb * 128:(tb + 1) * 128]
                    lo = xlo[:, g * 4:(g + 1) * 4, tb * 128:(tb + 1) * 128]
                    pv3 = ps[pb][:].rearrange("p (a b) -> p a b", a=4)
                    hb = [XB(k, tt) for k in range(g * 4, g * 4 + 4)]
                    lb = [XL(k, tt) for k in range(g * 4, g * 4 + 4)]
                    ACT(hi, pv3, AF.Copy, [PSB[pb]], hb)
                    TT("dve", lo, pv3, hi, ALU.subtract, [PSB[pb]] + hb, lb)
            posi = tmp[:, 0:4, :].rearrange("p a b -> p (a b)").bitcast(I32)
            ang = tmp[:, 4:8, :].rearrange("p a b -> p (a b)")
            kf = tmp[:, 0:4, :].rearrange("p a b -> p (a b)")
            S.barrier_all()
            DMA(posi[P, :], pos_d[s], (), ["posi"])
            CP("dve", ang[P, :], posi[P, :], ["posi"], ["ang"])
            TSC("dve", ang[P, :], ang[P, :], invf[P, :], None, ALU.mult, None, ["ang", "cst"], ["ang"])
            TSC("dve", posi[P, :], ang[P, :], 1.0 / TWO_PI, None, ALU.mult, None, ["ang"], ["posi"])
            CP("dve", kf[P, :], posi[P, :], ["posi"], ["posi"])
            STT("dve", ang[P, :], kf[P, :], -TWO_PI, ang[P, :], ALU.mult, ALU.add, ["posi", "ang"], ["ang"])

            def wrap():
                TSC("dve", kf[P, :], ang[P, :], PI, TWO_PI, ALU.is_gt, ALU.mult, ["ang"], ["posi"])
                TT("dve", ang[P, :], ang[P, :], kf[P, :], ALU.subtract, ["ang", "posi"], ["ang"])
                TSC("dve", kf[P, :], ang[P, :], -PI, TWO_PI, ALU.is_lt, ALU.mult, ["ang"], ["posi"])
                TT("dve", ang[P, :], ang[P, :], kf[P, :], ALU.add, ["ang", "posi"], ["ang"])
            wrap()
            ACT(sinT[P, :], ang[P, :], AF.Sin, ["ang"], ["sinT"])
            TSC("dve", ang[P, :], ang[P, :], PI / 2, None, ALU.add, None, ["ang"], ["ang"])
            wrap()
            ACT(cosT[P, :], ang[P, :], AF.Sin, ["ang"], ["cosT"])
            S.barrier_all()

            for li, l in enumerate(layers):
                base = l * PL
                pcol = lambda o, base=base: pv[:, base + o: base + o + 1]
                wlat = wbuf[:, 0:8 * 672].rearrange("p (k q) -> p k q", k=8)
                DMA(wlat, winL_s[l], [("winL", l)], ["wbA", "wbB"])
                DMA(wsm[:, 1, 0:768], wkrot_s[l], [("wkrot", l)], ["wkr"])
                for tt in range(NT):
                    xh = [xhi[:, k, tsl(tt)] for k in range(8)]
                    xrd = [XB(k, tt) for k in range(8)]
                    for (nch, c0, dst, inv_n, nm) in [(3, 0, qn, 1.0 / 384, "qn"), (2, 384, kvn, 1.0 / 256, "kvn")]:
                        for oc in range(nch):
                            pb = pbank()
                            mm_group(pb, [wlat[:, k, c0 + oc * 128: c0 + (oc + 1) * 128] for k in range(8)], xh, xrd + ["wbA"])
                            ACT(tmp[:, oc, :], ps[pb][:], AF.Copy, [PSB[pb]], [("t", oc)])
                            ACT(PT[:, oc, :], ps[pb][:], AF.Square, [PSB[pb]], [("pt", oc)])
                        mm_group(7, [onesb[:]] * nch, [PT[:, oc, :] for oc in range(nch)],
                                 ["onesb"] + [("pt", oc) for oc in range(nch)])
                        ACT(tmp[:, 4, :], ps[7][:], AF.Sqrt, [PSB[7], "cst"], [("t", 4)], bias=epsc, scale=inv_n)
                        RCP(tmp[:, 5, :], tmp[:, 4, :], [("t", 4)], [("t", 5)])
                        for oc in range(nch):
                            TT("pool", dst[:, oc, tsl(tt)], tmp[:, oc, :], tmp[:, 5, :], ALU.mult,
                               [("t", oc), ("t", 5)], [(nm, oc, tt)])
                    pa = pbank()
                    mm_group(pa, [wlat[:, k, 576:672] for k in range(8)], xh, xrd + ["wbA"], out_ap=ps[pa][0:96, :])
                    pb2 = pbank()
                    mm_group(pb2, [wkr[:, k, :] for k in range(8)], xh, xrd + ["wkr"], out_ap=ps[pb2][0:96, :])
                    TT("dve", tmp[P, 6, :], ps[pa][P, :], cosT[P, tsl(tt)], ALU.mult, [PSB[pa], "cosT"], [("t", 6)])
                    TT("dve", tmp[P, 7, :], ps[pb2][P, :], sinT[P, tsl(tt)], ALU.mult, [PSB[pb2], "sinT"], [("t", 7)])
                    TT("pool", KT[P, 0, tsl(tt)], tmp[P, 6, :], tmp[P, 7, :], ALU.add, [("t", 6), ("t", 7)], [("kpe", tt)])
                    CP("pool", KT[P, 1, tsl(tt)], KT[P, 0, tsl(tt)], [("kpe", tt)], [("kpe1", tt)])
                S.barrier_all()

                def load_pair(c, l=l):
                    hb_ = c % 2
                    wb = wbuf[:, hb_ * 4096:(hb_ + 1) * 4096]
                    DMA(wb.rearrange("p (k q) -> p k q", k=8), winB_s[l][:, c, :, :], [("winB", l)], ["wbA" if hb_ == 0 else "wbB"])
                    sm = wsm[:, hb_, :]
                    DMA(sm[:, 0:576].rearrange("p (k q) -> p k q", k=3), wuqP_s[l][:, c, :, :], [("wuqP", l)], [("wsm", hb_, 0)])
                    DMA(sm[:, 576:1152], wqrP_s[l][:, c * 576:(c + 1) * 576], [("wqrP", l)], [("wsm", hb_, 1)])
                    DMA(sm[:, 1152:1664].rearrange("p (k q) -> p k q", k=2), wukvP_s[l][:, c, :, :], [("wukvP", l)], [("wsm", hb_, 2)])
                    DMA(sm[:, 1664:1792], gxw_s[l][:, c * 128:(c + 1) * 128], [("gxws", l)], [("wsm", hb_, 3)])
                    DMA(sm[:, 1792:1920], gaw_s[l][:, c * 128:(c + 1) * 128], [("gaws", l)], [("wsm", hb_, 4)])

                nproj[0] = 2
                MS("pool", VA[:, :, 0, 64:128], 1.0, ["VA"])
                MS("pool", VA[:, :, 1, 0:64], 1.0, ["VA"])
                def make_rnn(c, l=l):
                    hb_ = c % 2
                    WB = "wbA" if hb_ == 0 else "wbB"
                    wc = wbuf[:, hb_ * 4096:(hb_ + 1) * 4096].rearrange("p (k j q) -> p k j q", k=8, j=4)
                    sm = wsm[:, hb_, :]
                    gxw = sm[:, 1664:1792]
                    gaw = sm[:, 1792:1920]
                    cwc = lambda tap: pcol(O_CW + tap * 8 + c)
                    def rnn_p1(qt):
                        xh = [xhi[:, k, tsl(qt)] for k in range(8)]
                        xrd = [XB(k, qt) for k in range(8)]
                        pc = lambda o: pvd[:, l * 32 + o + c: l * 32 + o + c + 1]
                        px = rbank()
                        mm_group(px, [wc[:, k, 0, :] for k in range(8)], xh, xrd + [WB])
                        yield
                        X = ps[px]
                        cv = tmp[:, 0, :]
                        TSC("dve", cv, X[:], cwc(3), pcol(O_CB + c), ALU.mult, ALU.add, [PSB[px], "pv"], [("t", 0)])
                        yield
                        pg = rbank()
                        mm_group(pg, [wc[:, k, 1, :] for k in range(8)], xh, xrd + [WB])
                        yield
                        for sh, tap in ((1, 2), (2, 1), (3, 0)):
                            STT("dve", cv[:, sh:TS], X[:, 0:TS - sh], cwc(tap), cv[:, sh:TS], ALU.mult, ALU.add,
                                [PSB[px], ("t", 0), "pv"], [("t", 0)])
                            yield
                        if qt > 0:
                            for sh, tap in ((1, 2), (2, 1), (3, 0)):
                                STT("dve", cv[:, 0:sh], hr[:, 3 - sh:3], cwc(tap), cv[:, 0:sh], ALU.mult, ALU.add,
                                    ["hr", ("t", 0), "pv"], [("t", 0)])
                        if qt < NT - 1:
                            ACT(hr[:, 0:3], X[:, TS - 3:TS], AF.Copy, [PSB[px]], ["hr"])
                        yield
                        CP("pool", PT[:, 4, :], cv, [("t", 0)], [("pt", 4)])
                        ACT(tmp[:, 3, :], ps[pg][:], AF.Square, [PSB[pg]], [("t", 3)])
                        ACT(tmp[:, 4, :], ps[pg][:], AF.Copy, [PSB[pg]], [("t", 4)])
                        yield
                        TSC("dve", tmp[:, 3, :], tmp[:, 3, :], 0.044715, 1.0, ALU.mult, ALU.add, [("t", 3)], [("t", 3)])
                        yield
                        TT("dve", tmp[:, 3, :], tmp[:, 3, :], tmp[:, 4, :], ALU.mult, [("t", 3), ("t", 4)], [("t", 3)])
                        yield
                        pgx = rbank()
                        mm_group(pgx, [gxw], [PT[:, 4, :]], [("pt", 4), ("wsm", hb_, 3)])
                        pga = rbank()
                        mm_group(pga, [gaw], [PT[:, 4, :]], [("pt", 4), ("wsm", hb_, 4)])
                        yield
                        ACT(tmp[:, 3, :], tmp[:, 3, :], AF.Tanh, [("t", 3)], [("t", 3)], scale=0.7978845608028654)
                        yield
                        ACT(tmp[:, 1, :], ps[pgx][:], AF.Tanh, [PSB[pgx], "pvd"], [("t", 1)], bias=pc(16), scale=0.5)
                        yield
                        ACT(tmp[:, 2, :], ps[pga][:], AF.Tanh, [PSB[pga], "pvd"], [("t", 2)], bias=pc(24), scale=0.5)
                        STT("dve", tmp[:, 4, :], tmp[:, 3, :], 1.0, tmp[:, 4, :], ALU.add, ALU.mult, [("t", 3), ("t", 4)], [("t", 4)])
                        yield
                        ACT(tmp[:, 3, :], tmp[:, 2, :], AF.Exp, [("t", 2), "pvd"], [("t", 3)], bias=pc(0), scale=pc(0))
                        yield
                        ACT(tmp[:, 2, :], tmp[:, 2, :], AF.Exp, [("t", 2), "pvd"], [("t", 2)], bias=pc(8), scale=pc(8))
                        STT("dve", tmp[:, 1, :], tmp[:, 1, :], 1.0, tmp[:, 0, :], ALU.add, ALU.mult, [("t", 1), ("t", 0)], [("t", 1)])
                        yield
                        ACT(tmp[:, 2, :], tmp[:, 2, :], AF.Sqrt, [("t", 2)], [("t", 2)], bias=1.0, scale=-1.0)
                        yield
                        STT("dve", tmp[:, 1, :], tmp[:, 1, :], 0.5, tmp[:, 2, :], ALU.mult, ALU.mult, [("t", 1), ("t", 2)], [("t", 1)])
                        yield
                        init = 0.0 if qt == 0 else hlast[:, 0:1]
                        S.op("dve", "tensor_tensor_scan", [("t", 3), ("t", 1), "hlast"], [("t", 2)],
                             out=tmp[:, 2, :], data0=tmp[:, 3, :], data1=tmp[:, 1, :], initial=init, op0=ALU.mult, op1=ALU.add)
                        CP("dve", hlast[:, 0:1], tmp[:, 2, TS - 1:TS], [("t", 2)], ["hlast"])
                        yield
                        TT("pool", tmp[:, 4, :], tmp[:, 4, :], tmp[:, 2, :], ALU.mult, [("t", 4), ("t", 2)], [("t", 4)])
                        yield

                    def rnn_p2(qt):
                        xh = [xhi[:, k, tsl(qt)] for k in range(8)]
                        xrd = [XB(k, qt) for k in range(8)]
                        pa_ = rbank()
                        mm_group(pa_, [wc[:, k, 2, :] for k in range(8)], xh, xrd + [WB])
                        yield
                        ACT(tmp[:, 1, :], ps[pa_][:], AF.Tanh, [PSB[pa_]], [("t", 1)], scale=0.5)
                        yield
                        pbg = rbank()
                        mm_group(pbg, [wc[:, k, 3, :] for k in range(8)], xh, xrd + [WB])
                        yield
                        STT("dve", tmo[:, 0, :], tmp[:, 1, :], 1.0, tmp[:, 4, :], ALU.add, ALU.mult, [("t", 4), ("t", 1)], ["tM"])
                        ACT(tmo[:, 1, :], ps[pbg][:], AF.Tanh, [PSB[pbg]], ["tB"], scale=0.5)
                        yield


                    return [g for q_ in range(NT) for g in (rnn_p1(q_), rnn_p2(q_))]

                gq = []

                def pump(n, limit):
                    k = 0
                    while k < n and gq_i[0] <= limit and gq_i[0] < len(gq):
                        try:
                            next(gq[gq_i[0]])
                            k += 1
                        except StopIteration:
                            gq_i[0] += 1

                def flush(limit):
                    while gq_i[0] <= limit and gq_i[0] < len(gq):
                        try:
                            next(gq[gq_i[0]])
                        except StopIteration:
                            gq_i[0] += 1

                gq_i = [0]
                load_pair(0)
                gq += make_rnn(0)
                for c in range(8):
                    hb_ = c % 2
                    WB = "wbA" if hb_ == 0 else "wbB"
                    if c + 1 < 8:
                        load_pair(c + 1)
                        gq += make_rnn(c + 1)
                    wc = wbuf[:, hb_ * 4096:(hb_ + 1) * 4096].rearrange("p (k j q) -> p k j q", k=8, j=4)
                    sm = wsm[:, hb_, :]
                    wuq = sm[:, 0:576].rearrange("p (k q) -> p k q", k=3)
                    wqr = sm[:, 576:1152].rearrange("p (k h q) -> p k h q", k=3, h=2)
                    wukv = sm[:, 1152:1664].rearrange("p (k q) -> p k q", k=2)
                    gxw = sm[:, 1664:1792]
                    gaw = sm[:, 1792:1920]
                    cwc = lambda tap, c=c: pcol(O_CW + tap * 8 + c)
                    for hh in range(2):
                        for tt in range(NT):
                            pb = abank()
                            mm_group(pb, [wukv[:, k, hh * 128: hh * 128 + 64] for k in range(2)],
                                     [kvn[:, k, tsl(tt)] for k in range(2)], [("kvn", 0, tt), ("kvn", 1, tt), ("wsm", hb_, 2)],
                                     out_ap=ps[pb][0:64, :])
                            ACT(KT[0:64, hh, tsl(tt)], ps[pb][0:64, :], AF.Copy, [PSB[pb]], [("KT", hh, tt)])
                    for g4 in range(4):
                        pb = abank()
                        pv4 = ps[pb][:].rearrange("p (t h d) -> p t h d", t=4, h=2)
                        for t4 in range(4):
                            tb = g4 * 4 + t4
                            for k in range(2):
                                MM(pv4[:, t4, :, :], kvn[:, k, tb * 128:(tb + 1) * 128],
                                   wukv[:, k, :].rearrange("p (h d) -> p h d", h=2)[:, :, 64:128], k == 0, k == 1,
                                   [("kvn", k, g4), ("wsm", hb_, 2)], [PSB[pb]])
                        ACT(VA[:, g4 * 4:(g4 + 1) * 4, 0, 0:64], pv4[:, :, 0, :], AF.Copy, [PSB[pb]], [("VA", g4)])
                        CP("dve", VA[:, g4 * 4:(g4 + 1) * 4, 1, 64:128], pv4[:, :, 1, :], [PSB[pb]], [("VA", g4)])
                    nproj[0] = 2
                    for qt in range(NT):
                        LIM = min(8 * c + 2 * qt + 2, 8 * c + 7) if not XC_RUNAHEAD else 8 * c + 2 * qt + 2
                        pump(2, LIM)
                        qr = [("qn", k, qt) for k in range(3)]
                        qrh = [qn[:, k, tsl(qt)] for k in range(3)]
                        for hh in range(2):
                            pq = 4 + 2 * hh
                            mm_group(pq, [wuq[:, k, hh * 96:(hh + 1) * 96] for k in range(3)], qrh,
                                     qr + [("wsm", hb_, 0)], out_ap=ps[pq][0:96, :])
                            pr = 5 + 2 * hh
                            mm_group(pr, [wqr[:, k, hh, :] for k in range(3)], qrh,
                                     qr + [("wsm", hb_, 1)], out_ap=ps[pr][0:96, :])
                            ACT(QT[0:64, hh, :], ps[pq][0:64, :], AF.Copy, [PSB[pq]], [("QT", hh)])
                            TT("dve", tmp[P, 5, :], ps[pq][P, :], cosT[P, tsl(qt)], ALU.mult, [PSB[pq], "cosT"], [("t", 5)])
                            TT("dve", tmp[P, 6, :], ps[pr][P, :], sinT[P, tsl(qt)], ALU.mult, [PSB[pr], "sinT"], [("t", 6)])
                            TT("pool", QT[P, hh, :], tmp[P, 5, :], tmp[P, 6, :], ALU.add, [("t", 5), ("t", 6)], [("QT", hh)])
                            pump(2, LIM)
                        nkb = 4 * qt + 4
                        for i in range(nkb + 1):
                            if i < nkb:
                                kb = i
                                n0 = max(kb - 4 * qt, 0) * 128
                                sl = kb % 2
                                for hh in range(2):
                                    bk = 4 + 2 * sl + hh
                                    MM(ps[bk][:, n0:TS], KT[0:96, hh, kb * 128:(kb + 1) * 128], QT[0:96, hh, n0:TS], True, True,
                                       [("KT", hh, kb // 4), ("kpe" if hh == 0 else "kpe1", kb // 4), ("QT", hh)], [PSB[bk]])
                            if i >= 1:
                                kb = i - 1
                                j = kb - 4 * qt
                                n0 = max(j, 0) * 128
                                sl = kb % 2
                                ACT(PT4[:, sl, :, n0:TS], ppair[sl][:, :, n0:TS], AF.Exp,
                                    [PSB[4 + 2 * sl], PSB[5 + 2 * sl]], [("pt", sl)], scale=SCALE)
                                if j >= 0:
                                    for hh in range(2):
                                        TT("pool", PT4[:, sl, hh, n0:n0 + 128], PT4[:, sl, hh, n0:n0 + 128], trib[:], ALU.mult,
                                           [("pt", sl), "trib"], [("pt", sl)])
                                for hh in range(2):
                                    MM(ps[2 + hh][:, n0:TS], VA[:, kb, hh, :], PT4[:, sl, hh, n0:TS], kb == 0, kb == nkb - 1,
                                       [("VA", kb // 4), ("pt", sl), "VA"], [PSB[2 + hh]])
                            pump(3, LIM)
                        RCP(tmp[0:64, 7, :], ps[2][64:128, :], [PSB[2]], [("t", 7)])
                        TT("dve", tmp[0:64, 8, :], ps[2][0:64, :], tmp[0:64, 7, :], ALU.mult, [PSB[2], ("t", 7)], [("t", 8)])
                        RCP(tmp[64:128, 7, :], ps[3][0:64, :], [PSB[3]], [("t", 7)])
                        TT("dve", tmp[64:128, 8, :], ps[3][64:128, :], tmp[64:128, 7, :], ALU.mult, [PSB[3], ("t", 7)], [("t", 8)])
                        flush(8 * c + 2 * qt + 1)
                        STT("dve", tmp[:, 8, :], tmo[:, 1, :], 1.0, tmp[:, 8, :], ALU.add, ALU.mult, [("t", 8), "tB"], [("t", 8)])
                        STT("dve", tmp[:, 8, :], tmo[:, 0, :], 0.5, tmp[:, 8, :], ALU.mult, ALU.add, [("t", 8), "tM"], [("t", 8)])
                        ACT(merged[:, c, tsl(qt)], tmp[:, 8, :], AF.Copy, [("t", 8)], [("mg", c, qt)], scale=0.5)
                nproj[0] = 3
                S.barrier_all()

                def stat_evac(pm, pq_, tm, tr):
                    TSC("dve", tmp[:, tm, :], ps[pm][:], 1.0 / D, None, ALU.mult, None, [PSB[pm]], [("t", tm)])
                    TT("pool", tmp[:, tr, :], tmp[:, tm, :], tmp[:, tm, :], ALU.mult, [("t", tm)], [("t", tr)])
                    STT("dve", tmp[:, tr, :], ps[pq_][:], 1.0 / D, tmp[:, tr, :], ALU.mult, ALU.subtract,
                        [PSB[pq_], ("t", tr)], [("t", tr)])
                    ACT(tmp[:, tr, :], tmp[:, tr, :], AF.Sqrt, [("t", tr), "cst"], [("t", tr)], bias=epsc, scale=1.0)
                    RCP(tmp[:, tr, :], tmp[:, tr, :], [("t", tr)], [("t", tr)])

                def ln_norm_gen(tt, tm, tr, og, ob, zi):
                    zt = tmp[:, zi, :]
                    ZB = ("t", zi)
                    for m in range(8):
                        TT("pool", zt, xhi[:, m, tsl(tt)], xlo[:, m, tsl(tt)], ALU.add, [XB(m, tt), XL(m, tt)], [ZB])
                        yield
                        TT("dve", zt, zt, tmp[:, tm, :], ALU.subtract, [ZB, ("t", tm)], [ZB])
                        yield
                        TT("pool", zt, zt, tmp[:, tr, :], ALU.mult, [ZB, ("t", tr)], [ZB])
                        yield
                        ACT(zt, zt, AF.Identity, [ZB, "pv"], [ZB], bias=pcol(ob + m), scale=pcol(og + m))
                        yield
                        ACT(xhi[:, m, tsl(tt)], zt, AF.Copy, [ZB], [XB(m, tt)])
                        yield
                        TT("dve", xlo[:, m, tsl(tt)], zt, xhi[:, m, tsl(tt)], ALU.subtract, [ZB, XB(m, tt)], [XL(m, tt)])
                        yield

                def gpump(gen, n):
                    if gen is None:
                        return
                    for _ in range(n):
                        try:
                            next(gen)
                        except StopIteration:
                            return

                def chain2(g1, g2):
                    yield from g1
                    yield from g2

                rcnt = [0]

                def resid(pb, m, tt, pm, pq_):
                    k = rcnt[0]
                    rcnt[0] += 1
                    zt = tmp[:, k % 2, :]
                    ZB = ("t", k % 2)
                    STT("dve", zt, xhi[:, m, tsl(tt)], ALPHA, ps[pb][:], ALU.mult, ALU.add, [XB(m, tt), PSB[pb]], [ZB])
                    STT("dve", zt, xlo[:, m, tsl(tt)], ALPHA, zt, ALU.mult, ALU.add, [XL(m, tt), ZB], [ZB])
                    ACT(xhi[:, m, tsl(tt)], zt, AF.Copy, [ZB], [XB(m, tt)])
                    TT("pool", xlo[:, m, tsl(tt)], zt, xhi[:, m, tsl(tt)], ALU.subtract, [ZB, XB(m, tt)], [XL(m, tt)])
                    sq = PT[:, k % 4, :]
                    ACT(sq, zt, AF.Square, [ZB], [("pt", k % 4)])

                    def stats():
                        MM(ps[pm][:], onesb[:], xhi[:, m, tsl(tt)], m == 0, m == 7, [XB(m, tt), "onesb"], [PSB[pm]])
                        MM(ps[pq_][:], onesb[:], sq, m == 0, m == 7, [("pt", k % 4), "onesb"], [PSB[pq_]])
                    return stats

                wo = wbuf[:].rearrange("p (k q) -> p k q", k=8)
                DMA(wo, wout_s[l], [("wout", l)], ["wbA", "wbB"])
                gen = None
                for tt in range(NT):
                    pm, pq_ = (4, 5) if tt % 2 == 0 else (6, 7)
                    pending = None
                    for m in range(8):
                        pb = pbank()
                        mm_group(pb, [wo[:, cc, m * 128:(m + 1) * 128] for cc in range(8)], [merged[:, cc, tsl(tt)] for cc in range(8)],
                                 [("mg", cc, tt) for cc in range(8)] + ["wbA"])
                        if pending is not None:
                            pending()
                        pending = resid(pb, m, tt, pm, pq_)
                        gpump(gen, 6)
                    pending()
                    gpump(gen, 1000)
                    stat_evac(pm, pq_, 4, 5)
                    gen = ln_norm_gen(tt, 4, 5, O_L1G, O_L1B, 8)
                gpump(gen, 1000)
                S.barrier_all()

                def load_up(j, l=l):
                    bi = j % 4
                    DMA(wbuf[:, bi * 2048:(bi + 1) * 2048].rearrange("p (k q) -> p k q", k=8), wupP_s[l][:, j, :, :],
                        [("wupP", l)], [("wup", bi)])

                def load_dn(m, l=l):
                    DMA(wdn_sb[m % 2], wdnP_s[l][:, m, :, :], [("wdnP", l)], [("wdn", m % 2)])

                gen = None
                for half in range(2):
                    for j in range(3):
                        load_up(j)
                    for j in range(24):
                        if j + 3 < 24:
                            load_up(j + 3)
                        wu = wbuf[:, (j % 4) * 2048:(j % 4 + 1) * 2048].rearrange("p (k g q) -> p k g q", k=8, g=2)
                        for t2 in range(2):
                            tt = half * 2 + t2
                            xh = [xhi[:, k, tsl(tt)] for k in range(8)]
                            xrd = [XB(k, tt) for k in range(8)]
                            for gv in range(2):
                                jj = gv * 24 + j
                                pb = pbank()
                                mm_group(pb, [wu[:, k, gv, :] for k in range(8)], xh, xrd + [("wup", j % 4)])
                                ti = 2 * t2 + gv
                                ht = tmp[:, ti, :]
                                HB = ("t", ti)
                                fw = lambda tap, jj=jj: pcol(O_FCW + tap * 48 + jj)
                                ACT(ht, ps[pb][:], AF.Identity, [PSB[pb], "pv"], [HB], bias=pcol(O_FCB + jj), scale=fw(2))
                                STT("dve", ht[:, 1:TS], ps[pb][:, 0:TS - 1], fw(1), ht[:, 1:TS], ALU.mult, ALU.add, [PSB[pb], HB, "pv"], [HB])
                                STT("dve", ht[:, 2:TS], ps[pb][:, 0:TS - 2], fw(0), ht[:, 2:TS], ALU.mult, ALU.add, [PSB[pb], HB, "pv"], [HB])
                                if tt > 0:
                                    STT("dve", ht[:, 0:1], hal[:, jj, 1:2], fw(1), ht[:, 0:1], ALU.mult, ALU.add, [("hal", jj), HB, "pv"], [HB])
                                    STT("dve", ht[:, 0:2], hal[:, jj, 0:2], fw(0), ht[:, 0:2], ALU.mult, ALU.add, [("hal", jj), HB, "pv"], [HB])
                                if tt < NT - 1:
                                    ACT(hal[:, jj, :], ps[pb][:, TS - 2:TS], AF.Copy, [PSB[pb]], [("hal", jj)])
                            tg, tv = 2 * t2, 2 * t2 + 1
                            ACT(tmp[:, tg, :], tmp[:, tg, :], AF.Gelu_apprx_tanh, [("t", tg)], [("t", tg)])
                            TT("pool", abuf[:, j, t2 * TS:(t2 + 1) * TS], tmp[:, tg, :], tmp[:, tv, :], ALU.mult,
                               [("t", tg), ("t", tv)], [("a", j, t2)])
                            gpump(gen, 2)
                    gpump(gen, 1000)
                    load_dn(0)
                    pending = None
                    for m in range(8):
                        if m + 1 < 8:
                            load_dn(m + 1)
                        wd = wdn_sb[m % 2]
                        for t2 in range(2):
                            tt = half * 2 + t2
                            pb = pbank()
                            mm_group(pb, [wd[:, j, :] for j in range(24)], [abuf[:, j, t2 * TS:(t2 + 1) * TS] for j in range(24)],
                                     [("a", j, t2) for j in range(24)] + [("wdn", m % 2)])
                            if pending is not None:
                                pending()
                            pending = resid(pb, m, tt, 4 + 2 * t2, 5 + 2 * t2)
                    pending()
                    stat_evac(4, 5, 4, 5)
                    stat_evac(6, 7, 6, 7)
                    gen = chain2(ln_norm_gen(half * 2, 4, 5, O_L2G, O_L2B, 8), ln_norm_gen(half * 2 + 1, 6, 7, O_L2G, O_L2B, 8))
                gpump(gen, 1000)
                S.barrier_all()

            S.barrier_all()
            cnt = 0
            for tt in range(NT):
                for g in range(2):
                    for kk in range(4):
                        m = g * 4 + kk
                        TT("pool", tmp[:, kk, :], xhi[:, m, tsl(tt)], xlo[:, m, tsl(tt)], ALU.add, [XB(m, tt), XL(m, tt)], [("t", kk)])
                    for t4 in range(4):
                        tb = tt * 4 + t4
                        pb = pbank()
                        for kk in range(4):
                            TR(ps[pb][:, kk * 128:(kk + 1) * 128], tmp[:, kk, t4 * 128:(t4 + 1) * 128], [("t", kk), "cst"], [PSB[pb]])
                        oi = 4 + cnt % 4
                        cnt += 1
                        if cnt % 2 == 0:
                            ACT(tmp[:, oi, :], ps[pb][:], AF.Copy, [PSB[pb]], [("t", oi)])
                        else:
                            CP("dve", tmp[:, oi, :], ps[pb][:], [PSB[pb]], [("t", oi)])
                        DMA(y_d[s, tb * 128:(tb + 1) * 128, g * 512:(g + 1) * 512], tmp[:, oi, :], [("t", oi)], [("y", s, tb, g)])
            S.barrier_all()
        S.barrier_all()
        S.run()
    return nc


def _fm(v):
    v = np.asarray(v, np.float32)
    return np.ascontiguousarray(v.reshape(-1, 128).T)


def _host_params(inp, nl):
    pvs = np.zeros((128, nl * PL), np.float32)
    for l in range(nl):
        b = l * PL
        for tap in range(4):
            pvs[:, b + O_CW + tap * 8: b + O_CW + tap * 8 + 8] = _fm(inp["conv_w"][l, tap])
        pvs[:, b + O_CB: b + O_CB + 8] = _fm(inp["conv_b"][l])
        pvs[:, b + O_GXB: b + O_GXB + 8] = _fm(inp["gx_b"][l])
        pvs[:, b + O_GAB: b + O_GAB + 8] = _fm(inp["ga_b"][l])
        pvs[:, b + O_LAM: b + O_LAM + 8] = _fm(inp["lru_lambda"][l])
        pvs[:, b + O_L1G: b + O_L1G + 8] = _fm(inp["ln1_g"][l])
        pvs[:, b + O_L1B: b + O_L1B + 8] = _fm(inp["ln1_b"][l])
        pvs[:, b + O_L2G: b + O_L2G + 8] = _fm(inp["ln2_g"][l])
        pvs[:, b + O_L2B: b + O_L2B + 8] = _fm(inp["ln2_b"][l])
        for tap in range(3):
            pvs[:, b + O_FCW + tap * 48: b + O_FCW + tap * 48 + 48] = _fm(inp["ffn_conv_w"][l, tap])
        pvs[:, b + O_FCB: b + O_FCB + 48] = _fm(inp["ffn_conv_b"][l])
        pvs[:, b + O_QG: b + O_QG + 3] = _fm(inp["q_norm_g"][l])
        pvs[:, b + O_KVG: b + O_KVG + 2] = _fm(inp["kv_norm_g"][l])
    return pvs


def _host_bd(w, nl):
    w = np.asarray(w, np.float32)
    out = np.zeros((nl, 128, 8, 128), np.float32)
    for c in range(8):
        for b in range(2):
            out[:, b * 64:(b + 1) * 64, c, b * 64:(b + 1) * 64] = w[:, 2 * c + b]
    return out.reshape(nl, 128, 1024)


def _consts():
    c = np.zeros((128, 258), np.float32)
    c[:, 0:128] = np.eye(128, dtype=np.float32)
    k = np.arange(128)[:, None]
    q = np.arange(128)[None, :]
    c[:, 128:256] = (q >= k).astype(np.float32)
    inv = (10000.0 ** (-np.arange(0, 32, 2, dtype=np.float32) / np.float32(32))).astype(np.float32)
    for p in range(64, 96):
        c[p, 256] = inv[(p - 64) % 16]
    c[:, 257] = EPS
    return c


_NC_CACHE = {}
_TRACE = False
_LAST = [None]


def run(inp, NS, layers, ncores, seq_of_core):
    nl = L_ALL
    key = (NS, tuple(layers))
    if key not in _NC_CACHE:
        _NC_CACHE[key] = build(NS, list(layers))
    nc = _NC_CACHE[key]
    shared = {
        "w_in": np.ascontiguousarray(inp["w_in"], np.float32),
        "w_uq": np.ascontiguousarray(inp["w_uq"], np.float32),
        "w_ukv": np.ascontiguousarray(inp["w_ukv"], np.float32),
        "w_out": np.ascontiguousarray(inp["w_out"], np.float32),
        "w_up": np.ascontiguousarray(inp["w_up"], np.float32),
        "w_down": np.ascontiguousarray(inp["w_down"], np.float32),
        "gxw": _host_bd(inp["gx_w"], nl),
        "gaw": _host_bd(inp["ga_w"], nl),
        "pv": _host_params(inp, nl),
        "cst": _consts(),
    }
    x = np.asarray(inp["x"], np.float32)
    pos = np.asarray(inp["positions"], np.int32)
    in_maps = []
    for ci in range(ncores):
        seqs = seq_of_core[ci]
        m = dict(shared)
        m["x"] = np.ascontiguousarray(x[seqs])
        m["pos"] = np.ascontiguousarray(np.broadcast_to(pos[seqs][:, None, :], (len(seqs), 32, T)))
        in_maps.append(m)
    res = run_bass_kernel_spmd(nc, in_maps, core_ids=list(range(ncores)), trace=_TRACE)
    _LAST[0] = res
    return [r["y"] for r in res.results]


def kernel(**inputs):
    B = inputs["x"].shape[0]
    ncores = 8
    NS = B // ncores
    seq_of_core = [list(range(ci * NS, (ci + 1) * NS)) for ci in range(ncores)]
    outs = run(inputs, NS, list(range(L_ALL)), ncores, seq_of_core)
    return np.concatenate(outs, axis=0).astype(np.float32)
```

```python
from contextlib import ExitStack
import math
import numpy as np
import concourse.bass as bass
import concourse.mybir as mybir
from concourse.bass_utils import run_bass_kernel_spmd

F32 = mybir.dt.float32
BF16 = mybir.dt.bfloat16
I32 = mybir.dt.int32
AF = mybir.ActivationFunctionType
ALU = mybir.AluOpType

ENGS = ["pe", "act", "dve", "pool", "sp"]

D = 1024
T = 2048
TS = 512
NT = T // TS
L_ALL = 4
INW = 4768
DFF = 3072
ALPHA = float((2 * L_ALL) ** 0.25)
EPS = 1e-6
SCALE = float(96 ** -0.5)
PL = 296
O_CW, O_CB, O_GXB, O_GAB, O_LAM = 0, 32, 40, 48, 56
O_L1G, O_L1B, O_L2G, O_L2B = 64, 72, 80, 88
O_FCW, O_FCB, O_QG, O_KVG = 96, 240, 288, 291
PAIR_EXP = True
XC_RUNAHEAD = True
TRANSITIVE = True
TWO_PI = float(2 * math.pi)
PI = float(math.pi)


class Buf:
    __slots__ = ("name", "lw", "rd")

    def __init__(self, name=""):
        self.name = name
        self.lw = None
        self.rd = {}


class Sched:
    K = 8
    R = 32

    def __init__(self, nc, stack):
        self.nc = nc
        self.sems = {e: [stack.enter_context(nc.semaphore(f"s_{e}_{i}")) for i in range(self.K)]
                     for e in ENGS}
        self.dsems = [stack.enter_context(nc.semaphore(f"d_{i}")) for i in range(self.R)]
        self.cnt = {e: 0 for e in ENGS}
        self.seen = {e: {f: 0 for f in ENGS} for e in ENGS}
        self.seen_d = {e: [0] * self.R for e in ENGS}
        self.clock = {e: [] for e in ENGS}
        self.nd = 0
        self.thunks = {e: [] for e in ENGS}
        self.bufs = {}

    def b(self, key):
        x = self.bufs.get(key)
        if x is None:
            x = self.bufs[key] = Buf(str(key))
        return x

    def _need(self, e, tok, out):
        if tok is None:
            return
        if tok[0] == "e":
            _, f, i = tok
            if f == e and e == "pe":
                return
            if self.seen[e][f] >= i + 1:
                return
            out.append(tok)
        else:
            n = tok[1]
            if self.seen_d[e][n % self.R] >= n // self.R + 1:
                return
            out.append(tok)

    def _emit_waits(self, e, toks):
        best = {}
        dm = {}
        for t in toks:
            if t[0] == "e":
                best[t[1]] = max(best.get(t[1], -1), t[2])
            else:
                s = t[1] % self.R
                dm[s] = max(dm.get(s, -1), t[1])
        for f, i in best.items():
            if self.seen[e][f] >= i + 1:
                continue
            self.thunks[e].append(("w", self.sems[f][i % self.K], i // self.K + 1))
            self.seen[e][f] = i + 1
            clk = self.clock[f][i] if TRANSITIVE else {}
            for g in (ENGS if TRANSITIVE else []):
                if g != e and clk[g] > self.seen[e][g]:
                    self.seen[e][g] = clk[g]
        for s, n in dm.items():
            v = n // self.R + 1
            if self.seen_d[e][s] >= v:
                continue
            self.thunks[e].append(("w", self.dsems[s], 16 * v))
            self.seen_d[e][s] = v

    def _deps(self, e, reads, writes):
        toks = []
        for r in reads:
            self._need(e, r.lw, toks)
        for w in writes:
            self._need(e, w.lw, toks)
            for f, i in w.rd.items():
                if f == "d":
                    for n in i:
                        self._need(e, ("d", n), toks)
                else:
                    self._need(e, ("e", f, i), toks)
        return toks

    def _bl(self, xs):
        return [x if isinstance(x, Buf) else self.b(x) for x in xs]

    def op(self, e, name, reads=(), writes=(), **kw):
        fn = (name, kw)
        reads = self._bl(reads)
        writes = self._bl(writes)
        self._emit_waits(e, self._deps(e, reads, writes))
        i = self.cnt[e]
        self.cnt[e] = i + 1
        self.thunks[e].append(("i", fn, self.sems[e][i % self.K], 1))
        self.clock[e].append(dict(self.seen[e]))
        tok = ("e", e, i)
        for r in reads:
            if r.rd.get(e, -1) < i:
                r.rd[e] = i
        for w in writes:
            w.lw = tok
            w.rd = {}
        return tok

    def dma(self, out, in_, reads=(), writes=(), q="sp"):
        fn = ("dma_start", dict(out=out, in_=in_))
        reads = self._bl(reads)
        writes = self._bl(writes)
        n = self.nd
        self.nd += 1
        s = n % self.R
        toks = self._deps(q, reads, writes)
        if n >= self.R:
            self._need(q, ("d", n - self.R), toks)
        self._emit_waits(q, toks)
        self.thunks[q].append(("i", fn, self.dsems[s], 16))
        tok = ("d", n)
        for r in reads:
            r.rd.setdefault("d", []).append(n)
        for w in writes:
            w.lw = tok
            w.rd = {}
        return tok

    def barrier_all(self):
        for e in ENGS:
            toks = []
            for f in ENGS:
                if self.cnt[f] > 0:
                    self._need(e, ("e", f, self.cnt[f] - 1), toks)
            for n in range(max(0, self.nd - self.R), self.nd):
                self._need(e, ("d", n), toks)
            self._emit_waits(e, toks)

    def run(self):
        nc = self.nc
        with nc.Block() as block:
            def mk(e):
                def body(eng):
                    for t in self.thunks[e]:
                        if t[0] == "w":
                            eng.wait_ge(t[1], t[2])
                        else:
                            getattr(eng, t[1][0])(**t[1][1]).then_inc(t[2], t[3])
                return body
            block.tensor(mk("pe"))
            block.scalar(mk("act"))
            block.vector(mk("dve"))
            block.gpsimd(mk("pool"))
            block.sync(mk("sp"))


def build(NS, layers, nlw=L_ALL):
    nc = bass.Bass("TRN2", target_bir_lowering=False)
    dt_in = lambda n, s, d=F32: nc.dram_tensor(n, s, d, kind="ExternalInput").ap()
    x_d = dt_in("x", [NS, T, D])
    pos_d = dt_in("pos", [NS, 32, T], I32)
    w_in_d = dt_in("w_in", [nlw, D, INW])
    w_uq_d = dt_in("w_uq", [nlw, 384, 1536])
    w_ukv_d = dt_in("w_ukv", [nlw, 256, 2048])
    w_out_d = dt_in("w_out", [nlw, D, D])
    w_up_d = dt_in("w_up", [nlw, D, 2 * DFF])
    w_dn_d = dt_in("w_down", [nlw, DFF, D])
    gxw_d = dt_in("gxw", [nlw, 128, 8 * 128])
    gaw_d = dt_in("gaw", [nlw, 128, 8 * 128])
    pv_d = dt_in("pv", [128, nlw * PL])
    cst_d = dt_in("cst", [128, 258])
    y_d = nc.dram_tensor("y", [NS, T, D], F32, kind="ExternalOutput").ap()

    scr = lambda n, s: nc.dram_tensor(n, s, BF16, kind="Internal").ap()
    winL_s = scr("winL", [nlw, 128, 8, 672])
    winB_s = scr("winB", [nlw, 128, 8, 8, 512])
    wkrot_s = scr("wkrot", [nlw, 128, 8 * 96])
    wuqP_s = scr("wuqP", [nlw, 128, 8, 3, 192])
    wqrP_s = scr("wqrP", [nlw, 128, 8 * 3 * 192])
    wukvP_s = scr("wukvP", [nlw, 128, 8, 2, 256])
    wout_s = scr("wout", [nlw, 128, 8, 1024])
    wupP_s = scr("wupP", [nlw, 128, 24, 8, 256])
    wdnP_s = scr("wdnP", [nlw, 128, 8, 24, 128])
    gxw_s = scr("gxws", [nlw, 128, 1024])
    gaw_s = scr("gaws", [nlw, 128, 1024])

    with ExitStack() as st:
        S = Sched(nc, st)
        sb = lambda n, s, d: st.enter_context(nc.sbuf_tensor("sb_" + n, s, d))
        xhi = sb("xhi", [128, 8, T], BF16)
        xlo = sb("xlo", [128, 8, T], BF16)
        big = sb("big", [128, 26624], BF16)
        tmp = sb("tmp", [128, 9, TS], F32)
        hr = sb("hr", [128, 4], F32)
        tmo = sb("tmo", [128, 2, TS], F32)
        cosT = sb("cosT", [128, T], F32)
        sinT = sb("sinT", [128, T], F32)
        KT = sb("KT", [128, 2, T], BF16)
        VA = sb("VA", [128, 16, 2, 128], BF16)
        QT = sb("QT", [128, 2, TS], BF16)
        PT = sb("PT", [128, 5, TS], BF16)
        PT4 = PT[:, 0:4, :].rearrange("p (s h) t -> p s h t", s=2)
        wbuf = sb("wbuf", [128, 8192], BF16)
        wsm = sb("wsm", [128, 2, 1920], BF16)
        pv = sb("pv", [128, nlw * PL], F32)
        pvd = sb("pvd", [128, nlw * 32], F32)
        cst = sb("cst", [128, 258], F32)
        onesb = sb("onesb", [128, 128], BF16)
        trib = sb("trib", [128, 128], BF16)
        hal = sb("hal", [128, 48, 2], F32)
        hlast = sb("hlast", [128, 2], F32)
        ps = [st.enter_context(nc.psum_tensor(f"ps{i}", [128, TS], F32)) for i in range(4)]
        ppair = [st.enter_context(nc.psum_tensor(f"pp{i}", [128, 2, TS], F32)) for i in range(2)]
        ps += [ppair[0][:, 0, :], ppair[0][:, 1, :], ppair[1][:, 0, :], ppair[1][:, 1, :]]
        PSB = [S.b(("ps", i)) for i in range(8)]
        ident = cst[:, 0:128]
        invf = cst[:, 256:257]
        epsc = cst[:, 257:258]
        wkr = wsm[:, 1, 0:768].rearrange("p (k d) -> p k d", k=8)

        merged = big[:, 0:16384].rearrange("p (c t) -> p c t", c=8)
        qn = big[:, 16384:22528].rearrange("p (c t) -> p c t", c=3)
        kvn = big[:, 22528:26624].rearrange("p (c t) -> p c t", c=2)
        abuf = big[:, 0:24576].rearrange("p (j t) -> p j t", j=24)
        wdn_sb = [KT[:].rearrange("p a t -> p (a t)")[:, 0:3072].rearrange("p (j q) -> p j q", j=24),
                  VA[:].rearrange("p a b c -> p (a b c)")[:, 0:3072].rearrange("p (j q) -> p j q", j=24)]

        def ACT(out, in_, func, rd, wr, **kw):
            S.op("act", "activation", rd, wr, out=out, in_=in_, func=func, **kw)

        def TT(e, out, in0, in1, op, rd, wr):
            S.op(e, "tensor_tensor", rd, wr, out=out, in0=in0, in1=in1, op=op)

        def TSC(e, out, in0, s1, s2, op0, op1, rd, wr):
            if s2 is None:
                S.op(e, "tensor_scalar", rd, wr, out=out, in0=in0, scalar1=s1, scalar2=None, op0=op0)
            else:
                S.op(e, "tensor_scalar", rd, wr, out=out, in0=in0, scalar1=s1, scalar2=s2, op0=op0, op1=op1)

        def STT(e, out, in0, scalar, in1, op0, op1, rd, wr):
            S.op(e, "scalar_tensor_tensor", rd, wr, out=out, in0=in0, scalar=scalar, in1=in1, op0=op0, op1=op1)

        def CP(e, out, in_, rd, wr):
            S.op(e, "tensor_copy", rd, wr, out=out, in_=in_)

        def RCP(out, in_, rd, wr):
            S.op("dve", "reciprocal", rd, wr, out=out, in_=in_)

        def MM(out, lhsT, rhs, start, stop, rd, wr):
            S.op("pe", "matmul", rd, wr, out=out, lhsT=lhsT, rhs=rhs, start=start, stop=stop)

        def TR(out, in_, rd, wr):
            S.op("pe", "transpose", rd, wr, out=out, in_=in_, identity=ident)

        def MS(e, ap, v, wr):
            S.op(e, "memset", (), wr, ap=ap, constant=v)

        DMA = S.dma
        proj_rr = [0]

        nproj = [3]

        def pbank():
            i = proj_rr[0] % nproj[0]
            proj_rr[0] += 1
            return i

        rb_rr = [0]
        ab_rr = [0]

        def rbank():
            i = rb_rr[0] % 2
            rb_rr[0] += 1
            return i

        def abank():
            i = 4 + ab_rr[0] % 4
            ab_rr[0] += 1
            return i

        def mm_group(pb, lhs_list, rhs_list, reads, out_ap=None):
            n = len(lhs_list)
            o = ps[pb][:] if out_ap is None else out_ap
            for i in range(n):
                MM(o, lhs_list[i], rhs_list[i], i == 0, i == n - 1, reads, [PSB[pb]])

        def tsl(tt):
            return slice(tt * TS, (tt + 1) * TS)

        XB = lambda k, tt: ("xhi", k, tt)
        XL = lambda k, tt: ("xlo", k, tt)

        DMA(cst[:], cst_d, (), ["cst"])
        DMA(pv[:], pv_d, (), ["pv"])
        MS("pool", onesb[:], 1.0, ["onesb"])
        CP("dve", trib[:], cst[:, 128:256], ["cst"], ["trib"])
        MS("pool", hal[:], 0.0, ["hal"])
        for l in layers:
            lam = pv[:, l * PL + O_LAM: l * PL + O_LAM + 8]
            t0 = tmp[:, 0, 0:8]
            ACT(t0, lam, AF.Exp, ["pv"], ["t0"], scale=-1.0)
            ACT(t0, t0, AF.Ln, ["t0"], ["t0"], bias=1.0)
            TSC("dve", pvd[:, l * 32: l * 32 + 8], t0, -4.0, None, ALU.mult, None, ["t0"], ["pvd"])
            TSC("dve", pvd[:, l * 32 + 8: l * 32 + 16], t0, -8.0, None, ALU.mult, None, ["t0"], ["pvd"])
            TSC("dve", pvd[:, l * 32 + 16: l * 32 + 24], pv[:, l * PL + O_GXB: l * PL + O_GXB + 8], 0.5, None, ALU.mult, None, ["pv"], ["pvd"])
            TSC("dve", pvd[:, l * 32 + 24: l * 32 + 32], pv[:, l * PL + O_GAB: l * PL + O_GAB + 8], 0.5, None, ALU.mult, None, ["pv"], ["pvd"])
        S.barrier_all()

        stf = big[:, 0:24576].bitcast(F32).rearrange("p (a b) -> p a b", a=2)
        stb = xhi[:].rearrange("p k t -> p (k t)")[:, 0:12288].rearrange("p (a b) -> p a b", a=2)
        tb16 = tmp[:].rearrange("p a b -> p (a b)").bitcast(BF16)
        wqr_sb = tb16[:, 0:4608].rearrange("p (c k h d) -> p c k h d", c=8, k=3, h=2)
        ccnt = [0]

        def cast(i, out_ap, in_ap, scale=None):
            if scale is None:
                ccnt[0] += 1
                if ccnt[0] % 2 == 0:
                    ACT(out_ap, in_ap, AF.Copy, [("stf", i)], [("stb", i)])
                else:
                    CP("dve", out_ap, in_ap, [("stf", i)], [("stb", i)])
            else:
                TSC("dve", out_ap, in_ap, scale, None, ALU.mult, None, [("stf", i), "pv"], [("stb", i)])

        MS("pool", wkr, 0.0, ["wkr"])
        MS("pool", tb16[:, 0:4608], 0.0, ["wqr_sb"])
        jobs = []
        for l in layers:
            base = l * PL
            for k in range(8):
                def job(i, l=l, k=k):
                    sf = stf[:, i, :]
                    so = stb[:, i, :]
                    soB = so[:, 0:4096].rearrange("p (c j q) -> p c j q", c=8, j=4)
                    for j, b0 in enumerate([0, 1024, 2720, 3744]):
                        cast(i, soB[:, :, j, :], sf[:, b0:b0 + 1024].rearrange("p (c q) -> p c q", c=8))
                    cast(i, so[:, 4096:4768], sf[:, 2048:2720])
                    TSC("pool", wkr[:, k, 64:80], sf[:, 2704:2720], -1.0, None, ALU.mult, None, [("stf", i)], ["wkr"])
                    CP("pool", wkr[:, k, 80:96], sf[:, 2688:2704], [("stf", i)], ["wkr"])
                    DMA(winB_s[l][:, :, k, :], so[:, 0:4096].rearrange("p (c q) -> p c q", c=8), [("stb", i)], [("winB", l)])
                    DMA(winL_s[l][:, k, :], so[:, 4096:4768], [("stb", i)], [("winL", l)])
                    if k == 7:
                        DMA(wkrot_s[l], wsm[:, 1, 0:768], ["wkr"], [("wkrot", l)])
                jobs.append((w_in_d[l, k * 128:(k + 1) * 128, :], lambda i: stf[:, i, 0:INW], job))

            def job_uq(i, l=l, base=base):
                sf3 = stf[:, i, 0:4608].rearrange("p (k q) -> p k q", k=3)
                so3 = stb[:, i, 0:4608].rearrange("p (k q) -> p k q", k=3)
                for k in range(3):
                    g = pv[:, base + O_QG + k: base + O_QG + k + 1]
                    cast(i, so3[:, k, :], sf3[:, k, :], scale=g)
                    sfv = sf3[:, k, :].rearrange("p (c h d) -> p c h d", c=8, h=2)
                    TSC("pool", wqr_sb[:, :, k, :, 64:80], sfv[:, :, :, 80:96], g, -1.0, ALU.mult, ALU.mult,
                        [("stf", i), "pv"], ["wqr_sb"])
                    TSC("pool", wqr_sb[:, :, k, :, 80:96], sfv[:, :, :, 64:80], g, None, ALU.mult, None,
                        [("stf", i), "pv"], ["wqr_sb"])
                    DMA(wuqP_s[l][:, :, k, :], so3[:, k, :].rearrange("p (c q) -> p c q", c=8), [("stb", i)], [("wuqP", l)])
                DMA(wqrP_s[l], tb16[:, 0:4608], ["wqr_sb"], [("wqrP", l)])
            jobs.append((w_uq_d[l].rearrange("(k p) q -> p k q", p=128),
                         lambda i: stf[:, i, 0:4608].rearrange("p (k q) -> p k q", k=3), job_uq))

            def job_ukv(i, l=l, base=base):
                sf2 = stf[:, i, 0:4096].rearrange("p (k q) -> p k q", k=2)
                so2 = stb[:, i, 0:4096].rearrange("p (k q) -> p k q", k=2)
                for k in range(2):
                    g = pv[:, base + O_KVG + k: base + O_KVG + k + 1]
                    cast(i, so2[:, k, :], sf2[:, k, :], scale=g)
                    DMA(wukvP_s[l][:, :, k, :], so2[:, k, :].rearrange("p (c q) -> p c q", c=8), [("stb", i)], [("wukvP", l)])
            jobs.append((w_ukv_d[l].rearrange("(k p) q -> p k q", p=128),
                         lambda i: stf[:, i, 0:4096].rearrange("p (k q) -> p k q", k=2), job_ukv))

            for k0 in range(0, 8, 4):
                def job_out(i, l=l, k0=k0):
                    cast(i, stb[:, i, 0:4096], stf[:, i, 0:4096])
                    DMA(wout_s[l][:, k0:k0 + 4, :], stb[:, i, 0:4096].rearrange("p (k q) -> p k q", k=4), [("stb", i)], [("wout", l)])
                jobs.append((w_out_d[l, k0 * 128:(k0 + 4) * 128, :].rearrange("(k p) q -> p k q", p=128),
                             lambda i: stf[:, i, 0:4096].rearrange("p (k q) -> p k q", k=4), job_out))

            for k in range(8):
                def job_up(i, l=l, k=k):
                    sfv = stf[:, i, :].rearrange("p (g j q) -> p g j q", g=2, j=24)
                    sov = stb[:, i, :].rearrange("p (j g q) -> p j g q", j=24, g=2)
                    cast(i, sov[:, :, 0, :], sfv[:, 0, :, :])
                    cast(i, sov[:, :, 1, :], sfv[:, 1, :, :])
                    DMA(wupP_s[l][:, :, k, :], stb[:, i, :].rearrange("p (j q) -> p j q", j=24), [("stb", i)], [("wupP", l)])
                jobs.append((w_up_d[l, k * 128:(k + 1) * 128, :], lambda i: stf[:, i, 0:6144], job_up))

            for k0 in range(0, 24, 4):
                def job_dn(i, l=l, k0=k0):
                    cast(i, stb[:, i, 0:4096], stf[:, i, 0:4096])
                    so4 = stb[:, i, 0:4096].rearrange("p (k m q) -> p k m q", k=4, m=8)
                    for kk in range(4):
                        DMA(wdnP_s[l][:, :, k0 + kk, :], so4[:, kk, :, :], [("stb", i)], [("wdnP", l)])
                jobs.append((w_dn_d[l, k0 * 128:(k0 + 4) * 128, :].rearrange("(k p) q -> p k q", p=128),
                             lambda i: stf[:, i, 0:4096].rearrange("p (k q) -> p k q", k=4), job_dn))

            for src, dst, nm in [(gxw_d, gxw_s, "gxws"), (gaw_d, gaw_s, "gaws")]:
                def job_g(i, l=l, dst=dst, nm=nm):
                    cast(i, stb[:, i, 0:1024], stf[:, i, 0:1024])
                    DMA(dst[l], stb[:, i, 0:1024], [("stb", i)], [(nm, l)])
                jobs.append((src[l], lambda i: stf[:, i, 0:1024], job_g))

        nj = len(jobs)

        def jload(n):
            i = n % 2
            DMA(jobs[n][1](i), jobs[n][0], (), [("stf", i)])

        if nj:
            jload(0)
        for n in range(nj):
            if n + 1 < nj:
                jload(n + 1)
            jobs[n][2](n % 2)
        S.barrier_all()

        P = slice(64, 96)
        for s in range(NS):
            for tb in range(T // 128):
                tt = tb // 4
                xt = tmp[:, 2 * (tb % 2):2 * (tb % 2) + 2, :].rearrange("p a b -> p (a b)")
                xtb = ("xt", tb % 2)
                DMA(xt, x_d[s, tb * 128:(tb + 1) * 128, :], (), [xtb])
                for g in range(2):
                    pb = pbank()
                    for kk in range(4):
                        k = g * 4 + kk
                        TR(ps[pb][:, kk * 128:(kk + 1) * 128], xt[:, k * 128:(k + 1) * 128], [xtb, "cst"], [PSB[pb]])
                    hi = xhi[:, g * 4:(g + 1) * 4, tb * 128:(tb + 1) * 128]
                    lo = xlo[:, g * 4:(g + 1) * 4, tb * 128:(tb + 1) * 128]
                    pv3 = ps[pb][:].rearrange("p (a b) -> p a b", a=4)
                    hb = [XB(k, tt) for k in range(g * 4, g * 4 + 4)]
                    lb = [XL(k, tt) for k in range(g * 4, g * 4 + 4)]
                    ACT(hi, pv3, AF.Copy, [PSB[pb]], hb)
                    TT("dve", lo, pv3, hi, ALU.subtract, [PSB[pb]] + hb, lb)
            posi = tmp[:, 0:4, :].rearrange("p a b -> p (a b)").bitcast(I32)
            ang = tmp[:, 4:8, :].rearrange("p a b -> p (a b)")
            kf = tmp[:, 0:4, :].rearrange("p a b -> p (a b)")
            S.barrier_all()
            DMA(posi[P, :], pos_d[s], (), ["posi"])
            CP("dve", ang[P, :], posi[P, :], ["posi"], ["ang"])
            TSC("dve", ang[P, :], ang[P, :], invf[P, :], None, ALU.mult, None, ["ang", "cst"], ["ang"])
            TSC("dve", posi[P, :], ang[P, :], 1.0 / TWO_PI, None, ALU.mult, None, ["ang"], ["posi"])
            CP("dve", kf[P, :], posi[P, :], ["posi"], ["posi"])
            STT("dve", ang[P, :], kf[P, :], -TWO_PI, ang[P, :], ALU.mult, ALU.add, ["posi", "ang"], ["ang"])

            def wrap():
                TSC("dve", kf[P, :], ang[P, :], PI, TWO_PI, ALU.is_gt, ALU.mult, ["ang"], ["posi"])
                TT("dve", ang[P, :], ang[P, :], kf[P, :], ALU.subtract, ["ang", "posi"], ["ang"])
                TSC("dve", kf[P, :], ang[P, :], -PI, TWO_PI, ALU.is_lt, ALU.mult, ["ang"], ["posi"])
                TT("dve", ang[P, :], ang[P, :], kf[P, :], ALU.add, ["ang", "posi"], ["ang"])
            wrap()
            ACT(sinT[P, :], ang[P, :], AF.Sin, ["ang"], ["sinT"])
            TSC("dve", ang[P, :], ang[P, :], PI / 2, None, ALU.add, None, ["ang"], ["ang"])
            wrap()
            ACT(cosT[P, :], ang[P, :], AF.Sin, ["ang"], ["cosT"])
            S.barrier_all()

            for li, l in enumerate(layers):
                base = l * PL
                pcol = lambda o, base=base: pv[:, base + o: base + o + 1]
                wlat = wbuf[:, 0:8 * 672].rearrange("p (k q) -> p k q", k=8)
                DMA(wlat, winL_s[l], [("winL", l)], ["wbA", "wbB"])
                DMA(wsm[:, 1, 0:768], wkrot_s[l], [("wkrot", l)], ["wkr"])
                for tt in range(NT):
                    xh = [xhi[:, k, tsl(tt)] for k in range(8)]
                    xrd = [XB(k, tt) for k in range(8)]
                    for (nch, c0, dst, inv_n, nm) in [(3, 0, qn, 1.0 / 384, "qn"), (2, 384, kvn, 1.0 / 256, "kvn")]:
                        for oc in range(nch):
                            pb = pbank()
                            mm_group(pb, [wlat[:, k, c0 + oc * 128: c0 + (oc + 1) * 128] for k in range(8)], xh, xrd + ["wbA"])
                            ACT(tmp[:, oc, :], ps[pb][:], AF.Copy, [PSB[pb]], [("t", oc)])
                            ACT(PT[:, oc, :], ps[pb][:], AF.Square, [PSB[pb]], [("pt", oc)])
                        mm_group(7, [onesb[:]] * nch, [PT[:, oc, :] for oc in range(nch)],
                                 ["onesb"] + [("pt", oc) for oc in range(nch)])
                        ACT(tmp[:, 4, :], ps[7][:], AF.Sqrt, [PSB[7], "cst"], [("t", 4)], bias=epsc, scale=inv_n)
                        RCP(tmp[:, 5, :], tmp[:, 4, :], [("t", 4)], [("t", 5)])
                        for oc in range(nch):
                            TT("pool", dst[:, oc, tsl(tt)], tmp[:, oc, :], tmp[:, 5, :], ALU.mult,
                               [("t", oc), ("t", 5)], [(nm, oc, tt)])
                    pa = pbank()
                    mm_group(pa, [wlat[:, k, 576:672] for k in range(8)], xh, xrd + ["wbA"], out_ap=ps[pa][0:96, :])
                    pb2 = pbank()
                    mm_group(pb2, [wkr[:, k, :] for k in range(8)], xh, xrd + ["wkr"], out_ap=ps[pb2][0:96, :])
                    TT("dve", tmp[P, 6, :], ps[pa][P, :], cosT[P, tsl(tt)], ALU.mult, [PSB[pa], "cosT"], [("t", 6)])
                    TT("dve", tmp[P, 7, :], ps[pb2][P, :], sinT[P, tsl(tt)], ALU.mult, [PSB[pb2], "sinT"], [("t", 7)])
                    TT("pool", KT[P, 0, tsl(tt)], tmp[P, 6, :], tmp[P, 7, :], ALU.add, [("t", 6), ("t", 7)], [("kpe", tt)])
                    CP("pool", KT[P, 1, tsl(tt)], KT[P, 0, tsl(tt)], [("kpe", tt)], [("kpe1", tt)])
                S.barrier_all()

                def load_pair(c, l=l):
                    hb_ = c % 2
                    wb = wbuf[:, hb_ * 4096:(hb_ + 1) * 4096]
                    DMA(wb.rearrange("p (k q) -> p k q", k=8), winB_s[l][:, c, :, :], [("winB", l)], ["wbA" if hb_ == 0 else "wbB"])
                    sm = wsm[:, hb_, :]
                    DMA(sm[:, 0:576].rearrange("p (k q) -> p k q", k=3), wuqP_s[l][:, c, :, :], [("wuqP", l)], [("wsm", hb_, 0)])
                    DMA(sm[:, 576:1152], wqrP_s[l][:, c * 576:(c + 1) * 576], [("wqrP", l)], [("wsm", hb_, 1)])
                    DMA(sm[:, 1152:1664].rearrange("p (k q) -> p k q", k=2), wukvP_s[l][:, c, :, :], [("wukvP", l)], [("wsm", hb_, 2)])
                    DMA(sm[:, 1664:1792], gxw_s[l][:, c * 128:(c + 1) * 128], [("gxws", l)], [("wsm", hb_, 3)])
                    DMA(sm[:, 1792:1920], gaw_s[l][:, c * 128:(c + 1) * 128], [("gaws", l)], [("wsm", hb_, 4)])

                nproj[0] = 2
                MS("pool", VA[:, :, 0, 64:128], 1.0, ["VA"])
                MS("pool", VA[:, :, 1, 0:64], 1.0, ["VA"])
                def make_rnn(c, l=l):
                    hb_ = c % 2
                    WB = "wbA" if hb_ == 0 else "wbB"
                    wc = wbuf[:, hb_ * 4096:(hb_ + 1) * 4096].rearrange("p (k j q) -> p k j q", k=8, j=4)
                    sm = wsm[:, hb_, :]
                    gxw = sm[:, 1664:1792]
                    gaw = sm[:, 1792:1920]
                    cwc = lambda tap: pcol(O_CW + tap * 8 + c)
                    def rnn_p1(qt):
                        xh = [xhi[:, k, tsl(qt)] for k in range(8)]
                        xrd = [XB(k, qt) for k in range(8)]
                        pc = lambda o: pvd[:, l * 32 + o + c: l * 32 + o + c + 1]
                        px = rbank()
                        mm_group(px, [wc[:, k, 0, :] for k in range(8)], xh, xrd + [WB])
                        yield
                        X = ps[px]
                        cv = tmp[:, 0, :]
                        TSC("dve", cv, X[:], cwc(3), pcol(O_CB + c), ALU.mult, ALU.add, [PSB[px], "pv"], [("t", 0)])
                        yield
                        pg = rbank()
                        mm_group(pg, [wc[:, k, 1, :] for k in range(8)], xh, xrd + [WB])
                        yield
                        for sh, tap in ((1, 2), (2, 1), (3, 0)):
                            STT("dve", cv[:, sh:TS], X[:, 0:TS - sh], cwc(tap), cv[:, sh:TS], ALU.mult, ALU.add,
                                [PSB[px], ("t", 0), "pv"], [("t", 0)])
                            yield
                        if qt > 0:
                            for sh, tap in ((1, 2), (2, 1), (3, 0)):
                                STT("dve", cv[:, 0:sh], hr[:, 3 - sh:3], cwc(tap), cv[:, 0:sh], ALU.mult, ALU.add,
                                    ["hr", ("t", 0), "pv"], [("t", 0)])
                        if qt < NT - 1:
                            CP("dve", hr[:, 0:3], X[:, TS - 3:TS], [PSB[px]], ["hr"])
                        yield
                        ACT(PT[:, 4, :], cv, AF.Copy, [("t", 0)], [("pt", 4)])
                        ACT(tmp[:, 3, :], ps[pg][:], AF.Square, [PSB[pg]], [("t", 3)])
                        ACT(tmp[:, 4, :], ps[pg][:], AF.Copy, [PSB[pg]], [("t", 4)])
                        yield
                        TSC("dve", tmp[:, 3, :], tmp[:, 3, :], 0.044715, 1.0, ALU.mult, ALU.add, [("t", 3)], [("t", 3)])
                        yield
                        TT("dve", tmp[:, 3, :], tmp[:, 3, :], tmp[:, 4, :], ALU.mult, [("t", 3), ("t", 4)], [("t", 3)])
                        yield
                        pgx = rbank()
                        mm_group(pgx, [gxw], [PT[:, 4, :]], [("pt", 4), ("wsm", hb_, 3)])
                        pga = rbank()
                        mm_group(pga, [gaw], [PT[:, 4, :]], [("pt", 4), ("wsm", hb_, 4)])
                        yield
                        ACT(tmp[:, 3, :], tmp[:, 3, :], AF.Tanh, [("t", 3)], [("t", 3)], scale=0.7978845608028654)
                        yield
                        ACT(tmp[:, 1, :], ps[pgx][:], AF.Tanh, [PSB[pgx], "pvd"], [("t", 1)], bias=pc(16), scale=0.5)
                        yield
                        ACT(tmp[:, 2, :], ps[pga][:], AF.Tanh, [PSB[pga], "pvd"], [("t", 2)], bias=pc(24), scale=0.5)
                        STT("dve", tmp[:, 4, :], tmp[:, 3, :], 1.0, tmp[:, 4, :], ALU.add, ALU.mult, [("t", 3), ("t", 4)], [("t", 4)])
                        yield
                        ACT(tmp[:, 3, :], tmp[:, 2, :], AF.Exp, [("t", 2), "pvd"], [("t", 3)], bias=pc(0), scale=pc(0))
                        yield
                        ACT(tmp[:, 2, :], tmp[:, 2, :], AF.Exp, [("t", 2), "pvd"], [("t", 2)], bias=pc(8), scale=pc(8))
                        STT("dve", tmp[:, 1, :], tmp[:, 1, :], 1.0, tmp[:, 0, :], ALU.add, ALU.mult, [("t", 1), ("t", 0)], [("t", 1)])
                        yield
                        ACT(tmp[:, 2, :], tmp[:, 2, :], AF.Sqrt, [("t", 2)], [("t", 2)], bias=1.0, scale=-1.0)
                        yield
                        STT("dve", tmp[:, 1, :], tmp[:, 1, :], 0.5, tmp[:, 2, :], ALU.mult, ALU.mult, [("t", 1), ("t", 2)], [("t", 1)])
                        yield
                        init = 0.0 if qt == 0 else hlast[:, 0:1]
                        S.op("dve", "tensor_tensor_scan", [("t", 3), ("t", 1), "hlast"], [("t", 2)],
                             out=tmp[:, 2, :], data0=tmp[:, 3, :], data1=tmp[:, 1, :], initial=init, op0=ALU.mult, op1=ALU.add)
                        CP("dve", hlast[:, 0:1], tmp[:, 2, TS - 1:TS], [("t", 2)], ["hlast"])
                        yield
                        TT("pool", tmp[:, 4, :], tmp[:, 4, :], tmp[:, 2, :], ALU.mult, [("t", 4), ("t", 2)], [("t", 4)])
                        yield

                    def rnn_p2(qt):
                        xh = [xhi[:, k, tsl(qt)] for k in range(8)]
                        xrd = [XB(k, qt) for k in range(8)]
                        pa_ = rbank()
                        mm_group(pa_, [wc[:, k, 2, :] for k in range(8)], xh, xrd + [WB])
                        yield
                        ACT(tmp[:, 1, :], ps[pa_][:], AF.Tanh, [PSB[pa_]], [("t", 1)], scale=0.5)
                        yield
                        pbg = rbank()
                        mm_group(pbg, [wc[:, k, 3, :] for k in range(8)], xh, xrd + [WB])
                        yield
                        STT("dve", tmo[:, 0, :], tmp[:, 1, :], 1.0, tmp[:, 4, :], ALU.add, ALU.mult, [("t", 4), ("t", 1)], ["tM"])
                        ACT(tmo[:, 1, :], ps[pbg][:], AF.Tanh, [PSB[pbg]], ["tB"], scale=0.5)
                        yield


                    return [g for q_ in range(NT) for g in (rnn_p1(q_), rnn_p2(q_))]

                gq = []

                def pump(n, limit):
                    k = 0
                    while k < n and gq_i[0] <= limit and gq_i[0] < len(gq):
                        try:
                            next(gq[gq_i[0]])
                            k += 1
                        except StopIteration:
                            gq_i[0] += 1

                def flush(limit):
                    while gq_i[0] <= limit and gq_i[0] < len(gq):
                        try:
                            next(gq[gq_i[0]])
                        except StopIteration:
                            gq_i[0] += 1

                gq_i = [0]
                load_pair(0)
                gq += make_rnn(0)
                for c in range(8):
                    hb_ = c % 2
                    WB = "wbA" if hb_ == 0 else "wbB"
                    if c + 1 < 8:
                        load_pair(c + 1)
                        gq += make_rnn(c + 1)
                    wc = wbuf[:, hb_ * 4096:(hb_ + 1) * 4096].rearrange("p (k j q) -> p k j q", k=8, j=4)
                    sm = wsm[:, hb_, :]
                    wuq = sm[:, 0:576].rearrange("p (k q) -> p k q", k=3)
                    wqr = sm[:, 576:1152].rearrange("p (k h q) -> p k h q", k=3, h=2)
                    wukv = sm[:, 1152:1664].rearrange("p (k q) -> p k q", k=2)
                    gxw = sm[:, 1664:1792]
                    gaw = sm[:, 1792:1920]
                    cwc = lambda tap, c=c: pcol(O_CW + tap * 8 + c)
                    for hh in range(2):
                        for tt in range(NT):
                            pb = abank()
                            mm_group(pb, [wukv[:, k, hh * 128: hh * 128 + 64] for k in range(2)],
                                     [kvn[:, k, tsl(tt)] for k in range(2)], [("kvn", 0, tt), ("kvn", 1, tt), ("wsm", hb_, 2)],
                                     out_ap=ps[pb][0:64, :])
                            ACT(KT[0:64, hh, tsl(tt)], ps[pb][0:64, :], AF.Copy, [PSB[pb]], [("KT", hh, tt)])
                            pump(2, 8 * c + 1)
                    for g4 in range(4):
                        pb = abank()
                        pv4 = ps[pb][:].rearrange("p (t h d) -> p t h d", t=4, h=2)
                        for t4 in range(4):
                            tb = g4 * 4 + t4
                            for k in range(2):
                                MM(pv4[:, t4, :, :], kvn[:, k, tb * 128:(tb + 1) * 128],
                                   wukv[:, k, :].rearrange("p (h d) -> p h d", h=2)[:, :, 64:128], k == 0, k == 1,
                                   [("kvn", k, g4), ("wsm", hb_, 2)], [PSB[pb]])
                        ACT(VA[:, g4 * 4:(g4 + 1) * 4, 0, 0:64], pv4[:, :, 0, :], AF.Copy, [PSB[pb]], [("VA", g4)])
                        ACT(VA[:, g4 * 4:(g4 + 1) * 4, 1, 64:128], pv4[:, :, 1, :], AF.Copy, [PSB[pb]], [("VA", g4)])
                        pump(2, 8 * c + 1)
                    nproj[0] = 2
                    for qt in range(NT):
                        LIM = min(8 * c + 2 * qt + 2, 8 * c + 7) if not XC_RUNAHEAD else 8 * c + 2 * qt + 2
                        pump(6 if qt == 0 else 2, LIM)
                        qr = [("qn", k, qt) for k in range(3)]
                        qrh = [qn[:, k, tsl(qt)] for k in range(3)]
                        for hh in range(2):
                            pq = 4 + 2 * hh
                            mm_group(pq, [wuq[:, k, hh * 96:(hh + 1) * 96] for k in range(3)], qrh,
                                     qr + [("wsm", hb_, 0)], out_ap=ps[pq][0:96, :])
                            pr = 5 + 2 * hh
                            mm_group(pr, [wqr[:, k, hh, :] for k in range(3)], qrh,
                                     qr + [("wsm", hb_, 1)], out_ap=ps[pr][0:96, :])
                            CP("dve", QT[0:64, hh, :], ps[pq][0:64, :], [PSB[pq]], [("QT", hh)])
                            TT("dve", tmp[P, 5, :], ps[pq][P, :], cosT[P, tsl(qt)], ALU.mult, [PSB[pq], "cosT"], [("t", 5)])
                            TT("dve", tmp[P, 6, :], ps[pr][P, :], sinT[P, tsl(qt)], ALU.mult, [PSB[pr], "sinT"], [("t", 6)])
                            TT("pool", QT[P, hh, :], tmp[P, 5, :], tmp[P, 6, :], ALU.add, [("t", 5), ("t", 6)], [("QT", hh)])
                            pump(2, LIM)
                        nkb = 4 * qt + 4
                        for i in range(nkb + 1):
                            if i < nkb:
                                kb = i
                                n0 = max(kb - 4 * qt, 0) * 128
                                sl = kb % 2
                                for hh in range(2):
                                    bk = 4 + 2 * sl + hh
                                    MM(ps[bk][:, n0:TS], KT[0:96, hh, kb * 128:(kb + 1) * 128], QT[0:96, hh, n0:TS], True, True,
                                       [("KT", hh, kb // 4), ("kpe" if hh == 0 else "kpe1", kb // 4), ("QT", hh)], [PSB[bk]])
                            if i >= 1:
                                kb = i - 1
                                j = kb - 4 * qt
                                n0 = max(j, 0) * 128
                                sl = kb % 2
                                ACT(PT4[:, sl, :, n0:TS], ppair[sl][:, :, n0:TS], AF.Exp,
                                    [PSB[4 + 2 * sl], PSB[5 + 2 * sl]], [("pt", sl)], scale=SCALE)
                                if j >= 0:
                                    for hh in range(2):
                                        TT("pool", PT4[:, sl, hh, n0:n0 + 128], PT4[:, sl, hh, n0:n0 + 128], trib[:], ALU.mult,
                                           [("pt", sl), "trib"], [("pt", sl)])
                                for hh in range(2):
                                    MM(ps[2 + hh][:, n0:TS], VA[:, kb, hh, :], PT4[:, sl, hh, n0:TS], kb == 0, kb == nkb - 1,
                                       [("VA", kb // 4), ("pt", sl), "VA"], [PSB[2 + hh]])
                            pump(3, LIM)
                        RCP(tmp[0:64, 7, :], ps[2][64:128, :], [PSB[2]], [("t", 7)])
                        TT("dve", tmp[0:64, 8, :], ps[2][0:64, :], tmp[0:64, 7, :], ALU.mult, [PSB[2], ("t", 7)], [("t", 8)])
                        RCP(tmp[64:128, 7, :], ps[3][0:64, :], [PSB[3]], [("t", 7)])
                        TT("dve", tmp[64:128, 8, :], ps[3][64:128, :], tmp[64:128, 7, :], ALU.mult, [PSB[3], ("t", 7)], [("t", 8)])
                        flush(8 * c + 2 * qt + 1)
                        STT("dve", tmp[:, 8, :], tmo[:, 1, :], 1.0, tmp[:, 8, :], ALU.add, ALU.mult, [("t", 8), "tB"], [("t", 8)])
                        STT("dve", tmp[:, 8, :], tmo[:, 0, :], 0.5, tmp[:, 8, :], ALU.mult, ALU.add, [("t", 8), "tM"], [("t", 8)])
                        ACT(merged[:, c, tsl(qt)], tmp[:, 8, :], AF.Copy, [("t", 8)], [("mg", c, qt)], scale=0.5)
                nproj[0] = 3
                S.barrier_all()

                def stat_evac(pm, pq_, tm, tr):
                    TSC("dve", tmp[:, tm, :], ps[pm][:], 1.0 / D, None, ALU.mult, None, [PSB[pm]], [("t", tm)])
                    TT("pool", tmp[:, tr, :], tmp[:, tm, :], tmp[:, tm, :], ALU.mult, [("t", tm)], [("t", tr)])
                    STT("dve", tmp[:, tr, :], ps[pq_][:], 1.0 / D, tmp[:, tr, :], ALU.mult, ALU.subtract,
                        [PSB[pq_], ("t", tr)], [("t", tr)])
                    ACT(tmp[:, tr, :], tmp[:, tr, :], AF.Sqrt, [("t", tr), "cst"], [("t", tr)], bias=epsc, scale=1.0)
                    RCP(tmp[:, tr, :], tmp[:, tr, :], [("t", tr)], [("t", tr)])

                def ln_norm_gen(tt, tm, tr, og, ob, zi):
                    zt = tmp[:, zi, :]
                    ZB = ("t", zi)
                    for m in range(8):
                        TT("pool", zt, xhi[:, m, tsl(tt)], xlo[:, m, tsl(tt)], ALU.add, [XB(m, tt), XL(m, tt)], [ZB])
                        yield
                        TT("dve", zt, zt, tmp[:, tm, :], ALU.subtract, [ZB, ("t", tm)], [ZB])
                        yield
                        TT("pool", zt, zt, tmp[:, tr, :], ALU.mult, [ZB, ("t", tr)], [ZB])
                        yield
                        ACT(zt, zt, AF.Identity, [ZB, "pv"], [ZB], bias=pcol(ob + m), scale=pcol(og + m))
                        yield
                        ACT(xhi[:, m, tsl(tt)], zt, AF.Copy, [ZB], [XB(m, tt)])
                        yield
                        TT("dve", xlo[:, m, tsl(tt)], zt, xhi[:, m, tsl(tt)], ALU.subtract, [ZB, XB(m, tt)], [XL(m, tt)])
                        yield

                def gpump(gen, n):
                    if gen is None:
                        return
                    for _ in range(n):
                        try:
                            next(gen)
                        except StopIteration:
                            return

                def chain2(g1, g2):
                    yield from g1
                    yield from g2

                rcnt = [0]

                def resid(pb, m, tt, pm, pq_):
                    k = rcnt[0]
                    rcnt[0] += 1
                    zt = tmp[:, k % 2, :]
                    ZB = ("t", k % 2)
                    STT("dve", zt, xhi[:, m, tsl(tt)], ALPHA, ps[pb][:], ALU.mult, ALU.add, [XB(m, tt), PSB[pb]], [ZB])
                    STT("dve", zt, xlo[:, m, tsl(tt)], ALPHA, zt, ALU.mult, ALU.add, [XL(m, tt), ZB], [ZB])
                    ACT(xhi[:, m, tsl(tt)], zt, AF.Copy, [ZB], [XB(m, tt)])
                    TT("pool", xlo[:, m, tsl(tt)], zt, xhi[:, m, tsl(tt)], ALU.subtract, [ZB, XB(m, tt)], [XL(m, tt)])
                    sq = PT[:, k % 4, :]
                    ACT(sq, zt, AF.Square, [ZB], [("pt", k % 4)])

                    def stats():
                        MM(ps[pm][:], onesb[:], xhi[:, m, tsl(tt)], m == 0, m == 7, [XB(m, tt), "onesb"], [PSB[pm]])
                        MM(ps[pq_][:], onesb[:], sq, m == 0, m == 7, [("pt", k % 4), "onesb"], [PSB[pq_]])
                    return stats

                wo = wbuf[:].rearrange("p (k q) -> p k q", k=8)
                DMA(wo, wout_s[l], [("wout", l)], ["wbA", "wbB"])
                gen = None
                for tt in range(NT):
                    pm, pq_ = (4, 5) if tt % 2 == 0 else (6, 7)
                    pending = None
                    for m in range(8):
                        pb = pbank()
                        mm_group(pb, [wo[:, cc, m * 128:(m + 1) * 128] for cc in range(8)], [merged[:, cc, tsl(tt)] for cc in range(8)],
                                 [("mg", cc, tt) for cc in range(8)] + ["wbA"])
                        if pending is not None:
                            pending()
                        pending = resid(pb, m, tt, pm, pq_)
                        gpump(gen, 6)
                    pending()
                    gpump(gen, 1000)
                    stat_evac(pm, pq_, 4, 5)
                    gen = ln_norm_gen(tt, 4, 5, O_L1G, O_L1B, 8)
                gpump(gen, 1000)
                S.barrier_all()

                def load_up(j, l=l):
                    bi = j % 4
                    DMA(wbuf[:, bi * 2048:(bi + 1) * 2048].rearrange("p (k q) -> p k q", k=8), wupP_s[l][:, j, :, :],
                        [("wupP", l)], [("wup", bi)])

                def load_dn(m, l=l):
                    DMA(wdn_sb[m % 2], wdnP_s[l][:, m, :, :], [("wdnP", l)], [("wdn", m % 2)])

                gen = None
                for half in range(2):
                    for j in range(3):
                        load_up(j)
                    for j in range(24):
                        if j + 3 < 24:
                            load_up(j + 3)
                        wu = wbuf[:, (j % 4) * 2048:(j % 4 + 1) * 2048].rearrange("p (k g q) -> p k g q", k=8, g=2)
                        for t2 in range(2):
                            tt = half * 2 + t2
                            xh = [xhi[:, k, tsl(tt)] for k in range(8)]
                            xrd = [XB(k, tt) for k in range(8)]
                            for gv in range(2):
                                jj = gv * 24 + j
                                pb = pbank()
                                mm_group(pb, [wu[:, k, gv, :] for k in range(8)], xh, xrd + [("wup", j % 4)])
                                ti = 2 * t2 + gv
                                ht = tmp[:, ti, :]
                                HB = ("t", ti)
                                fw = lambda tap, jj=jj: pcol(O_FCW + tap * 48 + jj)
                                ACT(ht, ps[pb][:], AF.Identity, [PSB[pb], "pv"], [HB], bias=pcol(O_FCB + jj), scale=fw(2))
                                STT("dve", ht[:, 1:TS], ps[pb][:, 0:TS - 1], fw(1), ht[:, 1:TS], ALU.mult, ALU.add, [PSB[pb], HB, "pv"], [HB])
                                STT("dve", ht[:, 2:TS], ps[pb][:, 0:TS - 2], fw(0), ht[:, 2:TS], ALU.mult, ALU.add, [PSB[pb], HB, "pv"], [HB])
                                if tt > 0:
                                    STT("dve", ht[:, 0:1], hal[:, jj, 1:2], fw(1), ht[:, 0:1], ALU.mult, ALU.add, [("hal", jj), HB, "pv"], [HB])
                                    STT("dve", ht[:, 0:2], hal[:, jj, 0:2], fw(0), ht[:, 0:2], ALU.mult, ALU.add, [("hal", jj), HB, "pv"], [HB])
                                if tt < NT - 1:
                                    CP("dve", hal[:, jj, :], ps[pb][:, TS - 2:TS], [PSB[pb]], [("hal", jj)])
                            tg, tv = 2 * t2, 2 * t2 + 1
                            ACT(tmp[:, tg, :], tmp[:, tg, :], AF.Gelu_apprx_tanh, [("t", tg)], [("t", tg)])
                            TT("pool", abuf[:, j, t2 * TS:(t2 + 1) * TS], tmp[:, tg, :], tmp[:, tv, :], ALU.mult,
                               [("t", tg), ("t", tv)], [("a", j, t2)])
                            gpump(gen, 2)
                    gpump(gen, 1000)
                    load_dn(0)
                    pending = None
                    for m in range(8):
                        if m + 1 < 8:
                            load_dn(m + 1)
                        wd = wdn_sb[m % 2]
                        for t2 in range(2):
                            tt = half * 2 + t2
                            pb = pbank()
                            mm_group(pb, [wd[:, j, :] for j in range(24)], [abuf[:, j, t2 * TS:(t2 + 1) * TS] for j in range(24)],
                                     [("a", j, t2) for j in range(24)] + [("wdn", m % 2)])
                            if pending is not None:
                                pending()
                            pending = resid(pb, m, tt, 4 + 2 * t2, 5 + 2 * t2)
                    pending()
                    stat_evac(4, 5, 4, 5)
                    stat_evac(6, 7, 6, 7)
                    gen = chain2(ln_norm_gen(half * 2, 4, 5, O_L2G, O_L2B, 8), ln_norm_gen(half * 2 + 1, 6, 7, O_L2G, O_L2B, 8))
                gpump(gen, 1000)
                S.barrier_all()

            S.barrier_all()
            cnt = 0
            for tt in range(NT):
                for g in range(2):
                    for kk in range(4):
                        m = g * 4 + kk
                        TT("pool", tmp[:, kk, :], xhi[:, m, tsl(tt)], xlo[:, m, tsl(tt)], ALU.add, [XB(m, tt), XL(m, tt)], [("t", kk)])
                    for t4 in range(4):
                        tb = tt * 4 + t4
                        pb = pbank()
                        for kk in range(4):
                            TR(ps[pb][:, kk * 128:(kk + 1) * 128], tmp[:, kk, t4 * 128:(t4 + 1) * 128], [("t", kk), "cst"], [PSB[pb]])
                        oi = 4 + cnt % 4
                        cnt += 1
                        if cnt % 2 == 0:
                            ACT(tmp[:, oi, :], ps[pb][:], AF.Copy, [PSB[pb]], [("t", oi)])
                        else:
                            CP("dve", tmp[:, oi, :], ps[pb][:], [PSB[pb]], [("t", oi)])
                        DMA(y_d[s, tb * 128:(tb + 1) * 128, g * 512:(g + 1) * 512], tmp[:, oi, :], [("t", oi)], [("y", s, tb, g)])
            S.barrier_all()
        S.barrier_all()
        S.run()
    return nc


def _fm(v):
    v = np.asarray(v, np.float32)
    return np.ascontiguousarray(v.reshape(-1, 128).T)


def _host_params(inp, nl):
    pvs = np.zeros((128, nl * PL), np.float32)
    for l in range(nl):
        b = l * PL
        for tap in range(4):
            pvs[:, b + O_CW + tap * 8: b + O_CW + tap * 8 + 8] = _fm(inp["conv_w"][l, tap])
        pvs[:, b + O_CB: b + O_CB + 8] = _fm(inp["conv_b"][l])
        pvs[:, b + O_GXB: b + O_GXB + 8] = _fm(inp["gx_b"][l])
        pvs[:, b + O_GAB: b + O_GAB + 8] = _fm(inp["ga_b"][l])
        pvs[:, b + O_LAM: b + O_LAM + 8] = _fm(inp["lru_lambda"][l])
        pvs[:, b + O_L1G: b + O_L1G + 8] = _fm(inp["ln1_g"][l])
        pvs[:, b + O_L1B: b + O_L1B + 8] = _fm(inp["ln1_b"][l])
        pvs[:, b + O_L2G: b + O_L2G + 8] = _fm(inp["ln2_g"][l])
        pvs[:, b + O_L2B: b + O_L2B + 8] = _fm(inp["ln2_b"][l])
        for tap in range(3):
            pvs[:, b + O_FCW + tap * 48: b + O_FCW + tap * 48 + 48] = _fm(inp["ffn_conv_w"][l, tap])
        pvs[:, b + O_FCB: b + O_FCB + 48] = _fm(inp["ffn_conv_b"][l])
        pvs[:, b + O_QG: b + O_QG + 3] = _fm(inp["q_norm_g"][l])
        pvs[:, b + O_KVG: b + O_KVG + 2] = _fm(inp["kv_norm_g"][l])
    return pvs


def _host_bd(w, nl):
    w = np.asarray(w, np.float32)
    out = np.zeros((nl, 128, 8, 128), np.float32)
    for c in range(8):
        for b in range(2):
            out[:, b * 64:(b + 1) * 64, c, b * 64:(b + 1) * 64] = w[:, 2 * c + b]
    return out.reshape(nl, 128, 1024)


def _consts():
    c = np.zeros((128, 258), np.float32)
    c[:, 0:128] = np.eye(128, dtype=np.float32)
    k = np.arange(128)[:, None]
    q = np.arange(128)[None, :]
    c[:, 128:256] = (q >= k).astype(np.float32)
    inv = (10000.0 ** (-np.arange(0, 32, 2, dtype=np.float32) / np.float32(32))).astype(np.float32)
    for p in range(64, 96):
        c[p, 256] = inv[(p - 64) % 16]
    c[:, 257] = EPS
    return c


_NC_CACHE = {}
_TRACE = False
_LAST = [None]


def run(inp, NS, layers, ncores, seq_of_core):
    nl = L_ALL
    key = (NS, tuple(layers))
    if key not in _NC_CACHE:
        _NC_CACHE[key] = build(NS, list(layers))
    nc = _NC_CACHE[key]
    shared = {
        "w_in": np.ascontiguousarray(inp["w_in"], np.float32),
        "w_uq": np.ascontiguousarray(inp["w_uq"], np.float32),
        "w_ukv": np.ascontiguousarray(inp["w_ukv"], np.float32),
        "w_out": np.ascontiguousarray(inp["w_out"], np.float32),
        "w_up": np.ascontiguousarray(inp["w_up"], np.float32),
        "w_down": np.ascontiguousarray(inp["w_down"], np.float32),
        "gxw": _host_bd(inp["gx_w"], nl),
        "gaw": _host_bd(inp["ga_w"], nl),
        "pv": _host_params(inp, nl),
        "cst": _consts(),
    }
    x = np.asarray(inp["x"], np.float32)
    pos = np.asarray(inp["positions"], np.int32)
    in_maps = []
    for ci in range(ncores):
        seqs = seq_of_core[ci]
        m = dict(shared)
        m["x"] = np.ascontiguousarray(x[seqs])
        m["pos"] = np.ascontiguousarray(np.broadcast_to(pos[seqs][:, None, :], (len(seqs), 32, T)))
        in_maps.append(m)
    res = run_bass_kernel_spmd(nc, in_maps, core_ids=list(range(ncores)), trace=_TRACE)
    _LAST[0] = res
    return [r["y"] for r in res.results]


def kernel(**inputs):
    B = inputs["x"].shape[0]
    ncores = 8
    NS = B // ncores
    seq_of_core = [list(range(ci * NS, (ci + 1) * NS)) for ci in range(ncores)]
    outs = run(inputs, NS, list(range(L_ALL)), ncores, seq_of_core)
    return np.concatenate(outs, axis=0).astype(np.float32)
```

```python
from contextlib import ExitStack
import math
import numpy as np
import concourse.bass as bass
import concourse.mybir as mybir
from concourse.bass_utils import run_bass_kernel_spmd

F32 = mybir.dt.float32
BF16 = mybir.dt.bfloat16
I32 = mybir.dt.int32
AF = mybir.ActivationFunctionType
ALU = mybir.AluOpType

ENGS = ["pe", "act", "dve", "pool", "sp"]

D = 1024
T = 2048
TS = 512
NT = T // TS
L_ALL = 4
INW = 4768
DFF = 3072
ALPHA = float((2 * L_ALL) ** 0.25)
EPS = 1e-6
SCALE = float(96 ** -0.5)
PL = 296
O_CW, O_CB, O_GXB, O_GAB, O_LAM = 0, 32, 40, 48, 56
O_L1G, O_L1B, O_L2G, O_L2B = 64, 72, 80, 88
O_FCW, O_FCB, O_QG, O_KVG = 96, 240, 288, 291
PAIR_EXP = True
XC_RUNAHEAD = True
TRANSITIVE = True
TWO_PI = float(2 * math.pi)
PI = float(math.pi)


class Buf:
    __slots__ = ("name", "lw", "rd")

    def __init__(self, name=""):
        self.name = name
        self.lw = None
        self.rd = {}


class Sched:
    K = 8
    R = 32

    def __init__(self, nc, stack):
        self.nc = nc
        self.sems = {e: [stack.enter_context(nc.semaphore(f"s_{e}_{i}")) for i in range(self.K)]
                     for e in ENGS}
        self.dsems = [stack.enter_context(nc.semaphore(f"d_{i}")) for i in range(self.R)]
        self.cnt = {e: 0 for e in ENGS}
        self.seen = {e: {f: 0 for f in ENGS} for e in ENGS}
        self.seen_d = {e: [0] * self.R for e in ENGS}
        self.clock = {e: [] for e in ENGS}
        self.nd = 0
        self.thunks = {e: [] for e in ENGS}
        self.bufs = {}

    def b(self, key):
        x = self.bufs.get(key)
        if x is None:
            x = self.bufs[key] = Buf(str(key))
        return x

    def _need(self, e, tok, out):
        if tok is None:
            return
        if tok[0] == "e":
            _, f, i = tok
            if f == e and e == "pe":
                return
            if self.seen[e][f] >= i + 1:
                return
            out.append(tok)
        else:
            n = tok[1]
            if self.seen_d[e][n % self.R] >= n // self.R + 1:
                return
            out.append(tok)

    def _emit_waits(self, e, toks):
        best = {}
        dm = {}
        for t in toks:
            if t[0] == "e":
                best[t[1]] = max(best.get(t[1], -1), t[2])
            else:
                s = t[1] % self.R
                dm[s] = max(dm.get(s, -1), t[1])
        for f, i in best.items():
            if self.seen[e][f] >= i + 1:
                continue
            self.thunks[e].append(("w", self.sems[f][i % self.K], i // self.K + 1))
            self.seen[e][f] = i + 1
            clk = self.clock[f][i] if TRANSITIVE else {}
            for g in (ENGS if TRANSITIVE else []):
                if g != e and clk[g] > self.seen[e][g]:
                    self.seen[e][g] = clk[g]
        for s, n in dm.items():
            v = n // self.R + 1
            if self.seen_d[e][s] >= v:
                continue
            self.thunks[e].append(("w", self.dsems[s], 16 * v))
            self.seen_d[e][s] = v

    def _deps(self, e, reads, writes):
        toks = []
        for r in reads:
            self._need(e, r.lw, toks)
        for w in writes:
            self._need(e, w.lw, toks)
            for f, i in w.rd.items():
                if f == "d":
                    for n in i:
                        self._need(e, ("d", n), toks)
                else:
                    self._need(e, ("e", f, i), toks)
        return toks

    def _bl(self, xs):
        return [x if isinstance(x, Buf) else self.b(x) for x in xs]

    def op(self, e, name, reads=(), writes=(), **kw):
        fn = (name, kw)
        reads = self._bl(reads)
        writes = self._bl(writes)
        self._emit_waits(e, self._deps(e, reads, writes))
        i = self.cnt[e]
        self.cnt[e] = i + 1
        self.thunks[e].append(("i", fn, self.sems[e][i % self.K], 1))
        self.clock[e].append(dict(self.seen[e]))
        tok = ("e", e, i)
        for r in reads:
            if r.rd.get(e, -1) < i:
                r.rd[e] = i
        for w in writes:
            w.lw = tok
            w.rd = {}
        return tok

    def dma(self, out, in_, reads=(), writes=(), q="sp"):
        fn = ("dma_start", dict(out=out, in_=in_))
        reads = self._bl(reads)
        writes = self._bl(writes)
        n = self.nd
        self.nd += 1
        s = n % self.R
        toks = self._deps(q, reads, writes)
        if n >= self.R:
            self._need(q, ("d", n - self.R), toks)
        self._emit_waits(q, toks)
        self.thunks[q].append(("i", fn, self.dsems[s], 16))
        tok = ("d", n)
        for r in reads:
            r.rd.setdefault("d", []).append(n)
        for w in writes:
            w.lw = tok
            w.rd = {}
        return tok

    def barrier_all(self):
        for e in ENGS:
            toks = []
            for f in ENGS:
                if self.cnt[f] > 0:
                    self._need(e, ("e", f, self.cnt[f] - 1), toks)
            for n in range(max(0, self.nd - self.R), self.nd):
                self._need(e, ("d", n), toks)
            self._emit_waits(e, toks)

    def run(self):
        nc = self.nc
        with nc.Block() as block:
            def mk(e):
                def body(eng):
                    for t in self.thunks[e]:
                        if t[0] == "w":
                            eng.wait_ge(t[1], t[2])
                        else:
                            getattr(eng, t[1][0])(**t[1][1]).then_inc(t[2], t[3])
                return body
            block.tensor(mk("pe"))
            block.scalar(mk("act"))
            block.vector(mk("dve"))
            block.gpsimd(mk("pool"))
            block.sync(mk("sp"))


def build(NS, layers, nlw=L_ALL):
    nc = bass.Bass("TRN2", target_bir_lowering=False)
    dt_in = lambda n, s, d=F32: nc.dram_tensor(n, s, d, kind="ExternalInput").ap()
    x_d = dt_in("x", [NS, T, D])
    pos_d = dt_in("pos", [NS, 32, T], I32)
    w_in_d = dt_in("w_in", [nlw, D, INW])
    w_uq_d = dt_in("w_uq", [nlw, 384, 1536])
    w_ukv_d = dt_in("w_ukv", [nlw, 256, 2048])
    w_out_d = dt_in("w_out", [nlw, D, D])
    w_up_d = dt_in("w_up", [nlw, D, 2 * DFF])
    w_dn_d = dt_in("w_down", [nlw, DFF, D])
    gxw_d = dt_in("gxw", [nlw, 128, 8 * 128])
    gaw_d = dt_in("gaw", [nlw, 128, 8 * 128])
    pv_d = dt_in("pv", [128, nlw * PL])
    cst_d = dt_in("cst", [128, 258])
    y_d = nc.dram_tensor("y", [NS, T, D], F32, kind="ExternalOutput").ap()

    scr = lambda n, s: nc.dram_tensor(n, s, BF16, kind="Internal").ap()
    winL_s = scr("winL", [nlw, 128, 8, 672])
    winB_s = scr("winB", [nlw, 128, 8, 8, 512])
    wkrot_s = scr("wkrot", [nlw, 128, 8 * 96])
    wuqP_s = scr("wuqP", [nlw, 128, 8, 3, 192])
    wqrP_s = scr("wqrP", [nlw, 128, 8 * 3 * 192])
    wukvP_s = scr("wukvP", [nlw, 128, 8, 2, 256])
    wout_s = scr("wout", [nlw, 128, 8, 1024])
    wupP_s = scr("wupP", [nlw, 128, 24, 8, 256])
    wdnP_s = scr("wdnP", [nlw, 128, 8, 24, 128])
    gxw_s = scr("gxws", [nlw, 128, 1024])
    gaw_s = scr("gaws", [nlw, 128, 1024])

    with ExitStack() as st:
        S = Sched(nc, st)
        sb = lambda n, s, d: st.enter_context(nc.sbuf_tensor("sb_" + n, s, d))
        xhi = sb("xhi", [128, 8, T], BF16)
        xlo = sb("xlo", [128, 8, T], BF16)
        big = sb("big", [128, 26624], BF16)
        tmp = sb("tmp", [128, 9, TS], F32)
        hr = sb("hr", [128, 4], F32)
        tmo = sb("tmo", [128, 2, TS], F32)
        cosT = sb("cosT", [128, T], F32)
        sinT = sb("sinT", [128, T], F32)
        KT = sb("KT", [128, 2, T], BF16)
        VA = sb("VA", [128, 16, 2, 128], BF16)
        QT = sb("QT", [128, 2, TS], BF16)
        PT = sb("PT", [128, 5, TS], BF16)
        PT4 = PT[:, 0:4, :].rearrange("p (s h) t -> p s h t", s=2)
        wbuf = sb("wbuf", [128, 8192], BF16)
        wsm = sb("wsm", [128, 2, 1920], BF16)
        pv = sb("pv", [128, nlw * PL], F32)
        pvd = sb("pvd", [128, nlw * 32], F32)
        cst = sb("cst", [128, 258], F32)
        onesb = sb("onesb", [128, 128], BF16)
        trib = sb("trib", [128, 128], BF16)
        hal = sb("hal", [128, 48, 2], F32)
        hlast = sb("hlast", [128, 2], F32)
        ps = [st.enter_context(nc.psum_tensor(f"ps{i}", [128, TS], F32)) for i in range(4)]
        ppair = [st.enter_context(nc.psum_tensor(f"pp{i}", [128, 2, TS], F32)) for i in range(2)]
        ps += [ppair[0][:, 0, :], ppair[0][:, 1, :], ppair[1][:, 0, :], ppair[1][:, 1, :]]
        PSB = [S.b(("ps", i)) for i in range(8)]
        ident = cst[:, 0:128]
        invf = cst[:, 256:257]
        epsc = cst[:, 257:258]
        wkr = wsm[:, 1, 0:768].rearrange("p (k d) -> p k d", k=8)

        merged = big[:, 0:16384].rearrange("p (c t) -> p c t", c=8)
        qn = big[:, 16384:22528].rearrange("p (c t) -> p c t", c=3)
        kvn = big[:, 22528:26624].rearrange("p (c t) -> p c t", c=2)
        abuf = big[:, 0:24576].rearrange("p (j t) -> p j t", j=24)
        wdn_sb = [KT[:].rearrange("p a t -> p (a t)")[:, 0:3072].rearrange("p (j q) -> p j q", j=24),
                  VA[:].rearrange("p a b c -> p (a b c)")[:, 0:3072].rearrange("p (j q) -> p j q", j=24)]

        def ACT(out, in_, func, rd, wr, **kw):
            S.op("act", "activation", rd, wr, out=out, in_=in_, func=func, **kw)

        def TT(e, out, in0, in1, op, rd, wr):
            S.op(e, "tensor_tensor", rd, wr, out=out, in0=in0, in1=in1, op=op)

        def TSC(e, out, in0, s1, s2, op0, op1, rd, wr):
            if s2 is None:
                S.op(e, "tensor_scalar", rd, wr, out=out, in0=in0, scalar1=s1, scalar2=None, op0=op0)
            else:
                S.op(e, "tensor_scalar", rd, wr, out=out, in0=in0, scalar1=s1, scalar2=s2, op0=op0, op1=op1)

        def STT(e, out, in0, scalar, in1, op0, op1, rd, wr):
            S.op(e, "scalar_tensor_tensor", rd, wr, out=out, in0=in0, scalar=scalar, in1=in1, op0=op0, op1=op1)

        def CP(e, out, in_, rd, wr):
            S.op(e, "tensor_copy", rd, wr, out=out, in_=in_)

        def RCP(out, in_, rd, wr):
            S.op("dve", "reciprocal", rd, wr, out=out, in_=in_)

        def MM(out, lhsT, rhs, start, stop, rd, wr):
            S.op("pe", "matmul", rd, wr, out=out, lhsT=lhsT, rhs=rhs, start=start, stop=stop)

        def TR(out, in_, rd, wr):
            S.op("pe", "transpose", rd, wr, out=out, in_=in_, identity=ident)

        def MS(e, ap, v, wr):
            S.op(e, "memset", (), wr, ap=ap, constant=v)

        DMA = S.dma
        proj_rr = [0]

        nproj = [3]

        def pbank():
            i = proj_rr[0] % nproj[0]
            proj_rr[0] += 1
            return i

        rb_rr = [0]
        ab_rr = [0]

        def rbank():
            i = rb_rr[0] % 2
            rb_rr[0] += 1
            return i

        def abank():
            i = 4 + ab_rr[0] % 4
            ab_rr[0] += 1
            return i

        def mm_group(pb, lhs_list, rhs_list, reads, out_ap=None):
            n = len(lhs_list)
            o = ps[pb][:] if out_ap is None else out_ap
            for i in range(n):
                MM(o, lhs_list[i], rhs_list[i], i == 0, i == n - 1, reads, [PSB[pb]])

        def tsl(tt):
            return slice(tt * TS, (tt + 1) * TS)

        XB = lambda k, tt: ("xhi", k, tt)
        XL = lambda k, tt: ("xlo", k, tt)

        DMA(cst[:], cst_d, (), ["cst"])
        DMA(pv[:], pv_d, (), ["pv"])
        MS("pool", onesb[:], 1.0, ["onesb"])
        CP("dve", trib[:], cst[:, 128:256], ["cst"], ["trib"])
        MS("pool", hal[:], 0.0, ["hal"])
        for l in layers:
            lam = pv[:, l * PL + O_LAM: l * PL + O_LAM + 8]
            t0 = tmp[:, 0, 0:8]
            ACT(t0, lam, AF.Exp, ["pv"], ["t0"], scale=-1.0)
            ACT(t0, t0, AF.Ln, ["t0"], ["t0"], bias=1.0)
            TSC("dve", pvd[:, l * 32: l * 32 + 8], t0, -4.0, None, ALU.mult, None, ["t0"], ["pvd"])
            TSC("dve", pvd[:, l * 32 + 8: l * 32 + 16], t0, -8.0, None, ALU.mult, None, ["t0"], ["pvd"])
            TSC("dve", pvd[:, l * 32 + 16: l * 32 + 24], pv[:, l * PL + O_GXB: l * PL + O_GXB + 8], 0.5, None, ALU.mult, None, ["pv"], ["pvd"])
            TSC("dve", pvd[:, l * 32 + 24: l * 32 + 32], pv[:, l * PL + O_GAB: l * PL + O_GAB + 8], 0.5, None, ALU.mult, None, ["pv"], ["pvd"])
        S.barrier_all()

        stf = big[:, 0:24576].bitcast(F32).rearrange("p (a b) -> p a b", a=2)
        stb = xhi[:].rearrange("p k t -> p (k t)")[:, 0:12288].rearrange("p (a b) -> p a b", a=2)
        tb16 = tmp[:].rearrange("p a b -> p (a b)").bitcast(BF16)
        wqr_sb = tb16[:, 0:4608].rearrange("p (c k h d) -> p c k h d", c=8, k=3, h=2)
        ccnt = [0]

        def cast(i, out_ap, in_ap, scale=None):
            if scale is None:
                ccnt[0] += 1
                if ccnt[0] % 2 == 0:
                    ACT(out_ap, in_ap, AF.Copy, [("stf", i)], [("stb", i)])
                else:
                    CP("dve", out_ap, in_ap, [("stf", i)], [("stb", i)])
            else:
                TSC("dve", out_ap, in_ap, scale, None, ALU.mult, None, [("stf", i), "pv"], [("stb", i)])

        MS("pool", wkr, 0.0, ["wkr"])
        MS("pool", tb16[:, 0:4608], 0.0, ["wqr_sb"])
        jobs = []
        for l in layers:
            base = l * PL
            for k in range(8):
                def job(i, l=l, k=k):
                    sf = stf[:, i, :]
                    so = stb[:, i, :]
                    soB = so[:, 0:4096].rearrange("p (c j q) -> p c j q", c=8, j=4)
                    for j, b0 in enumerate([0, 1024, 2720, 3744]):
                        cast(i, soB[:, :, j, :], sf[:, b0:b0 + 1024].rearrange("p (c q) -> p c q", c=8))
                    cast(i, so[:, 4096:4768], sf[:, 2048:2720])
                    TSC("pool", wkr[:, k, 64:80], sf[:, 2704:2720], -1.0, None, ALU.mult, None, [("stf", i)], ["wkr"])
                    CP("pool", wkr[:, k, 80:96], sf[:, 2688:2704], [("stf", i)], ["wkr"])
                    DMA(winB_s[l][:, :, k, :], so[:, 0:4096].rearrange("p (c q) -> p c q", c=8), [("stb", i)], [("winB", l)])
                    DMA(winL_s[l][:, k, :], so[:, 4096:4768], [("stb", i)], [("winL", l)])
                    if k == 7:
                        DMA(wkrot_s[l], wsm[:, 1, 0:768], ["wkr"], [("wkrot", l)])
                jobs.append((w_in_d[l, k * 128:(k + 1) * 128, :], lambda i: stf[:, i, 0:INW], job))

            def job_uq(i, l=l, base=base):
                sf3 = stf[:, i, 0:4608].rearrange("p (k q) -> p k q", k=3)
                so3 = stb[:, i, 0:4608].rearrange("p (k q) -> p k q", k=3)
                for k in range(3):
                    g = pv[:, base + O_QG + k: base + O_QG + k + 1]
                    cast(i, so3[:, k, :], sf3[:, k, :], scale=g)
                    sfv = sf3[:, k, :].rearrange("p (c h d) -> p c h d", c=8, h=2)
                    TSC("pool", wqr_sb[:, :, k, :, 64:80], sfv[:, :, :, 80:96], g, -1.0, ALU.mult, ALU.mult,
                        [("stf", i), "pv"], ["wqr_sb"])
                    TSC("pool", wqr_sb[:, :, k, :, 80:96], sfv[:, :, :, 64:80], g, None, ALU.mult, None,
                        [("stf", i), "pv"], ["wqr_sb"])
                    DMA(wuqP_s[l][:, :, k, :], so3[:, k, :].rearrange("p (c q) -> p c q", c=8), [("stb", i)], [("wuqP", l)])
                DMA(wqrP_s[l], tb16[:, 0:4608], ["wqr_sb"], [("wqrP", l)])
            jobs.append((w_uq_d[l].rearrange("(k p) q -> p k q", p=128),
                         lambda i: stf[:, i, 0:4608].rearrange("p (k q) -> p k q", k=3), job_uq))

            def job_ukv(i, l=l, base=base):
                sf2 = stf[:, i, 0:4096].rearrange("p (k q) -> p k q", k=2)
                so2 = stb[:, i, 0:4096].rearrange("p (k q) -> p k q", k=2)
                for k in range(2):
                    g = pv[:, base + O_KVG + k: base + O_KVG + k + 1]
                    cast(i, so2[:, k, :], sf2[:, k, :], scale=g)
                    DMA(wukvP_s[l][:, :, k, :], so2[:, k, :].rearrange("p (c q) -> p c q", c=8), [("stb", i)], [("wukvP", l)])
            jobs.append((w_ukv_d[l].rearrange("(k p) q -> p k q", p=128),
                         lambda i: stf[:, i, 0:4096].rearrange("p (k q) -> p k q", k=2), job_ukv))

            for k0 in range(0, 8, 4):
                def job_out(i, l=l, k0=k0):
                    cast(i, stb[:, i, 0:4096], stf[:, i, 0:4096])
                    DMA(wout_s[l][:, k0:k0 + 4, :], stb[:, i, 0:4096].rearrange("p (k q) -> p k q", k=4), [("stb", i)], [("wout", l)])
                jobs.append((w_out_d[l, k0 * 128:(k0 + 4) * 128, :].rearrange("(k p) q -> p k q", p=128),
                             lambda i: stf[:, i, 0:4096].rearrange("p (k q) -> p k q", k=4), job_out))

            for k in range(8):
                def job_up(i, l=l, k=k):
                    sfv = stf[:, i, :].rearrange("p (g j q) -> p g j q", g=2, j=24)
                    sov = stb[:, i, :].rearrange("p (j g q) -> p j g q", j=24, g=2)
                    cast(i, sov[:, :, 0, :], sfv[:, 0, :, :])
                    cast(i, sov[:, :, 1, :], sfv[:, 1, :, :])
                    DMA(wupP_s[l][:, :, k, :], stb[:, i, :].rearrange("p (j q) -> p j q", j=24), [("stb", i)], [("wupP", l)])
                jobs.append((w_up_d[l, k * 128:(k + 1) * 128, :], lambda i: stf[:, i, 0:6144], job_up))

            for k0 in range(0, 24, 4):
                def job_dn(i, l=l, k0=k0):
                    cast(i, stb[:, i, 0:4096], stf[:, i, 0:4096])
                    so4 = stb[:, i, 0:4096].rearrange("p (k m q) -> p k m q", k=4, m=8)
                    for kk in range(4):
                        DMA(wdnP_s[l][:, :, k0 + kk, :], so4[:, kk, :, :], [("stb", i)], [("wdnP", l)])
                jobs.append((w_dn_d[l, k0 * 128:(k0 + 4) * 128, :].rearrange("(k p) q -> p k q", p=128),
                             lambda i: stf[:, i, 0:4096].rearrange("p (k q) -> p k q", k=4), job_dn))

            for src, dst, nm in [(gxw_d, gxw_s, "gxws"), (gaw_d, gaw_s, "gaws")]:
                def job_g(i, l=l, dst=dst, nm=nm):
                    cast(i, stb[:, i, 0:1024], stf[:, i, 0:1024])
                    DMA(dst[l], stb[:, i, 0:1024], [("stb", i)], [(nm, l)])
                jobs.append((src[l], lambda i: stf[:, i, 0:1024], job_g))

        nj = len(jobs)

        def jload(n):
            i = n % 2
            DMA(jobs[n][1](i), jobs[n][0], (), [("stf", i)])

        if nj:
            jload(0)
        for n in range(nj):
            if n + 1 < nj:
                jload(n + 1)
            jobs[n][2](n % 2)
        S.barrier_all()

        P = slice(64, 96)
        for s in range(NS):
            for tb in range(T // 128):
                tt = tb // 4
                xt = tmp[:, 2 * (tb % 2):2 * (tb % 2) + 2, :].rearrange("p a b -> p (a b)")
                xtb = ("xt", tb % 2)
                DMA(xt, x_d[s, tb * 128:(tb + 1) * 128, :], (), [xtb])
                for g in range(2):
                    pb = pbank()
                    for kk in range(4):
                        k = g * 4 + kk
                        TR(ps[pb][:, kk * 128:(kk + 1) * 128], xt[:, k * 128:(k + 1) * 128], [xtb, "cst"], [PSB[pb]])
                    hi = xhi[:, g * 4:(g + 1) * 4, tb * 128:(tb + 1) * 128]
                    lo = xlo[:, g * 4:(g + 1) * 4, tb * 128:(tb + 1) * 128]
                    pv3 = ps[pb][:].rearrange("p (a b) -> p a b", a=4)
                    hb = [XB(k, tt) for k in range(g * 4, g * 4 + 4)]
                    lb = [XL(k, tt) for k in range(g * 4, g * 4 + 4)]
                    ACT(hi, pv3, AF.Copy, [PSB[pb]], hb)
                    TT("dve", lo, pv3, hi, ALU.subtract, [PSB[pb]] + hb, lb)
            posi = tmp[:, 0:4, :].rearrange("p a b -> p (a b)").bitcast(I32)
            ang = tmp[:, 4:8, :].rearrange("p a b -> p (a b)")
            kf = tmp[:, 0:4, :].rearrange("p a b -> p (a b)")
            S.barrier_all()
            DMA(posi[P, :], pos_d[s], (), ["posi"])
            CP("dve", ang[P, :], posi[P, :], ["posi"], ["ang"])
            TSC("dve", ang[P, :], ang[P, :], invf[P, :], None, ALU.mult, None, ["ang", "cst"], ["ang"])
            TSC("dve", posi[P, :], ang[P, :], 1.0 / TWO_PI, None, ALU.mult, None, ["ang"], ["posi"])
            CP("dve", kf[P, :], posi[P, :], ["posi"], ["posi"])
            STT("dve", ang[P, :], kf[P, :], -TWO_PI, ang[P, :], ALU.mult, ALU.add, ["posi", "ang"], ["ang"])

            def wrap():
                TSC("dve", kf[P, :], ang[P, :], PI, TWO_PI, ALU.is_gt, ALU.mult, ["ang"], ["posi"])
                TT("dve", ang[P, :], ang[P, :], kf[P, :], ALU.subtract, ["ang", "posi"], ["ang"])
                TSC("dve", kf[P, :], ang[P, :], -PI, TWO_PI, ALU.is_lt, ALU.mult, ["ang"], ["posi"])
                TT("dve", ang[P, :], ang[P, :], kf[P, :], ALU.add, ["ang", "posi"], ["ang"])
            wrap()
            ACT(sinT[P, :], ang[P, :], AF.Sin, ["ang"], ["sinT"])
            TSC("dve", ang[P, :], ang[P, :], PI / 2, None, ALU.add, None, ["ang"], ["ang"])
            wrap()
            ACT(cosT[P, :], ang[P, :], AF.Sin, ["ang"], ["cosT"])
            S.barrier_all()

            for li, l in enumerate(layers):
                base = l * PL
                pcol = lambda o, base=base: pv[:, base + o: base + o + 1]
                wlat = wbuf[:, 0:8 * 672].rearrange("p (k q) -> p k q", k=8)
                DMA(wlat, winL_s[l], [("winL", l)], ["wbA", "wbB"])
                DMA(wsm[:, 1, 0:768], wkrot_s[l], [("wkrot", l)], ["wkr"])
                for tt in range(NT):
                    xh = [xhi[:, k, tsl(tt)] for k in range(8)]
                    xrd = [XB(k, tt) for k in range(8)]
                    for (nch, c0, dst, inv_n, nm) in [(3, 0, qn, 1.0 / 384, "qn"), (2, 384, kvn, 1.0 / 256, "kvn")]:
                        for oc in range(nch):
                            pb = pbank()
                            mm_group(pb, [wlat[:, k, c0 + oc * 128: c0 + (oc + 1) * 128] for k in range(8)], xh, xrd + ["wbA"])
                            ACT(tmp[:, oc, :], ps[pb][:], AF.Copy, [PSB[pb]], [("t", oc)])
                            ACT(PT[:, oc, :], ps[pb][:], AF.Square, [PSB[pb]], [("pt", oc)])
                        mm_group(7, [onesb[:]] * nch, [PT[:, oc, :] for oc in range(nch)],
                                 ["onesb"] + [("pt", oc) for oc in range(nch)])
                        ACT(tmp[:, 4, :], ps[7][:], AF.Sqrt, [PSB[7], "cst"], [("t", 4)], bias=epsc, scale=inv_n)
                        RCP(tmp[:, 5, :], tmp[:, 4, :], [("t", 4)], [("t", 5)])
                        for oc in range(nch):
                            TT("pool", dst[:, oc, tsl(tt)], tmp[:, oc, :], tmp[:, 5, :], ALU.mult,
                               [("t", oc), ("t", 5)], [(nm, oc, tt)])
                    pa = pbank()
                    mm_group(pa, [wlat[:, k, 576:672] for k in range(8)], xh, xrd + ["wbA"], out_ap=ps[pa][0:96, :])
                    pb2 = pbank()
                    mm_group(pb2, [wkr[:, k, :] for k in range(8)], xh, xrd + ["wkr"], out_ap=ps[pb2][0:96, :])
                    TT("dve", tmp[P, 6, :], ps[pa][P, :], cosT[P, tsl(tt)], ALU.mult, [PSB[pa], "cosT"], [("t", 6)])
                    TT("dve", tmp[P, 7, :], ps[pb2][P, :], sinT[P, tsl(tt)], ALU.mult, [PSB[pb2], "sinT"], [("t", 7)])
                    TT("pool", KT[P, 0, tsl(tt)], tmp[P, 6, :], tmp[P, 7, :], ALU.add, [("t", 6), ("t", 7)], [("kpe", tt)])
                    CP("pool", KT[P, 1, tsl(tt)], KT[P, 0, tsl(tt)], [("kpe", tt)], [("kpe1", tt)])
                S.barrier_all()

                def load_pair(c, l=l):
                    hb_ = c % 2
                    wb = wbuf[:, hb_ * 4096:(hb_ + 1) * 4096]
                    DMA(wb.rearrange("p (k q) -> p k q", k=8), winB_s[l][:, c, :, :], [("winB", l)], ["wbA" if hb_ == 0 else "wbB"])
                    sm = wsm[:, hb_, :]
                    DMA(sm[:, 0:576].rearrange("p (k q) -> p k q", k=3), wuqP_s[l][:, c, :, :], [("wuqP", l)], [("wsm", hb_, 0)])
                    DMA(sm[:, 576:1152], wqrP_s[l][:, c * 576:(c + 1) * 576], [("wqrP", l)], [("wsm", hb_, 1)])
                    DMA(sm[:, 1152:1664].rearrange("p (k q) -> p k q", k=2), wukvP_s[l][:, c, :, :], [("wukvP", l)], [("wsm", hb_, 2)])
                    DMA(sm[:, 1664:1792], gxw_s[l][:, c * 128:(c + 1) * 128], [("gxws", l)], [("wsm", hb_, 3)])
                    DMA(sm[:, 1792:1920], gaw_s[l][:, c * 128:(c + 1) * 128], [("gaws", l)], [("wsm", hb_, 4)])

                nproj[0] = 2
                MS("pool", VA[:, :, 0, 64:128], 1.0, ["VA"])
                MS("pool", VA[:, :, 1, 0:64], 1.0, ["VA"])
                def make_rnn(c, l=l):
                    hb_ = c % 2
                    WB = "wbA" if hb_ == 0 else "wbB"
                    wc = wbuf[:, hb_ * 4096:(hb_ + 1) * 4096].rearrange("p (k j q) -> p k j q", k=8, j=4)
                    sm = wsm[:, hb_, :]
                    gxw = sm[:, 1664:1792]
                    gaw = sm[:, 1792:1920]
                    cwc = lambda tap: pcol(O_CW + tap * 8 + c)
                    def rnn_p1(qt):
                        xh = [xhi[:, k, tsl(qt)] for k in range(8)]
                        xrd = [XB(k, qt) for k in range(8)]
                        pc = lambda o: pvd[:, l * 32 + o + c: l * 32 + o + c + 1]
                        px = rbank()
                        mm_group(px, [wc[:, k, 0, :] for k in range(8)], xh, xrd + [WB])
                        yield
                        X = ps[px]
                        cv = tmp[:, 0, :]
                        TSC("dve", cv, X[:], cwc(3), pcol(O_CB + c), ALU.mult, ALU.add, [PSB[px], "pv"], [("t", 0)])
                        yield
                        pg = rbank()
                        mm_group(pg, [wc[:, k, 1, :] for k in range(8)], xh, xrd + [WB])
                        yield
                        for sh, tap in ((1, 2), (2, 1), (3, 0)):
                            STT("dve", cv[:, sh:TS], X[:, 0:TS - sh], cwc(tap), cv[:, sh:TS], ALU.mult, ALU.add,
                                [PSB[px], ("t", 0), "pv"], [("t", 0)])
                            yield
                        if qt > 0:
                            for sh, tap in ((1, 2), (2, 1), (3, 0)):
                                STT("dve", cv[:, 0:sh], hr[:, 3 - sh:3], cwc(tap), cv[:, 0:sh], ALU.mult, ALU.add,
                                    ["hr", ("t", 0), "pv"], [("t", 0)])
                        if qt < NT - 1:
                            CP("dve", hr[:, 0:3], X[:, TS - 3:TS], [PSB[px]], ["hr"])
                        yield
                        ACT(PT[:, 4, :], cv, AF.Copy, [("t", 0)], [("pt", 4)])
                        ACT(tmp[:, 3, :], ps[pg][:], AF.Square, [PSB[pg]], [("t", 3)])
                        ACT(tmp[:, 4, :], ps[pg][:], AF.Copy, [PSB[pg]], [("t", 4)])
                        yield
                        TSC("dve", tmp[:, 3, :], tmp[:, 3, :], 0.044715, 1.0, ALU.mult, ALU.add, [("t", 3)], [("t", 3)])
                        yield
                        TT("dve", tmp[:, 3, :], tmp[:, 3, :], tmp[:, 4, :], ALU.mult, [("t", 3), ("t", 4)], [("t", 3)])
                        yield
                        pgx = rbank()
                        mm_group(pgx, [gxw], [PT[:, 4, :]], [("pt", 4), ("wsm", hb_, 3)])
                        pga = rbank()
                        mm_group(pga, [gaw], [PT[:, 4, :]], [("pt", 4), ("wsm", hb_, 4)])
                        yield
                        ACT(tmp[:, 3, :], tmp[:, 3, :], AF.Tanh, [("t", 3)], [("t", 3)], scale=0.7978845608028654)
                        yield
                        ACT(tmp[:, 1, :], ps[pgx][:], AF.Tanh, [PSB[pgx], "pvd"], [("t", 1)], bias=pc(16), scale=0.5)
                        yield
                        ACT(tmp[:, 2, :], ps[pga][:], AF.Tanh, [PSB[pga], "pvd"], [("t", 2)], bias=pc(24), scale=0.5)
                        STT("dve", tmp[:, 4, :], tmp[:, 3, :], 1.0, tmp[:, 4, :], ALU.add, ALU.mult, [("t", 3), ("t", 4)], [("t", 4)])
                        yield
                        ACT(tmp[:, 3, :], tmp[:, 2, :], AF.Exp, [("t", 2), "pvd"], [("t", 3)], bias=pc(0), scale=pc(0))
                        yield
                        ACT(tmp[:, 2, :], tmp[:, 2, :], AF.Exp, [("t", 2), "pvd"], [("t", 2)], bias=pc(8), scale=pc(8))
                        STT("dve", tmp[:, 1, :], tmp[:, 1, :], 1.0, tmp[:, 0, :], ALU.add, ALU.mult, [("t", 1), ("t", 0)], [("t", 1)])
                        yield
                        ACT(tmp[:, 2, :], tmp[:, 2, :], AF.Sqrt, [("t", 2)], [("t", 2)], bias=1.0, scale=-1.0)
                        yield
                        STT("dve", tmp[:, 1, :], tmp[:, 1, :], 0.5, tmp[:, 2, :], ALU.mult, ALU.mult, [("t", 1), ("t", 2)], [("t", 1)])
                        yield
                        init = 0.0 if qt == 0 else hlast[:, 0:1]
                        S.op("dve", "tensor_tensor_scan", [("t", 3), ("t", 1), "hlast"], [("t", 2)],
                             out=tmp[:, 2, :], data0=tmp[:, 3, :], data1=tmp[:, 1, :], initial=init, op0=ALU.mult, op1=ALU.add)
                        CP("dve", hlast[:, 0:1], tmp[:, 2, TS - 1:TS], [("t", 2)], ["hlast"])
                        yield
                        TT("pool", tmp[:, 4, :], tmp[:, 4, :], tmp[:, 2, :], ALU.mult, [("t", 4), ("t", 2)], [("t", 4)])
                        yield

                    def rnn_p2(qt):
                        xh = [xhi[:, k, tsl(qt)] for k in range(8)]
                        xrd = [XB(k, qt) for k in range(8)]
                        pa_ = rbank()
                        mm_group(pa_, [wc[:, k, 2, :] for k in range(8)], xh, xrd + [WB])
                        yield
                        ACT(tmp[:, 1, :], ps[pa_][:], AF.Tanh, [PSB[pa_]], [("t", 1)], scale=0.5)
                        yield
                        pbg = rbank()
                        mm_group(pbg, [wc[:, k, 3, :] for k in range(8)], xh, xrd + [WB])
                        yield
                        STT("dve", tmo[:, 0, :], tmp[:, 1, :], 1.0, tmp[:, 4, :], ALU.add, ALU.mult, [("t", 4), ("t", 1)], ["tM"])
                        ACT(tmo[:, 1, :], ps[pbg][:], AF.Tanh, [PSB[pbg]], ["tB"], scale=0.5)
                        yield


                    return [g for q_ in range(NT) for g in (rnn_p1(q_), rnn_p2(q_))]

                gq = []

                def pump(n, limit):
                    k = 0
                    while k < n and gq_i[0] <= limit and gq_i[0] < len(gq):
                        try:
                            next(gq[gq_i[0]])
                            k += 1
                        except StopIteration:
                            gq_i[0] += 1

                def flush(limit):
                    while gq_i[0] <= limit and gq_i[0] < len(gq):
                        try:
                            next(gq[gq_i[0]])
                        except StopIteration:
                            gq_i[0] += 1

                gq_i = [0]
                load_pair(0)
                gq += make_rnn(0)
                for c in range(8):
                    hb_ = c % 2
                    WB = "wbA" if hb_ == 0 else "wbB"
                    if c + 1 < 8:
                        load_pair(c + 1)
                        gq += make_rnn(c + 1)
                    wc = wbuf[:, hb_ * 4096:(hb_ + 1) * 4096].rearrange("p (k j q) -> p k j q", k=8, j=4)
                    sm = wsm[:, hb_, :]
                    wuq = sm[:, 0:576].rearrange("p (k q) -> p k q", k=3)
                    wqr = sm[:, 576:1152].rearrange("p (k h q) -> p k h q", k=3, h=2)
                    wukv = sm[:, 1152:1664].rearrange("p (k q) -> p k q", k=2)
                    gxw = sm[:, 1664:1792]
                    gaw = sm[:, 1792:1920]
                    cwc = lambda tap, c=c: pcol(O_CW + tap * 8 + c)
                    for hh in range(2):
                        for tt in range(NT):
                            pb = abank()
                            mm_group(pb, [wukv[:, k, hh * 128: hh * 128 + 64] for k in range(2)],
                                     [kvn[:, k, tsl(tt)] for k in range(2)], [("kvn", 0, tt), ("kvn", 1, tt), ("wsm", hb_, 2)],
                                     out_ap=ps[pb][0:64, :])
                            ACT(KT[0:64, hh, tsl(tt)], ps[pb][0:64, :], AF.Copy, [PSB[pb]], [("KT", hh, tt)])
                            pump(2, 8 * c + 1)
                    for g4 in range(4):
                        pb = abank()
                        pv4 = ps[pb][:].rearrange("p (t h d) -> p t h d", t=4, h=2)
                        for t4 in range(4):
                            tb = g4 * 4 + t4
                            for k in range(2):
                                MM(pv4[:, t4, :, :], kvn[:, k, tb * 128:(tb + 1) * 128],
                                   wukv[:, k, :].rearrange("p (h d) -> p h d", h=2)[:, :, 64:128], k == 0, k == 1,
                                   [("kvn", k, g4), ("wsm", hb_, 2)], [PSB[pb]])
                        ACT(VA[:, g4 * 4:(g4 + 1) * 4, 0, 0:64], pv4[:, :, 0, :], AF.Copy, [PSB[pb]], [("VA", g4)])
                        ACT(VA[:, g4 * 4:(g4 + 1) * 4, 1, 64:128], pv4[:, :, 1, :], AF.Copy, [PSB[pb]], [("VA", g4)])
                        pump(2, 8 * c + 1)
                    nproj[0] = 2
                    for qt in range(NT):
                        LIM = min(8 * c + 2 * qt + 2, 8 * c + 7) if not XC_RUNAHEAD else 8 * c + 2 * qt + 2
                        pump(6 if qt == 0 else 2, LIM)
                        qr = [("qn", k, qt) for k in range(3)]
                        qrh = [qn[:, k, tsl(qt)] for k in range(3)]
                        for hh in range(2):
                            pq = 4 + 2 * hh
                            mm_group(pq, [wuq[:, k, hh * 96:(hh + 1) * 96] for k in range(3)], qrh,
                                     qr + [("wsm", hb_, 0)], out_ap=ps[pq][0:96, :])
                            pr = 5 + 2 * hh
                            mm_group(pr, [wqr[:, k, hh, :] for k in range(3)], qrh,
                                     qr + [("wsm", hb_, 1)], out_ap=ps[pr][0:96, :])
                            CP("dve", QT[0:64, hh, :], ps[pq][0:64, :], [PSB[pq]], [("QT", hh)])
                            TT("dve", tmp[P, 5, :], ps[pq][P, :], cosT[P, tsl(qt)], ALU.mult, [PSB[pq], "cosT"], [("t", 5)])
                            TT("dve", tmp[P, 6, :], ps[pr][P, :], sinT[P, tsl(qt)], ALU.mult, [PSB[pr], "sinT"], [("t", 6)])
                            TT("pool", QT[P, hh, :], tmp[P, 5, :], tmp[P, 6, :], ALU.add, [("t", 5), ("t", 6)], [("QT", hh)])
                            pump(2, LIM)
                        nkb = 4 * qt + 4
                        for i in range(nkb + 1):
                            if i < nkb:
                                kb = i
                                n0 = max(kb - 4 * qt, 0) * 128
                                sl = kb % 2
                                for hh in range(2):
                                    bk = 4 + 2 * sl + hh
                                    MM(ps[bk][:, n0:TS], KT[0:96, hh, kb * 128:(kb + 1) * 128], QT[0:96, hh, n0:TS], True, True,
                                       [("KT", hh, kb // 4), ("kpe" if hh == 0 else "kpe1", kb // 4), ("QT", hh)], [PSB[bk]])
                            if i >= 1:
                                kb = i - 1
                                j = kb - 4 * qt
                                n0 = max(j, 0) * 128
                                sl = kb % 2
                                ACT(PT4[:, sl, :, n0:TS], ppair[sl][:, :, n0:TS], AF.Exp,
                                    [PSB[4 + 2 * sl], PSB[5 + 2 * sl]], [("pt", sl)], scale=SCALE)
                                if j >= 0:
                                    for hh in range(2):
                                        TT("pool", PT4[:, sl, hh, n0:n0 + 128], PT4[:, sl, hh, n0:n0 + 128], trib[:], ALU.mult,
                                           [("pt", sl), "trib"], [("pt", sl)])
                                for hh in range(2):
                                    MM(ps[2 + hh][:, n0:TS], VA[:, kb, hh, :], PT4[:, sl, hh, n0:TS], kb == 0, kb == nkb - 1,
                                       [("VA", kb // 4), ("pt", sl), "VA"], [PSB[2 + hh]])
                            pump(3, LIM)
                        RCP(tmp[0:64, 7, :], ps[2][64:128, :], [PSB[2]], [("t", 7)])
                        TT("dve", tmp[0:64, 8, :], ps[2][0:64, :], tmp[0:64, 7, :], ALU.mult, [PSB[2], ("t", 7)], [("t", 8)])
                        RCP(tmp[64:128, 7, :], ps[3][0:64, :], [PSB[3]], [("t", 7)])
                        TT("dve", tmp[64:128, 8, :], ps[3][64:128, :], tmp[64:128, 7, :], ALU.mult, [PSB[3], ("t", 7)], [("t", 8)])
                        flush(8 * c + 2 * qt + 1)
                        STT("dve", tmp[:, 8, :], tmo[:, 1, :], 1.0, tmp[:, 8, :], ALU.add, ALU.mult, [("t", 8), "tB"], [("t", 8)])
                        STT("dve", tmp[:, 8, :], tmo[:, 0, :], 0.5, tmp[:, 8, :], ALU.mult, ALU.add, [("t", 8), "tM"], [("t", 8)])
                        ACT(merged[:, c, tsl(qt)], tmp[:, 8, :], AF.Copy, [("t", 8)], [("mg", c, qt)], scale=0.5)
                nproj[0] = 3
                S.barrier_all()

                def stat_evac(pm, pq_, tm, tr):
                    TSC("dve", tmp[:, tm, :], ps[pm][:], 1.0 / D, None, ALU.mult, None, [PSB[pm]], [("t", tm)])
                    TT("pool", tmp[:, tr, :], tmp[:, tm, :], tmp[:, tm, :], ALU.mult, [("t", tm)], [("t", tr)])
                    STT("dve", tmp[:, tr, :], ps[pq_][:], 1.0 / D, tmp[:, tr, :], ALU.mult, ALU.subtract,
                        [PSB[pq_], ("t", tr)], [("t", tr)])
                    ACT(tmp[:, tr, :], tmp[:, tr, :], AF.Sqrt, [("t", tr), "cst"], [("t", tr)], bias=epsc, scale=1.0)
                    RCP(tmp[:, tr, :], tmp[:, tr, :], [("t", tr)], [("t", tr)])

                def ln_norm_gen(tt, tm, tr, og, ob, zts):
                    G = len(zts)
                    for m0 in range(0, 8, G):
                        ms = list(range(m0, min(8, m0 + G)))
                        for stage in range(6):
                            for gi, m in enumerate(ms):
                                zt, ZB = zts[gi]
                                if stage == 0:
                                    TT("pool", zt, xhi[:, m, tsl(tt)], xlo[:, m, tsl(tt)], ALU.add, [XB(m, tt), XL(m, tt)], ZB)
                                elif stage == 1:
                                    TT("dve", zt, zt, tmp[:, tm, :], ALU.subtract, ZB + [("t", tm)], ZB)
                                elif stage == 2:
                                    TT("pool", zt, zt, tmp[:, tr, :], ALU.mult, ZB + [("t", tr)], ZB)
                                elif stage == 3:
                                    ACT(zt, zt, AF.Identity, ZB + ["pv"], ZB, bias=pcol(ob + m), scale=pcol(og + m))
                                elif stage == 4:
                                    ACT(xhi[:, m, tsl(tt)], zt, AF.Copy, ZB, [XB(m, tt)])
                                else:
                                    TT("dve", xlo[:, m, tsl(tt)], zt, xhi[:, m, tsl(tt)], ALU.subtract, ZB + [XB(m, tt)], [XL(m, tt)])
                                yield

                def gpump(gen, n):
                    if gen is None:
                        return
                    for _ in range(n):
                        try:
                            next(gen)
                        except StopIteration:
                            return

                def chain2(g1, g2):
                    yield from g1
                    yield from g2

                rcnt = [0]
                rtemps = [[0, 1, 8]]

                def resid(pb, m, tt, pm, pq_):
                    k = rcnt[0]
                    rcnt[0] += 1
                    ti_ = rtemps[0][k % len(rtemps[0])]
                    zt = tmp[:, ti_, :]
                    ZB = ("t", ti_)
                    STT("dve", zt, xhi[:, m, tsl(tt)], ALPHA, ps[pb][:], ALU.mult, ALU.add, [XB(m, tt), PSB[pb]], [ZB])
                    STT("dve", zt, xlo[:, m, tsl(tt)], ALPHA, zt, ALU.mult, ALU.add, [XL(m, tt), ZB], [ZB])
                    ACT(xhi[:, m, tsl(tt)], zt, AF.Copy, [ZB], [XB(m, tt)])
                    TT("pool", xlo[:, m, tsl(tt)], zt, xhi[:, m, tsl(tt)], ALU.subtract, [ZB, XB(m, tt)], [XL(m, tt)])
                    sq = PT[:, k % 4, :]
                    ACT(sq, zt, AF.Square, [ZB], [("pt", k % 4)])

                    def stats():
                        MM(ps[pm][:], onesb[:], xhi[:, m, tsl(tt)], m == 0, m == 7, [XB(m, tt), "onesb"], [PSB[pm]])
                        MM(ps[pq_][:], onesb[:], sq, m == 0, m == 7, [("pt", k % 4), "onesb"], [PSB[pq_]])
                    return stats

                wo = wbuf[:].rearrange("p (k q) -> p k q", k=8)
                DMA(wo, wout_s[l], [("wout", l)], ["wbA", "wbB"])
                gen = None
                for tt in range(NT):
                    pm, pq_ = (4, 5) if tt % 2 == 0 else (6, 7)
                    pending = None
                    for m in range(8):
                        pb = pbank()
                        mm_group(pb, [wo[:, cc, m * 128:(m + 1) * 128] for cc in range(8)], [merged[:, cc, tsl(tt)] for cc in range(8)],
                                 [("mg", cc, tt) for cc in range(8)] + ["wbA"])
                        if pending is not None:
                            pending()
                        pending = resid(pb, m, tt, pm, pq_)
                        gpump(gen, 6)
                    pending()
                    gpump(gen, 1000)
                    stat_evac(pm, pq_, 4, 5)
                    gen = ln_norm_gen(tt, 4, 5, O_L1G, O_L1B, [(tmp[:, i_, :], [("t", i_)]) for i_ in (2, 3, 6, 7)])
                gpump(gen, 1000)
                S.barrier_all()

                def load_up(j, l=l):
                    bi = j % 4
                    DMA(wbuf[:, bi * 2048:(bi + 1) * 2048].rearrange("p (k q) -> p k q", k=8), wupP_s[l][:, j, :, :],
                        [("wupP", l)], [("wup", bi)])

                def load_dn(m, l=l):
                    DMA(wdn_sb[m % 2], wdnP_s[l][:, m, :, :], [("wdnP", l)], [("wdn", m % 2)])

                gen = None
                for half in range(2):
                    for j in range(3):
                        load_up(j)
                    for j in range(24):
                        if j + 3 < 24:
                            load_up(j + 3)
                        wu = wbuf[:, (j % 4) * 2048:(j % 4 + 1) * 2048].rearrange("p (k g q) -> p k g q", k=8, g=2)
                        for t2 in range(2):
                            tt = half * 2 + t2
                            xh = [xhi[:, k, tsl(tt)] for k in range(8)]
                            xrd = [XB(k, tt) for k in range(8)]
                            for gv in range(2):
                                jj = gv * 24 + j
                                pb = pbank()
                                mm_group(pb, [wu[:, k, gv, :] for k in range(8)], xh, xrd + [("wup", j % 4)])
                                ti = 2 * t2 + gv
                                ht = tmp[:, ti, :]
                                HB = ("t", ti)
                                fw = lambda tap, jj=jj: pcol(O_FCW + tap * 48 + jj)
                                ACT(ht, ps[pb][:], AF.Identity, [PSB[pb], "pv"], [HB], bias=pcol(O_FCB + jj), scale=fw(2))
                                STT("dve", ht[:, 1:TS], ps[pb][:, 0:TS - 1], fw(1), ht[:, 1:TS], ALU.mult, ALU.add, [PSB[pb], HB, "pv"], [HB])
                                STT("dve", ht[:, 2:TS], ps[pb][:, 0:TS - 2], fw(0), ht[:, 2:TS], ALU.mult, ALU.add, [PSB[pb], HB, "pv"], [HB])
                                if tt > 0:
                                    STT("dve", ht[:, 0:1], hal[:, jj, 1:2], fw(1), ht[:, 0:1], ALU.mult, ALU.add, [("hal", jj), HB, "pv"], [HB])
                                    STT("dve", ht[:, 0:2], hal[:, jj, 0:2], fw(0), ht[:, 0:2], ALU.mult, ALU.add, [("hal", jj), HB, "pv"], [HB])
                                if tt < NT - 1:
                                    CP("dve", hal[:, jj, :], ps[pb][:, TS - 2:TS], [PSB[pb]], [("hal", jj)])
                            tg, tv = 2 * t2, 2 * t2 + 1
                            ACT(tmp[:, tg, :], tmp[:, tg, :], AF.Gelu_apprx_tanh, [("t", tg)], [("t", tg)])
                            TT("pool", abuf[:, j, t2 * TS:(t2 + 1) * TS], tmp[:, tg, :], tmp[:, tv, :], ALU.mult,
                               [("t", tg), ("t", tv)], [("a", j, t2)])
                            gpump(gen, 2)
                    gpump(gen, 1000)
                    load_dn(0)
                    pending = None
                    rtemps[0] = [0, 1, 2, 3]
                    for m in range(8):
                        if m + 1 < 8:
                            load_dn(m + 1)
                        wd = wdn_sb[m % 2]
                        for t2 in range(2):
                            tt = half * 2 + t2
                            pb = pbank()
                            mm_group(pb, [wd[:, j, :] for j in range(24)], [abuf[:, j, t2 * TS:(t2 + 1) * TS] for j in range(24)],
                                     [("a", j, t2) for j in range(24)] + [("wdn", m % 2)])
                            if pending is not None:
                                pending()
                            pending = resid(pb, m, tt, 4 + 2 * t2, 5 + 2 * t2)
                    pending()
                    stat_evac(4, 5, 4, 5)
                    stat_evac(6, 7, 6, 7)
                    ptv = PT[:, 0:2, :].rearrange("p a b -> p (a b)").bitcast(F32)
                    zts2 = [(tmp[:, 8, :], [("t", 8)]), (ptv, [("pt", 0), ("pt", 1)])]
                    gen = chain2(ln_norm_gen(half * 2, 4, 5, O_L2G, O_L2B, zts2), ln_norm_gen(half * 2 + 1, 6, 7, O_L2G, O_L2B, zts2))
                gpump(gen, 1000)
                S.barrier_all()

            S.barrier_all()
            cnt = 0
            for tt in range(NT):
                for g in range(2):
                    for kk in range(4):
                        m = g * 4 + kk
                        TT("pool", tmp[:, kk, :], xhi[:, m, tsl(tt)], xlo[:, m, tsl(tt)], ALU.add, [XB(m, tt), XL(m, tt)], [("t", kk)])
                    for t4 in range(4):
                        tb = tt * 4 + t4
                        pb = pbank()
                        for kk in range(4):
                            TR(ps[pb][:, kk * 128:(kk + 1) * 128], tmp[:, kk, t4 * 128:(t4 + 1) * 128], [("t", kk), "cst"], [PSB[pb]])
                        oi = 4 + cnt % 4
                        cnt += 1
                        if cnt % 2 == 0:
                            ACT(tmp[:, oi, :], ps[pb][:], AF.Copy, [PSB[pb]], [("t", oi)])
                        else:
                            CP("dve", tmp[:, oi, :], ps[pb][:], [PSB[pb]], [("t", oi)])
                        DMA(y_d[s, tb * 128:(tb + 1) * 128, g * 512:(g + 1) * 512], tmp[:, oi, :], [("t", oi)], [("y", s, tb, g)])
            S.barrier_all()
        S.barrier_all()
        S.run()
    return nc


def _fm(v):
    v = np.asarray(v, np.float32)
    return np.ascontiguousarray(v.reshape(-1, 128).T)


def _host_params(inp, nl):
    pvs = np.zeros((128, nl * PL), np.float32)
    for l in range(nl):
        b = l * PL
        for tap in range(4):
            pvs[:, b + O_CW + tap * 8: b + O_CW + tap * 8 + 8] = _fm(inp["conv_w"][l, tap])
        pvs[:, b + O_CB: b + O_CB + 8] = _fm(inp["conv_b"][l])
        pvs[:, b + O_GXB: b + O_GXB + 8] = _fm(inp["gx_b"][l])
        pvs[:, b + O_GAB: b + O_GAB + 8] = _fm(inp["ga_b"][l])
        pvs[:, b + O_LAM: b + O_LAM + 8] = _fm(inp["lru_lambda"][l])
        pvs[:, b + O_L1G: b + O_L1G + 8] = _fm(inp["ln1_g"][l])
        pvs[:, b + O_L1B: b + O_L1B + 8] = _fm(inp["ln1_b"][l])
        pvs[:, b + O_L2G: b + O_L2G + 8] = _fm(inp["ln2_g"][l])
        pvs[:, b + O_L2B: b + O_L2B + 8] = _fm(inp["ln2_b"][l])
        for tap in range(3):
            pvs[:, b + O_FCW + tap * 48: b + O_FCW + tap * 48 + 48] = _fm(inp["ffn_conv_w"][l, tap])
        pvs[:, b + O_FCB: b + O_FCB + 48] = _fm(inp["ffn_conv_b"][l])
        pvs[:, b + O_QG: b + O_QG + 3] = _fm(inp["q_norm_g"][l])
        pvs[:, b + O_KVG: b + O_KVG + 2] = _fm(inp["kv_norm_g"][l])
    return pvs


def _host_bd(w, nl):
    w = np.asarray(w, np.float32)
    out = np.zeros((nl, 128, 8, 128), np.float32)
    for c in range(8):
        for b in range(2):
            out[:, b * 64:(b + 1) * 64, c, b * 64:(b + 1) * 64] = w[:, 2 * c + b]
    return out.reshape(nl, 128, 1024)


def _consts():
    c = np.zeros((128, 258), np.float32)
    c[:, 0:128] = np.eye(128, dtype=np.float32)
    k = np.arange(128)[:, None]
    q = np.arange(128)[None, :]
    c[:, 128:256] = (q >= k).astype(np.float32)
    inv = (10000.0 ** (-np.arange(0, 32, 2, dtype=np.float32) / np.float32(32))).astype(np.float32)
    for p in range(64, 96):
        c[p, 256] = inv[(p - 64) % 16]
    c[:, 257] = EPS
    return c


_NC_CACHE = {}
_TRACE = False
_LAST = [None]


def run(inp, NS, layers, ncores, seq_of_core):
    nl = L_ALL
    key = (NS, tuple(layers))
    if key not in _NC_CACHE:
        _NC_CACHE[key] = build(NS, list(layers))
    nc = _NC_CACHE[key]
    shared = {
        "w_in": np.ascontiguousarray(inp["w_in"], np.float32),
        "w_uq": np.ascontiguousarray(inp["w_uq"], np.float32),
        "w_ukv": np.ascontiguousarray(inp["w_ukv"], np.float32),
        "w_out": np.ascontiguousarray(inp["w_out"], np.float32),
        "w_up": np.ascontiguousarray(inp["w_up"], np.float32),
        "w_down": np.ascontiguousarray(inp["w_down"], np.float32),
        "gxw": _host_bd(inp["gx_w"], nl),
        "gaw": _host_bd(inp["ga_w"], nl),
        "pv": _host_params(inp, nl),
        "cst": _consts(),
    }
    x = np.asarray(inp["x"], np.float32)
    pos = np.asarray(inp["positions"], np.int32)
    in_maps = []
    for ci in range(ncores):
        seqs = seq_of_core[ci]
        m = dict(shared)
        m["x"] = np.ascontiguousarray(x[seqs])
        m["pos"] = np.ascontiguousarray(np.broadcast_to(pos[seqs][:, None, :], (len(seqs), 32, T)))
        in_maps.append(m)
    res = run_bass_kernel_spmd(nc, in_maps, core_ids=list(range(ncores)), trace=_TRACE)
    _LAST[0] = res
    return [r["y"] for r in res.results]


def kernel(**inputs):
    B = inputs["x"].shape[0]
    ncores = 8
    NS = B // ncores
    seq_of_core = [list(range(ci * NS, (ci + 1) * NS)) for ci in range(ncores)]
    outs = run(inputs, NS, list(range(L_ALL)), ncores, seq_of_core)
    return np.concatenate(outs, axis=0).astype(np.float32)
```

```python
from contextlib import ExitStack
import math
import numpy as np
import concourse.bass as bass
import concourse.mybir as mybir
from concourse.bass_utils import run_bass_kernel_spmd

F32 = mybir.dt.float32
BF16 = mybir.dt.bfloat16
I32 = mybir.dt.int32
AF = mybir.ActivationFunctionType
ALU = mybir.AluOpType

ENGS = ["pe", "act", "dve", "pool", "sp"]

D = 1024
T = 2048
TS = 512
NT = T // TS
L_ALL = 4
INW = 4768
DFF = 3072
ALPHA = float((2 * L_ALL) ** 0.25)
EPS = 1e-6
SCALE = float(96 ** -0.5)
PL = 296
O_CW, O_CB, O_GXB, O_GAB, O_LAM = 0, 32, 40, 48, 56
O_L1G, O_L1B, O_L2G, O_L2B = 64, 72, 80, 88
O_FCW, O_FCB, O_QG, O_KVG = 96, 240, 288, 291
PAIR_EXP = True
XC_RUNAHEAD = True
PUMP_N = 2
PUMP_S = 5
TRANSITIVE = True
TWO_PI = float(2 * math.pi)
PI = float(math.pi)


class Buf:
    __slots__ = ("name", "lw", "rd")

    def __init__(self, name=""):
        self.name = name
        self.lw = None
        self.rd = {}


class Sched:
    K = 8
    R = 32

    def __init__(self, nc, stack):
        self.nc = nc
        self.sems = {e: [stack.enter_context(nc.semaphore(f"s_{e}_{i}")) for i in range(self.K)]
                     for e in ENGS}
        self.dsems = [stack.enter_context(nc.semaphore(f"d_{i}")) for i in range(self.R)]
        self.cnt = {e: 0 for e in ENGS}
        self.seen = {e: {f: 0 for f in ENGS} for e in ENGS}
        self.seen_d = {e: [0] * self.R for e in ENGS}
        self.clock = {e: [] for e in ENGS}
        self.nd = 0
        self.thunks = {e: [] for e in ENGS}
        self.bufs = {}

    def b(self, key):
        x = self.bufs.get(key)
        if x is None:
            x = self.bufs[key] = Buf(str(key))
        return x

    def _need(self, e, tok, out):
        if tok is None:
            return
        if tok[0] == "e":
            _, f, i = tok
            if f == e and e == "pe":
                return
            if self.seen[e][f] >= i + 1:
                return
            out.append(tok)
        else:
            n = tok[1]
            if self.seen_d[e][n % self.R] >= n // self.R + 1:
                return
            out.append(tok)

    def _emit_waits(self, e, toks):
        best = {}
        dm = {}
        for t in toks:
            if t[0] == "e":
                best[t[1]] = max(best.get(t[1], -1), t[2])
            else:
                s = t[1] % self.R
                dm[s] = max(dm.get(s, -1), t[1])
        for f, i in best.items():
            if self.seen[e][f] >= i + 1:
                continue
            self.thunks[e].append(("w", self.sems[f][i % self.K], i // self.K + 1))
            self.seen[e][f] = i + 1
            clk = self.clock[f][i] if TRANSITIVE else {}
            for g in (ENGS if TRANSITIVE else []):
                if g != e and clk[g] > self.seen[e][g]:
                    self.seen[e][g] = clk[g]
        for s, n in dm.items():
            v = n // self.R + 1
            if self.seen_d[e][s] >= v:
                continue
            self.thunks[e].append(("w", self.dsems[s], 16 * v))
            self.seen_d[e][s] = v

    def _deps(self, e, reads, writes):
        toks = []
        for r in reads:
            self._need(e, r.lw, toks)
        for w in writes:
            self._need(e, w.lw, toks)
            for f, i in w.rd.items():
                if f == "d":
                    for n in i:
                        self._need(e, ("d", n), toks)
                else:
                    self._need(e, ("e", f, i), toks)
        return toks

    def _bl(self, xs):
        return [x if isinstance(x, Buf) else self.b(x) for x in xs]

    def op(self, e, name, reads=(), writes=(), **kw):
        fn = (name, kw)
        reads = self._bl(reads)
        writes = self._bl(writes)
        self._emit_waits(e, self._deps(e, reads, writes))
        i = self.cnt[e]
        self.cnt[e] = i + 1
        self.thunks[e].append(("i", fn, self.sems[e][i % self.K], 1))
        self.clock[e].append(dict(self.seen[e]))
        tok = ("e", e, i)
        for r in reads:
            if r.rd.get(e, -1) < i:
                r.rd[e] = i
        for w in writes:
            w.lw = tok
            w.rd = {}
        return tok

    def dma(self, out, in_, reads=(), writes=(), q="sp"):
        fn = ("dma_start", dict(out=out, in_=in_))
        reads = self._bl(reads)
        writes = self._bl(writes)
        n = self.nd
        self.nd += 1
        s = n % self.R
        toks = self._deps(q, reads, writes)
        if n >= self.R:
            self._need(q, ("d", n - self.R), toks)
        self._emit_waits(q, toks)
        self.thunks[q].append(("i", fn, self.dsems[s], 16))
        tok = ("d", n)
        for r in reads:
            r.rd.setdefault("d", []).append(n)
        for w in writes:
            w.lw = tok
            w.rd = {}
        return tok

    def barrier_all(self):
        for e in ENGS:
            toks = []
            for f in ENGS:
                if self.cnt[f] > 0:
                    self._need(e, ("e", f, self.cnt[f] - 1), toks)
            for n in range(max(0, self.nd - self.R), self.nd):
                self._need(e, ("d", n), toks)
            self._emit_waits(e, toks)

    def run(self):
        nc = self.nc
        with nc.Block() as block:
            def mk(e):
                def body(eng):
                    for t in self.thunks[e]:
                        if t[0] == "w":
                            eng.wait_ge(t[1], t[2])
                        else:
                            getattr(eng, t[1][0])(**t[1][1]).then_inc(t[2], t[3])
                return body
            block.tensor(mk("pe"))
            block.scalar(mk("act"))
            block.vector(mk("dve"))
            block.gpsimd(mk("pool"))
            block.sync(mk("sp"))


def build(NS, layers, nlw=L_ALL):
    nc = bass.Bass("TRN2", target_bir_lowering=False)
    dt_in = lambda n, s, d=F32: nc.dram_tensor(n, s, d, kind="ExternalInput").ap()
    x_d = dt_in("x", [NS, T, D])
    pos_d = dt_in("pos", [NS, 32, T], I32)
    w_in_d = dt_in("w_in", [nlw, D, INW])
    w_uq_d = dt_in("w_uq", [nlw, 384, 1536])
    w_ukv_d = dt_in("w_ukv", [nlw, 256, 2048])
    w_out_d = dt_in("w_out", [nlw, D, D])
    w_up_d = dt_in("w_up", [nlw, D, 2 * DFF])
    w_dn_d = dt_in("w_down", [nlw, DFF, D])
    gxw_d = dt_in("gxw", [nlw, 128, 8 * 128])
    gaw_d = dt_in("gaw", [nlw, 128, 8 * 128])
    pv_d = dt_in("pv", [128, nlw * PL])
    cst_d = dt_in("cst", [128, 258])
    y_d = nc.dram_tensor("y", [NS, T, D], F32, kind="ExternalOutput").ap()

    scr = lambda n, s: nc.dram_tensor(n, s, BF16, kind="Internal").ap()
    winL_s = scr("winL", [nlw, 128, 8, 672])
    winB_s = scr("winB", [nlw, 128, 8, 8, 512])
    wkrot_s = scr("wkrot", [nlw, 128, 8 * 96])
    wuqP_s = scr("wuqP", [nlw, 128, 8, 3, 192])
    wqrP_s = scr("wqrP", [nlw, 128, 8 * 3 * 192])
    wukvP_s = scr("wukvP", [nlw, 128, 8, 2, 256])
    wout_s = scr("wout", [nlw, 128, 8, 1024])
    wupP_s = scr("wupP", [nlw, 128, 24, 8, 256])
    wdnP_s = scr("wdnP", [nlw, 128, 8, 24, 128])
    gxw_s = scr("gxws", [nlw, 128, 1024])
    gaw_s = scr("gaws", [nlw, 128, 1024])

    with ExitStack() as st:
        S = Sched(nc, st)
        sb = lambda n, s, d: st.enter_context(nc.sbuf_tensor("sb_" + n, s, d))
        xhi = sb("xhi", [128, 8, T], BF16)
        xlo = sb("xlo", [128, 8, T], BF16)
        big = sb("big", [128, 26624], BF16)
        tmp = sb("tmp", [128, 9, TS], F32)
        hr = sb("hr", [128, 4], F32)
        tmo = sb("tmo", [128, 2, TS], F32)
        cosT = sb("cosT", [128, T], F32)
        sinT = sb("sinT", [128, T], F32)
        KT = sb("KT", [128, 2, T], BF16)
        VA = sb("VA", [128, 16, 2, 128], BF16)
        QT = sb("QT", [128, 2, TS], BF16)
        PT = sb("PT", [128, 5, TS], BF16)
        PT4 = PT[:, 0:4, :].rearrange("p (s h) t -> p s h t", s=2)
        wbuf = sb("wbuf", [128, 8192], BF16)
        wsm = sb("wsm", [128, 2, 1920], BF16)
        pv = sb("pv", [128, nlw * PL], F32)
        pvd = sb("pvd", [128, nlw * 32], F32)
        cst = sb("cst", [128, 258], F32)
        onesb = sb("onesb", [128, 128], BF16)
        trib = sb("trib", [128, 128], BF16)
        hal = sb("hal", [128, 48, 2], F32)
        hlast = sb("hlast", [128, 2], F32)
        ps = [st.enter_context(nc.psum_tensor(f"ps{i}", [128, TS], F32)) for i in range(4)]
        ppair = [st.enter_context(nc.psum_tensor(f"pp{i}", [128, 2, TS], F32)) for i in range(2)]
        ps += [ppair[0][:, 0, :], ppair[0][:, 1, :], ppair[1][:, 0, :], ppair[1][:, 1, :]]
        PSB = [S.b(("ps", i)) for i in range(8)]
        ident = cst[:, 0:128]
        invf = cst[:, 256:257]
        epsc = cst[:, 257:258]
        wkr = wsm[:, 1, 0:768].rearrange("p (k d) -> p k d", k=8)

        merged = big[:, 0:16384].rearrange("p (c t) -> p c t", c=8)
        qn = big[:, 16384:22528].rearrange("p (c t) -> p c t", c=3)
        kvn = big[:, 22528:26624].rearrange("p (c t) -> p c t", c=2)
        abuf = big[:, 0:24576].rearrange("p (j t) -> p j t", j=24)
        wdn_sb = [KT[:].rearrange("p a t -> p (a t)")[:, 0:3072].rearrange("p (j q) -> p j q", j=24),
                  VA[:].rearrange("p a b c -> p (a b c)")[:, 0:3072].rearrange("p (j q) -> p j q", j=24)]

        def ACT(out, in_, func, rd, wr, **kw):
            S.op("act", "activation", rd, wr, out=out, in_=in_, func=func, **kw)

        def TT(e, out, in0, in1, op, rd, wr):
            S.op(e, "tensor_tensor", rd, wr, out=out, in0=in0, in1=in1, op=op)

        def TSC(e, out, in0, s1, s2, op0, op1, rd, wr):
            if s2 is None:
                S.op(e, "tensor_scalar", rd, wr, out=out, in0=in0, scalar1=s1, scalar2=None, op0=op0)
            else:
                S.op(e, "tensor_scalar", rd, wr, out=out, in0=in0, scalar1=s1, scalar2=s2, op0=op0, op1=op1)

        def STT(e, out, in0, scalar, in1, op0, op1, rd, wr):
            S.op(e, "scalar_tensor_tensor", rd, wr, out=out, in0=in0, scalar=scalar, in1=in1, op0=op0, op1=op1)

        def CP(e, out, in_, rd, wr):
            S.op(e, "tensor_copy", rd, wr, out=out, in_=in_)

        def RCP(out, in_, rd, wr):
            S.op("dve", "reciprocal", rd, wr, out=out, in_=in_)

        def MM(out, lhsT, rhs, start, stop, rd, wr):
            S.op("pe", "matmul", rd, wr, out=out, lhsT=lhsT, rhs=rhs, start=start, stop=stop)

        def TR(out, in_, rd, wr):
            S.op("pe", "transpose", rd, wr, out=out, in_=in_, identity=ident)

        def MS(e, ap, v, wr):
            S.op(e, "memset", (), wr, ap=ap, constant=v)

        DMA = S.dma
        proj_rr = [0]

        nproj = [3]

        def pbank():
            i = proj_rr[0] % nproj[0]
            proj_rr[0] += 1
            return i

        rb_rr = [0]
        ab_rr = [0]

        def rbank():
            i = rb_rr[0] % 2
            rb_rr[0] += 1
            return i

        def abank():
            i = 4 + ab_rr[0] % 4
            ab_rr[0] += 1
            return i

        def mm_group(pb, lhs_list, rhs_list, reads, out_ap=None):
            n = len(lhs_list)
            o = ps[pb][:] if out_ap is None else out_ap
            for i in range(n):
                MM(o, lhs_list[i], rhs_list[i], i == 0, i == n - 1, reads, [PSB[pb]])

        def tsl(tt):
            return slice(tt * TS, (tt + 1) * TS)

        XB = lambda k, tt: ("xhi", k, tt)
        XL = lambda k, tt: ("xlo", k, tt)

        DMA(cst[:], cst_d, (), ["cst"])
        DMA(pv[:], pv_d, (), ["pv"])
        MS("pool", onesb[:], 1.0, ["onesb"])
        CP("dve", trib[:], cst[:, 128:256], ["cst"], ["trib"])
        MS("pool", hal[:], 0.0, ["hal"])
        for l in layers:
            lam = pv[:, l * PL + O_LAM: l * PL + O_LAM + 8]
            t0 = tmp[:, 0, 0:8]
            ACT(t0, lam, AF.Exp, ["pv"], ["t0"], scale=-1.0)
            ACT(t0, t0, AF.Ln, ["t0"], ["t0"], bias=1.0)
            TSC("dve", pvd[:, l * 32: l * 32 + 8], t0, -4.0, None, ALU.mult, None, ["t0"], ["pvd"])
            TSC("dve", pvd[:, l * 32 + 8: l * 32 + 16], t0, -8.0, None, ALU.mult, None, ["t0"], ["pvd"])
            TSC("dve", pvd[:, l * 32 + 16: l * 32 + 24], pv[:, l * PL + O_GXB: l * PL + O_GXB + 8], 0.5, None, ALU.mult, None, ["pv"], ["pvd"])
            TSC("dve", pvd[:, l * 32 + 24: l * 32 + 32], pv[:, l * PL + O_GAB: l * PL + O_GAB + 8], 0.5, None, ALU.mult, None, ["pv"], ["pvd"])
        S.barrier_all()

        stf = big[:, 0:24576].bitcast(F32).rearrange("p (a b) -> p a b", a=2)
        stb = xhi[:].rearrange("p k t -> p (k t)")[:, 0:12288].rearrange("p (a b) -> p a b", a=2)
        tb16 = tmp[:].rearrange("p a b -> p (a b)").bitcast(BF16)
        wqr_sb = tb16[:, 0:4608].rearrange("p (c k h d) -> p c k h d", c=8, k=3, h=2)
        ccnt = [0]

        def cast(i, out_ap, in_ap, scale=None):
            if scale is None:
                ccnt[0] += 1
                if ccnt[0] % 2 == 0:
                    ACT(out_ap, in_ap, AF.Copy, [("stf", i)], [("stb", i)])
                else:
                    CP("dve", out_ap, in_ap, [("stf", i)], [("stb", i)])
            else:
                TSC("dve", out_ap, in_ap, scale, None, ALU.mult, None, [("stf", i), "pv"], [("stb", i)])

        MS("pool", wkr, 0.0, ["wkr"])
        MS("pool", tb16[:, 0:4608], 0.0, ["wqr_sb"])
        jobs = []
        for l in layers:
            base = l * PL
            for k in range(8):
                def job(i, l=l, k=k):
                    sf = stf[:, i, :]
                    so = stb[:, i, :]
                    soB = so[:, 0:4096].rearrange("p (c j q) -> p c j q", c=8, j=4)
                    for j, b0 in enumerate([0, 1024, 2720, 3744]):
                        cast(i, soB[:, :, j, :], sf[:, b0:b0 + 1024].rearrange("p (c q) -> p c q", c=8))
                    cast(i, so[:, 4096:4768], sf[:, 2048:2720])
                    TSC("pool", wkr[:, k, 64:80], sf[:, 2704:2720], -1.0, None, ALU.mult, None, [("stf", i)], ["wkr"])
                    CP("pool", wkr[:, k, 80:96], sf[:, 2688:2704], [("stf", i)], ["wkr"])
                    DMA(winB_s[l][:, :, k, :], so[:, 0:4096].rearrange("p (c q) -> p c q", c=8), [("stb", i)], [("winB", l)])
                    DMA(winL_s[l][:, k, :], so[:, 4096:4768], [("stb", i)], [("winL", l)])
                    if k == 7:
                        DMA(wkrot_s[l], wsm[:, 1, 0:768], ["wkr"], [("wkrot", l)])
                jobs.append((w_in_d[l, k * 128:(k + 1) * 128, :], lambda i: stf[:, i, 0:INW], job))

            def job_uq(i, l=l, base=base):
                sf3 = stf[:, i, 0:4608].rearrange("p (k q) -> p k q", k=3)
                so3 = stb[:, i, 0:4608].rearrange("p (k q) -> p k q", k=3)
                for k in range(3):
                    g = pv[:, base + O_QG + k: base + O_QG + k + 1]
                    cast(i, so3[:, k, :], sf3[:, k, :], scale=g)
                    sfv = sf3[:, k, :].rearrange("p (c h d) -> p c h d", c=8, h=2)
                    TSC("pool", wqr_sb[:, :, k, :, 64:80], sfv[:, :, :, 80:96], g, -1.0, ALU.mult, ALU.mult,
                        [("stf", i), "pv"], ["wqr_sb"])
                    TSC("pool", wqr_sb[:, :, k, :, 80:96], sfv[:, :, :, 64:80], g, None, ALU.mult, None,
                        [("stf", i), "pv"], ["wqr_sb"])
                    DMA(wuqP_s[l][:, :, k, :], so3[:, k, :].rearrange("p (c q) -> p c q", c=8), [("stb", i)], [("wuqP", l)])
                DMA(wqrP_s[l], tb16[:, 0:4608], ["wqr_sb"], [("wqrP", l)])
            jobs.append((w_uq_d[l].rearrange("(k p) q -> p k q", p=128),
                         lambda i: stf[:, i, 0:4608].rearrange("p (k q) -> p k q", k=3), job_uq))

            def job_ukv(i, l=l, base=base):
                sf2 = stf[:, i, 0:4096].rearrange("p (k q) -> p k q", k=2)
                so2 = stb[:, i, 0:4096].rearrange("p (k q) -> p k q", k=2)
                for k in range(2):
                    g = pv[:, base + O_KVG + k: base + O_KVG + k + 1]
                    cast(i, so2[:, k, :], sf2[:, k, :], scale=g)
                    DMA(wukvP_s[l][:, :, k, :], so2[:, k, :].rearrange("p (c q) -> p c q", c=8), [("stb", i)], [("wukvP", l)])
            jobs.append((w_ukv_d[l].rearrange("(k p) q -> p k q", p=128),
                         lambda i: stf[:, i, 0:4096].rearrange("p (k q) -> p k q", k=2), job_ukv))

            for k0 in range(0, 8, 4):
                def job_out(i, l=l, k0=k0):
                    cast(i, stb[:, i, 0:4096], stf[:, i, 0:4096])
                    DMA(wout_s[l][:, k0:k0 + 4, :], stb[:, i, 0:4096].rearrange("p (k q) -> p k q", k=4), [("stb", i)], [("wout", l)])
                jobs.append((w_out_d[l, k0 * 128:(k0 + 4) * 128, :].rearrange("(k p) q -> p k q", p=128),
                             lambda i: stf[:, i, 0:4096].rearrange("p (k q) -> p k q", k=4), job_out))

            for k in range(8):
                def job_up(i, l=l, k=k):
                    sfv = stf[:, i, :].rearrange("p (g j q) -> p g j q", g=2, j=24)
                    sov = stb[:, i, :].rearrange("p (j g q) -> p j g q", j=24, g=2)
                    cast(i, sov[:, :, 0, :], sfv[:, 0, :, :])
                    cast(i, sov[:, :, 1, :], sfv[:, 1, :, :])
                    DMA(wupP_s[l][:, :, k, :], stb[:, i, :].rearrange("p (j q) -> p j q", j=24), [("stb", i)], [("wupP", l)])
                jobs.append((w_up_d[l, k * 128:(k + 1) * 128, :], lambda i: stf[:, i, 0:6144], job_up))

            for k0 in range(0, 24, 4):
                def job_dn(i, l=l, k0=k0):
                    cast(i, stb[:, i, 0:4096], stf[:, i, 0:4096])
                    so4 = stb[:, i, 0:4096].rearrange("p (k m q) -> p k m q", k=4, m=8)
                    for kk in range(4):
                        DMA(wdnP_s[l][:, :, k0 + kk, :], so4[:, kk, :, :], [("stb", i)], [("wdnP", l)])
                jobs.append((w_dn_d[l, k0 * 128:(k0 + 4) * 128, :].rearrange("(k p) q -> p k q", p=128),
                             lambda i: stf[:, i, 0:4096].rearrange("p (k q) -> p k q", k=4), job_dn))

            for src, dst, nm in [(gxw_d, gxw_s, "gxws"), (gaw_d, gaw_s, "gaws")]:
                def job_g(i, l=l, dst=dst, nm=nm):
                    cast(i, stb[:, i, 0:1024], stf[:, i, 0:1024])
                    DMA(dst[l], stb[:, i, 0:1024], [("stb", i)], [(nm, l)])
                jobs.append((src[l], lambda i: stf[:, i, 0:1024], job_g))

        nj = len(jobs)

        def jload(n):
            i = n % 2
            DMA(jobs[n][1](i), jobs[n][0], (), [("stf", i)])

        if nj:
            jload(0)
        for n in range(nj):
            if n + 1 < nj:
                jload(n + 1)
            jobs[n][2](n % 2)
        S.barrier_all()

        P = slice(64, 96)
        for s in range(NS):
            for tb in range(T // 128):
                tt = tb // 4
                xt = tmp[:, 2 * (tb % 2):2 * (tb % 2) + 2, :].rearrange("p a b -> p (a b)")
                xtb = ("xt", tb % 2)
                DMA(xt, x_d[s, tb * 128:(tb + 1) * 128, :], (), [xtb])
                for g in range(2):
                    pb = pbank()
                    for kk in range(4):
                        k = g * 4 + kk
                        TR(ps[pb][:, kk * 128:(kk + 1) * 128], xt[:, k * 128:(k + 1) * 128], [xtb, "cst"], [PSB[pb]])
                    hi = xhi[:, g * 4:(g + 1) * 4, tb * 128:(tb + 1) * 128]
                    lo = xlo[:, g * 4:(g + 1) * 4, tb * 128:(tb + 1) * 128]
                    pv3 = ps[pb][:].rearrange("p (a b) -> p a b", a=4)
                    hb = [XB(k, tt) for k in range(g * 4, g * 4 + 4)]
                    lb = [XL(k, tt) for k in range(g * 4, g * 4 + 4)]
                    ACT(hi, pv3, AF.Copy, [PSB[pb]], hb)
                    TT("dve", lo, pv3, hi, ALU.subtract, [PSB[pb]] + hb, lb)
            posi = tmp[:, 0:4, :].rearrange("p a b -> p (a b)").bitcast(I32)
            ang = tmp[:, 4:8, :].rearrange("p a b -> p (a b)")
            kf = tmp[:, 0:4, :].rearrange("p a b -> p (a b)")
            S.barrier_all()
            DMA(posi[P, :], pos_d[s], (), ["posi"])
            CP("dve", ang[P, :], posi[P, :], ["posi"], ["ang"])
            TSC("dve", ang[P, :], ang[P, :], invf[P, :], None, ALU.mult, None, ["ang", "cst"], ["ang"])
            TSC("dve", posi[P, :], ang[P, :], 1.0 / TWO_PI, None, ALU.mult, None, ["ang"], ["posi"])
            CP("dve", kf[P, :], posi[P, :], ["posi"], ["posi"])
            STT("dve", ang[P, :], kf[P, :], -TWO_PI, ang[P, :], ALU.mult, ALU.add, ["posi", "ang"], ["ang"])

            def wrap():
                TSC("dve", kf[P, :], ang[P, :], PI, TWO_PI, ALU.is_gt, ALU.mult, ["ang"], ["posi"])
                TT("dve", ang[P, :], ang[P, :], kf[P, :], ALU.subtract, ["ang", "posi"], ["ang"])
                TSC("dve", kf[P, :], ang[P, :], -PI, TWO_PI, ALU.is_lt, ALU.mult, ["ang"], ["posi"])
                TT("dve", ang[P, :], ang[P, :], kf[P, :], ALU.add, ["ang", "posi"], ["ang"])
            wrap()
            ACT(sinT[P, :], ang[P, :], AF.Sin, ["ang"], ["sinT"])
            TSC("dve", ang[P, :], ang[P, :], PI / 2, None, ALU.add, None, ["ang"], ["ang"])
            wrap()
            ACT(cosT[P, :], ang[P, :], AF.Sin, ["ang"], ["cosT"])
            S.barrier_all()

            for li, l in enumerate(layers):
                base = l * PL
                pcol = lambda o, base=base: pv[:, base + o: base + o + 1]
                wlat = wbuf[:, 0:8 * 672].rearrange("p (k q) -> p k q", k=8)
                DMA(wlat, winL_s[l], [("winL", l)], ["wbA", "wbB"])
                DMA(wsm[:, 1, 0:768], wkrot_s[l], [("wkrot", l)], ["wkr"])
                for tt in range(NT):
                    xh = [xhi[:, k, tsl(tt)] for k in range(8)]
                    xrd = [XB(k, tt) for k in range(8)]
                    for (nch, c0, dst, inv_n, nm) in [(3, 0, qn, 1.0 / 384, "qn"), (2, 384, kvn, 1.0 / 256, "kvn")]:
                        for oc in range(nch):
                            pb = pbank()
                            mm_group(pb, [wlat[:, k, c0 + oc * 128: c0 + (oc + 1) * 128] for k in range(8)], xh, xrd + ["wbA"])
                            ACT(tmp[:, oc, :], ps[pb][:], AF.Copy, [PSB[pb]], [("t", oc)])
                            ACT(PT[:, oc, :], ps[pb][:], AF.Square, [PSB[pb]], [("pt", oc)])
                        mm_group(7, [onesb[:]] * nch, [PT[:, oc, :] for oc in range(nch)],
                                 ["onesb"] + [("pt", oc) for oc in range(nch)])
                        ACT(tmp[:, 4, :], ps[7][:], AF.Sqrt, [PSB[7], "cst"], [("t", 4)], bias=epsc, scale=inv_n)
                        RCP(tmp[:, 5, :], tmp[:, 4, :], [("t", 4)], [("t", 5)])
                        for oc in range(nch):
                            TT("pool", dst[:, oc, tsl(tt)], tmp[:, oc, :], tmp[:, 5, :], ALU.mult,
                               [("t", oc), ("t", 5)], [(nm, oc, tt)])
                    pa = pbank()
                    mm_group(pa, [wlat[:, k, 576:672] for k in range(8)], xh, xrd + ["wbA"], out_ap=ps[pa][0:96, :])
                    pb2 = pbank()
                    mm_group(pb2, [wkr[:, k, :] for k in range(8)], xh, xrd + ["wkr"], out_ap=ps[pb2][0:96, :])
                    TT("dve", tmp[P, 6, :], ps[pa][P, :], cosT[P, tsl(tt)], ALU.mult, [PSB[pa], "cosT"], [("t", 6)])
                    TT("dve", tmp[P, 7, :], ps[pb2][P, :], sinT[P, tsl(tt)], ALU.mult, [PSB[pb2], "sinT"], [("t", 7)])
                    TT("pool", KT[P, 0, tsl(tt)], tmp[P, 6, :], tmp[P, 7, :], ALU.add, [("t", 6), ("t", 7)], [("kpe", tt)])
                    CP("pool", KT[P, 1, tsl(tt)], KT[P, 0, tsl(tt)], [("kpe", tt)], [("kpe1", tt)])
                S.barrier_all()

                def load_pair(c, l=l):
                    hb_ = c % 2
                    wb = wbuf[:, hb_ * 4096:(hb_ + 1) * 4096]
                    DMA(wb.rearrange("p (k q) -> p k q", k=8), winB_s[l][:, c, :, :], [("winB", l)], ["wbA" if hb_ == 0 else "wbB"])
                    sm = wsm[:, hb_, :]
                    DMA(sm[:, 0:576].rearrange("p (k q) -> p k q", k=3), wuqP_s[l][:, c, :, :], [("wuqP", l)], [("wsm", hb_, 0)])
                    DMA(sm[:, 576:1152], wqrP_s[l][:, c * 576:(c + 1) * 576], [("wqrP", l)], [("wsm", hb_, 1)])
                    DMA(sm[:, 1152:1664].rearrange("p (k q) -> p k q", k=2), wukvP_s[l][:, c, :, :], [("wukvP", l)], [("wsm", hb_, 2)])
                    DMA(sm[:, 1664:1792], gxw_s[l][:, c * 128:(c + 1) * 128], [("gxws", l)], [("wsm", hb_, 3)])
                    DMA(sm[:, 1792:1920], gaw_s[l][:, c * 128:(c + 1) * 128], [("gaws", l)], [("wsm", hb_, 4)])

                nproj[0] = 2
                MS("pool", VA[:, :, 0, 64:128], 1.0, ["VA"])
                MS("pool", VA[:, :, 1, 0:64], 1.0, ["VA"])
                def make_rnn(c, l=l):
                    hb_ = c % 2
                    WB = "wbA" if hb_ == 0 else "wbB"
                    wc = wbuf[:, hb_ * 4096:(hb_ + 1) * 4096].rearrange("p (k j q) -> p k j q", k=8, j=4)
                    sm = wsm[:, hb_, :]
                    gxw = sm[:, 1664:1792]
                    gaw = sm[:, 1792:1920]
                    cwc = lambda tap: pcol(O_CW + tap * 8 + c)
                    def rnn_p1(qt):
                        xh = [xhi[:, k, tsl(qt)] for k in range(8)]
                        xrd = [XB(k, qt) for k in range(8)]
                        pc = lambda o: pvd[:, l * 32 + o + c: l * 32 + o + c + 1]
                        px = rbank()
                        mm_group(px, [wc[:, k, 0, :] for k in range(8)], xh, xrd + [WB])
                        yield
                        X = ps[px]
                        cv = tmp[:, 0, :]
                        TSC("dve", cv, X[:], cwc(3), pcol(O_CB + c), ALU.mult, ALU.add, [PSB[px], "pv"], [("t", 0)])
                        yield
                        pg = rbank()
                        mm_group(pg, [wc[:, k, 1, :] for k in range(8)], xh, xrd + [WB])
                        yield
                        for sh, tap in ((1, 2), (2, 1), (3, 0)):
                            STT("dve", cv[:, sh:TS], X[:, 0:TS - sh], cwc(tap), cv[:, sh:TS], ALU.mult, ALU.add,
                                [PSB[px], ("t", 0), "pv"], [("t", 0)])
                            yield
                        if qt > 0:
                            for sh, tap in ((1, 2), (2, 1), (3, 0)):
                                STT("dve", cv[:, 0:sh], hr[:, 3 - sh:3], cwc(tap), cv[:, 0:sh], ALU.mult, ALU.add,
                                    ["hr", ("t", 0), "pv"], [("t", 0)])
                        if qt < NT - 1:
                            CP("dve", hr[:, 0:3], X[:, TS - 3:TS], [PSB[px]], ["hr"])
                        yield
                        ACT(PT[:, 4, :], cv, AF.Copy, [("t", 0)], [("pt", 4)])
                        ACT(tmp[:, 3, :], ps[pg][:], AF.Square, [PSB[pg]], [("t", 3)])
                        ACT(tmp[:, 4, :], ps[pg][:], AF.Copy, [PSB[pg]], [("t", 4)])
                        yield
                        TSC("dve", tmp[:, 3, :], tmp[:, 3, :], 0.044715, 1.0, ALU.mult, ALU.add, [("t", 3)], [("t", 3)])
                        yield
                        TT("dve", tmp[:, 3, :], tmp[:, 3, :], tmp[:, 4, :], ALU.mult, [("t", 3), ("t", 4)], [("t", 3)])
                        yield
                        pgx = rbank()
                        mm_group(pgx, [gxw], [PT[:, 4, :]], [("pt", 4), ("wsm", hb_, 3)])
                        pga = rbank()
                        mm_group(pga, [gaw], [PT[:, 4, :]], [("pt", 4), ("wsm", hb_, 4)])
                        yield
                        ACT(tmp[:, 3, :], tmp[:, 3, :], AF.Tanh, [("t", 3)], [("t", 3)], scale=0.7978845608028654)
                        yield
                        ACT(tmp[:, 1, :], ps[pgx][:], AF.Tanh, [PSB[pgx], "pvd"], [("t", 1)], bias=pc(16), scale=0.5)
                        yield
                        ACT(tmp[:, 2, :], ps[pga][:], AF.Tanh, [PSB[pga], "pvd"], [("t", 2)], bias=pc(24), scale=0.5)
                        STT("dve", tmp[:, 4, :], tmp[:, 3, :], 1.0, tmp[:, 4, :], ALU.add, ALU.mult, [("t", 3), ("t", 4)], [("t", 4)])
                        yield
                        ACT(tmp[:, 3, :], tmp[:, 2, :], AF.Exp, [("t", 2), "pvd"], [("t", 3)], bias=pc(0), scale=pc(0))
                        yield
                        ACT(tmp[:, 2, :], tmp[:, 2, :], AF.Exp, [("t", 2), "pvd"], [("t", 2)], bias=pc(8), scale=pc(8))
                        STT("dve", tmp[:, 1, :], tmp[:, 1, :], 1.0, tmp[:, 0, :], ALU.add, ALU.mult, [("t", 1), ("t", 0)], [("t", 1)])
                        yield
                        ACT(tmp[:, 2, :], tmp[:, 2, :], AF.Sqrt, [("t", 2)], [("t", 2)], bias=1.0, scale=-1.0)
                        yield
                        STT("dve", tmp[:, 1, :], tmp[:, 1, :], 0.5, tmp[:, 2, :], ALU.mult, ALU.mult, [("t", 1), ("t", 2)], [("t", 1)])
                        yield
                        init = 0.0 if qt == 0 else hlast[:, 0:1]
                        S.op("dve", "tensor_tensor_scan", [("t", 3), ("t", 1), "hlast"], [("t", 2)],
                             out=tmp[:, 2, :], data0=tmp[:, 3, :], data1=tmp[:, 1, :], initial=init, op0=ALU.mult, op1=ALU.add)
                        CP("dve", hlast[:, 0:1], tmp[:, 2, TS - 1:TS], [("t", 2)], ["hlast"])
                        yield
                        TT("pool", tmp[:, 4, :], tmp[:, 4, :], tmp[:, 2, :], ALU.mult, [("t", 4), ("t", 2)], [("t", 4)])
                        yield

                    def rnn_p2(qt):
                        xh = [xhi[:, k, tsl(qt)] for k in range(8)]
                        xrd = [XB(k, qt) for k in range(8)]
                        pa_ = rbank()
                        mm_group(pa_, [wc[:, k, 2, :] for k in range(8)], xh, xrd + [WB])
                        yield
                        ACT(tmp[:, 1, :], ps[pa_][:], AF.Tanh, [PSB[pa_]], [("t", 1)], scale=0.5)
                        yield
                        pbg = rbank()
                        mm_group(pbg, [wc[:, k, 3, :] for k in range(8)], xh, xrd + [WB])
                        yield
                        STT("dve", tmo[:, 0, :], tmp[:, 1, :], 1.0, tmp[:, 4, :], ALU.add, ALU.mult, [("t", 4), ("t", 1)], ["tM"])
                        ACT(tmo[:, 1, :], ps[pbg][:], AF.Tanh, [PSB[pbg]], ["tB"], scale=0.5)
                        yield


                    return [g for q_ in range(NT) for g in (rnn_p1(q_), rnn_p2(q_))]

                gq = []

                def pump(n, limit):
                    k = 0
                    while k < n and gq_i[0] <= limit and gq_i[0] < len(gq):
                        try:
                            next(gq[gq_i[0]])
                            k += 1
                        except StopIteration:
                            gq_i[0] += 1

                def flush(limit):
                    while gq_i[0] <= limit and gq_i[0] < len(gq):
                        try:
                            next(gq[gq_i[0]])
                        except StopIteration:
                            gq_i[0] += 1

                gq_i = [0]
                load_pair(0)
                gq += make_rnn(0)
                for c in range(8):
                    hb_ = c % 2
                    WB = "wbA" if hb_ == 0 else "wbB"
                    if c + 1 < 8:
                        load_pair(c + 1)
                        gq += make_rnn(c + 1)
                    wc = wbuf[:, hb_ * 4096:(hb_ + 1) * 4096].rearrange("p (k j q) -> p k j q", k=8, j=4)
                    sm = wsm[:, hb_, :]
                    wuq = sm[:, 0:576].rearrange("p (k q) -> p k q", k=3)
                    wqr = sm[:, 576:1152].rearrange("p (k h q) -> p k h q", k=3, h=2)
                    wukv = sm[:, 1152:1664].rearrange("p (k q) -> p k q", k=2)
                    gxw = sm[:, 1664:1792]
                    gaw = sm[:, 1792:1920]
                    cwc = lambda tap, c=c: pcol(O_CW + tap * 8 + c)
                    for hh in range(2):
                        for tt in range(NT):
                            pb = abank()
                            mm_group(pb, [wukv[:, k, hh * 128: hh * 128 + 64] for k in range(2)],
                                     [kvn[:, k, tsl(tt)] for k in range(2)], [("kvn", 0, tt), ("kvn", 1, tt), ("wsm", hb_, 2)],
                                     out_ap=ps[pb][0:64, :])
                            ACT(KT[0:64, hh, tsl(tt)], ps[pb][0:64, :], AF.Copy, [PSB[pb]], [("KT", hh, tt)])
                            pump(2, 8 * c + 1)
                    for g4 in range(4):
                        pb = abank()
                        pv4 = ps[pb][:].rearrange("p (t h d) -> p t h d", t=4, h=2)
                        for t4 in range(4):
                            tb = g4 * 4 + t4
                            for k in range(2):
                                MM(pv4[:, t4, :, :], kvn[:, k, tb * 128:(tb + 1) * 128],
                                   wukv[:, k, :].rearrange("p (h d) -> p h d", h=2)[:, :, 64:128], k == 0, k == 1,
                                   [("kvn", k, g4), ("wsm", hb_, 2)], [PSB[pb]])
                        ACT(VA[:, g4 * 4:(g4 + 1) * 4, 0, 0:64], pv4[:, :, 0, :], AF.Copy, [PSB[pb]], [("VA", g4)])
                        ACT(VA[:, g4 * 4:(g4 + 1) * 4, 1, 64:128], pv4[:, :, 1, :], AF.Copy, [PSB[pb]], [("VA", g4)])
                        pump(2, 8 * c + 1)
                    nproj[0] = 2
                    for qt in range(NT):
                        LIM = min(8 * c + 2 * qt + 2, 8 * c + 7) if not XC_RUNAHEAD else 8 * c + 2 * qt + 2
                        pump(6 if qt == 0 else 2, LIM)
                        qr = [("qn", k, qt) for k in range(3)]
                        qrh = [qn[:, k, tsl(qt)] for k in range(3)]
                        for hh in range(2):
                            pq = 4 + 2 * hh
                            mm_group(pq, [wuq[:, k, hh * 96:(hh + 1) * 96] for k in range(3)], qrh,
                                     qr + [("wsm", hb_, 0)], out_ap=ps[pq][0:96, :])
                            pr = 5 + 2 * hh
                            mm_group(pr, [wqr[:, k, hh, :] for k in range(3)], qrh,
                                     qr + [("wsm", hb_, 1)], out_ap=ps[pr][0:96, :])
                            CP("dve", QT[0:64, hh, :], ps[pq][0:64, :], [PSB[pq]], [("QT", hh)])
                            TT("dve", tmp[P, 5, :], ps[pq][P, :], cosT[P, tsl(qt)], ALU.mult, [PSB[pq], "cosT"], [("t", 5)])
                            TT("dve", tmp[P, 6, :], ps[pr][P, :], sinT[P, tsl(qt)], ALU.mult, [PSB[pr], "sinT"], [("t", 6)])
                            TT("pool", QT[P, hh, :], tmp[P, 5, :], tmp[P, 6, :], ALU.add, [("t", 5), ("t", 6)], [("QT", hh)])
                            pump(2, LIM)
                        nkb = 4 * qt + 4
                        for i in range(nkb + 1):
                            if i < nkb:
                                kb = i
                                n0 = max(kb - 4 * qt, 0) * 128
                                sl = kb % 2
                                for hh in range(2):
                                    bk = 4 + 2 * sl + hh
                                    MM(ps[bk][:, n0:TS], KT[0:96, hh, kb * 128:(kb + 1) * 128], QT[0:96, hh, n0:TS], True, True,
                                       [("KT", hh, kb // 4), ("kpe" if hh == 0 else "kpe1", kb // 4), ("QT", hh)], [PSB[bk]])
                            if i >= 1:
                                kb = i - 1
                                j = kb - 4 * qt
                                n0 = max(j, 0) * 128
                                sl = kb % 2
                                ACT(PT4[:, sl, :, n0:TS], ppair[sl][:, :, n0:TS], AF.Exp,
                                    [PSB[4 + 2 * sl], PSB[5 + 2 * sl]], [("pt", sl)], scale=SCALE)
                                if j >= 0:
                                    for hh in range(2):
                                        TT("pool", PT4[:, sl, hh, n0:n0 + 128], PT4[:, sl, hh, n0:n0 + 128], trib[:], ALU.mult,
                                           [("pt", sl), "trib"], [("pt", sl)])
                                for hh in range(2):
                                    MM(ps[2 + hh][:, n0:TS], VA[:, kb, hh, :], PT4[:, sl, hh, n0:TS], kb == 0, kb == nkb - 1,
                                       [("VA", kb // 4), ("pt", sl), "VA"], [PSB[2 + hh]])
                            pump(PUMP_N, LIM)
                        RCP(tmp[0:64, 7, :], ps[2][64:128, :], [PSB[2]], [("t", 7)])
                        TT("dve", tmp[0:64, 8, :], ps[2][0:64, :], tmp[0:64, 7, :], ALU.mult, [PSB[2], ("t", 7)], [("t", 8)])
                        RCP(tmp[64:128, 7, :], ps[3][0:64, :], [PSB[3]], [("t", 7)])
                        TT("dve", tmp[64:128, 8, :], ps[3][64:128, :], tmp[64:128, 7, :], ALU.mult, [PSB[3], ("t", 7)], [("t", 8)])
                        flush(8 * c + 2 * qt + 1)
                        STT("dve", tmp[:, 8, :], tmo[:, 1, :], 1.0, tmp[:, 8, :], ALU.add, ALU.mult, [("t", 8), "tB"], [("t", 8)])
                        STT("dve", tmp[:, 8, :], tmo[:, 0, :], 0.5, tmp[:, 8, :], ALU.mult, ALU.add, [("t", 8), "tM"], [("t", 8)])
                        ACT(merged[:, c, tsl(qt)], tmp[:, 8, :], AF.Copy, [("t", 8)], [("mg", c, qt)], scale=0.5)
                nproj[0] = 3
                S.barrier_all()

                def stat_evac(pm, pq_, tm, tr):
                    TSC("dve", tmp[:, tm, :], ps[pm][:], 1.0 / D, None, ALU.mult, None, [PSB[pm]], [("t", tm)])
                    TT("pool", tmp[:, tr, :], tmp[:, tm, :], tmp[:, tm, :], ALU.mult, [("t", tm)], [("t", tr)])
                    STT("dve", tmp[:, tr, :], ps[pq_][:], 1.0 / D, tmp[:, tr, :], ALU.mult, ALU.subtract,
                        [PSB[pq_], ("t", tr)], [("t", tr)])
                    ACT(tmp[:, tr, :], tmp[:, tr, :], AF.Sqrt, [("t", tr), "cst"], [("t", tr)], bias=epsc, scale=1.0)
                    RCP(tmp[:, tr, :], tmp[:, tr, :], [("t", tr)], [("t", tr)])

                def ln_norm_gen(tt, tm, tr, og, ob, zts):
                    G = len(zts)
                    for m0 in range(0, 8, G):
                        ms = list(range(m0, min(8, m0 + G)))
                        for stage in range(6):
                            for gi, m in enumerate(ms):
                                zt, ZB = zts[gi]
                                if stage == 0:
                                    TT("pool", zt, xhi[:, m, tsl(tt)], xlo[:, m, tsl(tt)], ALU.add, [XB(m, tt), XL(m, tt)], ZB)
                                elif stage == 1:
                                    TT("dve", zt, zt, tmp[:, tm, :], ALU.subtract, ZB + [("t", tm)], ZB)
                                elif stage == 2:
                                    TT("pool", zt, zt, tmp[:, tr, :], ALU.mult, ZB + [("t", tr)], ZB)
                                elif stage == 3:
                                    ACT(zt, zt, AF.Identity, ZB + ["pv"], ZB, bias=pcol(ob + m), scale=pcol(og + m))
                                elif stage == 4:
                                    ACT(xhi[:, m, tsl(tt)], zt, AF.Copy, ZB, [XB(m, tt)])
                                else:
                                    TT("dve", xlo[:, m, tsl(tt)], zt, xhi[:, m, tsl(tt)], ALU.subtract, ZB + [XB(m, tt)], [XL(m, tt)])
                                yield

                def gpump(gen, n):
                    if gen is None:
                        return
                    for _ in range(n):
                        try:
                            next(gen)
                        except StopIteration:
                            return

                def chain2(g1, g2):
                    yield from g1
                    yield from g2

                rcnt = [0]
                rtemps = [[0, 1, 8]]

                def resid(pb, m, tt, pm, pq_):
                    k = rcnt[0]
                    rcnt[0] += 1
                    ti_ = rtemps[0][k % len(rtemps[0])]
                    zt = tmp[:, ti_, :]
                    ZB = ("t", ti_)
                    STT("dve", zt, xhi[:, m, tsl(tt)], ALPHA, ps[pb][:], ALU.mult, ALU.add, [XB(m, tt), PSB[pb]], [ZB])
                    STT("dve", zt, xlo[:, m, tsl(tt)], ALPHA, zt, ALU.mult, ALU.add, [XL(m, tt), ZB], [ZB])
                    ACT(xhi[:, m, tsl(tt)], zt, AF.Copy, [ZB], [XB(m, tt)])
                    TT("pool", xlo[:, m, tsl(tt)], zt, xhi[:, m, tsl(tt)], ALU.subtract, [ZB, XB(m, tt)], [XL(m, tt)])
                    sq = PT[:, k % 4, :]
                    ACT(sq, zt, AF.Square, [ZB], [("pt", k % 4)])

                    def stats():
                        MM(ps[pm][:], onesb[:], xhi[:, m, tsl(tt)], m == 0, m == 7, [XB(m, tt), "onesb"], [PSB[pm]])
                        MM(ps[pq_][:], onesb[:], sq, m == 0, m == 7, [("pt", k % 4), "onesb"], [PSB[pq_]])
                    return stats

                wo = wbuf[:].rearrange("p (k q) -> p k q", k=8)
                DMA(wo, wout_s[l], [("wout", l)], ["wbA", "wbB"])
                gen = None
                for tt in range(NT):
                    pm, pq_ = (4, 5) if tt % 2 == 0 else (6, 7)
                    pending = None
                    for m in range(8):
                        pb = pbank()
                        mm_group(pb, [wo[:, cc, m * 128:(m + 1) * 128] for cc in range(8)], [merged[:, cc, tsl(tt)] for cc in range(8)],
                                 [("mg", cc, tt) for cc in range(8)] + ["wbA"])
                        if pending is not None:
                            pending()
                        pending = resid(pb, m, tt, pm, pq_)
                        gpump(gen, 6)
                    pending()
                    gpump(gen, 1000)
                    stat_evac(pm, pq_, 4, 5)
                    gen = ln_norm_gen(tt, 4, 5, O_L1G, O_L1B, [(tmp[:, i_, :], [("t", i_)]) for i_ in (2, 3, 6, 7)])
                gpump(gen, 1000)
                S.barrier_all()

                def load_up(j, l=l):
                    bi = j % 4
                    DMA(wbuf[:, bi * 2048:(bi + 1) * 2048].rearrange("p (k q) -> p k q", k=8), wupP_s[l][:, j, :, :],
                        [("wupP", l)], [("wup", bi)])

                def load_dn(m, l=l):
                    DMA(wdn_sb[m % 2], wdnP_s[l][:, m, :, :], [("wdnP", l)], [("wdn", m % 2)])

                gen = None
                for half in range(2):
                    for j in range(3):
                        load_up(j)
                    for j in range(24):
                        if j + 3 < 24:
                            load_up(j + 3)
                        wu = wbuf[:, (j % 4) * 2048:(j % 4 + 1) * 2048].rearrange("p (k g q) -> p k g q", k=8, g=2)
                        for t2 in range(2):
                            tt = half * 2 + t2
                            xh = [xhi[:, k, tsl(tt)] for k in range(8)]
                            xrd = [XB(k, tt) for k in range(8)]
                            for gv in range(2):
                                jj = gv * 24 + j
                                pb = pbank()
                                mm_group(pb, [wu[:, k, gv, :] for k in range(8)], xh, xrd + [("wup", j % 4)])
                                ti = 2 * t2 + gv
                                ht = tmp[:, ti, :]
                                HB = ("t", ti)
                                fw = lambda tap, jj=jj: pcol(O_FCW + tap * 48 + jj)
                                ACT(ht, ps[pb][:], AF.Identity, [PSB[pb], "pv"], [HB], bias=pcol(O_FCB + jj), scale=fw(2))
                                STT("dve", ht[:, 1:TS], ps[pb][:, 0:TS - 1], fw(1), ht[:, 1:TS], ALU.mult, ALU.add, [PSB[pb], HB, "pv"], [HB])
                                STT("dve", ht[:, 2:TS], ps[pb][:, 0:TS - 2], fw(0), ht[:, 2:TS], ALU.mult, ALU.add, [PSB[pb], HB, "pv"], [HB])
                                if tt > 0:
                                    STT("dve", ht[:, 0:1], hal[:, jj, 1:2], fw(1), ht[:, 0:1], ALU.mult, ALU.add, [("hal", jj), HB, "pv"], [HB])
                                    STT("dve", ht[:, 0:2], hal[:, jj, 0:2], fw(0), ht[:, 0:2], ALU.mult, ALU.add, [("hal", jj), HB, "pv"], [HB])
                                if tt < NT - 1:
                                    CP("dve", hal[:, jj, :], ps[pb][:, TS - 2:TS], [PSB[pb]], [("hal", jj)])
                            tg, tv = 2 * t2, 2 * t2 + 1
                            ACT(tmp[:, tg, :], tmp[:, tg, :], AF.Gelu_apprx_tanh, [("t", tg)], [("t", tg)])
                            TT("pool", abuf[:, j, t2 * TS:(t2 + 1) * TS], tmp[:, tg, :], tmp[:, tv, :], ALU.mult,
                               [("t", tg), ("t", tv)], [("a", j, t2)])
                            gpump(gen, 2)
                    gpump(gen, 1000)
                    load_dn(0)
                    pending = None
                    rtemps[0] = [0, 1, 2, 3]
                    for m in range(8):
                        if m + 1 < 8:
                            load_dn(m + 1)
                        wd = wdn_sb[m % 2]
                        for t2 in range(2):
                            tt = half * 2 + t2
                            pb = pbank()
                            mm_group(pb, [wd[:, j, :] for j in range(24)], [abuf[:, j, t2 * TS:(t2 + 1) * TS] for j in range(24)],
                                     [("a", j, t2) for j in range(24)] + [("wdn", m % 2)])
                            if pending is not None:
                                pending()
                            pending = resid(pb, m, tt, 4 + 2 * t2, 5 + 2 * t2)
                    pending()
                    stat_evac(4, 5, 4, 5)
                    stat_evac(6, 7, 6, 7)
                    ptv = PT[:, 0:2, :].rearrange("p a b -> p (a b)").bitcast(F32)
                    zts2 = [(tmp[:, 8, :], [("t", 8)]), (ptv, [("pt", 0), ("pt", 1)])]
                    gen = chain2(ln_norm_gen(half * 2, 4, 5, O_L2G, O_L2B, zts2), ln_norm_gen(half * 2 + 1, 6, 7, O_L2G, O_L2B, zts2))
                gpump(gen, 1000)
                S.barrier_all()

            S.barrier_all()
            cnt = 0
            for tt in range(NT):
                for g in range(2):
                    for kk in range(4):
                        m = g * 4 + kk
                        TT("pool", tmp[:, kk, :], xhi[:, m, tsl(tt)], xlo[:, m, tsl(tt)], ALU.add, [XB(m, tt), XL(m, tt)], [("t", kk)])
                    for t4 in range(4):
                        tb = tt * 4 + t4
                        pb = pbank()
                        for kk in range(4):
                            TR(ps[pb][:, kk * 128:(kk + 1) * 128], tmp[:, kk, t4 * 128:(t4 + 1) * 128], [("t", kk), "cst"], [PSB[pb]])
                        oi = 4 + cnt % 4
                        cnt += 1
                        if cnt % 2 == 0:
                            ACT(tmp[:, oi, :], ps[pb][:], AF.Copy, [PSB[pb]], [("t", oi)])
                        else:
                            CP("dve", tmp[:, oi, :], ps[pb][:], [PSB[pb]], [("t", oi)])
                        DMA(y_d[s, tb * 128:(tb + 1) * 128, g * 512:(g + 1) * 512], tmp[:, oi, :], [("t", oi)], [("y", s, tb, g)])
            S.barrier_all()
        S.barrier_all()
        S.run()
    return nc


def _fm(v):
    v = np.asarray(v, np.float32)
    return np.ascontiguousarray(v.reshape(-1, 128).T)


def _host_params(inp, nl):
    pvs = np.zeros((128, nl * PL), np.float32)
    for l in range(nl):
        b = l * PL
        for tap in range(4):
            pvs[:, b + O_CW + tap * 8: b + O_CW + tap * 8 + 8] = _fm(inp["conv_w"][l, tap])
        pvs[:, b + O_CB: b + O_CB + 8] = _fm(inp["conv_b"][l])
        pvs[:, b + O_GXB: b + O_GXB + 8] = _fm(inp["gx_b"][l])
        pvs[:, b + O_GAB: b + O_GAB + 8] = _fm(inp["ga_b"][l])
        pvs[:, b + O_LAM: b + O_LAM + 8] = _fm(inp["lru_lambda"][l])
        pvs[:, b + O_L1G: b + O_L1G + 8] = _fm(inp["ln1_g"][l])
        pvs[:, b + O_L1B: b + O_L1B + 8] = _fm(inp["ln1_b"][l])
        pvs[:, b + O_L2G: b + O_L2G + 8] = _fm(inp["ln2_g"][l])
        pvs[:, b + O_L2B: b + O_L2B + 8] = _fm(inp["ln2_b"][l])
        for tap in range(3):
            pvs[:, b + O_FCW + tap * 48: b + O_FCW + tap * 48 + 48] = _fm(inp["ffn_conv_w"][l, tap])
        pvs[:, b + O_FCB: b + O_FCB + 48] = _fm(inp["ffn_conv_b"][l])
        pvs[:, b + O_QG: b + O_QG + 3] = _fm(inp["q_norm_g"][l])
        pvs[:, b + O_KVG: b + O_KVG + 2] = _fm(inp["kv_norm_g"][l])
    return pvs


def _host_bd(w, nl):
    w = np.asarray(w, np.float32)
    out = np.zeros((nl, 128, 8, 128), np.float32)
    for c in range(8):
        for b in range(2):
            out[:, b * 64:(b + 1) * 64, c, b * 64:(b + 1) * 64] = w[:, 2 * c + b]
    return out.reshape(nl, 128, 1024)


def _consts():
    c = np.zeros((128, 258), np.float32)
    c[:, 0:128] = np.eye(128, dtype=np.float32)
    k = np.arange(128)[:, None]
    q = np.arange(128)[None, :]
    c[:, 128:256] = (q >= k).astype(np.float32)
    inv = (10000.0 ** (-np.arange(0, 32, 2, dtype=np.float32) / np.float32(32))).astype(np.float32)
    for p in range(64, 96):
        c[p, 256] = inv[(p - 64) % 16]
    c[:, 257] = EPS
    return c


_NC_CACHE = {}
_TRACE = False
_LAST = [None]


def run(inp, NS, layers, ncores, seq_of_core):
    nl = L_ALL
    key = (NS, tuple(layers))
    if key not in _NC_CACHE:
        _NC_CACHE[key] = build(NS, list(layers))
    nc = _NC_CACHE[key]
    shared = {
        "w_in": np.ascontiguousarray(inp["w_in"], np.float32),
        "w_uq": np.ascontiguousarray(inp["w_uq"], np.float32),
        "w_ukv": np.ascontiguousarray(inp["w_ukv"], np.float32),
        "w_out": np.ascontiguousarray(inp["w_out"], np.float32),
        "w_up": np.ascontiguousarray(inp["w_up"], np.float32),
        "w_down": np.ascontiguousarray(inp["w_down"], np.float32),
        "gxw": _host_bd(inp["gx_w"], nl),
        "gaw": _host_bd(inp["ga_w"], nl),
        "pv": _host_params(inp, nl),
        "cst": _consts(),
    }
    x = np.asarray(inp["x"], np.float32)
    pos = np.asarray(inp["positions"], np.int32)
    in_maps = []
    for ci in range(ncores):
        seqs = seq_of_core[ci]
        m = dict(shared)
        m["x"] = np.ascontiguousarray(x[seqs])
        m["pos"] = np.ascontiguousarray(np.broadcast_to(pos[seqs][:, None, :], (len(seqs), 32, T)))
        in_maps.append(m)
    res = run_bass_kernel_spmd(nc, in_maps, core_ids=list(range(ncores)), trace=_TRACE)
    _LAST[0] = res
    return [r["y"] for r in res.results]


def kernel(**inputs):
    B = inputs["x"].shape[0]
    ncores = 8
    NS = B // ncores
    seq_of_core = [list(range(ci * NS, (ci + 1) * NS)) for ci in range(ncores)]
    outs = run(inputs, NS, list(range(L_ALL)), ncores, seq_of_core)
    return np.concatenate(outs, axis=0).astype(np.float32)
```

```python
from contextlib import ExitStack
import math
import numpy as np
import concourse.bass as bass
import concourse.mybir as mybir
from concourse.bass_utils import run_bass_kernel_spmd

F32 = mybir.dt.float32
BF16 = mybir.dt.bfloat16
I32 = mybir.dt.int32
AF = mybir.ActivationFunctionType
ALU = mybir.AluOpType

ENGS = ["pe", "act", "dve", "pool", "sp"]

D = 1024
T = 2048
TS = 512
NT = T // TS
L_ALL = 4
INW = 4768
DFF = 3072
ALPHA = float((2 * L_ALL) ** 0.25)
EPS = 1e-6
SCALE = float(96 ** -0.5)
PL = 296
O_CW, O_CB, O_GXB, O_GAB, O_LAM = 0, 32, 40, 48, 56
O_L1G, O_L1B, O_L2G, O_L2B = 64, 72, 80, 88
O_FCW, O_FCB, O_QG, O_KVG = 96, 240, 288, 291
PAIR_EXP = True
XC_RUNAHEAD = True
PUMP_N = 2
MASK_PE = True
SQRT_HOLD = False
QPROJ_EARLY = False
PUMP_S = 5
TRANSITIVE = True
TWO_PI = float(2 * math.pi)
PI = float(math.pi)


class Buf:
    __slots__ = ("name", "lw", "rd")

    def __init__(self, name=""):
        self.name = name
        self.lw = None
        self.rd = {}


class Sched:
    K = 8
    R = 32

    def __init__(self, nc, stack):
        self.nc = nc
        self.sems = {e: [stack.enter_context(nc.semaphore(f"s_{e}_{i}")) for i in range(self.K)]
                     for e in ENGS}
        self.dsems = [stack.enter_context(nc.semaphore(f"d_{i}")) for i in range(self.R)]
        self.cnt = {e: 0 for e in ENGS}
        self.seen = {e: {f: 0 for f in ENGS} for e in ENGS}
        self.seen_d = {e: [0] * self.R for e in ENGS}
        self.clock = {e: [] for e in ENGS}
        self.nd = 0
        self.thunks = {e: [] for e in ENGS}
        self.bufs = {}

    def b(self, key):
        x = self.bufs.get(key)
        if x is None:
            x = self.bufs[key] = Buf(str(key))
        return x

    def _need(self, e, tok, out):
        if tok is None:
            return
        if tok[0] == "e":
            _, f, i = tok
            if f == e and e == "pe":
                return
            if self.seen[e][f] >= i + 1:
                return
            out.append(tok)
        else:
            n = tok[1]
            if self.seen_d[e][n % self.R] >= n // self.R + 1:
                return
            out.append(tok)

    def _emit_waits(self, e, toks):
        best = {}
        dm = {}
        for t in toks:
            if t[0] == "e":
                best[t[1]] = max(best.get(t[1], -1), t[2])
            else:
                s = t[1] % self.R
                dm[s] = max(dm.get(s, -1), t[1])
        for f, i in best.items():
            if self.seen[e][f] >= i + 1:
                continue
            self.thunks[e].append(("w", self.sems[f][i % self.K], i // self.K + 1))
            self.seen[e][f] = i + 1
            clk = self.clock[f][i] if TRANSITIVE else {}
            for g in (ENGS if TRANSITIVE else []):
                if g != e and clk[g] > self.seen[e][g]:
                    self.seen[e][g] = clk[g]
        for s, n in dm.items():
            v = n // self.R + 1
            if self.seen_d[e][s] >= v:
                continue
            self.thunks[e].append(("w", self.dsems[s], 16 * v))
            self.seen_d[e][s] = v

    def _deps(self, e, reads, writes):
        toks = []
        for r in reads:
            self._need(e, r.lw, toks)
        for w in writes:
            self._need(e, w.lw, toks)
            for f, i in w.rd.items():
                if f == "d":
                    for n in i:
                        self._need(e, ("d", n), toks)
                else:
                    self._need(e, ("e", f, i), toks)
        return toks

    def _bl(self, xs):
        return [x if isinstance(x, Buf) else self.b(x) for x in xs]

    def op(self, e, name, reads=(), writes=(), **kw):
        fn = (name, kw)
        reads = self._bl(reads)
        writes = self._bl(writes)
        self._emit_waits(e, self._deps(e, reads, writes))
        i = self.cnt[e]
        self.cnt[e] = i + 1
        self.thunks[e].append(("i", fn, self.sems[e][i % self.K], 1))
        self.clock[e].append(dict(self.seen[e]))
        tok = ("e", e, i)
        for r in reads:
            if r.rd.get(e, -1) < i:
                r.rd[e] = i
        for w in writes:
            w.lw = tok
            w.rd = {}
        return tok

    def dma(self, out, in_, reads=(), writes=(), q="sp"):
        fn = ("dma_start", dict(out=out, in_=in_))
        reads = self._bl(reads)
        writes = self._bl(writes)
        n = self.nd
        self.nd += 1
        s = n % self.R
        toks = self._deps(q, reads, writes)
        if n >= self.R:
            self._need(q, ("d", n - self.R), toks)
        self._emit_waits(q, toks)
        self.thunks[q].append(("i", fn, self.dsems[s], 16))
        tok = ("d", n)
        for r in reads:
            r.rd.setdefault("d", []).append(n)
        for w in writes:
            w.lw = tok
            w.rd = {}
        return tok

    def barrier_all(self):
        for e in ENGS:
            toks = []
            for f in ENGS:
                if self.cnt[f] > 0:
                    self._need(e, ("e", f, self.cnt[f] - 1), toks)
            for n in range(max(0, self.nd - self.R), self.nd):
                self._need(e, ("d", n), toks)
            self._emit_waits(e, toks)

    def run(self):
        nc = self.nc
        with nc.Block() as block:
            def mk(e):
                def body(eng):
                    for t in self.thunks[e]:
                        if t[0] == "w":
                            eng.wait_ge(t[1], t[2])
                        else:
                            getattr(eng, t[1][0])(**t[1][1]).then_inc(t[2], t[3])
                return body
            block.tensor(mk("pe"))
            block.scalar(mk("act"))
            block.vector(mk("dve"))
            block.gpsimd(mk("pool"))
            block.sync(mk("sp"))


def build(NS, layers, nlw=L_ALL):
    nc = bass.Bass("TRN2", target_bir_lowering=False)
    dt_in = lambda n, s, d=F32: nc.dram_tensor(n, s, d, kind="ExternalInput").ap()
    x_d = dt_in("x", [NS, T, D])
    pos_d = dt_in("pos", [NS, 32, T], I32)
    w_in_d = dt_in("w_in", [nlw, D, INW])
    w_uq_d = dt_in("w_uq", [nlw, 384, 1536])
    w_ukv_d = dt_in("w_ukv", [nlw, 256, 2048])
    w_out_d = dt_in("w_out", [nlw, D, D])
    w_up_d = dt_in("w_up", [nlw, D, 2 * DFF])
    w_dn_d = dt_in("w_down", [nlw, DFF, D])
    gxw_d = dt_in("gxw", [nlw, 128, 8 * 128])
    gaw_d = dt_in("gaw", [nlw, 128, 8 * 128])
    pv_d = dt_in("pv", [128, nlw * PL])
    cst_d = dt_in("cst", [128, 258])
    y_d = nc.dram_tensor("y", [NS, T, D], F32, kind="ExternalOutput").ap()

    scr = lambda n, s: nc.dram_tensor(n, s, BF16, kind="Internal").ap()
    winL_s = scr("winL", [nlw, 128, 8, 672])
    winB_s = scr("winB", [nlw, 128, 8, 8, 512])
    wkrot_s = scr("wkrot", [nlw, 128, 8 * 96])
    wuqP_s = scr("wuqP", [nlw, 128, 8, 3, 192])
    wqrP_s = scr("wqrP", [nlw, 128, 8 * 3 * 192])
    wukvP_s = scr("wukvP", [nlw, 128, 8, 2, 256])
    wout_s = scr("wout", [nlw, 128, 8, 1024])
    wupP_s = scr("wupP", [nlw, 128, 24, 8, 256])
    wdnP_s = scr("wdnP", [nlw, 128, 8, 24, 128])
    gxw_s = scr("gxws", [nlw, 128, 1024])
    gaw_s = scr("gaws", [nlw, 128, 1024])

    with ExitStack() as st:
        S = Sched(nc, st)
        sb = lambda n, s, d: st.enter_context(nc.sbuf_tensor("sb_" + n, s, d))
        xhi = sb("xhi", [128, 8, T], BF16)
        xlo = sb("xlo", [128, 8, T], BF16)
        big = sb("big", [128, 26624], BF16)
        tmp = sb("tmp", [128, 9, TS], F32)
        hr = sb("hr", [128, 4], F32)
        tmo = sb("tmo", [128, 2, TS], F32)
        cosT = sb("cosT", [128, T], F32)
        sinT = sb("sinT", [128, T], F32)
        KT = sb("KT", [128, 2, T], BF16)
        VA = sb("VA", [128, 16, 2, 128], BF16)
        QT = sb("QT", [128, 2, TS], BF16)
        PT = sb("PT", [128, 5, TS], BF16)
        PT4 = PT[:, 0:4, :].rearrange("p (s h) t -> p s h t", s=2)
        wbuf = sb("wbuf", [128, 8192], BF16)
        wsm = sb("wsm", [128, 2, 1920], BF16)
        pv = sb("pv", [128, nlw * PL], F32)
        pvd = sb("pvd", [128, nlw * 32], F32)
        cst = sb("cst", [128, 258], F32)
        onesb = sb("onesb", [128, 128], BF16)
        identb = sb("identb", [128, 128], BF16)
        mnegb = sb("mnegb", [128, 128], BF16)
        trib = mnegb
        hal = sb("hal", [128, 48, 2], F32)
        hlast = sb("hlast", [128, 2], F32)
        ps = [st.enter_context(nc.psum_tensor(f"ps{i}", [128, TS], F32)) for i in range(4)]
        ppair = [st.enter_context(nc.psum_tensor(f"pp{i}", [128, 2, TS], F32)) for i in range(2)]
        ps += [ppair[0][:, 0, :], ppair[0][:, 1, :], ppair[1][:, 0, :], ppair[1][:, 1, :]]
        PSB = [S.b(("ps", i)) for i in range(8)]
        ident = cst[:, 0:128]
        invf = cst[:, 256:257]
        epsc = cst[:, 257:258]
        wkr = wsm[:, 1, 0:768].rearrange("p (k d) -> p k d", k=8)

        merged = big[:, 0:16384].rearrange("p (c t) -> p c t", c=8)
        qn = big[:, 16384:22528].rearrange("p (c t) -> p c t", c=3)
        kvn = big[:, 22528:26624].rearrange("p (c t) -> p c t", c=2)
        abuf = big[:, 0:24576].rearrange("p (j t) -> p j t", j=24)
        wdn_sb = [KT[:].rearrange("p a t -> p (a t)")[:, 0:3072].rearrange("p (j q) -> p j q", j=24),
                  VA[:].rearrange("p a b c -> p (a b c)")[:, 0:3072].rearrange("p (j q) -> p j q", j=24)]

        def ACT(out, in_, func, rd, wr, **kw):
            S.op("act", "activation", rd, wr, out=out, in_=in_, func=func, **kw)

        def TT(e, out, in0, in1, op, rd, wr):
            S.op(e, "tensor_tensor", rd, wr, out=out, in0=in0, in1=in1, op=op)

        def TSC(e, out, in0, s1, s2, op0, op1, rd, wr):
            if s2 is None:
                S.op(e, "tensor_scalar", rd, wr, out=out, in0=in0, scalar1=s1, scalar2=None, op0=op0)
            else:
                S.op(e, "tensor_scalar", rd, wr, out=out, in0=in0, scalar1=s1, scalar2=s2, op0=op0, op1=op1)

        def STT(e, out, in0, scalar, in1, op0, op1, rd, wr):
            S.op(e, "scalar_tensor_tensor", rd, wr, out=out, in0=in0, scalar=scalar, in1=in1, op0=op0, op1=op1)

        def CP(e, out, in_, rd, wr):
            S.op(e, "tensor_copy", rd, wr, out=out, in_=in_)

        def RCP(out, in_, rd, wr):
            S.op("dve", "reciprocal", rd, wr, out=out, in_=in_)

        def MM(out, lhsT, rhs, start, stop, rd, wr):
            S.op("pe", "matmul", rd, wr, out=out, lhsT=lhsT, rhs=rhs, start=start, stop=stop)

        def TR(out, in_, rd, wr):
            S.op("pe", "transpose", rd, wr, out=out, in_=in_, identity=ident)

        def MS(e, ap, v, wr):
            S.op(e, "memset", (), wr, ap=ap, constant=v)

        DMA = S.dma
        proj_rr = [0]

        nproj = [3]

        def pbank():
            i = proj_rr[0] % nproj[0]
            proj_rr[0] += 1
            return i

        rb_rr = [0]
        ab_rr = [0]

        def rbank():
            i = rb_rr[0] % 2
            rb_rr[0] += 1
            return i

        def abank():
            i = 4 + ab_rr[0] % 4
            ab_rr[0] += 1
            return i

        def mm_group(pb, lhs_list, rhs_list, reads, out_ap=None):
            n = len(lhs_list)
            o = ps[pb][:] if out_ap is None else out_ap
            for i in range(n):
                MM(o, lhs_list[i], rhs_list[i], i == 0, i == n - 1, reads, [PSB[pb]])

        def tsl(tt):
            return slice(tt * TS, (tt + 1) * TS)

        XB = lambda k, tt: ("xhi", k, tt)
        XL = lambda k, tt: ("xlo", k, tt)

        DMA(cst[:], cst_d, (), ["cst"])
        DMA(pv[:], pv_d, (), ["pv"])
        MS("pool", onesb[:], 1.0, ["onesb"])
        CP("dve", identb[:], cst[:, 0:128], ["cst"], ["identb"])
        TSC("dve", mnegb[:], cst[:, 128:256], -1.0, 30000.0, ALU.add, ALU.mult, ["cst"], ["mnegb"])
        MS("pool", hal[:], 0.0, ["hal"])
        for l in layers:
            lam = pv[:, l * PL + O_LAM: l * PL + O_LAM + 8]
            t0 = tmp[:, 0, 0:8]
            ACT(t0, lam, AF.Exp, ["pv"], ["t0"], scale=-1.0)
            ACT(t0, t0, AF.Ln, ["t0"], ["t0"], bias=1.0)
            TSC("dve", pvd[:, l * 32: l * 32 + 8], t0, -4.0, None, ALU.mult, None, ["t0"], ["pvd"])
            TSC("dve", pvd[:, l * 32 + 8: l * 32 + 16], t0, -8.0, None, ALU.mult, None, ["t0"], ["pvd"])
            TSC("dve", pvd[:, l * 32 + 16: l * 32 + 24], pv[:, l * PL + O_GXB: l * PL + O_GXB + 8], 0.5, None, ALU.mult, None, ["pv"], ["pvd"])
            TSC("dve", pvd[:, l * 32 + 24: l * 32 + 32], pv[:, l * PL + O_GAB: l * PL + O_GAB + 8], 0.5, None, ALU.mult, None, ["pv"], ["pvd"])
        S.barrier_all()

        stf = big[:, 0:24576].bitcast(F32).rearrange("p (a b) -> p a b", a=2)
        stb = xhi[:].rearrange("p k t -> p (k t)")[:, 0:12288].rearrange("p (a b) -> p a b", a=2)
        tb16 = tmp[:].rearrange("p a b -> p (a b)").bitcast(BF16)
        wqr_sb = tb16[:, 0:4608].rearrange("p (c k h d) -> p c k h d", c=8, k=3, h=2)
        ccnt = [0]

        def cast(i, out_ap, in_ap, scale=None):
            if scale is None:
                ccnt[0] += 1
                if ccnt[0] % 2 == 0:
                    ACT(out_ap, in_ap, AF.Copy, [("stf", i)], [("stb", i)])
                else:
                    CP("dve", out_ap, in_ap, [("stf", i)], [("stb", i)])
            else:
                TSC("dve", out_ap, in_ap, scale, None, ALU.mult, None, [("stf", i), "pv"], [("stb", i)])

        MS("pool", wkr, 0.0, ["wkr"])
        MS("pool", tb16[:, 0:4608], 0.0, ["wqr_sb"])
        jobs = []
        for l in layers:
            base = l * PL
            for k in range(8):
                def job(i, l=l, k=k):
                    sf = stf[:, i, :]
                    so = stb[:, i, :]
                    soB = so[:, 0:4096].rearrange("p (c j q) -> p c j q", c=8, j=4)
                    for j, b0 in enumerate([0, 1024, 2720, 3744]):
                        cast(i, soB[:, :, j, :], sf[:, b0:b0 + 1024].rearrange("p (c q) -> p c q", c=8))
                    cast(i, so[:, 4096:4768], sf[:, 2048:2720])
                    TSC("pool", wkr[:, k, 64:80], sf[:, 2704:2720], -1.0, None, ALU.mult, None, [("stf", i)], ["wkr"])
                    CP("pool", wkr[:, k, 80:96], sf[:, 2688:2704], [("stf", i)], ["wkr"])
                    DMA(winB_s[l][:, :, k, :], so[:, 0:4096].rearrange("p (c q) -> p c q", c=8), [("stb", i)], [("winB", l)])
                    DMA(winL_s[l][:, k, :], so[:, 4096:4768], [("stb", i)], [("winL", l)])
                    if k == 7:
                        DMA(wkrot_s[l], wsm[:, 1, 0:768], ["wkr"], [("wkrot", l)])
                jobs.append((w_in_d[l, k * 128:(k + 1) * 128, :], lambda i: stf[:, i, 0:INW], job))

            def job_uq(i, l=l, base=base):
                sf3 = stf[:, i, 0:4608].rearrange("p (k q) -> p k q", k=3)
                so3 = stb[:, i, 0:4608].rearrange("p (k q) -> p k q", k=3)
                for k in range(3):
                    g = pv[:, base + O_QG + k: base + O_QG + k + 1]
                    cast(i, so3[:, k, :], sf3[:, k, :], scale=g)
                    sfv = sf3[:, k, :].rearrange("p (c h d) -> p c h d", c=8, h=2)
                    TSC("pool", wqr_sb[:, :, k, :, 64:80], sfv[:, :, :, 80:96], g, -1.0, ALU.mult, ALU.mult,
                        [("stf", i), "pv"], ["wqr_sb"])
                    TSC("pool", wqr_sb[:, :, k, :, 80:96], sfv[:, :, :, 64:80], g, None, ALU.mult, None,
                        [("stf", i), "pv"], ["wqr_sb"])
                    DMA(wuqP_s[l][:, :, k, :], so3[:, k, :].rearrange("p (c q) -> p c q", c=8), [("stb", i)], [("wuqP", l)])
                DMA(wqrP_s[l], tb16[:, 0:4608], ["wqr_sb"], [("wqrP", l)])
            jobs.append((w_uq_d[l].rearrange("(k p) q -> p k q", p=128),
                         lambda i: stf[:, i, 0:4608].rearrange("p (k q) -> p k q", k=3), job_uq))

            def job_ukv(i, l=l, base=base):
                sf2 = stf[:, i, 0:4096].rearrange("p (k q) -> p k q", k=2)
                so2 = stb[:, i, 0:4096].rearrange("p (k q) -> p k q", k=2)
                for k in range(2):
                    g = pv[:, base + O_KVG + k: base + O_KVG + k + 1]
                    cast(i, so2[:, k, :], sf2[:, k, :], scale=g)
                    DMA(wukvP_s[l][:, :, k, :], so2[:, k, :].rearrange("p (c q) -> p c q", c=8), [("stb", i)], [("wukvP", l)])
            jobs.append((w_ukv_d[l].rearrange("(k p) q -> p k q", p=128),
                         lambda i: stf[:, i, 0:4096].rearrange("p (k q) -> p k q", k=2), job_ukv))

            for k0 in range(0, 8, 4):
                def job_out(i, l=l, k0=k0):
                    cast(i, stb[:, i, 0:4096], stf[:, i, 0:4096])
                    DMA(wout_s[l][:, k0:k0 + 4, :], stb[:, i, 0:4096].rearrange("p (k q) -> p k q", k=4), [("stb", i)], [("wout", l)])
                jobs.append((w_out_d[l, k0 * 128:(k0 + 4) * 128, :].rearrange("(k p) q -> p k q", p=128),
                             lambda i: stf[:, i, 0:4096].rearrange("p (k q) -> p k q", k=4), job_out))

            for k in range(8):
                def job_up(i, l=l, k=k):
                    sfv = stf[:, i, :].rearrange("p (g j q) -> p g j q", g=2, j=24)
                    sov = stb[:, i, :].rearrange("p (j g q) -> p j g q", j=24, g=2)
                    cast(i, sov[:, :, 0, :], sfv[:, 0, :, :])
                    cast(i, sov[:, :, 1, :], sfv[:, 1, :, :])
                    DMA(wupP_s[l][:, :, k, :], stb[:, i, :].rearrange("p (j q) -> p j q", j=24), [("stb", i)], [("wupP", l)])
                jobs.append((w_up_d[l, k * 128:(k + 1) * 128, :], lambda i: stf[:, i, 0:6144], job_up))

            for k0 in range(0, 24, 4):
                def job_dn(i, l=l, k0=k0):
                    cast(i, stb[:, i, 0:4096], stf[:, i, 0:4096])
                    so4 = stb[:, i, 0:4096].rearrange("p (k m q) -> p k m q", k=4, m=8)
                    for kk in range(4):
                        DMA(wdnP_s[l][:, :, k0 + kk, :], so4[:, kk, :, :], [("stb", i)], [("wdnP", l)])
                jobs.append((w_dn_d[l, k0 * 128:(k0 + 4) * 128, :].rearrange("(k p) q -> p k q", p=128),
                             lambda i: stf[:, i, 0:4096].rearrange("p (k q) -> p k q", k=4), job_dn))

            for src, dst, nm in [(gxw_d, gxw_s, "gxws"), (gaw_d, gaw_s, "gaws")]:
                def job_g(i, l=l, dst=dst, nm=nm):
                    cast(i, stb[:, i, 0:1024], stf[:, i, 0:1024])
                    DMA(dst[l], stb[:, i, 0:1024], [("stb", i)], [(nm, l)])
                jobs.append((src[l], lambda i: stf[:, i, 0:1024], job_g))

        nj = len(jobs)

        def jload(n):
            i = n % 2
            DMA(jobs[n][1](i), jobs[n][0], (), [("stf", i)])

        if nj:
            jload(0)
        for n in range(nj):
            if n + 1 < nj:
                jload(n + 1)
            jobs[n][2](n % 2)
        S.barrier_all()

        P = slice(64, 96)
        for s in range(NS):
            for tb in range(T // 128):
                tt = tb // 4
                xt = tmp[:, 2 * (tb % 2):2 * (tb % 2) + 2, :].rearrange("p a b -> p (a b)")
                xtb = ("xt", tb % 2)
                DMA(xt, x_d[s, tb * 128:(tb + 1) * 128, :], (), [xtb])
                for g in range(2):
                    pb = pbank()
                    for kk in range(4):
                        k = g * 4 + kk
                        TR(ps[pb][:, kk * 128:(kk + 1) * 128], xt[:, k * 128:(k + 1) * 128], [xtb, "cst"], [PSB[pb]])
                    hi = xhi[:, g * 4:(g + 1) * 4, tb * 128:(tb + 1) * 128]
                    lo = xlo[:, g * 4:(g + 1) * 4, tb * 128:(tb + 1) * 128]
                    pv3 = ps[pb][:].rearrange("p (a b) -> p a b", a=4)
                    hb = [XB(k, tt) for k in range(g * 4, g * 4 + 4)]
                    lb = [XL(k, tt) for k in range(g * 4, g * 4 + 4)]
                    ACT(hi, pv3, AF.Copy, [PSB[pb]], hb)
                    TT("dve", lo, pv3, hi, ALU.subtract, [PSB[pb]] + hb, lb)
            posi = tmp[:, 0:4, :].rearrange("p a b -> p (a b)").bitcast(I32)
            ang = tmp[:, 4:8, :].rearrange("p a b -> p (a b)")
            kf = tmp[:, 0:4, :].rearrange("p a b -> p (a b)")
            S.barrier_all()
            DMA(posi[P, :], pos_d[s], (), ["posi"])
            CP("dve", ang[P, :], posi[P, :], ["posi"], ["ang"])
            TSC("dve", ang[P, :], ang[P, :], invf[P, :], None, ALU.mult, None, ["ang", "cst"], ["ang"])
            TSC("dve", posi[P, :], ang[P, :], 1.0 / TWO_PI, None, ALU.mult, None, ["ang"], ["posi"])
            CP("dve", kf[P, :], posi[P, :], ["posi"], ["posi"])
            STT("dve", ang[P, :], kf[P, :], -TWO_PI, ang[P, :], ALU.mult, ALU.add, ["posi", "ang"], ["ang"])

            def wrap():
                TSC("dve", kf[P, :], ang[P, :], PI, TWO_PI, ALU.is_gt, ALU.mult, ["ang"], ["posi"])
                TT("dve", ang[P, :], ang[P, :], kf[P, :], ALU.subtract, ["ang", "posi"], ["ang"])
                TSC("dve", kf[P, :], ang[P, :], -PI, TWO_PI, ALU.is_lt, ALU.mult, ["ang"], ["posi"])
                TT("dve", ang[P, :], ang[P, :], kf[P, :], ALU.add, ["ang", "posi"], ["ang"])
            wrap()
            ACT(sinT[P, :], ang[P, :], AF.Sin, ["ang"], ["sinT"])
            TSC("dve", ang[P, :], ang[P, :], PI / 2, None, ALU.add, None, ["ang"], ["ang"])
            wrap()
            ACT(cosT[P, :], ang[P, :], AF.Sin, ["ang"], ["cosT"])
            S.barrier_all()

            for li, l in enumerate(layers):
                base = l * PL
                pcol = lambda o, base=base: pv[:, base + o: base + o + 1]
                wlat = wbuf[:, 0:8 * 672].rearrange("p (k q) -> p k q", k=8)
                DMA(wlat, winL_s[l], [("winL", l)], ["wbA", "wbB"])
                DMA(wsm[:, 1, 0:768], wkrot_s[l], [("wkrot", l)], ["wkr"])
                for tt in range(NT):
                    xh = [xhi[:, k, tsl(tt)] for k in range(8)]
                    xrd = [XB(k, tt) for k in range(8)]
                    for (nch, c0, dst, inv_n, nm) in [(3, 0, qn, 1.0 / 384, "qn"), (2, 384, kvn, 1.0 / 256, "kvn")]:
                        for oc in range(nch):
                            pb = pbank()
                            mm_group(pb, [wlat[:, k, c0 + oc * 128: c0 + (oc + 1) * 128] for k in range(8)], xh, xrd + ["wbA"])
                            ACT(tmp[:, oc, :], ps[pb][:], AF.Copy, [PSB[pb]], [("t", oc)])
                            ACT(PT[:, oc, :], ps[pb][:], AF.Square, [PSB[pb]], [("pt", oc)])
                        mm_group(7, [onesb[:]] * nch, [PT[:, oc, :] for oc in range(nch)],
                                 ["onesb"] + [("pt", oc) for oc in range(nch)])
                        ACT(tmp[:, 4, :], ps[7][:], AF.Sqrt, [PSB[7], "cst"], [("t", 4)], bias=epsc, scale=inv_n)
                        RCP(tmp[:, 5, :], tmp[:, 4, :], [("t", 4)], [("t", 5)])
                        for oc in range(nch):
                            TT("pool", dst[:, oc, tsl(tt)], tmp[:, oc, :], tmp[:, 5, :], ALU.mult,
                               [("t", oc), ("t", 5)], [(nm, oc, tt)])
                    pa = pbank()
                    mm_group(pa, [wlat[:, k, 576:672] for k in range(8)], xh, xrd + ["wbA"], out_ap=ps[pa][0:96, :])
                    pb2 = pbank()
                    mm_group(pb2, [wkr[:, k, :] for k in range(8)], xh, xrd + ["wkr"], out_ap=ps[pb2][0:96, :])
                    TT("dve", tmp[P, 6, :], ps[pa][P, :], cosT[P, tsl(tt)], ALU.mult, [PSB[pa], "cosT"], [("t", 6)])
                    TT("dve", tmp[P, 7, :], ps[pb2][P, :], sinT[P, tsl(tt)], ALU.mult, [PSB[pb2], "sinT"], [("t", 7)])
                    TT("pool", KT[P, 0, tsl(tt)], tmp[P, 6, :], tmp[P, 7, :], ALU.add, [("t", 6), ("t", 7)], [("kpe", tt)])
                    CP("pool", KT[P, 1, tsl(tt)], KT[P, 0, tsl(tt)], [("kpe", tt)], [("kpe1", tt)])
                S.barrier_all()

                def load_pair(c, l=l):
                    hb_ = c % 2
                    wb = wbuf[:, hb_ * 4096:(hb_ + 1) * 4096]
                    DMA(wb.rearrange("p (k q) -> p k q", k=8), winB_s[l][:, c, :, :], [("winB", l)], ["wbA" if hb_ == 0 else "wbB"])
                    sm = wsm[:, hb_, :]
                    DMA(sm[:, 0:576].rearrange("p (k q) -> p k q", k=3), wuqP_s[l][:, c, :, :], [("wuqP", l)], [("wsm", hb_, 0)])
                    DMA(sm[:, 576:1152], wqrP_s[l][:, c * 576:(c + 1) * 576], [("wqrP", l)], [("wsm", hb_, 1)])
                    DMA(sm[:, 1152:1664].rearrange("p (k q) -> p k q", k=2), wukvP_s[l][:, c, :, :], [("wukvP", l)], [("wsm", hb_, 2)])
                    DMA(sm[:, 1664:1792], gxw_s[l][:, c * 128:(c + 1) * 128], [("gxws", l)], [("wsm", hb_, 3)])
                    DMA(sm[:, 1792:1920], gaw_s[l][:, c * 128:(c + 1) * 128], [("gaws", l)], [("wsm", hb_, 4)])

                nproj[0] = 2
                MS("pool", VA[:, :, 0, 64:128], 1.0, ["VA"])
                MS("pool", VA[:, :, 1, 0:64], 1.0, ["VA"])
                def make_rnn(c, l=l):
                    hb_ = c % 2
                    WB = "wbA" if hb_ == 0 else "wbB"
                    wc = wbuf[:, hb_ * 4096:(hb_ + 1) * 4096].rearrange("p (k j q) -> p k j q", k=8, j=4)
                    sm = wsm[:, hb_, :]
                    gxw = sm[:, 1664:1792]
                    gaw = sm[:, 1792:1920]
                    cwc = lambda tap: pcol(O_CW + tap * 8 + c)
                    def rnn_p1(qt):
                        xh = [xhi[:, k, tsl(qt)] for k in range(8)]
                        xrd = [XB(k, qt) for k in range(8)]
                        pc = lambda o: pvd[:, l * 32 + o + c: l * 32 + o + c + 1]
                        px = rbank()
                        mm_group(px, [wc[:, k, 0, :] for k in range(8)], xh, xrd + [WB])
                        yield
                        X = ps[px]
                        cv = tmp[:, 0, :]
                        TSC("dve", cv, X[:], cwc(3), pcol(O_CB + c), ALU.mult, ALU.add, [PSB[px], "pv"], [("t", 0)])
                        yield
                        pg = rbank()
                        mm_group(pg, [wc[:, k, 1, :] for k in range(8)], xh, xrd + [WB])
                        yield
                        for sh, tap in ((1, 2), (2, 1), (3, 0)):
                            STT("dve", cv[:, sh:TS], X[:, 0:TS - sh], cwc(tap), cv[:, sh:TS], ALU.mult, ALU.add,
                                [PSB[px], ("t", 0), "pv"], [("t", 0)])
                            yield
                        if qt > 0:
                            for sh, tap in ((1, 2), (2, 1), (3, 0)):
                                STT("dve", cv[:, 0:sh], hr[:, 3 - sh:3], cwc(tap), cv[:, 0:sh], ALU.mult, ALU.add,
                                    ["hr", ("t", 0), "pv"], [("t", 0)])
                        if qt < NT - 1:
                            CP("dve", hr[:, 0:3], X[:, TS - 3:TS], [PSB[px]], ["hr"])
                        yield
                        ACT(PT[:, 4, :], cv, AF.Copy, [("t", 0)], [("pt", 4)])
                        ACT(tmp[:, 3, :], ps[pg][:], AF.Square, [PSB[pg]], [("t", 3)])
                        ACT(tmp[:, 4, :], ps[pg][:], AF.Copy, [PSB[pg]], [("t", 4)])
                        yield
                        TSC("dve", tmp[:, 3, :], tmp[:, 3, :], 0.044715, 1.0, ALU.mult, ALU.add, [("t", 3)], [("t", 3)])
                        yield
                        TT("dve", tmp[:, 3, :], tmp[:, 3, :], tmp[:, 4, :], ALU.mult, [("t", 3), ("t", 4)], [("t", 3)])
                        yield
                        pgx = rbank()
                        mm_group(pgx, [gxw], [PT[:, 4, :]], [("pt", 4), ("wsm", hb_, 3)])
                        pga = rbank()
                        mm_group(pga, [gaw], [PT[:, 4, :]], [("pt", 4), ("wsm", hb_, 4)])
                        yield
                        ACT(tmp[:, 3, :], tmp[:, 3, :], AF.Tanh, [("t", 3)], [("t", 3)], scale=0.7978845608028654)
                        yield
                        ACT(tmp[:, 1, :], ps[pgx][:], AF.Tanh, [PSB[pgx], "pvd"], [("t", 1)], bias=pc(16), scale=0.5)
                        yield
                        ACT(tmp[:, 2, :], ps[pga][:], AF.Tanh, [PSB[pga], "pvd"], [("t", 2)], bias=pc(24), scale=0.5)
                        STT("dve", tmp[:, 4, :], tmp[:, 3, :], 1.0, tmp[:, 4, :], ALU.add, ALU.mult, [("t", 3), ("t", 4)], [("t", 4)])
                        yield
                        ACT(tmp[:, 3, :], tmp[:, 2, :], AF.Exp, [("t", 2), "pvd"], [("t", 3)], bias=pc(0), scale=pc(0))
                        yield
                        ACT(tmp[:, 2, :], tmp[:, 2, :], AF.Exp, [("t", 2), "pvd"], [("t", 2)], bias=pc(8), scale=pc(8))
                        STT("dve", tmp[:, 1, :], tmp[:, 1, :], 1.0, tmp[:, 0, :], ALU.add, ALU.mult, [("t", 1), ("t", 0)], [("t", 1)])
                        yield
                        if SQRT_HOLD:
                            yield "HOLD"
                        ACT(tmp[:, 2, :], tmp[:, 2, :], AF.Sqrt, [("t", 2)], [("t", 2)], bias=1.0, scale=-1.0)
                        yield
                        STT("dve", tmp[:, 1, :], tmp[:, 1, :], 0.5, tmp[:, 2, :], ALU.mult, ALU.mult, [("t", 1), ("t", 2)], [("t", 1)])
                        yield
                        init = 0.0 if qt == 0 else hlast[:, 0:1]
                        S.op("dve", "tensor_tensor_scan", [("t", 3), ("t", 1), "hlast"], [("t", 2)],
                             out=tmp[:, 2, :], data0=tmp[:, 3, :], data1=tmp[:, 1, :], initial=init, op0=ALU.mult, op1=ALU.add)
                        CP("dve", hlast[:, 0:1], tmp[:, 2, TS - 1:TS], [("t", 2)], ["hlast"])
                        yield
                        TT("pool", tmp[:, 4, :], tmp[:, 4, :], tmp[:, 2, :], ALU.mult, [("t", 4), ("t", 2)], [("t", 4)])
                        yield

                    def rnn_p2(qt):
                        xh = [xhi[:, k, tsl(qt)] for k in range(8)]
                        xrd = [XB(k, qt) for k in range(8)]
                        pa_ = rbank()
                        mm_group(pa_, [wc[:, k, 2, :] for k in range(8)], xh, xrd + [WB])
                        yield
                        ACT(tmp[:, 1, :], ps[pa_][:], AF.Tanh, [PSB[pa_]], [("t", 1)], scale=0.5)
                        yield
                        pbg = rbank()
                        mm_group(pbg, [wc[:, k, 3, :] for k in range(8)], xh, xrd + [WB])
                        yield
                        STT("dve", tmo[:, 0, :], tmp[:, 1, :], 1.0, tmp[:, 4, :], ALU.add, ALU.mult, [("t", 4), ("t", 1)], ["tM"])
                        ACT(tmo[:, 1, :], ps[pbg][:], AF.Tanh, [PSB[pbg]], ["tB"], scale=0.5)
                        yield


                    return [g for q_ in range(NT) for g in (rnn_p1(q_), rnn_p2(q_))]

                gq = []

                held = [False]

                def pump(n, limit):
                    k = 0
                    while k < n and gq_i[0] <= limit and gq_i[0] < len(gq) and not held[0]:
                        try:
                            r = next(gq[gq_i[0]])
                            k += 1
                            if r == "HOLD":
                                held[0] = True
                        except StopIteration:
                            gq_i[0] += 1

                def release():
                    held[0] = False

                def flush(limit):
                    held[0] = False
                    while gq_i[0] <= limit and gq_i[0] < len(gq):
                        try:
                            next(gq[gq_i[0]])
                        except StopIteration:
                            gq_i[0] += 1

                gq_i = [0]
                load_pair(0)
                gq += make_rnn(0)
                for c in range(8):
                    hb_ = c % 2
                    WB = "wbA" if hb_ == 0 else "wbB"
                    if c + 1 < 8:
                        load_pair(c + 1)
                        gq += make_rnn(c + 1)
                    wc = wbuf[:, hb_ * 4096:(hb_ + 1) * 4096].rearrange("p (k j q) -> p k j q", k=8, j=4)
                    sm = wsm[:, hb_, :]
                    wuq = sm[:, 0:576].rearrange("p (k q) -> p k q", k=3)
                    wqr = sm[:, 576:1152].rearrange("p (k h q) -> p k h q", k=3, h=2)
                    wukv = sm[:, 1152:1664].rearrange("p (k q) -> p k q", k=2)
                    gxw = sm[:, 1664:1792]
                    gaw = sm[:, 1792:1920]
                    cwc = lambda tap, c=c: pcol(O_CW + tap * 8 + c)
                    for hh in range(2):
                        for tt in range(NT):
                            pb = abank()
                            mm_group(pb, [wukv[:, k, hh * 128: hh * 128 + 64] for k in range(2)],
                                     [kvn[:, k, tsl(tt)] for k in range(2)], [("kvn", 0, tt), ("kvn", 1, tt), ("wsm", hb_, 2)],
                                     out_ap=ps[pb][0:64, :])
                            ACT(KT[0:64, hh, tsl(tt)], ps[pb][0:64, :], AF.Copy, [PSB[pb]], [("KT", hh, tt)])
                            pump(2, 8 * c + 1)
                    for g4 in range(4):
                        pb = abank()
                        pv4 = ps[pb][:].rearrange("p (t h d) -> p t h d", t=4, h=2)
                        for t4 in range(4):
                            tb = g4 * 4 + t4
                            for k in range(2):
                                MM(pv4[:, t4, :, :], kvn[:, k, tb * 128:(tb + 1) * 128],
                                   wukv[:, k, :].rearrange("p (h d) -> p h d", h=2)[:, :, 64:128], k == 0, k == 1,
                                   [("kvn", k, g4), ("wsm", hb_, 2)], [PSB[pb]])
                        ACT(VA[:, g4 * 4:(g4 + 1) * 4, 0, 0:64], pv4[:, :, 0, :], AF.Copy, [PSB[pb]], [("VA", g4)])
                        ACT(VA[:, g4 * 4:(g4 + 1) * 4, 1, 64:128], pv4[:, :, 1, :], AF.Copy, [PSB[pb]], [("VA", g4)])
                        pump(2, 8 * c + 1)
                    def qproj(qt, LIM, wuq=wuq, wqr=wqr, hb_=hb_):
                        qr = [("qn", k, qt) for k in range(3)]
                        qrh = [qn[:, k, tsl(qt)] for k in range(3)]
                        for hh in range(2):
                            pq = 4 + 2 * hh
                            mm_group(pq, [wuq[:, k, hh * 96:(hh + 1) * 96] for k in range(3)], qrh,
                                     qr + [("wsm", hb_, 0)], out_ap=ps[pq][0:96, :])
                            pr = 5 + 2 * hh
                            mm_group(pr, [wqr[:, k, hh, :] for k in range(3)], qrh,
                                     qr + [("wsm", hb_, 1)], out_ap=ps[pr][0:96, :])
                            CP("dve", QT[0:64, hh, :], ps[pq][0:64, :], [PSB[pq]], [("QT", hh)])
                            TT("dve", tmp[P, 5, :], ps[pq][P, :], cosT[P, tsl(qt)], ALU.mult, [PSB[pq], "cosT"], [("t", 5)])
                            TT("dve", tmp[P, 6, :], ps[pr][P, :], sinT[P, tsl(qt)], ALU.mult, [PSB[pr], "sinT"], [("t", 6)])
                            TT("pool", QT[P, hh, :], tmp[P, 5, :], tmp[P, 6, :], ALU.add, [("t", 5), ("t", 6)], [("QT", hh)])
                            pump(2, LIM)

                    nproj[0] = 2
                    for qt in range(NT):
                        LIM = min(8 * c + 2 * qt + 2, 8 * c + 7) if not XC_RUNAHEAD else 8 * c + 2 * qt + 2
                        pump(6 if qt == 0 else 2, LIM)
                        if qt == 0 or not QPROJ_EARLY:
                            qproj(qt, LIM)
                        nkb = 4 * qt + 4
                        for i in range(nkb + 1):
                            if i < nkb:
                                kb = i
                                n0 = max(kb - 4 * qt, 0) * 128
                                sl = kb % 2
                                diag = (kb - 4 * qt) >= 0 and MASK_PE
                                for hh in range(2):
                                    bk = 4 + 2 * sl + hh
                                    MM(ps[bk][:, n0:TS], KT[0:96, hh, kb * 128:(kb + 1) * 128], QT[0:96, hh, n0:TS], True, not diag,
                                       [("KT", hh, kb // 4), ("kpe" if hh == 0 else "kpe1", kb // 4), ("QT", hh)], [PSB[bk]])
                                    if diag:
                                        MM(ps[bk][:, n0:n0 + 128], identb[:], mnegb[:], False, True, ["identb", "mnegb"], [PSB[bk]])
                            if i >= 1:
                                kb = i - 1
                                j = kb - 4 * qt
                                n0 = max(j, 0) * 128
                                sl = kb % 2
                                ACT(PT4[:, sl, :, n0:TS], ppair[sl][:, :, n0:TS], AF.Exp,
                                    [PSB[4 + 2 * sl], PSB[5 + 2 * sl]], [("pt", sl)], scale=SCALE)
                                if j >= 0 and not MASK_PE:
                                    for hh in range(2):
                                        TT("pool", PT4[:, sl, hh, n0:n0 + 128], PT4[:, sl, hh, n0:n0 + 128], trib[:], ALU.mult,
                                           [("pt", sl), "trib"], [("pt", sl)])
                                for hh in range(2):
                                    MM(ps[2 + hh][:, n0:TS], VA[:, kb, hh, :], PT4[:, sl, hh, n0:TS], kb == 0, kb == nkb - 1,
                                       [("VA", kb // 4), ("pt", sl), "VA"], [PSB[2 + hh]])
                            pump(PUMP_N, LIM)
                        if QPROJ_EARLY and qt + 1 < NT:
                            qproj(qt + 1, LIM)
                        release()
                        pump(2, LIM)
                        RCP(tmp[0:64, 7, :], ps[2][64:128, :], [PSB[2]], [("t", 7)])
                        TT("dve", tmp[0:64, 8, :], ps[2][0:64, :], tmp[0:64, 7, :], ALU.mult, [PSB[2], ("t", 7)], [("t", 8)])
                        RCP(tmp[64:128, 7, :], ps[3][0:64, :], [PSB[3]], [("t", 7)])
                        TT("dve", tmp[64:128, 8, :], ps[3][64:128, :], tmp[64:128, 7, :], ALU.mult, [PSB[3], ("t", 7)], [("t", 8)])
                        flush(8 * c + 2 * qt + 1)
                        STT("dve", tmp[:, 8, :], tmo[:, 1, :], 1.0, tmp[:, 8, :], ALU.add, ALU.mult, [("t", 8), "tB"], [("t", 8)])
                        STT("dve", tmp[:, 8, :], tmo[:, 0, :], 0.5, tmp[:, 8, :], ALU.mult, ALU.add, [("t", 8), "tM"], [("t", 8)])
                        ACT(merged[:, c, tsl(qt)], tmp[:, 8, :], AF.Copy, [("t", 8)], [("mg", c, qt)], scale=0.5)
                nproj[0] = 3
                S.barrier_all()

                def stat_evac(pm, pq_, tm, tr):
                    TSC("dve", tmp[:, tm, :], ps[pm][:], 1.0 / D, None, ALU.mult, None, [PSB[pm]], [("t", tm)])
                    TT("pool", tmp[:, tr, :], tmp[:, tm, :], tmp[:, tm, :], ALU.mult, [("t", tm)], [("t", tr)])
                    STT("dve", tmp[:, tr, :], ps[pq_][:], 1.0 / D, tmp[:, tr, :], ALU.mult, ALU.subtract,
                        [PSB[pq_], ("t", tr)], [("t", tr)])
                    ACT(tmp[:, tr, :], tmp[:, tr, :], AF.Sqrt, [("t", tr), "cst"], [("t", tr)], bias=epsc, scale=1.0)
                    RCP(tmp[:, tr, :], tmp[:, tr, :], [("t", tr)], [("t", tr)])

                def ln_norm_gen(tt, tm, tr, og, ob, zts):
                    G = len(zts)
                    for m0 in range(0, 8, G):
                        ms = list(range(m0, min(8, m0 + G)))
                        for stage in range(6):
                            for gi, m in enumerate(ms):
                                zt, ZB = zts[gi]
                                if stage == 0:
                                    TT("pool", zt, xhi[:, m, tsl(tt)], xlo[:, m, tsl(tt)], ALU.add, [XB(m, tt), XL(m, tt)], ZB)
                                elif stage == 1:
                                    TT("dve", zt, zt, tmp[:, tm, :], ALU.subtract, ZB + [("t", tm)], ZB)
                                elif stage == 2:
                                    TT("pool", zt, zt, tmp[:, tr, :], ALU.mult, ZB + [("t", tr)], ZB)
                                elif stage == 3:
                                    ACT(zt, zt, AF.Identity, ZB + ["pv"], ZB, bias=pcol(ob + m), scale=pcol(og + m))
                                elif stage == 4:
                                    ACT(xhi[:, m, tsl(tt)], zt, AF.Copy, ZB, [XB(m, tt)])
                                else:
                                    TT("dve", xlo[:, m, tsl(tt)], zt, xhi[:, m, tsl(tt)], ALU.subtract, ZB + [XB(m, tt)], [XL(m, tt)])
                                yield

                def gpump(gen, n):
                    if gen is None:
                        return
                    for _ in range(n):
                        try:
                            next(gen)
                        except StopIteration:
                            return

                def chain2(g1, g2):
                    yield from g1
                    yield from g2

                rcnt = [0]
                rtemps = [[0, 1, 8]]

                def resid(pb, m, tt, pm, pq_):
                    k = rcnt[0]
                    rcnt[0] += 1
                    ti_ = rtemps[0][k % len(rtemps[0])]
                    zt = tmp[:, ti_, :]
                    ZB = ("t", ti_)
                    STT("dve", zt, xhi[:, m, tsl(tt)], ALPHA, ps[pb][:], ALU.mult, ALU.add, [XB(m, tt), PSB[pb]], [ZB])
                    STT("dve", zt, xlo[:, m, tsl(tt)], ALPHA, zt, ALU.mult, ALU.add, [XL(m, tt), ZB], [ZB])
                    ACT(xhi[:, m, tsl(tt)], zt, AF.Copy, [ZB], [XB(m, tt)])
                    TT("pool", xlo[:, m, tsl(tt)], zt, xhi[:, m, tsl(tt)], ALU.subtract, [ZB, XB(m, tt)], [XL(m, tt)])
                    sq = PT[:, k % 4, :]
                    ACT(sq, zt, AF.Square, [ZB], [("pt", k % 4)])

                    def stats():
                        MM(ps[pm][:], onesb[:], xhi[:, m, tsl(tt)], m == 0, m == 7, [XB(m, tt), "onesb"], [PSB[pm]])
                        MM(ps[pq_][:], onesb[:], sq, m == 0, m == 7, [("pt", k % 4), "onesb"], [PSB[pq_]])
                    return stats

                wo = wbuf[:].rearrange("p (k q) -> p k q", k=8)
                DMA(wo, wout_s[l], [("wout", l)], ["wbA", "wbB"])
                gen = None
                for tt in range(NT):
                    pm, pq_ = (4, 5) if tt % 2 == 0 else (6, 7)
                    pending = None
                    for m in range(8):
                        pb = pbank()
                        mm_group(pb, [wo[:, cc, m * 128:(m + 1) * 128] for cc in range(8)], [merged[:, cc, tsl(tt)] for cc in range(8)],
                                 [("mg", cc, tt) for cc in range(8)] + ["wbA"])
                        if pending is not None:
                            pending()
                        pending = resid(pb, m, tt, pm, pq_)
                        gpump(gen, 6)
                    pending()
                    gpump(gen, 1000)
                    stat_evac(pm, pq_, 4, 5)
                    gen = ln_norm_gen(tt, 4, 5, O_L1G, O_L1B, [(tmp[:, i_, :], [("t", i_)]) for i_ in (2, 3, 6, 7)])
                gpump(gen, 1000)
                S.barrier_all()

                def load_up(j, l=l):
                    bi = j % 4
                    DMA(wbuf[:, bi * 2048:(bi + 1) * 2048].rearrange("p (k q) -> p k q", k=8), wupP_s[l][:, j, :, :],
                        [("wupP", l)], [("wup", bi)])

                def load_dn(m, l=l):
                    DMA(wdn_sb[m % 2], wdnP_s[l][:, m, :, :], [("wdnP", l)], [("wdn", m % 2)])

                gen = None
                for half in range(2):
                    for j in range(3):
                        load_up(j)
                    for j in range(24):
                        if j + 3 < 24:
                            load_up(j + 3)
                        wu = wbuf[:, (j % 4) * 2048:(j % 4 + 1) * 2048].rearrange("p (k g q) -> p k g q", k=8, g=2)
                        for t2 in range(2):
                            tt = half * 2 + t2
                            xh = [xhi[:, k, tsl(tt)] for k in range(8)]
                            xrd = [XB(k, tt) for k in range(8)]
                            for gv in range(2):
                                jj = gv * 24 + j
                                pb = pbank()
                                mm_group(pb, [wu[:, k, gv, :] for k in range(8)], xh, xrd + [("wup", j % 4)])
                                ti = 2 * t2 + gv
                                ht = tmp[:, ti, :]
                                HB = ("t", ti)
                                fw = lambda tap, jj=jj: pcol(O_FCW + tap * 48 + jj)
                                ACT(ht, ps[pb][:], AF.Identity, [PSB[pb], "pv"], [HB], bias=pcol(O_FCB + jj), scale=fw(2))
                                STT("dve", ht[:, 1:TS], ps[pb][:, 0:TS - 1], fw(1), ht[:, 1:TS], ALU.mult, ALU.add, [PSB[pb], HB, "pv"], [HB])
                                STT("dve", ht[:, 2:TS], ps[pb][:, 0:TS - 2], fw(0), ht[:, 2:TS], ALU.mult, ALU.add, [PSB[pb], HB, "pv"], [HB])
                                if tt > 0:
                                    STT("dve", ht[:, 0:1], hal[:, jj, 1:2], fw(1), ht[:, 0:1], ALU.mult, ALU.add, [("hal", jj), HB, "pv"], [HB])
                                    STT("dve", ht[:, 0:2], hal[:, jj, 0:2], fw(0), ht[:, 0:2], ALU.mult, ALU.add, [("hal", jj), HB, "pv"], [HB])
                                if tt < NT - 1:
                                    CP("dve", hal[:, jj, :], ps[pb][:, TS - 2:TS], [PSB[pb]], [("hal", jj)])
                            tg, tv = 2 * t2, 2 * t2 + 1
                            ACT(tmp[:, tg, :], tmp[:, tg, :], AF.Gelu_apprx_tanh, [("t", tg)], [("t", tg)])
                            TT("pool", abuf[:, j, t2 * TS:(t2 + 1) * TS], tmp[:, tg, :], tmp[:, tv, :], ALU.mult,
                               [("t", tg), ("t", tv)], [("a", j, t2)])
                            gpump(gen, 2)
                    gpump(gen, 1000)
                    load_dn(0)
                    pending = None
                    rtemps[0] = [0, 1, 2, 3]
                    for m in range(8):
                        if m + 1 < 8:
                            load_dn(m + 1)
                        wd = wdn_sb[m % 2]
                        for t2 in range(2):
                            tt = half * 2 + t2
                            pb = pbank()
                            mm_group(pb, [wd[:, j, :] for j in range(24)], [abuf[:, j, t2 * TS:(t2 + 1) * TS] for j in range(24)],
                                     [("a", j, t2) for j in range(24)] + [("wdn", m % 2)])
                            if pending is not None:
                                pending()
                            pending = resid(pb, m, tt, 4 + 2 * t2, 5 + 2 * t2)
                    pending()
                    stat_evac(4, 5, 4, 5)
                    stat_evac(6, 7, 6, 7)
                    ptv = PT[:, 0:2, :].rearrange("p a b -> p (a b)").bitcast(F32)
                    zts2 = [(tmp[:, 8, :], [("t", 8)]), (ptv, [("pt", 0), ("pt", 1)])]
                    gen = chain2(ln_norm_gen(half * 2, 4, 5, O_L2G, O_L2B, zts2), ln_norm_gen(half * 2 + 1, 6, 7, O_L2G, O_L2B, zts2))
                gpump(gen, 1000)
                S.barrier_all()

            S.barrier_all()
            cnt = 0
            for tt in range(NT):
                for g in range(2):
                    for kk in range(4):
                        m = g * 4 + kk
                        TT("pool", tmp[:, kk, :], xhi[:, m, tsl(tt)], xlo[:, m, tsl(tt)], ALU.add, [XB(m, tt), XL(m, tt)], [("t", kk)])
                    for t4 in range(4):
                        tb = tt * 4 + t4
                        pb = pbank()
                        for kk in range(4):
                            TR(ps[pb][:, kk * 128:(kk + 1) * 128], tmp[:, kk, t4 * 128:(t4 + 1) * 128], [("t", kk), "cst"], [PSB[pb]])
                        oi = 4 + cnt % 4
                        cnt += 1
                        if cnt % 2 == 0:
                            ACT(tmp[:, oi, :], ps[pb][:], AF.Copy, [PSB[pb]], [("t", oi)])
                        else:
                            CP("dve", tmp[:, oi, :], ps[pb][:], [PSB[pb]], [("t", oi)])
                        DMA(y_d[s, tb * 128:(tb + 1) * 128, g * 512:(g + 1) * 512], tmp[:, oi, :], [("t", oi)], [("y", s, tb, g)])
            S.barrier_all()
        S.barrier_all()
        S.run()
    return nc


def _fm(v):
    v = np.asarray(v, np.float32)
    return np.ascontiguousarray(v.reshape(-1, 128).T)


def _host_params(inp, nl):
    pvs = np.zeros((128, nl * PL), np.float32)
    for l in range(nl):
        b = l * PL
        for tap in range(4):
            pvs[:, b + O_CW + tap * 8: b + O_CW + tap * 8 + 8] = _fm(inp["conv_w"][l, tap])
        pvs[:, b + O_CB: b + O_CB + 8] = _fm(inp["conv_b"][l])
        pvs[:, b + O_GXB: b + O_GXB + 8] = _fm(inp["gx_b"][l])
        pvs[:, b + O_GAB: b + O_GAB + 8] = _fm(inp["ga_b"][l])
        pvs[:, b + O_LAM: b + O_LAM + 8] = _fm(inp["lru_lambda"][l])
        pvs[:, b + O_L1G: b + O_L1G + 8] = _fm(inp["ln1_g"][l])
        pvs[:, b + O_L1B: b + O_L1B + 8] = _fm(inp["ln1_b"][l])
        pvs[:, b + O_L2G: b + O_L2G + 8] = _fm(inp["ln2_g"][l])
        pvs[:, b + O_L2B: b + O_L2B + 8] = _fm(inp["ln2_b"][l])
        for tap in range(3):
            pvs[:, b + O_FCW + tap * 48: b + O_FCW + tap * 48 + 48] = _fm(inp["ffn_conv_w"][l, tap])
        pvs[:, b + O_FCB: b + O_FCB + 48] = _fm(inp["ffn_conv_b"][l])
        pvs[:, b + O_QG: b + O_QG + 3] = _fm(inp["q_norm_g"][l])
        pvs[:, b + O_KVG: b + O_KVG + 2] = _fm(inp["kv_norm_g"][l])
    return pvs


def _host_bd(w, nl):
    w = np.asarray(w, np.float32)
    out = np.zeros((nl, 128, 8, 128), np.float32)
    for c in range(8):
        for b in range(2):
            out[:, b * 64:(b + 1) * 64, c, b * 64:(b + 1) * 64] = w[:, 2 * c + b]
    return out.reshape(nl, 128, 1024)


def _consts():
    c = np.zeros((128, 258), np.float32)
    c[:, 0:128] = np.eye(128, dtype=np.float32)
    k = np.arange(128)[:, None]
    q = np.arange(128)[None, :]
    c[:, 128:256] = (q >= k).astype(np.float32)
    inv = (10000.0 ** (-np.arange(0, 32, 2, dtype=np.float32) / np.float32(32))).astype(np.float32)
    for p in range(64, 96):
        c[p, 256] = inv[(p - 64) % 16]
    c[:, 257] = EPS
    return c


_NC_CACHE = {}
_TRACE = False
_LAST = [None]


def run(inp, NS, layers, ncores, seq_of_core):
    nl = L_ALL
    key = (NS, tuple(layers))
    if key not in _NC_CACHE:
        _NC_CACHE[key] = build(NS, list(layers))
    nc = _NC_CACHE[key]
    shared = {
        "w_in": np.ascontiguousarray(inp["w_in"], np.float32),
        "w_uq": np.ascontiguousarray(inp["w_uq"], np.float32),
        "w_ukv": np.ascontiguousarray(inp["w_ukv"], np.float32),
        "w_out": np.ascontiguousarray(inp["w_out"], np.float32),
        "w_up": np.ascontiguousarray(inp["w_up"], np.float32),
        "w_down": np.ascontiguousarray(inp["w_down"], np.float32),
        "gxw": _host_bd(inp["gx_w"], nl),
        "gaw": _host_bd(inp["ga_w"], nl),
        "pv": _host_params(inp, nl),
        "cst": _consts(),
    }
    x = np.asarray(inp["x"], np.float32)
    pos = np.asarray(inp["positions"], np.int32)
    in_maps = []
    for ci in range(ncores):
        seqs = seq_of_core[ci]
        m = dict(shared)
        m["x"] = np.ascontiguousarray(x[seqs])
        m["pos"] = np.ascontiguousarray(np.broadcast_to(pos[seqs][:, None, :], (len(seqs), 32, T)))
        in_maps.append(m)
    res = run_bass_kernel_spmd(nc, in_maps, core_ids=list(range(ncores)), trace=_TRACE)
    _LAST[0] = res
    return [r["y"] for r in res.results]


def kernel(**inputs):
    B = inputs["x"].shape[0]
    ncores = 8
    NS = B // ncores
    seq_of_core = [list(range(ci * NS, (ci + 1) * NS)) for ci in range(ncores)]
    outs = run(inputs, NS, list(range(L_ALL)), ncores, seq_of_core)
    return np.concatenate(outs, axis=0).astype(np.float32)
```

```python
from contextlib import ExitStack
import math
import numpy as np
import concourse.bass as bass
import concourse.mybir as mybir
from concourse.bass_utils import run_bass_kernel_spmd

F32 = mybir.dt.float32
BF16 = mybir.dt.bfloat16
I32 = mybir.dt.int32
AF = mybir.ActivationFunctionType
ALU = mybir.AluOpType

ENGS = ["pe", "act", "dve", "pool", "sp"]

D = 1024
T = 2048
TS = 512
NT = T // TS
L_ALL = 4
INW = 4768
DFF = 3072
ALPHA = float((2 * L_ALL) ** 0.25)
EPS = 1e-6
SCALE = float(96 ** -0.5)
PL = 296
O_CW, O_CB, O_GXB, O_GAB, O_LAM = 0, 32, 40, 48, 56
O_L1G, O_L1B, O_L2G, O_L2B = 64, 72, 80, 88
O_FCW, O_FCB, O_QG, O_KVG = 96, 240, 288, 291
PAIR_EXP = True
XC_RUNAHEAD = True
PUMP_N = 2
LN_POOL = True
PUMP_Q = 2
MASK_PE = True
SQRT_HOLD = False
QPROJ_EARLY = False
PUMP_S = 5
TRANSITIVE = True
TWO_PI = float(2 * math.pi)
PI = float(math.pi)


class Buf:
    __slots__ = ("name", "lw", "rd")

    def __init__(self, name=""):
        self.name = name
        self.lw = None
        self.rd = {}


class Sched:
    K = 8
    R = 32

    def __init__(self, nc, stack):
        self.nc = nc
        self.sems = {e: [stack.enter_context(nc.semaphore(f"s_{e}_{i}")) for i in range(self.K)]
                     for e in ENGS}
        self.dsems = [stack.enter_context(nc.semaphore(f"d_{i}")) for i in range(self.R)]
        self.cnt = {e: 0 for e in ENGS}
        self.seen = {e: {f: 0 for f in ENGS} for e in ENGS}
        self.seen_d = {e: [0] * self.R for e in ENGS}
        self.clock = {e: [] for e in ENGS}
        self.nd = 0
        self.thunks = {e: [] for e in ENGS}
        self.bufs = {}

    def b(self, key):
        x = self.bufs.get(key)
        if x is None:
            x = self.bufs[key] = Buf(str(key))
        return x

    def _need(self, e, tok, out):
        if tok is None:
            return
        if tok[0] == "e":
            _, f, i = tok
            if f == e and e == "pe":
                return
            if self.seen[e][f] >= i + 1:
                return
            out.append(tok)
        else:
            n = tok[1]
            if self.seen_d[e][n % self.R] >= n // self.R + 1:
                return
            out.append(tok)

    def _emit_waits(self, e, toks):
        best = {}
        dm = {}
        for t in toks:
            if t[0] == "e":
                best[t[1]] = max(best.get(t[1], -1), t[2])
            else:
                s = t[1] % self.R
                dm[s] = max(dm.get(s, -1), t[1])
        for f, i in best.items():
            if self.seen[e][f] >= i + 1:
                continue
            self.thunks[e].append(("w", self.sems[f][i % self.K], i // self.K + 1))
            self.seen[e][f] = i + 1
            clk = self.clock[f][i] if TRANSITIVE else {}
            for g in (ENGS if TRANSITIVE else []):
                if g != e and clk[g] > self.seen[e][g]:
                    self.seen[e][g] = clk[g]
        for s, n in dm.items():
            v = n // self.R + 1
            if self.seen_d[e][s] >= v:
                continue
            self.thunks[e].append(("w", self.dsems[s], 16 * v))
            self.seen_d[e][s] = v

    def _deps(self, e, reads, writes):
        toks = []
        for r in reads:
            self._need(e, r.lw, toks)
        for w in writes:
            self._need(e, w.lw, toks)
            for f, i in w.rd.items():
                if f == "d":
                    for n in i:
                        self._need(e, ("d", n), toks)
                else:
                    self._need(e, ("e", f, i), toks)
        return toks

    def _bl(self, xs):
        return [x if isinstance(x, Buf) else self.b(x) for x in xs]

    def op(self, e, name, reads=(), writes=(), **kw):
        fn = (name, kw)
        reads = self._bl(reads)
        writes = self._bl(writes)
        self._emit_waits(e, self._deps(e, reads, writes))
        i = self.cnt[e]
        self.cnt[e] = i + 1
        self.thunks[e].append(("i", fn, self.sems[e][i % self.K], 1))
        self.clock[e].append(dict(self.seen[e]))
        tok = ("e", e, i)
        for r in reads:
            if r.rd.get(e, -1) < i:
                r.rd[e] = i
        for w in writes:
            w.lw = tok
            w.rd = {}
        return tok

    def dma(self, out, in_, reads=(), writes=(), q="sp"):
        fn = ("dma_start", dict(out=out, in_=in_))
        reads = self._bl(reads)
        writes = self._bl(writes)
        n = self.nd
        self.nd += 1
        s = n % self.R
        toks = self._deps(q, reads, writes)
        if n >= self.R:
            self._need(q, ("d", n - self.R), toks)
        self._emit_waits(q, toks)
        self.thunks[q].append(("i", fn, self.dsems[s], 16))
        tok = ("d", n)
        for r in reads:
            r.rd.setdefault("d", []).append(n)
        for w in writes:
            w.lw = tok
            w.rd = {}
        return tok

    def barrier_all(self):
        for e in ENGS:
            toks = []
            for f in ENGS:
                if self.cnt[f] > 0:
                    self._need(e, ("e", f, self.cnt[f] - 1), toks)
            for n in range(max(0, self.nd - self.R), self.nd):
                self._need(e, ("d", n), toks)
            self._emit_waits(e, toks)

    def run(self):
        nc = self.nc
        with nc.Block() as block:
            def mk(e):
                def body(eng):
                    for t in self.thunks[e]:
                        if t[0] == "w":
                            eng.wait_ge(t[1], t[2])
                        else:
                            getattr(eng, t[1][0])(**t[1][1]).then_inc(t[2], t[3])
                return body
            block.tensor(mk("pe"))
            block.scalar(mk("act"))
            block.vector(mk("dve"))
            block.gpsimd(mk("pool"))
            block.sync(mk("sp"))


def build(NS, layers, nlw=L_ALL):
    nc = bass.Bass("TRN2", target_bir_lowering=False)
    dt_in = lambda n, s, d=F32: nc.dram_tensor(n, s, d, kind="ExternalInput").ap()
    x_d = dt_in("x", [NS, T, D])
    pos_d = dt_in("pos", [NS, 32, T], I32)
    w_in_d = dt_in("w_in", [nlw, D, INW])
    w_uq_d = dt_in("w_uq", [nlw, 384, 1536])
    w_ukv_d = dt_in("w_ukv", [nlw, 256, 2048])
    w_out_d = dt_in("w_out", [nlw, D, D])
    w_up_d = dt_in("w_up", [nlw, D, 2 * DFF])
    w_dn_d = dt_in("w_down", [nlw, DFF, D])
    gxw_d = dt_in("gxw", [nlw, 128, 8 * 128])
    gaw_d = dt_in("gaw", [nlw, 128, 8 * 128])
    pv_d = dt_in("pv", [128, nlw * PL])
    cst_d = dt_in("cst", [128, 258])
    y_d = nc.dram_tensor("y", [NS, T, D], F32, kind="ExternalOutput").ap()

    scr = lambda n, s: nc.dram_tensor(n, s, BF16, kind="Internal").ap()
    winL_s = scr("winL", [nlw, 128, 8, 672])
    winB_s = scr("winB", [nlw, 128, 8, 8, 512])
    wkrot_s = scr("wkrot", [nlw, 128, 8 * 96])
    wuqP_s = scr("wuqP", [nlw, 128, 8, 3, 192])
    wqrP_s = scr("wqrP", [nlw, 128, 8 * 3 * 192])
    wukvP_s = scr("wukvP", [nlw, 128, 8, 2, 256])
    wout_s = scr("wout", [nlw, 128, 8, 1024])
    wupP_s = scr("wupP", [nlw, 128, 24, 8, 256])
    wdnP_s = scr("wdnP", [nlw, 128, 8, 24, 128])
    gxw_s = scr("gxws", [nlw, 128, 1024])
    gaw_s = scr("gaws", [nlw, 128, 1024])

    with ExitStack() as st:
        S = Sched(nc, st)
        sb = lambda n, s, d: st.enter_context(nc.sbuf_tensor("sb_" + n, s, d))
        xhi = sb("xhi", [128, 8, T], BF16)
        xlo = sb("xlo", [128, 8, T], BF16)
        big = sb("big", [128, 26624], BF16)
        tmp = sb("tmp", [128, 9, TS], F32)
        hr = sb("hr", [128, 4], F32)
        tmo = sb("tmo", [128, 2, TS], F32)
        cosT = sb("cosT", [128, T], F32)
        sinT = sb("sinT", [128, T], F32)
        KT = sb("KT", [128, 2, T], BF16)
        VA = sb("VA", [128, 16, 2, 128], BF16)
        QT = sb("QT", [128, 2, TS], BF16)
        PT = sb("PT", [128, 5, TS], BF16)
        PT4 = PT[:, 0:4, :].rearrange("p (s h) t -> p s h t", s=2)
        wbuf = sb("wbuf", [128, 8192], BF16)
        wsm = sb("wsm", [128, 2, 1920], BF16)
        pv = sb("pv", [128, nlw * PL], F32)
        pvd = sb("pvd", [128, nlw * 32], F32)
        cst = sb("cst", [128, 258], F32)
        onesb = sb("onesb", [128, 128], BF16)
        identb = sb("identb", [128, 128], BF16)
        mnegb = sb("mnegb", [128, 128], BF16)
        trib = mnegb
        hal = sb("hal", [128, 48, 2], F32)
        hlast = sb("hlast", [128, 2], F32)
        ps = [st.enter_context(nc.psum_tensor(f"ps{i}", [128, TS], F32)) for i in range(4)]
        ppair = [st.enter_context(nc.psum_tensor(f"pp{i}", [128, 2, TS], F32)) for i in range(2)]
        ps += [ppair[0][:, 0, :], ppair[0][:, 1, :], ppair[1][:, 0, :], ppair[1][:, 1, :]]
        PSB = [S.b(("ps", i)) for i in range(8)]
        ident = cst[:, 0:128]
        invf = cst[:, 256:257]
        epsc = cst[:, 257:258]
        wkr = wsm[:, 1, 0:768].rearrange("p (k d) -> p k d", k=8)

        merged = big[:, 0:16384].rearrange("p (c t) -> p c t", c=8)
        qn = big[:, 16384:22528].rearrange("p (c t) -> p c t", c=3)
        kvn = big[:, 22528:26624].rearrange("p (c t) -> p c t", c=2)
        abuf = big[:, 0:24576].rearrange("p (j t) -> p j t", j=24)
        wdn_sb = [KT[:].rearrange("p a t -> p (a t)")[:, 0:3072].rearrange("p (j q) -> p j q", j=24),
                  VA[:].rearrange("p a b c -> p (a b c)")[:, 0:3072].rearrange("p (j q) -> p j q", j=24)]

        def ACT(out, in_, func, rd, wr, **kw):
            S.op("act", "activation", rd, wr, out=out, in_=in_, func=func, **kw)

        def TT(e, out, in0, in1, op, rd, wr):
            S.op(e, "tensor_tensor", rd, wr, out=out, in0=in0, in1=in1, op=op)

        def TSC(e, out, in0, s1, s2, op0, op1, rd, wr):
            if s2 is None:
                S.op(e, "tensor_scalar", rd, wr, out=out, in0=in0, scalar1=s1, scalar2=None, op0=op0)
            else:
                S.op(e, "tensor_scalar", rd, wr, out=out, in0=in0, scalar1=s1, scalar2=s2, op0=op0, op1=op1)

        def STT(e, out, in0, scalar, in1, op0, op1, rd, wr):
            S.op(e, "scalar_tensor_tensor", rd, wr, out=out, in0=in0, scalar=scalar, in1=in1, op0=op0, op1=op1)

        def CP(e, out, in_, rd, wr):
            S.op(e, "tensor_copy", rd, wr, out=out, in_=in_)

        def RCP(out, in_, rd, wr):
            S.op("dve", "reciprocal", rd, wr, out=out, in_=in_)

        def MM(out, lhsT, rhs, start, stop, rd, wr):
            S.op("pe", "matmul", rd, wr, out=out, lhsT=lhsT, rhs=rhs, start=start, stop=stop)

        def TR(out, in_, rd, wr):
            S.op("pe", "transpose", rd, wr, out=out, in_=in_, identity=ident)

        def MS(e, ap, v, wr):
            S.op(e, "memset", (), wr, ap=ap, constant=v)

        DMA = S.dma
        proj_rr = [0]

        nproj = [3]

        def pbank():
            i = proj_rr[0] % nproj[0]
            proj_rr[0] += 1
            return i

        rb_rr = [0]
        ab_rr = [0]

        def rbank():
            i = rb_rr[0] % 2
            rb_rr[0] += 1
            return i

        def abank():
            i = 4 + ab_rr[0] % 4
            ab_rr[0] += 1
            return i

        def mm_group(pb, lhs_list, rhs_list, reads, out_ap=None):
            n = len(lhs_list)
            o = ps[pb][:] if out_ap is None else out_ap
            for i in range(n):
                MM(o, lhs_list[i], rhs_list[i], i == 0, i == n - 1, reads, [PSB[pb]])

        def tsl(tt):
            return slice(tt * TS, (tt + 1) * TS)

        XB = lambda k, tt: ("xhi", k, tt)
        XL = lambda k, tt: ("xlo", k, tt)

        DMA(cst[:], cst_d, (), ["cst"])
        DMA(pv[:], pv_d, (), ["pv"])
        MS("pool", onesb[:], 1.0, ["onesb"])
        CP("dve", identb[:], cst[:, 0:128], ["cst"], ["identb"])
        TSC("dve", mnegb[:], cst[:, 128:256], -1.0, 30000.0, ALU.add, ALU.mult, ["cst"], ["mnegb"])
        MS("pool", hal[:], 0.0, ["hal"])
        for l in layers:
            lam = pv[:, l * PL + O_LAM: l * PL + O_LAM + 8]
            t0 = tmp[:, 0, 0:8]
            ACT(t0, lam, AF.Exp, ["pv"], ["t0"], scale=-1.0)
            ACT(t0, t0, AF.Ln, ["t0"], ["t0"], bias=1.0)
            TSC("dve", pvd[:, l * 32: l * 32 + 8], t0, -4.0, None, ALU.mult, None, ["t0"], ["pvd"])
            TSC("dve", pvd[:, l * 32 + 8: l * 32 + 16], t0, -8.0, None, ALU.mult, None, ["t0"], ["pvd"])
            TSC("dve", pvd[:, l * 32 + 16: l * 32 + 24], pv[:, l * PL + O_GXB: l * PL + O_GXB + 8], 0.5, None, ALU.mult, None, ["pv"], ["pvd"])
            TSC("dve", pvd[:, l * 32 + 24: l * 32 + 32], pv[:, l * PL + O_GAB: l * PL + O_GAB + 8], 0.5, None, ALU.mult, None, ["pv"], ["pvd"])
        S.barrier_all()

        stf = big[:, 0:24576].bitcast(F32).rearrange("p (a b) -> p a b", a=2)
        stb = xhi[:].rearrange("p k t -> p (k t)")[:, 0:12288].rearrange("p (a b) -> p a b", a=2)
        tb16 = tmp[:].rearrange("p a b -> p (a b)").bitcast(BF16)
        wqr_sb = tb16[:, 0:4608].rearrange("p (c k h d) -> p c k h d", c=8, k=3, h=2)
        ccnt = [0]

        def cast(i, out_ap, in_ap, scale=None):
            if scale is None:
                ccnt[0] += 1
                if ccnt[0] % 2 == 0:
                    ACT(out_ap, in_ap, AF.Copy, [("stf", i)], [("stb", i)])
                else:
                    CP("dve", out_ap, in_ap, [("stf", i)], [("stb", i)])
            else:
                TSC("dve", out_ap, in_ap, scale, None, ALU.mult, None, [("stf", i), "pv"], [("stb", i)])

        MS("pool", wkr, 0.0, ["wkr"])
        MS("pool", tb16[:, 0:4608], 0.0, ["wqr_sb"])
        jobs = []
        for l in layers:
            base = l * PL
            for k in range(8):
                def job(i, l=l, k=k):
                    sf = stf[:, i, :]
                    so = stb[:, i, :]
                    soB = so[:, 0:4096].rearrange("p (c j q) -> p c j q", c=8, j=4)
                    for j, b0 in enumerate([0, 1024, 2720, 3744]):
                        cast(i, soB[:, :, j, :], sf[:, b0:b0 + 1024].rearrange("p (c q) -> p c q", c=8))
                    cast(i, so[:, 4096:4768], sf[:, 2048:2720])
                    TSC("pool", wkr[:, k, 64:80], sf[:, 2704:2720], -1.0, None, ALU.mult, None, [("stf", i)], ["wkr"])
                    CP("pool", wkr[:, k, 80:96], sf[:, 2688:2704], [("stf", i)], ["wkr"])
                    DMA(winB_s[l][:, :, k, :], so[:, 0:4096].rearrange("p (c q) -> p c q", c=8), [("stb", i)], [("winB", l)])
                    DMA(winL_s[l][:, k, :], so[:, 4096:4768], [("stb", i)], [("winL", l)])
                    if k == 7:
                        DMA(wkrot_s[l], wsm[:, 1, 0:768], ["wkr"], [("wkrot", l)])
                jobs.append((w_in_d[l, k * 128:(k + 1) * 128, :], lambda i: stf[:, i, 0:INW], job))

            def job_uq(i, l=l, base=base):
                sf3 = stf[:, i, 0:4608].rearrange("p (k q) -> p k q", k=3)
                so3 = stb[:, i, 0:4608].rearrange("p (k q) -> p k q", k=3)
                for k in range(3):
                    g = pv[:, base + O_QG + k: base + O_QG + k + 1]
                    cast(i, so3[:, k, :], sf3[:, k, :], scale=g)
                    sfv = sf3[:, k, :].rearrange("p (c h d) -> p c h d", c=8, h=2)
                    TSC("pool", wqr_sb[:, :, k, :, 64:80], sfv[:, :, :, 80:96], g, -1.0, ALU.mult, ALU.mult,
                        [("stf", i), "pv"], ["wqr_sb"])
                    TSC("pool", wqr_sb[:, :, k, :, 80:96], sfv[:, :, :, 64:80], g, None, ALU.mult, None,
                        [("stf", i), "pv"], ["wqr_sb"])
                    DMA(wuqP_s[l][:, :, k, :], so3[:, k, :].rearrange("p (c q) -> p c q", c=8), [("stb", i)], [("wuqP", l)])
                DMA(wqrP_s[l], tb16[:, 0:4608], ["wqr_sb"], [("wqrP", l)])
            jobs.append((w_uq_d[l].rearrange("(k p) q -> p k q", p=128),
                         lambda i: stf[:, i, 0:4608].rearrange("p (k q) -> p k q", k=3), job_uq))

            def job_ukv(i, l=l, base=base):
                sf2 = stf[:, i, 0:4096].rearrange("p (k q) -> p k q", k=2)
                so2 = stb[:, i, 0:4096].rearrange("p (k q) -> p k q", k=2)
                for k in range(2):
                    g = pv[:, base + O_KVG + k: base + O_KVG + k + 1]
                    cast(i, so2[:, k, :], sf2[:, k, :], scale=g)
                    DMA(wukvP_s[l][:, :, k, :], so2[:, k, :].rearrange("p (c q) -> p c q", c=8), [("stb", i)], [("wukvP", l)])
            jobs.append((w_ukv_d[l].rearrange("(k p) q -> p k q", p=128),
                         lambda i: stf[:, i, 0:4096].rearrange("p (k q) -> p k q", k=2), job_ukv))

            for k0 in range(0, 8, 4):
                def job_out(i, l=l, k0=k0):
                    cast(i, stb[:, i, 0:4096], stf[:, i, 0:4096])
                    DMA(wout_s[l][:, k0:k0 + 4, :], stb[:, i, 0:4096].rearrange("p (k q) -> p k q", k=4), [("stb", i)], [("wout", l)])
                jobs.append((w_out_d[l, k0 * 128:(k0 + 4) * 128, :].rearrange("(k p) q -> p k q", p=128),
                             lambda i: stf[:, i, 0:4096].rearrange("p (k q) -> p k q", k=4), job_out))

            for k in range(8):
                def job_up(i, l=l, k=k):
                    sfv = stf[:, i, :].rearrange("p (g j q) -> p g j q", g=2, j=24)
                    sov = stb[:, i, :].rearrange("p (j g q) -> p j g q", j=24, g=2)
                    cast(i, sov[:, :, 0, :], sfv[:, 0, :, :])
                    cast(i, sov[:, :, 1, :], sfv[:, 1, :, :])
                    DMA(wupP_s[l][:, :, k, :], stb[:, i, :].rearrange("p (j q) -> p j q", j=24), [("stb", i)], [("wupP", l)])
                jobs.append((w_up_d[l, k * 128:(k + 1) * 128, :], lambda i: stf[:, i, 0:6144], job_up))

            for k0 in range(0, 24, 4):
                def job_dn(i, l=l, k0=k0):
                    cast(i, stb[:, i, 0:4096], stf[:, i, 0:4096])
                    so4 = stb[:, i, 0:4096].rearrange("p (k m q) -> p k m q", k=4, m=8)
                    for kk in range(4):
                        DMA(wdnP_s[l][:, :, k0 + kk, :], so4[:, kk, :, :], [("stb", i)], [("wdnP", l)])
                jobs.append((w_dn_d[l, k0 * 128:(k0 + 4) * 128, :].rearrange("(k p) q -> p k q", p=128),
                             lambda i: stf[:, i, 0:4096].rearrange("p (k q) -> p k q", k=4), job_dn))

            for src, dst, nm in [(gxw_d, gxw_s, "gxws"), (gaw_d, gaw_s, "gaws")]:
                def job_g(i, l=l, dst=dst, nm=nm):
                    cast(i, stb[:, i, 0:1024], stf[:, i, 0:1024])
                    DMA(dst[l], stb[:, i, 0:1024], [("stb", i)], [(nm, l)])
                jobs.append((src[l], lambda i: stf[:, i, 0:1024], job_g))

        nj = len(jobs)

        def jload(n):
            i = n % 2
            DMA(jobs[n][1](i), jobs[n][0], (), [("stf", i)])

        if nj:
            jload(0)
        for n in range(nj):
            if n + 1 < nj:
                jload(n + 1)
            jobs[n][2](n % 2)
        S.barrier_all()

        P = slice(64, 96)
        for s in range(NS):
            for tb in range(T // 128):
                tt = tb // 4
                xt = tmp[:, 2 * (tb % 2):2 * (tb % 2) + 2, :].rearrange("p a b -> p (a b)")
                xtb = ("xt", tb % 2)
                DMA(xt, x_d[s, tb * 128:(tb + 1) * 128, :], (), [xtb])
                for g in range(2):
                    pb = pbank()
                    for kk in range(4):
                        k = g * 4 + kk
                        TR(ps[pb][:, kk * 128:(kk + 1) * 128], xt[:, k * 128:(k + 1) * 128], [xtb, "cst"], [PSB[pb]])
                    hi = xhi[:, g * 4:(g + 1) * 4, tb * 128:(tb + 1) * 128]
                    lo = xlo[:, g * 4:(g + 1) * 4, tb * 128:(tb + 1) * 128]
                    pv3 = ps[pb][:].rearrange("p (a b) -> p a b", a=4)
                    hb = [XB(k, tt) for k in range(g * 4, g * 4 + 4)]
                    lb = [XL(k, tt) for k in range(g * 4, g * 4 + 4)]
                    ACT(hi, pv3, AF.Copy, [PSB[pb]], hb)
                    TT("dve", lo, pv3, hi, ALU.subtract, [PSB[pb]] + hb, lb)
            posi = tmp[:, 0:4, :].rearrange("p a b -> p (a b)").bitcast(I32)
            ang = tmp[:, 4:8, :].rearrange("p a b -> p (a b)")
            kf = tmp[:, 0:4, :].rearrange("p a b -> p (a b)")
            S.barrier_all()
            DMA(posi[P, :], pos_d[s], (), ["posi"])
            CP("dve", ang[P, :], posi[P, :], ["posi"], ["ang"])
            TSC("dve", ang[P, :], ang[P, :], invf[P, :], None, ALU.mult, None, ["ang", "cst"], ["ang"])
            TSC("dve", posi[P, :], ang[P, :], 1.0 / TWO_PI, None, ALU.mult, None, ["ang"], ["posi"])
            CP("dve", kf[P, :], posi[P, :], ["posi"], ["posi"])
            STT("dve", ang[P, :], kf[P, :], -TWO_PI, ang[P, :], ALU.mult, ALU.add, ["posi", "ang"], ["ang"])

            def wrap():
                TSC("dve", kf[P, :], ang[P, :], PI, TWO_PI, ALU.is_gt, ALU.mult, ["ang"], ["posi"])
                TT("dve", ang[P, :], ang[P, :], kf[P, :], ALU.subtract, ["ang", "posi"], ["ang"])
                TSC("dve", kf[P, :], ang[P, :], -PI, TWO_PI, ALU.is_lt, ALU.mult, ["ang"], ["posi"])
                TT("dve", ang[P, :], ang[P, :], kf[P, :], ALU.add, ["ang", "posi"], ["ang"])
            wrap()
            ACT(sinT[P, :], ang[P, :], AF.Sin, ["ang"], ["sinT"])
            TSC("dve", ang[P, :], ang[P, :], PI / 2, None, ALU.add, None, ["ang"], ["ang"])
            wrap()
            ACT(cosT[P, :], ang[P, :], AF.Sin, ["ang"], ["cosT"])
            S.barrier_all()

            for li, l in enumerate(layers):
                base = l * PL
                pcol = lambda o, base=base: pv[:, base + o: base + o + 1]
                wlat = wbuf[:, 0:8 * 672].rearrange("p (k q) -> p k q", k=8)
                DMA(wlat, winL_s[l], [("winL", l)], ["wbA", "wbB"])
                DMA(wsm[:, 1, 0:768], wkrot_s[l], [("wkrot", l)], ["wkr"])
                for tt in range(NT):
                    xh = [xhi[:, k, tsl(tt)] for k in range(8)]
                    xrd = [XB(k, tt) for k in range(8)]
                    for (nch, c0, dst, inv_n, nm) in [(3, 0, qn, 1.0 / 384, "qn"), (2, 384, kvn, 1.0 / 256, "kvn")]:
                        for oc in range(nch):
                            pb = pbank()
                            mm_group(pb, [wlat[:, k, c0 + oc * 128: c0 + (oc + 1) * 128] for k in range(8)], xh, xrd + ["wbA"])
                            ACT(tmp[:, oc, :], ps[pb][:], AF.Copy, [PSB[pb]], [("t", oc)])
                            ACT(PT[:, oc, :], ps[pb][:], AF.Square, [PSB[pb]], [("pt", oc)])
                        mm_group(7, [onesb[:]] * nch, [PT[:, oc, :] for oc in range(nch)],
                                 ["onesb"] + [("pt", oc) for oc in range(nch)])
                        ACT(tmp[:, 4, :], ps[7][:], AF.Sqrt, [PSB[7], "cst"], [("t", 4)], bias=epsc, scale=inv_n)
                        RCP(tmp[:, 5, :], tmp[:, 4, :], [("t", 4)], [("t", 5)])
                        for oc in range(nch):
                            TT("pool", dst[:, oc, tsl(tt)], tmp[:, oc, :], tmp[:, 5, :], ALU.mult,
                               [("t", oc), ("t", 5)], [(nm, oc, tt)])
                    pa = pbank()
                    mm_group(pa, [wlat[:, k, 576:672] for k in range(8)], xh, xrd + ["wbA"], out_ap=ps[pa][0:96, :])
                    pb2 = pbank()
                    mm_group(pb2, [wkr[:, k, :] for k in range(8)], xh, xrd + ["wkr"], out_ap=ps[pb2][0:96, :])
                    TT("dve", tmp[P, 6, :], ps[pa][P, :], cosT[P, tsl(tt)], ALU.mult, [PSB[pa], "cosT"], [("t", 6)])
                    TT("dve", tmp[P, 7, :], ps[pb2][P, :], sinT[P, tsl(tt)], ALU.mult, [PSB[pb2], "sinT"], [("t", 7)])
                    TT("pool", KT[P, 0, tsl(tt)], tmp[P, 6, :], tmp[P, 7, :], ALU.add, [("t", 6), ("t", 7)], [("kpe", tt)])
                    CP("pool", KT[P, 1, tsl(tt)], KT[P, 0, tsl(tt)], [("kpe", tt)], [("kpe1", tt)])
                S.barrier_all()

                def load_pair(c, l=l):
                    hb_ = c % 2
                    wb = wbuf[:, hb_ * 4096:(hb_ + 1) * 4096]
                    DMA(wb.rearrange("p (k q) -> p k q", k=8), winB_s[l][:, c, :, :], [("winB", l)], ["wbA" if hb_ == 0 else "wbB"])
                    sm = wsm[:, hb_, :]
                    DMA(sm[:, 0:576].rearrange("p (k q) -> p k q", k=3), wuqP_s[l][:, c, :, :], [("wuqP", l)], [("wsm", hb_, 0)])
                    DMA(sm[:, 576:1152], wqrP_s[l][:, c * 576:(c + 1) * 576], [("wqrP", l)], [("wsm", hb_, 1)])
                    DMA(sm[:, 1152:1664].rearrange("p (k q) -> p k q", k=2), wukvP_s[l][:, c, :, :], [("wukvP", l)], [("wsm", hb_, 2)])
                    DMA(sm[:, 1664:1792], gxw_s[l][:, c * 128:(c + 1) * 128], [("gxws", l)], [("wsm", hb_, 3)])
                    DMA(sm[:, 1792:1920], gaw_s[l][:, c * 128:(c + 1) * 128], [("gaws", l)], [("wsm", hb_, 4)])

                nproj[0] = 2
                MS("pool", VA[:, :, 0, 64:128], 1.0, ["VA"])
                MS("pool", VA[:, :, 1, 0:64], 1.0, ["VA"])
                def make_rnn(c, l=l):
                    hb_ = c % 2
                    WB = "wbA" if hb_ == 0 else "wbB"
                    wc = wbuf[:, hb_ * 4096:(hb_ + 1) * 4096].rearrange("p (k j q) -> p k j q", k=8, j=4)
                    sm = wsm[:, hb_, :]
                    gxw = sm[:, 1664:1792]
                    gaw = sm[:, 1792:1920]
                    cwc = lambda tap: pcol(O_CW + tap * 8 + c)
                    def rnn_p1(qt):
                        xh = [xhi[:, k, tsl(qt)] for k in range(8)]
                        xrd = [XB(k, qt) for k in range(8)]
                        pc = lambda o: pvd[:, l * 32 + o + c: l * 32 + o + c + 1]
                        px = rbank()
                        mm_group(px, [wc[:, k, 0, :] for k in range(8)], xh, xrd + [WB])
                        yield
                        X = ps[px]
                        cv = tmp[:, 0, :]
                        TSC("dve", cv, X[:], cwc(3), pcol(O_CB + c), ALU.mult, ALU.add, [PSB[px], "pv"], [("t", 0)])
                        yield
                        pg = rbank()
                        mm_group(pg, [wc[:, k, 1, :] for k in range(8)], xh, xrd + [WB])
                        yield
                        for sh, tap in ((1, 2), (2, 1), (3, 0)):
                            STT("dve", cv[:, sh:TS], X[:, 0:TS - sh], cwc(tap), cv[:, sh:TS], ALU.mult, ALU.add,
                                [PSB[px], ("t", 0), "pv"], [("t", 0)])
                            yield
                        if qt > 0:
                            for sh, tap in ((1, 2), (2, 1), (3, 0)):
                                STT("dve", cv[:, 0:sh], hr[:, 3 - sh:3], cwc(tap), cv[:, 0:sh], ALU.mult, ALU.add,
                                    ["hr", ("t", 0), "pv"], [("t", 0)])
                        if qt < NT - 1:
                            CP("dve", hr[:, 0:3], X[:, TS - 3:TS], [PSB[px]], ["hr"])
                        yield
                        ACT(PT[:, 4, :], cv, AF.Copy, [("t", 0)], [("pt", 4)])
                        ACT(tmp[:, 3, :], ps[pg][:], AF.Square, [PSB[pg]], [("t", 3)])
                        ACT(tmp[:, 4, :], ps[pg][:], AF.Copy, [PSB[pg]], [("t", 4)])
                        yield
                        TSC("dve", tmp[:, 3, :], tmp[:, 3, :], 0.044715, 1.0, ALU.mult, ALU.add, [("t", 3)], [("t", 3)])
                        yield
                        TT("dve", tmp[:, 3, :], tmp[:, 3, :], tmp[:, 4, :], ALU.mult, [("t", 3), ("t", 4)], [("t", 3)])
                        yield
                        pgx = rbank()
                        mm_group(pgx, [gxw], [PT[:, 4, :]], [("pt", 4), ("wsm", hb_, 3)])
                        pga = rbank()
                        mm_group(pga, [gaw], [PT[:, 4, :]], [("pt", 4), ("wsm", hb_, 4)])
                        yield
                        ACT(tmp[:, 3, :], tmp[:, 3, :], AF.Tanh, [("t", 3)], [("t", 3)], scale=0.7978845608028654)
                        yield
                        ACT(tmp[:, 1, :], ps[pgx][:], AF.Tanh, [PSB[pgx], "pvd"], [("t", 1)], bias=pc(16), scale=0.5)
                        yield
                        ACT(tmp[:, 2, :], ps[pga][:], AF.Tanh, [PSB[pga], "pvd"], [("t", 2)], bias=pc(24), scale=0.5)
                        STT("dve", tmp[:, 4, :], tmp[:, 3, :], 1.0, tmp[:, 4, :], ALU.add, ALU.mult, [("t", 3), ("t", 4)], [("t", 4)])
                        yield
                        ACT(tmp[:, 3, :], tmp[:, 2, :], AF.Exp, [("t", 2), "pvd"], [("t", 3)], bias=pc(0), scale=pc(0))
                        yield
                        ACT(tmp[:, 2, :], tmp[:, 2, :], AF.Exp, [("t", 2), "pvd"], [("t", 2)], bias=pc(8), scale=pc(8))
                        STT("dve", tmp[:, 1, :], tmp[:, 1, :], 1.0, tmp[:, 0, :], ALU.add, ALU.mult, [("t", 1), ("t", 0)], [("t", 1)])
                        yield
                        if SQRT_HOLD:
                            yield "HOLD"
                        ACT(tmp[:, 2, :], tmp[:, 2, :], AF.Sqrt, [("t", 2)], [("t", 2)], bias=1.0, scale=-1.0)
                        yield
                        STT("dve", tmp[:, 1, :], tmp[:, 1, :], 0.5, tmp[:, 2, :], ALU.mult, ALU.mult, [("t", 1), ("t", 2)], [("t", 1)])
                        yield
                        init = 0.0 if qt == 0 else hlast[:, 0:1]
                        S.op("dve", "tensor_tensor_scan", [("t", 3), ("t", 1), "hlast"], [("t", 2)],
                             out=tmp[:, 2, :], data0=tmp[:, 3, :], data1=tmp[:, 1, :], initial=init, op0=ALU.mult, op1=ALU.add)
                        CP("dve", hlast[:, 0:1], tmp[:, 2, TS - 1:TS], [("t", 2)], ["hlast"])
                        yield
                        TT("pool", tmp[:, 4, :], tmp[:, 4, :], tmp[:, 2, :], ALU.mult, [("t", 4), ("t", 2)], [("t", 4)])
                        yield

                    def rnn_p2(qt):
                        xh = [xhi[:, k, tsl(qt)] for k in range(8)]
                        xrd = [XB(k, qt) for k in range(8)]
                        pa_ = rbank()
                        mm_group(pa_, [wc[:, k, 2, :] for k in range(8)], xh, xrd + [WB])
                        yield
                        ACT(tmp[:, 1, :], ps[pa_][:], AF.Tanh, [PSB[pa_]], [("t", 1)], scale=0.5)
                        yield
                        pbg = rbank()
                        mm_group(pbg, [wc[:, k, 3, :] for k in range(8)], xh, xrd + [WB])
                        yield
                        STT("dve", tmo[:, 0, :], tmp[:, 1, :], 1.0, tmp[:, 4, :], ALU.add, ALU.mult, [("t", 4), ("t", 1)], ["tM"])
                        ACT(tmo[:, 1, :], ps[pbg][:], AF.Tanh, [PSB[pbg]], ["tB"], scale=0.5)
                        yield


                    return [g for q_ in range(NT) for g in (rnn_p1(q_), rnn_p2(q_))]

                gq = []

                held = [False]

                def pump(n, limit):
                    k = 0
                    while k < n and gq_i[0] <= limit and gq_i[0] < len(gq) and not held[0]:
                        try:
                            r = next(gq[gq_i[0]])
                            k += 1
                            if r == "HOLD":
                                held[0] = True
                        except StopIteration:
                            gq_i[0] += 1

                def release():
                    held[0] = False

                def flush(limit):
                    held[0] = False
                    while gq_i[0] <= limit and gq_i[0] < len(gq):
                        try:
                            next(gq[gq_i[0]])
                        except StopIteration:
                            gq_i[0] += 1

                gq_i = [0]
                load_pair(0)
                gq += make_rnn(0)
                for c in range(8):
                    hb_ = c % 2
                    WB = "wbA" if hb_ == 0 else "wbB"
                    if c + 1 < 8:
                        load_pair(c + 1)
                        gq += make_rnn(c + 1)
                    wc = wbuf[:, hb_ * 4096:(hb_ + 1) * 4096].rearrange("p (k j q) -> p k j q", k=8, j=4)
                    sm = wsm[:, hb_, :]
                    wuq = sm[:, 0:576].rearrange("p (k q) -> p k q", k=3)
                    wqr = sm[:, 576:1152].rearrange("p (k h q) -> p k h q", k=3, h=2)
                    wukv = sm[:, 1152:1664].rearrange("p (k q) -> p k q", k=2)
                    gxw = sm[:, 1664:1792]
                    gaw = sm[:, 1792:1920]
                    cwc = lambda tap, c=c: pcol(O_CW + tap * 8 + c)
                    for hh in range(2):
                        for tt in range(NT):
                            pb = abank()
                            mm_group(pb, [wukv[:, k, hh * 128: hh * 128 + 64] for k in range(2)],
                                     [kvn[:, k, tsl(tt)] for k in range(2)], [("kvn", 0, tt), ("kvn", 1, tt), ("wsm", hb_, 2)],
                                     out_ap=ps[pb][0:64, :])
                            ACT(KT[0:64, hh, tsl(tt)], ps[pb][0:64, :], AF.Copy, [PSB[pb]], [("KT", hh, tt)])
                            pump(2, 8 * c + 1)
                    for g4 in range(4):
                        pb = abank()
                        pv4 = ps[pb][:].rearrange("p (t h d) -> p t h d", t=4, h=2)
                        for t4 in range(4):
                            tb = g4 * 4 + t4
                            for k in range(2):
                                MM(pv4[:, t4, :, :], kvn[:, k, tb * 128:(tb + 1) * 128],
                                   wukv[:, k, :].rearrange("p (h d) -> p h d", h=2)[:, :, 64:128], k == 0, k == 1,
                                   [("kvn", k, g4), ("wsm", hb_, 2)], [PSB[pb]])
                        ACT(VA[:, g4 * 4:(g4 + 1) * 4, 0, 0:64], pv4[:, :, 0, :], AF.Copy, [PSB[pb]], [("VA", g4)])
                        ACT(VA[:, g4 * 4:(g4 + 1) * 4, 1, 64:128], pv4[:, :, 1, :], AF.Copy, [PSB[pb]], [("VA", g4)])
                        pump(2, 8 * c + 1)
                    def qproj(qt, LIM, wuq=wuq, wqr=wqr, hb_=hb_):
                        qr = [("qn", k, qt) for k in range(3)]
                        qrh = [qn[:, k, tsl(qt)] for k in range(3)]
                        for hh in range(2):
                            pq = 4 + 2 * hh
                            mm_group(pq, [wuq[:, k, hh * 96:(hh + 1) * 96] for k in range(3)], qrh,
                                     qr + [("wsm", hb_, 0)], out_ap=ps[pq][0:96, :])
                            pr = 5 + 2 * hh
                            mm_group(pr, [wqr[:, k, hh, :] for k in range(3)], qrh,
                                     qr + [("wsm", hb_, 1)], out_ap=ps[pr][0:96, :])
                            CP("dve", QT[0:64, hh, :], ps[pq][0:64, :], [PSB[pq]], [("QT", hh)])
                            TT("dve", tmp[P, 5, :], ps[pq][P, :], cosT[P, tsl(qt)], ALU.mult, [PSB[pq], "cosT"], [("t", 5)])
                            TT("dve", tmp[P, 6, :], ps[pr][P, :], sinT[P, tsl(qt)], ALU.mult, [PSB[pr], "sinT"], [("t", 6)])
                            TT("pool", QT[P, hh, :], tmp[P, 5, :], tmp[P, 6, :], ALU.add, [("t", 5), ("t", 6)], [("QT", hh)])
                            pump(PUMP_Q, LIM)

                    nproj[0] = 2
                    for qt in range(NT):
                        LIM = min(8 * c + 2 * qt + 2, 8 * c + 7) if not XC_RUNAHEAD else 8 * c + 2 * qt + 2
                        pump(6 if qt == 0 else 2, LIM)
                        if qt == 0 or not QPROJ_EARLY:
                            qproj(qt, LIM)
                        nkb = 4 * qt + 4
                        for i in range(nkb + 1):
                            if i < nkb:
                                kb = i
                                n0 = max(kb - 4 * qt, 0) * 128
                                sl = kb % 2
                                diag = (kb - 4 * qt) >= 0 and MASK_PE
                                for hh in range(2):
                                    bk = 4 + 2 * sl + hh
                                    MM(ps[bk][:, n0:TS], KT[0:96, hh, kb * 128:(kb + 1) * 128], QT[0:96, hh, n0:TS], True, not diag,
                                       [("KT", hh, kb // 4), ("kpe" if hh == 0 else "kpe1", kb // 4), ("QT", hh)], [PSB[bk]])
                                    if diag:
                                        MM(ps[bk][:, n0:n0 + 128], identb[:], mnegb[:], False, True, ["identb", "mnegb"], [PSB[bk]])
                            if i >= 1:
                                kb = i - 1
                                j = kb - 4 * qt
                                n0 = max(j, 0) * 128
                                sl = kb % 2
                                ACT(PT4[:, sl, :, n0:TS], ppair[sl][:, :, n0:TS], AF.Exp,
                                    [PSB[4 + 2 * sl], PSB[5 + 2 * sl]], [("pt", sl)], scale=SCALE)
                                if j >= 0 and not MASK_PE:
                                    for hh in range(2):
                                        TT("pool", PT4[:, sl, hh, n0:n0 + 128], PT4[:, sl, hh, n0:n0 + 128], trib[:], ALU.mult,
                                           [("pt", sl), "trib"], [("pt", sl)])
                                for hh in range(2):
                                    MM(ps[2 + hh][:, n0:TS], VA[:, kb, hh, :], PT4[:, sl, hh, n0:TS], kb == 0, kb == nkb - 1,
                                       [("VA", kb // 4), ("pt", sl), "VA"], [PSB[2 + hh]])
                            pump(PUMP_N, LIM)
                        if QPROJ_EARLY and qt + 1 < NT:
                            qproj(qt + 1, LIM)
                        release()
                        pump(2, LIM)
                        RCP(tmp[0:64, 7, :], ps[2][64:128, :], [PSB[2]], [("t", 7)])
                        TT("dve", tmp[0:64, 8, :], ps[2][0:64, :], tmp[0:64, 7, :], ALU.mult, [PSB[2], ("t", 7)], [("t", 8)])
                        RCP(tmp[64:128, 7, :], ps[3][0:64, :], [PSB[3]], [("t", 7)])
                        TT("dve", tmp[64:128, 8, :], ps[3][64:128, :], tmp[64:128, 7, :], ALU.mult, [PSB[3], ("t", 7)], [("t", 8)])
                        flush(8 * c + 2 * qt + 1)
                        STT("dve", tmp[:, 8, :], tmo[:, 1, :], 1.0, tmp[:, 8, :], ALU.add, ALU.mult, [("t", 8), "tB"], [("t", 8)])
                        STT("dve", tmp[:, 8, :], tmo[:, 0, :], 0.5, tmp[:, 8, :], ALU.mult, ALU.add, [("t", 8), "tM"], [("t", 8)])
                        ACT(merged[:, c, tsl(qt)], tmp[:, 8, :], AF.Copy, [("t", 8)], [("mg", c, qt)], scale=0.5)
                nproj[0] = 3
                S.barrier_all()

                def stat_evac(pm, pq_, tm, tr):
                    TSC("dve", tmp[:, tm, :], ps[pm][:], 1.0 / D, None, ALU.mult, None, [PSB[pm]], [("t", tm)])
                    TT("pool", tmp[:, tr, :], tmp[:, tm, :], tmp[:, tm, :], ALU.mult, [("t", tm)], [("t", tr)])
                    STT("dve", tmp[:, tr, :], ps[pq_][:], 1.0 / D, tmp[:, tr, :], ALU.mult, ALU.subtract,
                        [PSB[pq_], ("t", tr)], [("t", tr)])
                    ACT(tmp[:, tr, :], tmp[:, tr, :], AF.Sqrt, [("t", tr), "cst"], [("t", tr)], bias=epsc, scale=1.0)
                    RCP(tmp[:, tr, :], tmp[:, tr, :], [("t", tr)], [("t", tr)])

                def ln_norm_gen(tt, tm, tr, og, ob, zts, e1="dve"):
                    G = len(zts)
                    for m0 in range(0, 8, G):
                        ms = list(range(m0, min(8, m0 + G)))
                        for stage in range(6):
                            for gi, m in enumerate(ms):
                                zt, ZB = zts[gi]
                                if stage == 0:
                                    TT("pool", zt, xhi[:, m, tsl(tt)], xlo[:, m, tsl(tt)], ALU.add, [XB(m, tt), XL(m, tt)], ZB)
                                elif stage == 1:
                                    TT(e1, zt, zt, tmp[:, tm, :], ALU.subtract, ZB + [("t", tm)], ZB)
                                elif stage == 2:
                                    TT("pool", zt, zt, tmp[:, tr, :], ALU.mult, ZB + [("t", tr)], ZB)
                                elif stage == 3:
                                    ACT(zt, zt, AF.Identity, ZB + ["pv"], ZB, bias=pcol(ob + m), scale=pcol(og + m))
                                elif stage == 4:
                                    ACT(xhi[:, m, tsl(tt)], zt, AF.Copy, ZB, [XB(m, tt)])
                                else:
                                    TT(e1, xlo[:, m, tsl(tt)], zt, xhi[:, m, tsl(tt)], ALU.subtract, ZB + [XB(m, tt)], [XL(m, tt)])
                                yield

                def gpump(gen, n):
                    if gen is None:
                        return
                    for _ in range(n):
                        try:
                            next(gen)
                        except StopIteration:
                            return

                def chain2(g1, g2):
                    yield from g1
                    yield from g2

                rcnt = [0]
                rtemps = [[0, 1, 8]]

                def resid(pb, m, tt, pm, pq_):
                    k = rcnt[0]
                    rcnt[0] += 1
                    ti_ = rtemps[0][k % len(rtemps[0])]
                    zt = tmp[:, ti_, :]
                    ZB = ("t", ti_)
                    STT("dve", zt, xhi[:, m, tsl(tt)], ALPHA, ps[pb][:], ALU.mult, ALU.add, [XB(m, tt), PSB[pb]], [ZB])
                    STT("dve", zt, xlo[:, m, tsl(tt)], ALPHA, zt, ALU.mult, ALU.add, [XL(m, tt), ZB], [ZB])
                    ACT(xhi[:, m, tsl(tt)], zt, AF.Copy, [ZB], [XB(m, tt)])
                    TT("pool", xlo[:, m, tsl(tt)], zt, xhi[:, m, tsl(tt)], ALU.subtract, [ZB, XB(m, tt)], [XL(m, tt)])
                    sq = PT[:, k % 4, :]
                    ACT(sq, zt, AF.Square, [ZB], [("pt", k % 4)])

                    def stats():
                        MM(ps[pm][:], onesb[:], xhi[:, m, tsl(tt)], m == 0, m == 7, [XB(m, tt), "onesb"], [PSB[pm]])
                        MM(ps[pq_][:], onesb[:], sq, m == 0, m == 7, [("pt", k % 4), "onesb"], [PSB[pq_]])
                    return stats

                wo = wbuf[:].rearrange("p (k q) -> p k q", k=8)
                DMA(wo, wout_s[l], [("wout", l)], ["wbA", "wbB"])
                gen = None
                for tt in range(NT):
                    pm, pq_ = (4, 5) if tt % 2 == 0 else (6, 7)
                    pending = None
                    for m in range(8):
                        pb = pbank()
                        mm_group(pb, [wo[:, cc, m * 128:(m + 1) * 128] for cc in range(8)], [merged[:, cc, tsl(tt)] for cc in range(8)],
                                 [("mg", cc, tt) for cc in range(8)] + ["wbA"])
                        if pending is not None:
                            pending()
                        pending = resid(pb, m, tt, pm, pq_)
                        gpump(gen, 6)
                    pending()
                    gpump(gen, 1000)
                    stat_evac(pm, pq_, 4, 5)
                    gen = ln_norm_gen(tt, 4, 5, O_L1G, O_L1B, [(tmp[:, i_, :], [("t", i_)]) for i_ in (2, 3, 6, 7)])
                gpump(gen, 1000)
                S.barrier_all()

                def load_up(j, l=l):
                    bi = j % 4
                    DMA(wbuf[:, bi * 2048:(bi + 1) * 2048].rearrange("p (k q) -> p k q", k=8), wupP_s[l][:, j, :, :],
                        [("wupP", l)], [("wup", bi)])

                def load_dn(m, l=l):
                    DMA(wdn_sb[m % 2], wdnP_s[l][:, m, :, :], [("wdnP", l)], [("wdn", m % 2)])

                gen = None
                for half in range(2):
                    for j in range(3):
                        load_up(j)
                    for j in range(24):
                        if j + 3 < 24:
                            load_up(j + 3)
                        wu = wbuf[:, (j % 4) * 2048:(j % 4 + 1) * 2048].rearrange("p (k g q) -> p k g q", k=8, g=2)
                        for t2 in range(2):
                            tt = half * 2 + t2
                            xh = [xhi[:, k, tsl(tt)] for k in range(8)]
                            xrd = [XB(k, tt) for k in range(8)]
                            for gv in range(2):
                                jj = gv * 24 + j
                                pb = pbank()
                                mm_group(pb, [wu[:, k, gv, :] for k in range(8)], xh, xrd + [("wup", j % 4)])
                                ti = 2 * t2 + gv
                                ht = tmp[:, ti, :]
                                HB = ("t", ti)
                                fw = lambda tap, jj=jj: pcol(O_FCW + tap * 48 + jj)
                                ACT(ht, ps[pb][:], AF.Identity, [PSB[pb], "pv"], [HB], bias=pcol(O_FCB + jj), scale=fw(2))
                                STT("dve", ht[:, 1:TS], ps[pb][:, 0:TS - 1], fw(1), ht[:, 1:TS], ALU.mult, ALU.add, [PSB[pb], HB, "pv"], [HB])
                                STT("dve", ht[:, 2:TS], ps[pb][:, 0:TS - 2], fw(0), ht[:, 2:TS], ALU.mult, ALU.add, [PSB[pb], HB, "pv"], [HB])
                                if tt > 0:
                                    STT("dve", ht[:, 0:1], hal[:, jj, 1:2], fw(1), ht[:, 0:1], ALU.mult, ALU.add, [("hal", jj), HB, "pv"], [HB])
                                    STT("dve", ht[:, 0:2], hal[:, jj, 0:2], fw(0), ht[:, 0:2], ALU.mult, ALU.add, [("hal", jj), HB, "pv"], [HB])
                                if tt < NT - 1:
                                    CP("dve", hal[:, jj, :], ps[pb][:, TS - 2:TS], [PSB[pb]], [("hal", jj)])
                            tg, tv = 2 * t2, 2 * t2 + 1
                            ACT(tmp[:, tg, :], tmp[:, tg, :], AF.Gelu_apprx_tanh, [("t", tg)], [("t", tg)])
                            TT("pool", abuf[:, j, t2 * TS:(t2 + 1) * TS], tmp[:, tg, :], tmp[:, tv, :], ALU.mult,
                               [("t", tg), ("t", tv)], [("a", j, t2)])
                            gpump(gen, 2)
                    gpump(gen, 1000)
                    load_dn(0)
                    pending = None
                    rtemps[0] = [0, 1, 2, 3]
                    for m in range(8):
                        if m + 1 < 8:
                            load_dn(m + 1)
                        wd = wdn_sb[m % 2]
                        for t2 in range(2):
                            tt = half * 2 + t2
                            pb = pbank()
                            mm_group(pb, [wd[:, j, :] for j in range(24)], [abuf[:, j, t2 * TS:(t2 + 1) * TS] for j in range(24)],
                                     [("a", j, t2) for j in range(24)] + [("wdn", m % 2)])
                            if pending is not None:
                                pending()
                            pending = resid(pb, m, tt, 4 + 2 * t2, 5 + 2 * t2)
                    pending()
                    stat_evac(4, 5, 4, 5)
                    stat_evac(6, 7, 6, 7)
                    ptv = PT[:, 0:2, :].rearrange("p a b -> p (a b)").bitcast(F32)
                    zts2 = [(tmp[:, 8, :], [("t", 8)]), (ptv, [("pt", 0), ("pt", 1)])]
                    e1_ = "pool" if (half == 0 and LN_POOL) else "dve"
                    gen = chain2(ln_norm_gen(half * 2, 4, 5, O_L2G, O_L2B, zts2, e1_), ln_norm_gen(half * 2 + 1, 6, 7, O_L2G, O_L2B, zts2, e1_))
                gpump(gen, 1000)
                S.barrier_all()

            S.barrier_all()
            cnt = 0
            for tt in range(NT):
                for g in range(2):
                    for kk in range(4):
                        m = g * 4 + kk
                        TT("pool", tmp[:, kk, :], xhi[:, m, tsl(tt)], xlo[:, m, tsl(tt)], ALU.add, [XB(m, tt), XL(m, tt)], [("t", kk)])
                    for t4 in range(4):
                        tb = tt * 4 + t4
                        pb = pbank()
                        for kk in range(4):
                            TR(ps[pb][:, kk * 128:(kk + 1) * 128], tmp[:, kk, t4 * 128:(t4 + 1) * 128], [("t", kk), "cst"], [PSB[pb]])
                        oi = 4 + cnt % 4
                        cnt += 1
                        if cnt % 2 == 0:
                            ACT(tmp[:, oi, :], ps[pb][:], AF.Copy, [PSB[pb]], [("t", oi)])
                        else:
                            CP("dve", tmp[:, oi, :], ps[pb][:], [PSB[pb]], [("t", oi)])
                        DMA(y_d[s, tb * 128:(tb + 1) * 128, g * 512:(g + 1) * 512], tmp[:, oi, :], [("t", oi)], [("y", s, tb, g)])
            S.barrier_all()
        S.barrier_all()
        S.run()
    return nc


def _fm(v):
    v = np.asarray(v, np.float32)
    return np.ascontiguousarray(v.reshape(-1, 128).T)


def _host_params(inp, nl):
    pvs = np.zeros((128, nl * PL), np.float32)
    for l in range(nl):
        b = l * PL
        for tap in range(4):
            pvs[:, b + O_CW + tap * 8: b + O_CW + tap * 8 + 8] = _fm(inp["conv_w"][l, tap])
        pvs[:, b + O_CB: b + O_CB + 8] = _fm(inp["conv_b"][l])
        pvs[:, b + O_GXB: b + O_GXB + 8] = _fm(inp["gx_b"][l])
        pvs[:, b + O_GAB: b + O_GAB + 8] = _fm(inp["ga_b"][l])
        pvs[:, b + O_LAM: b + O_LAM + 8] = _fm(inp["lru_lambda"][l])
        pvs[:, b + O_L1G: b + O_L1G + 8] = _fm(inp["ln1_g"][l])
        pvs[:, b + O_L1B: b + O_L1B + 8] = _fm(inp["ln1_b"][l])
        pvs[:, b + O_L2G: b + O_L2G + 8] = _fm(inp["ln2_g"][l])
        pvs[:, b + O_L2B: b + O_L2B + 8] = _fm(inp["ln2_b"][l])
        for tap in range(3):
            pvs[:, b + O_FCW + tap * 48: b + O_FCW + tap * 48 + 48] = _fm(inp["ffn_conv_w"][l, tap])
        pvs[:, b + O_FCB: b + O_FCB + 48] = _fm(inp["ffn_conv_b"][l])
        pvs[:, b + O_QG: b + O_QG + 3] = _fm(inp["q_norm_g"][l])
        pvs[:, b + O_KVG: b + O_KVG + 2] = _fm(inp["kv_norm_g"][l])
    return pvs


def _host_bd(w, nl):
    w = np.asarray(w, np.float32)
    out = np.zeros((nl, 128, 8, 128), np.float32)
    for c in range(8):
        for b in range(2):
            out[:, b * 64:(b + 1) * 64, c, b * 64:(b + 1) * 64] = w[:, 2 * c + b]
    return out.reshape(nl, 128, 1024)


def _consts():
    c = np.zeros((128, 258), np.float32)
    c[:, 0:128] = np.eye(128, dtype=np.float32)
    k = np.arange(128)[:, None]
    q = np.arange(128)[None, :]
    c[:, 128:256] = (q >= k).astype(np.float32)
    inv = (10000.0 ** (-np.arange(0, 32, 2, dtype=np.float32) / np.float32(32))).astype(np.float32)
    for p in range(64, 96):
        c[p, 256] = inv[(p - 64) % 16]
    c[:, 257] = EPS
    return c


_NC_CACHE = {}
_TRACE = False
_LAST = [None]


def run(inp, NS, layers, ncores, seq_of_core):
    nl = L_ALL
    key = (NS, tuple(layers))
    if key not in _NC_CACHE:
        _NC_CACHE[key] = build(NS, list(layers))
    nc = _NC_CACHE[key]
    shared = {
        "w_in": np.ascontiguousarray(inp["w_in"], np.float32),
        "w_uq": np.ascontiguousarray(inp["w_uq"], np.float32),
        "w_ukv": np.ascontiguousarray(inp["w_ukv"], np.float32),
        "w_out": np.ascontiguousarray(inp["w_out"], np.float32),
        "w_up": np.ascontiguousarray(inp["w_up"], np.float32),
        "w_down": np.ascontiguousarray(inp["w_down"], np.float32),
        "gxw": _host_bd(inp["gx_w"], nl),
        "gaw": _host_bd(inp["ga_w"], nl),
        "pv": _host_params(inp, nl),
        "cst": _consts(),
    }
    x = np.asarray(inp["x"], np.float32)
    pos = np.asarray(inp["positions"], np.int32)
    in_maps = []
    for ci in range(ncores):
        seqs = seq_of_core[ci]
        m = dict(shared)
        m["x"] = np.ascontiguousarray(x[seqs])
        m["pos"] = np.ascontiguousarray(np.broadcast_to(pos[seqs][:, None, :], (len(seqs), 32, T)))
        in_maps.append(m)
    res = run_bass_kernel_spmd(nc, in_maps, core_ids=list(range(ncores)), trace=_TRACE)
    _LAST[0] = res
    return [r["y"] for r in res.results]


def kernel(**inputs):
    B = inputs["x"].shape[0]
    ncores = 8
    NS = B // ncores
    seq_of_core = [list(range(ci * NS, (ci + 1) * NS)) for ci in range(ncores)]
    outs = run(inputs, NS, list(range(L_ALL)), ncores, seq_of_core)
    return np.concatenate(outs, axis=0).astype(np.float32)
```
